# Optimizing a Trainium2 kernel written in Bass

```python
import math
import jax, jax.numpy as jnp
from jax import lax
import numpy as np

D_MODEL = 1024
BATCH = 32
SEQ = 256
DEPTH = 4
DEC_BATCH = 2
DEC_SEQ = 4096
PAST_LEN = 512

GRID_W = 64
N_MIXERS = 2
N_MLA = (DEPTH + 1) // 2
N_FOURIER = DEPTH // 2
N_HEADS = 8
QK_NOPE = 128
QK_ROPE = 64
V_DIM = 128
Q_LORA = 512
KV_LORA = 256
AXIS_FREQS = QK_ROPE // 4
ROPE_THETA = 10000.0
N_FGROUPS = 4
FGROUP_DIM = D_MODEL // N_FGROUPS
D_FF = 2816
N_MOD = 9
Q_BLOCK = 128
EPS = 1e-6
ATTN_SCALE = 1.0 / math.sqrt(QK_NOPE + QK_ROPE)

kernel_name = "mla_fnet_macaron_dit_step"


def _rms(x, g):
    xf = x.astype(jnp.float32)
    y = xf * lax.rsqrt(jnp.mean(xf * xf, axis=-1, keepdims=True) + EPS)
    return (y * g.astype(jnp.float32)).astype(x.dtype)


def _modulate(h, shift, scale):
    return h * (1 + scale) + shift


def _modulation(cond, w, b):
    m = jax.nn.silu(cond) @ w + b
    return jnp.split(m, N_MOD, axis=-1)


def _swiglu(h, wg, wu, wd):
    return (jax.nn.silu(h @ wg) * (h @ wu)) @ wd


def _axial_rope_tables(n):
    rows = n // GRID_W
    row = jnp.repeat(jnp.arange(rows, dtype=jnp.float32), GRID_W)
    col = jnp.tile(jnp.arange(GRID_W, dtype=jnp.float32), rows)
    inv = 1.0 / (ROPE_THETA ** (jnp.arange(AXIS_FREQS, dtype=jnp.float32) / AXIS_FREQS))
    ang = jnp.stack([row[:, None] * inv, col[:, None] * inv], axis=1)
    return jnp.cos(ang), jnp.sin(ang)


def _apply_axial_rope(x, cos, sin):
    xs = x.astype(jnp.float32).reshape(x.shape[:-1] + (2, 2, AXIS_FREQS))
    x1 = xs[..., 0, :]
    x2 = xs[..., 1, :]
    c = cos[:, None]
    s = sin[:, None]
    out = jnp.stack([x1 * c - x2 * s, x2 * c + x1 * s], axis=-2)
    return out.reshape(x.shape).astype(x.dtype)


def _mla_qkv(h, w_dq, q_norm, w_uq, w_dkv, kv_norm, rope):
    B, S, _ = h.shape
    cq = _rms(h @ w_dq, q_norm)
    q = (cq @ w_uq).reshape(B, S, N_HEADS, QK_NOPE + QK_ROPE)
    kv = h @ w_dkv
    ckv = _rms(kv[..., :KV_LORA], kv_norm)
    kpe = kv[..., KV_LORA:]
    if rope is not None:
        cos, sin = rope
        q = jnp.concatenate([q[..., :QK_NOPE], _apply_axial_rope(q[..., QK_NOPE:], cos, sin)], axis=-1)
        kpe = _apply_axial_rope(kpe[:, :, None, :], cos, sin)[:, :, 0, :]
    return q, ckv, kpe


def _mla_up(ckv, kpe, w_ukv):
    B, T, _ = ckv.shape
    kvu = (ckv @ w_ukv).reshape(B, T, N_HEADS, QK_NOPE + V_DIM)
    k = jnp.concatenate(
        [kvu[..., :QK_NOPE], jnp.broadcast_to(kpe[:, :, None, :], (B, T, N_HEADS, QK_ROPE))], axis=-1)
    return k, kvu[..., QK_NOPE:]


def _attend(q, k, v):
    B, S, H, Dk = q.shape
    qb = Q_BLOCK if S % Q_BLOCK == 0 else S
    nb = S // qb
    kf = k.astype(jnp.float32)
    vf = v.astype(jnp.float32)
    qs = q.reshape(B, nb, qb, H, Dk).transpose(1, 0, 2, 3, 4)

    def block(qi):
        s = jnp.einsum('bqhd,bkhd->bhqk', qi.astype(jnp.float32), kf) * ATTN_SCALE
        p = jax.nn.softmax(s, axis=-1)
        return jnp.einsum('bhqk,bkhd->bqhd', p, vf).astype(q.dtype)

    o = lax.map(block, qs)
    return o.transpose(1, 0, 2, 3, 4).reshape(B, S, H * v.shape[-1])


def _fourier(h, w, b):
    B, S, D = h.shape
    hg = h.astype(jnp.float32).reshape(B, S, N_FGROUPS, FGROUP_DIM)
    f = jnp.fft.fft2(hg, axes=(1, 3), norm="ortho").real
    return f.reshape(B, S, D).astype(h.dtype) @ w + b


def setup_inputs(seed: int = 0) -> dict:
    key = jax.random.key(seed)
    ks = jax.random.split(key, 24)
    f32 = jnp.float32
    D = D_MODEL
    nrm = lambda k, shape, s: (jax.random.normal(k, shape, f32) * s)
    return {
        "x_prompt": nrm(ks[0], (BATCH, SEQ, D), 1.0),
        "x_sample": nrm(ks[1], (DEC_BATCH, DEC_SEQ, D), 1.0),
        "cache_ckv": nrm(ks[2], (DEC_BATCH, N_MLA, PAST_LEN, KV_LORA), 1.0),
        "cache_kpe": nrm(ks[3], (DEC_BATCH, N_MLA, PAST_LEN, QK_ROPE), 1.0),
        "c": nrm(ks[4], (DEC_BATCH, D), 1.0),
        "c_ctx": nrm(ks[5], (D,), 1.0),
        "w_mod": nrm(ks[6], (DEPTH, D, N_MOD * D), 0.5 * D ** -0.5),
        "b_mod": nrm(ks[7], (DEPTH, N_MOD * D), 0.01),
        "norm_g": 1.0 + nrm(ks[8], (DEPTH, 3, D), 0.05),
        "ffn_wg": nrm(ks[9], (DEPTH, 2, D, D_FF), D ** -0.5),
        "ffn_wu": nrm(ks[10], (DEPTH, 2, D, D_FF), D ** -0.5),
        "ffn_wd": nrm(ks[11], (DEPTH, 2, D_FF, D), D_FF ** -0.5),
        "mla_w_dq": nrm(ks[12], (N_MLA, D, Q_LORA), D ** -0.5),
        "mla_q_norm": 1.0 + nrm(ks[13], (N_MLA, Q_LORA), 0.05),
        "mla_w_uq": nrm(ks[14], (N_MLA, Q_LORA, N_HEADS * (QK_NOPE + QK_ROPE)), Q_LORA ** -0.5),
        "mla_w_dkv": nrm(ks[15], (N_MLA, D, KV_LORA + QK_ROPE), D ** -0.5),
        "mla_kv_norm": 1.0 + nrm(ks[16], (N_MLA, KV_LORA), 0.05),
        "mla_w_ukv": nrm(ks[17], (N_MLA, KV_LORA, N_HEADS * (QK_NOPE + V_DIM)), KV_LORA ** -0.5),
        "mla_w_o": nrm(ks[18], (N_MLA, N_HEADS * V_DIM, D), (N_HEADS * V_DIM) ** -0.5),
        "fourier_w": nrm(ks[19], (N_FOURIER, D, D), D ** -0.5),
        "fourier_b": nrm(ks[20], (N_FOURIER, D), 0.01),
        "final_norm": 1.0 + nrm(ks[21], (D,), 0.05),
    }


def reference(x_prompt, x_sample, cache_ckv, cache_kpe, c, c_ctx, w_mod, b_mod, norm_g,
              ffn_wg, ffn_wu, ffn_wd, mla_w_dq, mla_q_norm, mla_w_uq, mla_w_dkv, mla_kv_norm,
              mla_w_ukv, mla_w_o, fourier_w, fourier_b, final_norm):
    yp = x_prompt
    ys = x_sample
    rope_s = _axial_rope_tables(x_sample.shape[1])
    ckv_states = []
    kpe_states = []
    for l in range(DEPTH):
        mp = _modulation(c_ctx, w_mod[l], b_mod[l])
        ms = [m[:, None, :] for m in _modulation(c, w_mod[l], b_mod[l])]

        hp = _modulate(_rms(yp, norm_g[l, 0]), mp[0], mp[1])
        hs = _modulate(_rms(ys, norm_g[l, 0]), ms[0], ms[1])
        yp = yp + 0.5 * mp[2] * _swiglu(hp, ffn_wg[l, 0], ffn_wu[l, 0], ffn_wd[l, 0])
        ys = ys + 0.5 * ms[2] * _swiglu(hs, ffn_wg[l, 0], ffn_wu[l, 0], ffn_wd[l, 0])

        hp = _modulate(_rms(yp, norm_g[l, 1]), mp[3], mp[4])
        hs = _modulate(_rms(ys, norm_g[l, 1]), ms[3], ms[4])
        j = l // N_MIXERS
        if l % N_MIXERS == 0:
            qp, ckv_p, kpe_p = _mla_qkv(hp, mla_w_dq[j], mla_q_norm[j], mla_w_uq[j],
                                        mla_w_dkv[j], mla_kv_norm[j], None)
            kp, vp = _mla_up(ckv_p, kpe_p, mla_w_ukv[j])
            op = _attend(qp, kp, vp) @ mla_w_o[j]
            ckv_states.append(ckv_p)
            kpe_states.append(kpe_p)

            qs, ckv_s, kpe_s = _mla_qkv(hs, mla_w_dq[j], mla_q_norm[j], mla_w_uq[j],
                                        mla_w_dkv[j], mla_kv_norm[j], rope_s)
            k_ctx, v_ctx = _mla_up(cache_ckv[:, j], cache_kpe[:, j], mla_w_ukv[j])
            k_lat, v_lat = _mla_up(ckv_s, kpe_s, mla_w_ukv[j])
            k_all = jnp.concatenate([k_ctx, k_lat], axis=1)
            v_all = jnp.concatenate([v_ctx, v_lat], axis=1)
            os_ = _attend(qs, k_all, v_all) @ mla_w_o[j]
        else:
            op = _fourier(hp, fourier_w[j], fourier_b[j])
            os_ = _fourier(hs, fourier_w[j], fourier_b[j])
        yp = yp + mp[5] * op
        ys = ys + ms[5] * os_

        hp = _modulate(_rms(yp, norm_g[l, 2]), mp[6], mp[7])
        hs = _modulate(_rms(ys, norm_g[l, 2]), ms[6], ms[7])
        yp = yp + 0.5 * mp[8] * _swiglu(hp, ffn_wg[l, 1], ffn_wu[l, 1], ffn_wd[l, 1])
        ys = ys + 0.5 * ms[8] * _swiglu(hs, ffn_wg[l, 1], ffn_wu[l, 1], ffn_wd[l, 1])

    y_prompt = _rms(yp, final_norm)
    y_sample = _rms(ys, final_norm)
    new_ckv = jnp.stack(ckv_states, axis=1)
    new_kpe = jnp.stack(kpe_states, axis=1)
    return (y_prompt, y_sample, new_ckv, new_kpe)
```

```python
import math
from contextlib import ExitStack

import numpy as np
import ml_dtypes
import concourse.bass as bass
import concourse.mybir as mybir
from concourse.bass_utils import run_bass_kernel_spmd

F32 = mybir.dt.float32
BF16 = mybir.dt.bfloat16
AF = mybir.ActivationFunctionType
ALU = mybir.AluOpType

D = 1024
DFF = 2816
NCH = 8
TB = 512
NT = 2048
DEPTH = 4
EPS = 1e-6
ATTN_SCALE = 1.0 / math.sqrt(192.0)
GROUPS = [[0, 1, 2, 3], [4, 5, 6, 7]]
ENGINES = ("sp", "act", "dve", "pool", "pe")
SEM_ROT = 8000
SCR_BYTES = 102 * 1024
DBG_PHASES = "mf"


class Op:
    __slots__ = ("id", "eng", "fn", "deps", "is_dma", "slot", "sig", "inc", "needs_signal")

    def __init__(self, id, eng, fn, is_dma, slot, inc):
        self.id = id
        self.eng = eng
        self.fn = fn
        self.deps = set()
        self.is_dma = is_dma
        self.slot = slot
        self.sig = None
        self.inc = inc
        self.needs_signal = False


class Prog:
    def __init__(self):
        self.ops = []
        self.last_w = {}
        self.readers = {}
        self.eng_ops = {e: [] for e in ENGINES}
        self.fence_set = set()
        self.dma_since_fence = []
        self.fenced = {e: True for e in ENGINES}

    def fence(self):
        fs = set(self.dma_since_fence)
        for e in ENGINES:
            for o in reversed(self.eng_ops[e]):
                if not o.is_dma:
                    fs.add(o.id)
                    break
        self.fence_set = fs
        self.dma_since_fence = []
        self.fenced = {e: False for e in ENGINES}

    def op(self, eng, fn, reads=(), writes=(), dma=False, slot=None, inc=16):
        o = Op(len(self.ops), eng, fn, dma, slot, inc)
        deps = set()
        for k in reads:
            w = self.last_w.get(k)
            if w is not None:
                deps.add(w)
        for k in writes:
            w = self.last_w.get(k)
            if w is not None:
                deps.add(w)
            for r in self.readers.get(k, {}).values():
                if isinstance(r, list):
                    deps.update(r)
                else:
                    deps.add(r)
        if not self.fenced[eng]:
            deps |= self.fence_set
            self.fenced[eng] = True
        deps.discard(o.id)
        o.deps = deps
        for k in reads:
            rd = self.readers.setdefault(k, {})
            if dma:
                rd.setdefault("dma", []).append(o.id)
            else:
                rd[eng] = o.id
        for k in writes:
            self.last_w[k] = o.id
            self.readers[k] = {}
        if dma:
            if slot is None:
                o.slot = ("dma", writes[0])
            self.dma_since_fence.append(o.id)
        self.eng_ops[eng].append(o)
        self.ops.append(o)
        return o

    def emit(self, nc, final_waits=()):
        ops = self.ops
        for o in ops:
            for d in o.deps:
                p = ops[d]
                if p.is_dma:
                    p.needs_signal = True
                elif p.eng == "pe" and o.eng == "pe" and not o.is_dma:
                    continue
                else:
                    p.needs_signal = True
        for d in final_waits:
            d.needs_signal = True
        sem_keys = []
        comp_cnt = {e: 0 for e in ENGINES}
        slot_cnt = {}
        grp_total = {}
        for o in ops:
            if o.is_dma and isinstance(o.slot, tuple) and o.slot[0] == "grp":
                grp_total[o.slot] = grp_total.get(o.slot, 0) + o.inc
        for o in ops:
            if o.is_dma:
                c = slot_cnt.get(o.slot, 0) + o.inc
                slot_cnt[o.slot] = c
                o.sig = (o.slot, grp_total.get(o.slot, c))
                if o.slot not in slot_cnt or o.slot not in sem_keys:
                    sem_keys.append(o.slot)
            elif o.needs_signal:
                n = comp_cnt[o.eng]
                comp_cnt[o.eng] = n + 1
                key = ("c", o.eng, n // SEM_ROT)
                o.sig = (key, n % SEM_ROT + 1)
                if key not in sem_keys:
                    sem_keys.append(key)
        sem_keys = list(dict.fromkeys(sem_keys))
        self.n_sems = len(sem_keys)
        stack = ExitStack()
        sems = {}
        for i, k in enumerate(sem_keys):
            sems[k] = stack.enter_context(nc.semaphore("s%d" % i))

        def run(engname, e):
            waited = {}
            for o in self.eng_ops[engname]:
                need = {}
                for d in o.deps:
                    p = ops[d]
                    if p.sig is None:
                        continue
                    if (not p.is_dma) and p.eng == "pe" and engname == "pe" and not o.is_dma:
                        continue
                    k, c = p.sig
                    if need.get(k, 0) < c:
                        need[k] = c
                for k, c in need.items():
                    if waited.get(k, 0) >= c:
                        continue
                    e.wait_ge(sems[k], c)
                    waited[k] = c
                ins = o.fn(e)
                if o.sig is not None:
                    ins.then_inc(sems[o.sig[0]], o.inc if o.is_dma else 1)
            if engname == "sp":
                for d in final_waits:
                    k, c = d.sig
                    e.wait_ge(sems[k], c)

        with stack:
            with nc.Block() as block:
                @block.sync
                def _(e):
                    run("sp", e)

                @block.scalar
                def _(e):
                    run("act", e)

                @block.vector
                def _(e):
                    run("dve", e)

                @block.gpsimd
                def _(e):
                    run("pool", e)

                @block.tensor
                def _(e):
                    run("pe", e)


class Carve:
    def __init__(self, scr):
        self.scr = scr
        self.off = 0

    def reset(self, off=0):
        self.off = off

    def take(self, nelem, dtype, parts=128):
        nb = nelem * (4 if dtype == F32 else 2)
        nb = (nb + 63) // 64 * 64
        a = self.off // 2
        self.off += nb
        assert self.off <= SCR_BYTES, ("scratch overflow", self.off)
        v = self.scr[:, a:a + nb // 2]
        if dtype == F32:
            v = v.bitcast(F32)
        v = v[:, 0:nelem]
        return v


def build_program(depth=DEPTH):
    nc = bass.Bass("TRN2", target_bir_lowering=False)

    def din(name, shape, dt=F32):
        return nc.dram_tensor(name, list(shape), dt, kind="ExternalInput").ap()

    def dout(name, shape, dt=F32):
        return nc.dram_tensor(name, list(shape), dt, kind="ExternalOutput").ap()

    xT_d = din("xT", [128, NCH, NT])
    cckv_d = din("cckv", [2, 256, 512])
    ckpe_d = din("ckpe", [2, 64, 512])
    condT_d = din("condT", [128, NCH, 2])
    wmod_d = din("wmod", [4, D, 2304])
    bmodT_d = din("bmodT", [128, 4, 72])
    normgT_d = din("normgT", [128, 4, 3, 8])
    finalT_d = din("finalT", [128, 8])
    qnT_d = din("qnT", [128, 2, 4])
    kvnT_d = din("kvnT", [128, 2, 2])
    fbT_d = din("fbT", [128, 2, 8])
    wg_d = din("wg", [4, 2, D, DFF])
    wu_d = din("wu", [4, 2, D, DFF])
    wd_d = din("wd", [4, 2, DFF, D])
    wdq_d = din("wdq", [2, D, 512])
    wuq_d = din("wuq", [2, 512, 1536])
    wuqs_d = din("wuqs", [2, 512, 512])
    wdkv_d = din("wdkv", [2, D, 320])
    wdkvs_d = din("wdkvs", [2, D, 64])
    wukv_d = din("wukv", [2, 256, 2048])
    wo_d = din("wo", [2, D, D])
    fw_d = din("fw", [2, D, D])
    rope_d = din("ropeT", [2, 64, 1024])
    dftC_d = din("dftC", [256, 512])
    dftP_d = din("dftP", [2, 256, 256])
    dftS_d = din("dftS", [2, 4096, 1024])

    yT_d = dout("yT", [128, NCH, NT])
    ockv_d = dout("o_ckv", [2, 256, 1024])
    okpe_d = dout("o_kpe", [2, 64, 1024])

    mod_in = nc.dram_tensor("mod_in", [128, 144], F32).ap()
    mod_out = nc.dram_tensor("mod_out", [512, 144], F32).ap()
    x_in = [nc.dram_tensor("x_in%d" % j, [320, 1024], BF16).ap() for j in range(2)]
    x_out = [nc.dram_tensor("x_out%d" % j, [1280, 1024], BF16).ap() for j in range(2)]
    f_in = [[nc.dram_tensor("f_in%d_%d" % (j, q), [256, 2048], BF16).ap() for q in range(4)] for j in range(2)]
    f_out = [[nc.dram_tensor("f_out%d_%d" % (j, q), [1024, 2048], BF16).ap() for q in range(4)] for j in range(2)]

    dftS_bf = nc.dram_tensor("dftS_bf", [2, 4096, 1024], BF16).ap()
    DFT_KEYS = [("dftS_bf", a, q) for a in range(2) for q in range(4)]

    P = Prog()
    st = ExitStack()
    sb = lambda name, shape, dt: st.enter_context(nc.sbuf_tensor(name, list(shape), dt))
    xT = sb("xTs", [128, NCH, NT], F32)
    hT = sb("hTs", [128, NCH, NT], BF16)
    modS = sb("modS", [128, 4 * 72 * 2], F32)
    bmodT = sb("bmodTs", [128, 4, 72], F32)
    normgT = sb("normgTs", [128, 4, 3, 8], F32)
    finalT = sb("finalTs", [128, 8], F32)
    qnT = sb("qnTs", [128, 2, 4], F32)
    kvnT = sb("kvnTs", [128, 2, 2], F32)
    fbT = sb("fbTs", [128, 2, 8], F32)
    condT = sb("condTs", [128, NCH, 2], F32)
    scT = sb("scTs", [128, NCH, 2], F32)
    scB = sb("scBs", [128, NCH, 2], BF16)
    onesB = sb("onesB", [128, 128], BF16)
    onesF = sb("onesF", [128, 128], F32)
    epsT = sb("epsT", [128, 1], F32)
    scr = sb("scr", [128, SCR_BYTES // 2], BF16)
    ps = [st.enter_context(nc.psum_tensor("ps%d" % i, [128, 512], F32)) for i in range(8)]
    PK = [("ps", i) for i in range(8)]
    cv = Carve(scr)

    mod5 = modS[:].rearrange("p (l k c n) -> p l k c n", l=4, k=9, c=8)
    mod_lrx = modS[:].rearrange("p (l r x) -> p l r x", l=4, r=4)
    mod_lcn = modS[:].rearrange("p (l c n) -> p l c n", l=4, c=72)

    outs = []

    def mcol(l, k, c, cond):
        return mod5[:, l, k, c, cond:cond + 1]

    def cond_of(t):
        return 0 if t < 2 else 1

    def tsl(t):
        return slice(t * TB, (t + 1) * TB)

    cv.reset(0)
    sq = cv.take(2 * TB, BF16).rearrange("p (a b) -> p a b", a=2)
    rs = cv.take(2 * TB, F32).rearrange("p (a b) -> p a b", a=2)
    tmpf = cv.take(2 * TB, F32).rearrange("p (a b) -> p a b", a=2)
    COMMON = cv.off
    cnt = {"sq": 0, "tmpf": 0}

    def rot(name, n=2):
        v = cnt.get(name, 0)
        cnt[name] = v + 1
        return v % n

    P.op("sp", lambda e: e.dma_start(out=condT[:], in_=condT_d), writes=["condT"], dma=True, slot=("grp", "setup"))
    for (tl, td, nm) in ((bmodT, bmodT_d, "bmodT"), (normgT, normgT_d, "normgT"), (finalT, finalT_d, "finalT"),
                         (qnT, qnT_d, "qnT"), (kvnT, kvnT_d, "kvnT"), (fbT, fbT_d, "fbT")):
        P.op("sp", lambda e, tl=tl, td=td: e.dma_start(out=tl[:], in_=td), writes=[nm], dma=True, slot=("grp", "setup"))
    P.op("dve", lambda e: e.memset(onesB[:], 1.0), writes=["ones"])
    P.op("dve", lambda e: e.memset(onesF[:], 1.0), writes=["onesF"])
    P.op("dve", lambda e: e.memset(epsT[:], EPS), writes=["eps"])
    P.op("act", lambda e: e.activation(out=scB[:], in_=condT[:], func=AF.Silu), reads=["condT"], writes=["scT"])

    for c in range(NCH):
        P.op("sp", lambda e, c=c: e.dma_start(out=xT[:, c, :], in_=xT_d[:, c, :]),
             writes=[("xT", c, t) for t in range(4)], dma=True, slot=("grp", "xT"))

    cv.reset(COMMON)
    wm = [cv.take(NCH * 1152, BF16).rearrange("p (c f) -> p c f", c=NCH) for _ in range(4)]
    modin = cv.take(144, F32)
    for l in range(4):
        for hf in range(2):
            b = (l * 2 + hf) % 4
            src = wmod_d[l, :, hf * 1152:(hf + 1) * 1152].rearrange("(c p) f -> p c f", p=128)
            P.op("pool", lambda e, b=b, src=src: e.dma_start(out=wm[b], in_=src), writes=[("wm", b)], dma=True)
            for i in range(9):
                col = (hf * 9 + i) * 2
                for c in range(NCH):
                    P.op("pe", lambda e, b=b, i=i, c=c, col=col: e.matmul(
                        ps[0][:, col:col + 2], lhsT=wm[b][:, c, i * 128:(i + 1) * 128], rhs=scB[:, c, :],
                        start=(c == 0), stop=(c == NCH - 1)),
                        reads=[("wm", b), "scT"], writes=[PK[0]])
        P.op("dve", lambda e, l=l: e.tensor_copy(out=modin[:, l * 36:(l + 1) * 36], in_=ps[0][:, 0:36]),
             reads=[PK[0]], writes=["modin"])
    P.op("sp", lambda e: e.dma_start(out=mod_in, in_=modin), reads=["modin"], writes=["mod_in"], dma=True)
    P.op("pool", lambda e: e.collective_compute("AllGather", ALU.bypass, replica_groups=GROUPS,
                                                ins=[mod_in], outs=[mod_out]),
         reads=["mod_in"], writes=["mod_out"], dma=True, inc=1)
    for l in range(4):
        src = mod_out[:, l * 36:(l + 1) * 36].rearrange("(r p) x -> p r x", p=128)
        P.op("sp", lambda e, l=l, src=src: e.dma_start(out=mod_lrx[:, l], in_=src),
             reads=["mod_out"], writes=["modS"], dma=True, slot=("dma", "modS", l))
    for n in range(2):
        P.op("dve", lambda e, n=n: e.tensor_tensor(out=mod_lcn[:, :, :, n], in0=mod_lcn[:, :, :, n],
                                                   in1=bmodT[:], op=ALU.add),
             reads=["modS", "bmodT"], writes=["modS"])
    for i in range(3):
        for n in range(2):
            P.op("dve", lambda e, i=i, n=n: e.scalar_tensor_tensor(
                out=mod5[:, :, 3 * i + 1, :, n], in0=mod5[:, :, 3 * i + 1, :, n], scalar=1.0,
                in1=normgT[:, :, i, :], op0=ALU.add, op1=ALU.mult),
                reads=["modS", "normgT"], writes=["modS"])
            if i != 1:
                P.op("dve", lambda e, i=i, n=n: e.tensor_scalar(
                    out=mod5[:, :, 3 * i + 2, :, n], in0=mod5[:, :, 3 * i + 2, :, n], scalar1=0.5,
                    scalar2=None, op0=ALU.mult),
                    reads=["modS"], writes=["modS"])

    def rstd_block(src_keys, nchunks, get_src, inv_n, bank, extra_reads=()):
        for c in range(nchunks):
            s = rot("sq")
            src, rk = get_src(c)
            P.op("act", lambda e, s=s, src=src: e.activation(out=sq[:, s, :], in_=src, func=AF.Square),
                 reads=[rk], writes=[("sq", s)])
            P.op("pe", lambda e, s=s, c=c: e.matmul(ps[bank][:], lhsT=onesB[:], rhs=sq[:, s, :],
                                                    start=(c == 0), stop=(c == nchunks - 1)),
                 reads=[("sq", s), "ones"], writes=[PK[bank]])
        P.op("act", lambda e: e.activation(out=rs[:, 0, :], in_=ps[bank][:], func=AF.Ln,
                                           bias=epsT[:, 0:1], scale=inv_n),
             reads=[PK[bank], "eps"], writes=[("rs", 0)])
        P.op("act", lambda e: e.activation(out=rs[:, 1, :], in_=rs[:, 0, :], func=AF.Exp, scale=-0.5),
             reads=[("rs", 0)], writes=[("rs", 1)])

    def norm_phase(l, i, final=False):
        for t in range(4):
            n = cond_of(t)
            rstd_block(None, NCH, lambda c, t=t: (xT[:, c, tsl(t)], ("xT", c, t)), 1.0 / D, 7)
            for c in range(NCH):
                s = rot("tmpf")
                g = finalT[:, c:c + 1] if final else mcol(l, 3 * i + 1, c, n)
                P.op("dve", lambda e, s=s, c=c, t=t, g=g: e.scalar_tensor_tensor(
                    out=tmpf[:, s, :], in0=xT[:, c, tsl(t)], scalar=g, in1=rs[:, 1, :],
                    op0=ALU.mult, op1=ALU.mult),
                    reads=[("xT", c, t), ("rs", 1), "modS", "finalT"], writes=[("tmpf", s)])
                if final:
                    outs.append(P.op("sp", lambda e, s=s, c=c, t=t: e.dma_start(out=yT_d[:, c, tsl(t)], in_=tmpf[:, s, :]),
                                     reads=[("tmpf", s)], writes=[("yT", c, t)], dma=True, slot=("out", "tmpf", s)))
                else:
                    P.op("act", lambda e, s=s, c=c, t=t, l=l, i=i, n=n: e.activation(
                        out=hT[:, c, tsl(t)], in_=tmpf[:, s, :], func=AF.Identity,
                        bias=mcol(l, 3 * i, c, n), scale=1.0),
                        reads=[("tmpf", s), "modS"], writes=[("hT", c, t)])

    pending = []

    def norm_items(l, i, t, final=False):
        items = []
        n = cond_of(t)
        bank = 7

        def sq_item(c):
            def f():
                s = rot("sq")
                P.op("act", lambda e: e.activation(out=sq[:, s, :], in_=xT[:, c, tsl(t)], func=AF.Square),
                     reads=[("xT", c, t)], writes=[("sq", s)])
                P.op("pe", lambda e: e.matmul(ps[bank][:], lhsT=onesB[:], rhs=sq[:, s, :],
                                              start=(c == 0), stop=(c == NCH - 1)),
                     reads=[("sq", s), "ones"], writes=[PK[bank]])
            return f

        def rs_item():
            P.op("act", lambda e: e.activation(out=rs[:, 0, :], in_=ps[bank][:], func=AF.Ln,
                                               bias=epsT[:, 0:1], scale=1.0 / D),
                 reads=[PK[bank], "eps"], writes=[("rs", 0)])
            P.op("act", lambda e: e.activation(out=rs[:, 1, :], in_=rs[:, 0, :], func=AF.Exp, scale=-0.5),
                 reads=[("rs", 0)], writes=[("rs", 1)])

        def out_item(c):
            def f():
                s = rot("tmpf")
                g = finalT[:, c:c + 1] if final else mcol(l, 3 * i + 1, c, n)
                P.op("dve", lambda e: e.scalar_tensor_tensor(
                    out=tmpf[:, s, :], in0=xT[:, c, tsl(t)], scalar=g, in1=rs[:, 1, :],
                    op0=ALU.mult, op1=ALU.mult),
                    reads=[("xT", c, t), ("rs", 1), "modS", "finalT"], writes=[("tmpf", s)])
                if final:
                    outs.append(P.op("sp", lambda e: e.dma_start(out=yT_d[:, c, tsl(t)], in_=tmpf[:, s, :]),
                                     reads=[("tmpf", s)], writes=[("yT", c, t)], dma=True, slot=("out", "tmpf", s)))
                else:
                    P.op("act", lambda e: e.activation(
                        out=hT[:, c, tsl(t)], in_=tmpf[:, s, :], func=AF.Identity,
                        bias=mcol(l, 3 * i, c, n), scale=1.0),
                        reads=[("tmpf", s), "modS"], writes=[("hT", c, t)])
            return f

        for c in range(NCH):
            items.append((t, sq_item(c)))
        items.append((t, rs_item))
        for c in range(NCH):
            items.append((t, out_item(c)))
        return items

    def pop_pending(n):
        for _ in range(n):
            if pending:
                pending.pop(0)[1]()

    def flush_pending(upto_t=None):
        while pending and (upto_t is None or any(tg <= upto_t for tg, _ in pending)):
            pending.pop(0)[1]()

    def resid_update(bank, m, t, gate_ap, tok=None):
        sl = tsl(t) if tok is None else tok
        P.op("dve", lambda e: e.scalar_tensor_tensor(
            out=xT[:, m, sl], in0=ps[bank][:, 0:(sl.stop - sl.start)], scalar=gate_ap, in1=xT[:, m, sl],
            op0=ALU.mult, op1=ALU.add),
            reads=[PK[bank], ("xT", m, t), "modS"], writes=[("xT", m, t)])

    def ffn_phase(l, i, pre_normed=False, next_norm=None, skip_fence=False, pre_work=None, carry=False,
                  tail_order=(0, 1, 2, 3)):
        if not skip_fence:
            P.fence()
        cv.reset(COMMON)
        NB = 3
        wgb = [cv.take(NCH * 512, BF16).rearrange("p (c f) -> p c f", c=NCH) for _ in range(NB)]
        wub = [cv.take(NCH * 512, BF16).rearrange("p (c f) -> p c f", c=NCH) for _ in range(NB)]
        wdb = [cv.take(4 * D, BF16).rearrange("p (j d) -> p j d", j=4) for _ in range(NB)]
        actb = [cv.take(4 * TB, BF16).rearrange("p (j t) -> p j t", j=4) for _ in range(2)]
        slb = [cv.take(TB, BF16) for _ in range(2)]
        sizes = [4, 4, 4, 4, 3, 3]
        f0s = [0, 4, 8, 12, 16, 19]
        kidx = 2 * i
        sub = 0 if i == 0 else 2

        def load_chunk(k):
            s = k % NB
            G = sizes[k]
            f0 = f0s[k] * 128
            srcg = wg_d[l, i, :, f0:f0 + G * 128].rearrange("(c p) f -> p c f", p=128)
            srcu = wu_d[l, i, :, f0:f0 + G * 128].rearrange("(c p) f -> p c f", p=128)
            srcd = wd_d[l, i, f0:f0 + G * 128, :].rearrange("(j p) d -> p j d", p=128)
            P.op("pool", lambda e: e.dma_start(out=wgb[s][:, :, 0:G * 128], in_=srcg), writes=[("wg", s)], dma=True)
            P.op("pool", lambda e: e.dma_start(out=wub[s][:, :, 0:G * 128], in_=srcu), writes=[("wu", s)], dma=True)
            P.op("pool", lambda e: e.dma_start(out=wdb[s][:, 0:G, :], in_=srcd), writes=[("wd", s)], dma=True)

        gu_cnt = [0]

        def gate_up(k, t):
            s = k % NB
            G = sizes[k]
            ab = (gu_cnt[0]) % 2
            gu_cnt[0] += 1
            for j in range(G):
                bg = j % 2
                bu = 2 + j % 2
                for c in range(NCH):
                    P.op("pe", lambda e, j=j, c=c, bg=bg: e.matmul(
                        ps[bg][:], lhsT=wgb[s][:, c, j * 128:(j + 1) * 128], rhs=hT[:, c, tsl(t)],
                        start=(c == 0), stop=(c == NCH - 1)),
                        reads=[("wg", s), ("hT", c, t)], writes=[PK[bg]])
                for c in range(NCH):
                    P.op("pe", lambda e, j=j, c=c, bu=bu: e.matmul(
                        ps[bu][:], lhsT=wub[s][:, c, j * 128:(j + 1) * 128], rhs=hT[:, c, tsl(t)],
                        start=(c == 0), stop=(c == NCH - 1)),
                        reads=[("wu", s), ("hT", c, t)], writes=[PK[bu]])
                sl_i = rot("slb")
                P.op("act", lambda e, bg=bg, sl_i=sl_i: e.activation(out=slb[sl_i], in_=ps[bg][:], func=AF.Silu),
                     reads=[PK[bg]], writes=[("slb", sl_i)])
                P.op("dve", lambda e, bu=bu, sl_i=sl_i, j=j, ab=ab: e.tensor_tensor(
                    out=actb[ab][:, j, :], in0=ps[bu][:], in1=slb[sl_i], op=ALU.mult),
                    reads=[PK[bu], ("slb", sl_i)], writes=[("act", ab, j)])
                pop_pending(2)
            return ab

        def down(k, t, ab):
            s = k % NB
            G = sizes[k]
            n = cond_of(t)
            for m in range(NCH):
                by = 4 + m % 3
                for j in range(G):
                    P.op("pe", lambda e, j=j, m=m, by=by: e.matmul(
                        ps[by][:], lhsT=wdb[s][:, j, m * 128:(m + 1) * 128], rhs=actb[ab][:, j, :],
                        start=(j == 0), stop=(j == G - 1)),
                        reads=[("wd", s), ("act", ab, j)], writes=[PK[by]])
                resid_update(by, m, t, mcol(l, 3 * sub + 2, m, n))
                pop_pending(1)

        if pre_work is not None:
            tiles = [(wgb[2][:, :, i_ * 128:(i_ + 1) * 128], ("wg", 2)) for i_ in range(4)] + \
                    [(wub[2][:, :, i_ * 128:(i_ + 1) * 128], ("wu", 2)) for i_ in range(4)]
            pre_work(lambda: (load_chunk(0), load_chunk(1)), tiles)
        if not pre_normed:
            for t in range(4):
                pending.extend(norm_items(l, sub, t))
            flush_pending(0)
        nk_ = len(sizes)
        work = [(k, t) for k in range(nk_ - 1) for t in range(4)]
        if pre_work is None:
            load_chunk(0)
            load_chunk(1)
        prev = None
        for idx, (k, t) in enumerate(work):
            if k == 0:
                flush_pending(t)
            ab = gate_up(k, t)
            if prev is not None:
                down(*prev)
            if t == 0 and k >= 1 and k + 1 < nk_:
                load_chunk(k + 1)
                if l == 0 and i == 1:
                    for a in range(2):
                        q = k - 1
                        P.op("pool", lambda e, a=a, q=q: e.dma_start(out=dftS_bf[a, q * 1024:(q + 1) * 1024, :],
                                                                     in_=dftS_d[a, q * 1024:(q + 1) * 1024, :]),
                             writes=[("dftS_bf", a, q)], dma=True, slot="dftcast")
            prev = (k, t, ab)
        k = nk_ - 1
        for t in tail_order:
            ab = gate_up(k, t)
            if prev is not None:
                down(*prev)
                prev = None
            down(k, t, ab)
            if next_norm is not None:
                pending.extend(norm_items(next_norm[0], next_norm[1], t, final=next_norm[2]))
        if not carry:
            flush_pending()

    def mla_phase(l, pre_normed=False):
        jl = l // 2
        P.fence()
        flush_pending()
        cv.reset(COMMON)
        cqT_s = cv.take(4 * 1024, BF16).rearrange("p (c t) -> p c t", c=4)
        ckvT_s = cv.take(2 * 1024, BF16).rearrange("p (c t) -> p c t", c=2)
        kpeT_s = cv.take(1024, BF16)
        ropeT = cv.take(2 * 1024, F32).rearrange("p (a t) -> p a t", a=2)
        SHARED_S = cv.off
        cqT_p = cv.take(4 * 1024, BF16).rearrange("p (c t) -> p c t", c=4)
        ckvT_p = cv.take(2 * 1024, BF16).rearrange("p (c t) -> p c t", c=2)
        kpeT_p = cv.take(1024, BF16)
        SHARED_P = cv.off

        def lsl(t):
            return slice((t % 2) * TB, (t % 2 + 1) * TB)

        def cq(kc, t):
            return (cqT_p if t < 2 else cqT_s)[:, kc, lsl(t)]

        def ckv(mc, t):
            return (ckvT_p if t < 2 else ckvT_s)[:, mc, lsl(t)]

        def kpe(t):
            return (kpeT_p if t < 2 else kpeT_s)[0:64, lsl(t)]

        wdq = cv.take(NCH * 512, BF16).rearrange("p (c f) -> p c f", c=NCH)
        wdkv = cv.take(NCH * 384, BF16).rearrange("p (c f) -> p c f", c=NCH)
        cqraw = cv.take(4 * TB, F32).rearrange("p (c t) -> p c t", c=4)
        ckvf = [cv.take(2 * TB, F32).rearrange("p (c t) -> p c t", c=2) for _ in range(2)]
        kpef = [cv.take(TB, F32) for _ in range(2)]
        if not pre_normed:
            norm_phase(l, 1)
        P.op("pool", lambda e: e.dma_start(out=wdq, in_=wdq_d[jl].rearrange("(c p) f -> p c f", p=128)),
             writes=["wdq"], dma=True)
        P.op("pool", lambda e: e.dma_start(out=wdkv[:, :, 0:320], in_=wdkv_d[jl].rearrange("(c p) f -> p c f", p=128)),
             writes=["wdkv0"], dma=True)
        P.op("pool", lambda e: e.dma_start(out=wdkv[:, :, 320:384], in_=wdkvs_d[jl].rearrange("(c p) f -> p c f", p=128)),
             writes=["wdkv1"], dma=True)
        for a in range(2):
            P.op("sp", lambda e, a=a: e.dma_start(out=ropeT[0:64, a, :], in_=rope_d[a]), writes=[("rope", a)], dma=True)

        def stage_a(t):
            for mc in range(4):
                for c in range(NCH):
                    P.op("pe", lambda e, mc=mc, c=c: e.matmul(
                        ps[mc][:], lhsT=wdq[:, c, mc * 128:(mc + 1) * 128], rhs=hT[:, c, tsl(t)],
                        start=(c == 0), stop=(c == NCH - 1)),
                        reads=["wdq", ("hT", c, t)], writes=[PK[mc]])
                P.op("act", lambda e, mc=mc: e.activation(out=cqraw[:, mc, :], in_=ps[mc][:], func=AF.Identity),
                     reads=[PK[mc]], writes=[("cqraw", mc)])
            rstd_block(None, 4, lambda c: (cqraw[:, c, :], ("cqraw", c)), 1.0 / 512, 7)
            for mc in range(4):
                P.op("dve", lambda e, mc=mc: e.scalar_tensor_tensor(
                    out=cq(mc, t), in0=cqraw[:, mc, :], scalar=qnT[:, jl, mc:mc + 1], in1=rs[:, 1, :],
                    op0=ALU.mult, op1=ALU.mult),
                    reads=[("cqraw", mc), ("rs", 1), "qnT"], writes=[("cqT", mc, t)])
            fb = t % 2
            for mc in range(2):
                for c in range(NCH):
                    P.op("pe", lambda e, mc=mc, c=c: e.matmul(
                        ps[4 + mc][:], lhsT=wdkv[:, c, mc * 128:(mc + 1) * 128], rhs=hT[:, c, tsl(t)],
                        start=(c == 0), stop=(c == NCH - 1)),
                        reads=["wdkv0", ("hT", c, t)], writes=[PK[4 + mc]])
                P.op("act", lambda e, mc=mc: e.activation(out=cqraw[:, mc, :], in_=ps[4 + mc][:], func=AF.Identity),
                     reads=[PK[4 + mc]], writes=[("cqraw", mc)])
            rstd_block(None, 2, lambda c: (cqraw[:, c, :], ("cqraw", c)), 1.0 / 256, 7)
            for mc in range(2):
                if t < 2:
                    P.op("dve", lambda e, mc=mc: e.scalar_tensor_tensor(
                        out=ckvf[fb][:, mc, :], in0=cqraw[:, mc, :], scalar=kvnT[:, jl, mc:mc + 1], in1=rs[:, 1, :],
                        op0=ALU.mult, op1=ALU.mult),
                        reads=[("cqraw", mc), ("rs", 1), "kvnT"], writes=[("ckvf", fb, mc)])
                    P.op("act", lambda e, mc=mc: e.activation(out=ckv(mc, t), in_=ckvf[fb][:, mc, :], func=AF.Identity),
                         reads=[("ckvf", fb, mc)], writes=[("ckvT", mc, t)])
                else:
                    P.op("dve", lambda e, mc=mc: e.scalar_tensor_tensor(
                        out=ckv(mc, t), in0=cqraw[:, mc, :], scalar=kvnT[:, jl, mc:mc + 1], in1=rs[:, 1, :],
                        op0=ALU.mult, op1=ALU.mult),
                        reads=[("cqraw", mc), ("rs", 1), "kvnT"], writes=[("ckvT", mc, t)])
            if t < 2:
                outs.append(P.op("sp", lambda e: e.dma_start(
                    out=ockv_d[jl][:, tsl(t)].rearrange("(c p) t -> p c t", p=128), in_=ckvf[fb]),
                    reads=[("ckvf", fb, 0), ("ckvf", fb, 1)], writes=[("ockv", jl, t)], dma=True,
                    slot=("out", "ckvf", fb)))
            for c in range(NCH):
                P.op("pe", lambda e, c=c: e.matmul(
                    ps[6][0:64, :], lhsT=wdkv[:, c, 256:320], rhs=hT[:, c, tsl(t)],
                    start=(c == 0), stop=(c == NCH - 1)),
                    reads=["wdkv0", ("hT", c, t)], writes=[PK[6]])
            if t < 2:
                P.op("act", lambda e: e.activation(out=kpef[fb][0:64, :], in_=ps[6][0:64, :], func=AF.Identity),
                     reads=[PK[6]], writes=[("kpef", fb)])
                P.op("dve", lambda e: e.tensor_copy(out=kpe(t), in_=kpef[fb][0:64, :]),
                     reads=[("kpef", fb)], writes=[("kpeT", t)])
                outs.append(P.op("sp", lambda e: e.dma_start(out=okpe_d[jl][:, tsl(t)], in_=kpef[fb][0:64, :]),
                                 reads=[("kpef", fb)], writes=[("okpe", jl, t)], dma=True,
                                 slot=("out", "kpef", fb)))
            else:
                for c in range(NCH):
                    P.op("pe", lambda e, c=c: e.matmul(
                        ps[3][0:64, :], lhsT=wdkv[:, c, 320:384], rhs=hT[:, c, tsl(t)],
                        start=(c == 0), stop=(c == NCH - 1)),
                        reads=["wdkv1", ("hT", c, t)], writes=[PK[3]])
                tok = lsl(t)
                P.op("dve", lambda e: e.tensor_tensor(out=tmpf[0:64, 0, :], in0=ps[6][0:64, :],
                                                      in1=ropeT[0:64, 0, tok], op=ALU.mult),
                     reads=[PK[6], ("rope", 0)], writes=[("tmpf", 0)])
                P.op("dve", lambda e: e.tensor_tensor(out=tmpf[0:64, 1, :], in0=ps[3][0:64, :],
                                                      in1=ropeT[0:64, 1, tok], op=ALU.mult),
                     reads=[PK[3], ("rope", 1)], writes=[("tmpf", 1)])
                P.op("dve", lambda e: e.tensor_tensor(out=kpe(t), in0=tmpf[0:64, 0, :],
                                                      in1=tmpf[0:64, 1, :], op=ALU.add),
                     reads=[("tmpf", 0), ("tmpf", 1)], writes=[("kpeT", t)])

        for t in (2, 3, 0, 1):
            stage_a(t)
            if t == 3:
                P.op("sp", lambda e: e.dma_start(out=x_in[jl][0:256, :].rearrange("(c p) t -> p c t", p=128),
                                                 in_=ckvT_s),
                     reads=[("ckvT", 0, 2), ("ckvT", 0, 3), ("ckvT", 1, 2), ("ckvT", 1, 3)],
                     writes=[("x_in", jl, 0)], dma=True, slot=("grp", "x_in", jl))
                P.op("sp", lambda e: e.dma_start(out=x_in[jl][256:320, :], in_=kpeT_s[0:64, :]),
                     reads=[("kpeT", 2), ("kpeT", 3)], writes=[("x_in", jl, 1)], dma=True, slot=("grp", "x_in", jl))
                P.op("pool", lambda e: e.collective_compute("AllGather", ALU.bypass, replica_groups=GROUPS,
                                                            ins=[x_in[jl]], outs=[x_out[jl]]),
                     reads=[("x_in", jl, 0), ("x_in", jl, 1)], writes=[("x_out", jl)], dma=True, inc=1)

        def attention(stream):
            P.fence()
            cv.reset(SHARED_P if stream == 0 else SHARED_S)
            nk = 1024 if stream == 0 else 4608
            nkt = nk // 128
            tok0 = 0 if stream == 0 else 1024
            if stream == 1:
                ckv_all = cv.take(2 * nk, BF16).rearrange("p (c t) -> p c t", c=2)
                kpe_all = cv.take(nk, BF16)
            kn = cv.take(nk, BF16)
            Vh = cv.take(nk, BF16).rearrange("p (k d) -> p k d", d=128)
            qn = cv.take(1024, BF16)
            qr = cv.take(1024, BF16)
            Pt = [cv.take(TB, BF16) for _ in range(4)]
            dacc = [cv.take(TB, F32) for _ in range(2)]
            hw_q = [cv.take(4 * 256, BF16).rearrange("p (c f) -> p c f", c=4) for _ in range(2)]
            hw_kv = [cv.take(2 * 256, BF16).rearrange("p (c f) -> p c f", c=2) for _ in range(2)]
            wob = [cv.take(8 * 128, BF16).rearrange("p (h d) -> p h d", h=8) for _ in range(2)]
            rden = cv.take(TB, F32)
            P.op("dve", lambda e: e.memset(qr[64:128, :], 0.0), writes=["qr_pad"])
            if stream == 1:
                P.op("dve", lambda e: e.memset(kpe_all[64:128, :], 0.0), writes=["kpe_pad"])
            else:
                P.op("dve", lambda e: e.memset(kpeT_p[64:128, :], 0.0), writes=["kpe_pad"])
            if stream == 1:
                P.op("pool", lambda e: e.dma_start(out=ckv_all[:, :, 0:512],
                                                   in_=cckv_d[jl].rearrange("(c p) t -> p c t", p=128)),
                     writes=[("ckv_all", 0)], dma=True, slot=("grp", "kvall_p", jl))
                P.op("pool", lambda e: e.dma_start(out=kpe_all[0:64, 0:512], in_=ckpe_d[jl]),
                     writes=[("kpe_all", 0)], dma=True, slot=("grp", "kvall_p", jl))
                for r in range(4):
                    P.op("sp", lambda e, r=r: e.dma_start(
                        out=ckv_all[:, :, 512 + r * 1024:512 + (r + 1) * 1024],
                        in_=x_out[jl][r * 320:r * 320 + 256, :].rearrange("(c p) t -> p c t", p=128)),
                        reads=[("x_out", jl)], writes=[("ckv_all", 1 + r)], dma=True, slot=("grp", "kvall", jl))
                    P.op("sp", lambda e, r=r: e.dma_start(
                        out=kpe_all[0:64, 512 + r * 1024:512 + (r + 1) * 1024],
                        in_=x_out[jl][r * 320 + 256:(r + 1) * 320, :]),
                        reads=[("x_out", jl)], writes=[("kpe_all", 1 + r)], dma=True, slot=("grp", "kvall", jl))
                ckvsrc, kpesrc = ckv_all, kpe_all
                ckv_keys = [("ckv_all", r) for r in range(5)]
                kpe_keys = [("kpe_all", r) for r in range(5)]
            else:
                ckvsrc, kpesrc = ckvT_p, kpeT_p
                ckv_keys = [("ckvT", mc, t) for mc in range(2) for t in range(2)]
                kpe_keys = [("kpeT", 0), ("kpeT", 1)]

            gcnt = {"tile": 0}

            def head_tiles(h, qblocks, kn_prep, v_prep):
                flat = [(qi, ki) for qi, (q0, qn_, kts) in enumerate(qblocks) for ki in range(len(kts))]
                info = {}

                def s_mm(qi, ki):
                    q0, qn_, kts = qblocks[qi]
                    kt = kts[ki]
                    kn_prep(kt // 4)
                    sbk = gcnt["tile"] % 3
                    gcnt["tile"] += 1
                    da = dacc[qi % 2]
                    P.op("pe", lambda e: e.matmul(
                        ps[sbk][:, 0:qn_], lhsT=kn[:, kt * 128:(kt + 1) * 128], rhs=qn[:, q0:q0 + qn_],
                        start=True, stop=False),
                        reads=[("kn", kt // 4), ("qn", q0 // TB)], writes=[PK[sbk]])
                    P.op("pe", lambda e: e.matmul(
                        ps[sbk][:, 0:qn_], lhsT=kpesrc[:, kt * 128:(kt + 1) * 128], rhs=qr[:, q0:q0 + qn_],
                        start=False, stop=True),
                        reads=kpe_keys + [("qr", q0 // TB), "qr_pad", "kpe_pad"], writes=[PK[sbk]])
                    pi = rot("Pt", 4)
                    info[(qi, ki)] = pi
                    P.op("act", lambda e: e.activation(out=Pt[pi][:, 0:qn_], in_=ps[sbk][:, 0:qn_], func=AF.Exp),
                         reads=[PK[sbk]], writes=[("Pt", pi)])
                    if ki == 0:
                        P.op("dve", lambda e: e.tensor_copy(out=da[:, 0:qn_], in_=Pt[pi][:, 0:qn_]),
                             reads=[("Pt", pi)], writes=[("dacc", qi % 2)])
                    else:
                        P.op("dve", lambda e: e.tensor_tensor(out=da[:, 0:qn_], in0=da[:, 0:qn_], in1=Pt[pi][:, 0:qn_],
                                                              op=ALU.add),
                             reads=[("Pt", pi), ("dacc", qi % 2)], writes=[("dacc", qi % 2)])

                def pv_mm(qi, ki):
                    q0, qn_, kts = qblocks[qi]
                    kt = kts[ki]
                    v_prep(kt // 4)
                    if kt % 4 == 0:
                        v_prep(kt // 4 + 1)
                        kn_prep(kt // 4 + 2)
                    pi = info[(qi, ki)]
                    ob = 4 + (qi % 2)
                    nkt_ = len(kts)
                    P.op("pe", lambda e: e.matmul(
                        ps[ob][:, 0:qn_], lhsT=Vh[:, kt, :], rhs=Pt[pi][:, 0:qn_],
                        start=(ki == 0), stop=(ki == nkt_ - 1)),
                        reads=[("Vh", kt // 4), ("Pt", pi)], writes=[PK[ob]])
                    if ki == nkt_ - 1:
                        db = 6 + (qi % 2)
                        da = dacc[qi % 2]
                        tq = (tok0 + q0) // TB
                        P.op("pe", lambda e: e.matmul(ps[db][:, 0:qn_], lhsT=onesF[:], rhs=da[:, 0:qn_],
                                                      start=True, stop=True),
                             reads=["onesF", ("dacc", qi % 2)], writes=[PK[db]])
                        P.op("act", lambda e: e.activation(out=rden[:, 0:qn_], in_=ps[db][:, 0:qn_], func=AF.Ln),
                             reads=[PK[db]], writes=["rden"])
                        P.op("act", lambda e: e.activation(out=rden[:, 0:qn_], in_=rden[:, 0:qn_], func=AF.Exp, scale=-1.0),
                             reads=["rden"], writes=["rden"])
                        P.op("dve", lambda e: e.tensor_tensor(out=hT[:, h, tok0 + q0:tok0 + q0 + qn_], in0=ps[ob][:, 0:qn_],
                                                              in1=rden[:, 0:qn_], op=ALU.mult),
                             reads=[PK[ob], "rden"], writes=[("hT", h, tq)])

                depth_ = 2
                for idx in range(len(flat) + depth_):
                    if idx < len(flat):
                        s_mm(*flat[idx])
                    if idx >= depth_:
                        pv_mm(*flat[idx - depth_])

            def head(h):
                hb = h % 2
                P.op("pool", lambda e: e.dma_start(
                    out=hw_q[hb][:, :, 0:192], in_=wuq_d[jl][:, h * 192:(h + 1) * 192].rearrange("(c p) f -> p c f", p=128)),
                    writes=[("hwq0", hb)], dma=True)
                P.op("pool", lambda e: e.dma_start(
                    out=hw_q[hb][:, :, 192:256], in_=wuqs_d[jl][:, h * 64:(h + 1) * 64].rearrange("(c p) f -> p c f", p=128)),
                    writes=[("hwq1", hb)], dma=True)
                P.op("pool", lambda e: e.dma_start(
                    out=hw_kv[hb], in_=wukv_d[jl][:, h * 256:(h + 1) * 256].rearrange("(c p) f -> p c f", p=128)),
                    writes=[("hwkv", hb)], dma=True)
                for tb in range(2):
                    t = (tok0 // TB) + tb
                    for kc in range(4):
                        P.op("pe", lambda e, kc=kc, t=t: e.matmul(
                            ps[0][:], lhsT=hw_q[hb][:, kc, 0:128], rhs=cq(kc, t),
                            start=(kc == 0), stop=(kc == 3)),
                            reads=[("hwq0", hb), ("cqT", kc, t)], writes=[PK[0]])
                    P.op("act", lambda e, tb=tb: e.activation(out=qn[:, tb * TB:(tb + 1) * TB], in_=ps[0][:],
                                                              func=AF.Identity, scale=ATTN_SCALE),
                         reads=[PK[0]], writes=[("qn", tb)])
                    for kc in range(4):
                        P.op("pe", lambda e, kc=kc, t=t: e.matmul(
                            ps[1][0:64, :], lhsT=hw_q[hb][:, kc, 128:192], rhs=cq(kc, t),
                            start=(kc == 0), stop=(kc == 3)),
                            reads=[("hwq0", hb), ("cqT", kc, t)], writes=[PK[1]])
                    if stream == 0:
                        P.op("act", lambda e, tb=tb: e.activation(out=qr[0:64, tb * TB:(tb + 1) * TB], in_=ps[1][0:64, :],
                                                                  func=AF.Identity, scale=ATTN_SCALE),
                             reads=[PK[1]], writes=[("qr", tb)])
                    else:
                        for kc in range(4):
                            P.op("pe", lambda e, kc=kc, t=t: e.matmul(
                                ps[2][0:64, :], lhsT=hw_q[hb][:, kc, 192:256], rhs=cq(kc, t),
                                start=(kc == 0), stop=(kc == 3)),
                                reads=[("hwq1", hb), ("cqT", kc, t)], writes=[PK[2]])
                        tok = slice(tb * TB, (tb + 1) * TB)
                        P.op("dve", lambda e, tok=tok: e.tensor_tensor(out=tmpf[0:64, 0, :], in0=ps[1][0:64, :],
                                                                       in1=ropeT[0:64, 0, tok], op=ALU.mult),
                             reads=[PK[1], ("rope", 0)], writes=[("tmpf", 0)])
                        P.op("dve", lambda e, tok=tok: e.tensor_tensor(out=tmpf[0:64, 1, :], in0=ps[2][0:64, :],
                                                                       in1=ropeT[0:64, 1, tok], op=ALU.mult),
                             reads=[PK[2], ("rope", 1)], writes=[("tmpf", 1)])
                        P.op("dve", lambda e: e.tensor_tensor(out=tmpf[0:64, 0, :], in0=tmpf[0:64, 0, :],
                                                              in1=tmpf[0:64, 1, :], op=ALU.add),
                             reads=[("tmpf", 0), ("tmpf", 1)], writes=[("tmpf", 0)])
                        P.op("act", lambda e, tb=tb: e.activation(out=qr[0:64, tb * TB:(tb + 1) * TB], in_=tmpf[0:64, 0, :],
                                                                  func=AF.Identity, scale=ATTN_SCALE),
                             reads=[("tmpf", 0)], writes=[("qr", tb)])
                done_kn, done_v = set(), set()

                def kn_prep(kb):
                    if kb in done_kn or kb >= nk // TB:
                        return
                    done_kn.add(kb)
                    bank = 3
                    for kc in range(2):
                        P.op("pe", lambda e, kc=kc: e.matmul(
                            ps[bank][:], lhsT=hw_kv[hb][:, kc, 0:128], rhs=ckvsrc[:, kc, kb * TB:(kb + 1) * TB],
                            start=(kc == 0), stop=(kc == 1)),
                            reads=[("hwkv", hb)] + ckv_keys, writes=[PK[bank]])
                    if kb % 2 == 0:
                        P.op("act", lambda e: e.activation(out=kn[:, kb * TB:(kb + 1) * TB], in_=ps[bank][:], func=AF.Identity),
                             reads=[PK[bank]], writes=[("kn", kb)])
                    else:
                        P.op("dve", lambda e: e.tensor_copy(out=kn[:, kb * TB:(kb + 1) * TB], in_=ps[bank][:]),
                             reads=[PK[bank]], writes=[("kn", kb)])

                def v_prep(vb):
                    if vb in done_v or vb >= nkt // 4:
                        return
                    done_v.add(vb)
                    bank = 3
                    for q4 in range(4):
                        kt = vb * 4 + q4
                        for kc in range(2):
                            P.op("pe", lambda e, kc=kc, kt=kt, q4=q4: e.matmul(
                                ps[bank][:, q4 * 128:(q4 + 1) * 128], lhsT=ckvsrc[:, kc, kt * 128:(kt + 1) * 128],
                                rhs=hw_kv[hb][:, kc, 128:256], start=(kc == 0), stop=(kc == 1)),
                                reads=[("hwkv", hb)] + ckv_keys, writes=[PK[bank]])
                    vdst = Vh[:, vb * 4:(vb + 1) * 4, :]
                    vsrc = ps[bank][:].rearrange("p (k d) -> p k d", d=128)
                    if vb % 2 == 0:
                        P.op("dve", lambda e: e.tensor_copy(out=vdst, in_=vsrc),
                             reads=[PK[bank]], writes=[("Vh", vb)])
                    else:
                        P.op("act", lambda e: e.activation(out=vdst, in_=vsrc, func=AF.Identity),
                             reads=[PK[bank]], writes=[("Vh", vb)])

                kn_prep(0)
                v_prep(0)
                kn_prep(1)
                if stream == 0:
                    qblocks = [(s * 256, 256, [2 * s, 2 * s + 1]) for s in range(4)]
                else:
                    qblocks = [(qb * TB, TB, list(range(nkt))) for qb in range(2)]
                head_tiles(h, qblocks, kn_prep, v_prep)

            for h in range(8):
                head(h)
            if stream == 0:
                wo_stage(0, [(w_, None) for w_ in wob])

        def wo_stage(stream, wob, mid_loads=None):
            n = 0 if stream == 0 else 1
            tok0 = 0 if stream == 0 else 1024
            nb_ = len(wob)

            def ld(m):
                wb_i = m % nb_
                P.op("pool", lambda e: e.dma_start(
                    out=wob[wb_i][0], in_=wo_d[jl][:, m * 128:(m + 1) * 128].rearrange("(h p) d -> p h d", p=128)),
                    writes=[("wob", wb_i)], dma=True)

            for m in range(min(nb_, NCH)):
                ld(m)
            if mid_loads is not None:
                mid_loads()
            for m in range(NCH):
                wb_i = m % nb_
                for tb in range(2):
                    t = tok0 // TB + tb
                    bank = (m * 2 + tb) % 4
                    for hh in range(8):
                        P.op("pe", lambda e, hh=hh, t=t, bank=bank, wb_i=wb_i: e.matmul(
                            ps[bank][:], lhsT=wob[wb_i][0][:, hh, :], rhs=hT[:, hh, tsl(t)],
                            start=(hh == 0), stop=(hh == 7)),
                            reads=[("wob", wb_i), ("hT", hh, t)] + ([wob[wb_i][1]] if wob[wb_i][1] else []),
                            writes=[PK[bank]])
                    resid_update(bank, m, t, mcol(l, 5, m, n))
                if m + nb_ < NCH:
                    ld(m + nb_)

        attention(0)
        attention(1)

        def pre_work(mid_loads, tiles):
            wo_stage(1, tiles, mid_loads)
        return pre_work

    def fourier_phase(l, pre_normed=False):
        jl = l // 2
        P.fence()
        flush_pending()
        cv.reset(COMMON)
        cs = cv.take(2 * 512, BF16).rearrange("p (c f) -> p c f", c=2)
        dp = cv.take(2 * 2 * 256, BF16).rearrange("p (a n k) -> p a n k", a=2, n=2)
        ABp = cv.take(8 * 2048, BF16).rearrange("p (t f) -> p t f", t=8)
        ABs = cv.take(8 * 2048, BF16).rearrange("p (t f) -> p t f", t=8)
        NSB = 3
        abn = [cv.take(2048, BF16) for _ in range(NSB)]
        tbn = [cv.take(2 * 512, BF16).rearrange("p (a k) -> p a k", a=2) for _ in range(NSB)]
        fwm = [cv.take(NCH * 128, BF16).rearrange("p (c d) -> p c d", c=NCH) for _ in range(2)]
        if not pre_normed:
            norm_phase(l, 1)
        P.op("pool", lambda e: e.dma_start(out=cs, in_=dftC_d.rearrange("(c p) f -> p c f", p=128)), writes=["cs"], dma=True)
        for a in range(2):
            P.op("pool", lambda e, a=a: e.dma_start(out=dp[:, a], in_=dftP_d[a].rearrange("(n p) k -> p n k", p=128)),
                 writes=[("dp", a)], dma=True)
        for tile in list(range(8, 16)) + list(range(8)):
            dst = ABs if tile >= 8 else ABp
            ti = tile % 8
            t = tile // 4
            for g in range(4):
                bank = g % 4
                for kc in range(2):
                    P.op("pe", lambda e, g=g, kc=kc, tile=tile, bank=bank: e.matmul(
                        ps[bank][:], lhsT=hT[:, 2 * g + kc, tile * 128:(tile + 1) * 128], rhs=cs[:, kc, :],
                        start=(kc == 0), stop=(kc == 1)),
                        reads=["cs", ("hT", 2 * g + kc, t)], writes=[PK[bank]])
                dv = dst[:, ti, :].rearrange("p (s g c) -> p s g c", s=2, g=4)[:, :, g, :]
                sv = ps[bank][:].rearrange("p (s c) -> p s c", s=2)
                key = ("AB", tile)
                if g % 2 == 0:
                    P.op("act", lambda e, dv=dv, sv=sv: e.activation(out=dv, in_=sv, func=AF.Identity),
                         reads=[PK[bank]], writes=[(key, g)])
                else:
                    P.op("dve", lambda e, dv=dv, sv=sv: e.tensor_copy(out=dv, in_=sv),
                         reads=[PK[bank]], writes=[(key, g)])
            if tile >= 8 and tile % 2 == 1:
                part = (tile - 8) // 2
                P.op("sp", lambda e, part=part: e.dma_start(
                    out=f_in[jl][part].rearrange("(t p) f -> p t f", p=128), in_=ABs[:, 2 * part:2 * part + 2, :]),
                    reads=[(("AB", tl), g) for tl in (tile - 1, tile) for g in range(4)],
                    writes=[("f_in", jl, part)], dma=True)
                P.op("pool", lambda e, part=part: e.collective_compute(
                    "AllGather", ALU.bypass, replica_groups=GROUPS, ins=[f_in[jl][part]], outs=[f_out[jl][part]]),
                    reads=[("f_in", jl, part)], writes=[("f_out", jl, part)], dma=True, inc=1)
        for s in range(4):
            t = s // 2
            for m in range(NCH):
                bank = m % 4
                first = True
                for nt in range(2):
                    for a in range(2):
                        P.op("pe", lambda e, nt=nt, a=a, m=m, s=s, bank=bank, first=first: e.matmul(
                            ps[bank][:, 0:256], lhsT=ABp[:, s * 2 + nt, a * 1024 + m * 128:a * 1024 + (m + 1) * 128],
                            rhs=dp[:, a, nt, :], start=first, stop=(nt == 1 and a == 1)),
                            reads=[(("AB", s * 2 + nt), gg) for gg in range(4)] + [("dp", a)], writes=[PK[bank]])
                        first = False
                P.op("act", lambda e, m=m, s=s, bank=bank: e.activation(
                    out=hT[:, m, s * 256:(s + 1) * 256], in_=ps[bank][:, 0:256], func=AF.Identity, scale=1.0 / 256.0),
                    reads=[PK[bank]], writes=[("hT", m, t)])
        def fc_stage(tblocks, fwbuf, mid_loads=None):
            nb_ = len(fwbuf)

            def ld(m):
                wi = m % nb_
                P.op("pool", lambda e: e.dma_start(
                    out=fwbuf[wi][0], in_=fw_d[jl][:, m * 128:(m + 1) * 128].rearrange("(c p) d -> p c d", p=128)),
                    writes=[("fwm", wi)], dma=True)

            for m in range(min(nb_, NCH)):
                ld(m)
            if mid_loads is not None:
                mid_loads()
            for m in range(NCH):
                wi = m % nb_
                if m >= nb_ and False:
                    pass
                for t in tblocks:
                    n = cond_of(t)
                    bank = (m * 2 + t) % 4
                    for c in range(NCH):
                        P.op("pe", lambda e, c=c, t=t, bank=bank, wi=wi: e.matmul(
                            ps[bank][:], lhsT=fwbuf[wi][0][:, c, :], rhs=hT[:, c, tsl(t)],
                            start=(c == 0), stop=(c == NCH - 1)),
                            reads=[("fwm", wi), ("hT", c, t)] + ([fwbuf[wi][1]] if fwbuf[wi][1] else []),
                            writes=[PK[bank]])
                    s_ = rot("tmpf")
                    P.op("dve", lambda e, s_=s_, m=m, bank=bank, n=n: e.tensor_scalar(
                        out=tmpf[:, s_, :], in0=ps[bank][:], scalar1=fbT[:, jl, m:m + 1], scalar2=mcol(l, 5, m, n),
                        op0=ALU.add, op1=ALU.mult),
                        reads=[PK[bank], "fbT", "modS"], writes=[("tmpf", s_)])
                    P.op("dve", lambda e, s_=s_, m=m, t=t: e.tensor_tensor(out=xT[:, m, tsl(t)], in0=xT[:, m, tsl(t)],
                                                                           in1=tmpf[:, s_, :], op=ALU.add),
                         reads=[("tmpf", s_), ("xT", m, t)], writes=[("xT", m, t)])
                if m + nb_ < NCH:
                    ld(m + nb_)

        fc_stage((0, 1), [(w_, None) for w_ in fwm])
        for kb in range(2):
            t = 2 + kb
            nt_order = [r_ * 8 + part * 2 + j_ for part in range(4) for r_ in range(4) for j_ in range(2)]
            for ni, nt in enumerate(nt_order):
                b = (kb * 32 + ni) % NSB
                r_, w_ = nt // 8, nt % 8
                part, j_ = w_ // 2, w_ % 2
                src = f_out[jl][part][r_ * 256 + j_ * 128:r_ * 256 + (j_ + 1) * 128, :]
                P.op("sp", lambda e, src=src, b=b: e.dma_start(out=abn[b], in_=src),
                     reads=[("f_out", jl, part)], writes=[("abn", b)], dma=True)
                P.op("sp", lambda e, nt=nt, b=b, kb=kb: e.dma_start(
                    out=tbn[b], in_=dftS_bf[:, nt * 128:(nt + 1) * 128, kb * 512:(kb + 1) * 512].rearrange("a p k -> p a k")),
                    reads=DFT_KEYS, writes=[("tbn", b)], dma=True)
                for m in range(NCH):
                    for a in range(2):
                        P.op("pe", lambda e, ni=ni, a=a, m=m, b=b: e.matmul(
                            ps[m][:], lhsT=abn[b][:, a * 1024 + m * 128:a * 1024 + (m + 1) * 128], rhs=tbn[b][:, a, :],
                            start=(ni == 0 and a == 0), stop=(ni == 31 and a == 1)),
                            reads=[("abn", b), ("tbn", b)], writes=[PK[m]])
            for m in range(NCH):
                if m % 2 == 0:
                    P.op("act", lambda e, m=m, t=t: e.activation(out=hT[:, m, tsl(t)], in_=ps[m][:], func=AF.Identity,
                                                                 scale=1.0 / 1024.0),
                         reads=[PK[m]], writes=[("hT", m, t)])
                else:
                    P.op("dve", lambda e, m=m, t=t: e.tensor_scalar(out=hT[:, m, tsl(t)], in0=ps[m][:], scalar1=1.0 / 1024.0,
                                                                    scalar2=None, op0=ALU.mult),
                         reads=[PK[m]], writes=[("hT", m, t)])
        def pre_work(mid_loads, tiles):
            fc_stage((2, 3), tiles, mid_loads)
        return pre_work

    for l in range(depth):
        mixer_on = ("m" if l % 2 == 0 else "f") in DBG_PHASES
        ffn_phase(l, 0, pre_normed=(l > 0), next_norm=(l, 1, False) if mixer_on else None, skip_fence=(l > 0),
                  carry=mixer_on, tail_order=(2, 3, 0, 1) if mixer_on else (0, 1, 2, 3))
        pw = None
        if mixer_on:
            pw = mla_phase(l, pre_normed=True) if l % 2 == 0 else fourier_phase(l, pre_normed=True)
        last = (l == depth - 1)
        ffn_phase(l, 1, pre_normed=False, next_norm=(0, 0, True) if last else (l + 1, 0, False),
                  pre_work=pw, carry=not last)
    if depth == 0:
        P.fence()
        norm_phase(0, 0, final=True)
    P.emit(nc, final_waits=outs)
    st.close()
    return nc, P


def _fm(a):
    t = a.shape[0]
    return np.ascontiguousarray(a.T.reshape(NCH, 128, t).transpose(1, 0, 2))


def _vec(a, nch):
    sh = a.shape[:-1]
    b = a.reshape(sh + (nch, 128))
    return np.ascontiguousarray(np.moveaxis(b, -1, 0))


def _const_tables():
    f32 = np.float32
    k = np.arange(256)
    ang = 2 * np.pi * np.outer(k, k) / 256.0
    dftC = np.concatenate([np.cos(ang), np.sin(ang)], axis=1).astype(np.float32)
    dftP = np.stack([np.cos(ang), -np.sin(ang)]).astype(np.float32)
    n = np.arange(4096, dtype=np.int64)
    dftS = []
    for qd in range(4):
        kk = np.arange(qd * 1024, (qd + 1) * 1024, dtype=np.int64)
        a = 2 * np.pi * ((np.outer(n, kk) % 4096).astype(np.float64)) / 4096.0
        dftS.append(np.stack([np.cos(a), -np.sin(a)]).astype(np.float32))
    inv = 1.0 / (10000.0 ** (np.arange(16, dtype=np.float32) / 16.0))
    pos = np.arange(4096)
    row = (pos // 64).astype(np.float32)
    col = (pos % 64).astype(np.float32)
    ang = np.stack([row[:, None] * inv, col[:, None] * inv], axis=1).astype(np.float32)
    cos = np.cos(ang)
    sin = np.sin(ang)
    cosT = np.zeros((64, 4096), f32)
    sinT = np.zeros((64, 4096), f32)
    for a in range(2):
        for hf in range(2):
            for f in range(16):
                p = a * 32 + hf * 16 + f
                cosT[p] = cos[:, a, f]
                sinT[p] = -sin[:, a, f] if hf == 0 else sin[:, a, f]
    rope = [np.ascontiguousarray(np.stack([cosT[:, q * 1024:(q + 1) * 1024], sinT[:, q * 1024:(q + 1) * 1024]]))
            for q in range(4)]
    return dftC, dftP, dftS, rope


def _swap_cols(w, nheads, base, stride):
    cols = []
    for h in range(nheads):
        o = h * stride + base
        for a in range(2):
            cols += list(range(o + a * 32 + 16, o + a * 32 + 32)) + list(range(o + a * 32, o + a * 32 + 16))
    return np.ascontiguousarray(w[..., cols])


_CACHE = {}
DEPTH_RUN = DEPTH


def kernel(x_prompt, x_sample, cache_ckv, cache_kpe, c, c_ctx, w_mod, b_mod, norm_g,
           ffn_wg, ffn_wu, ffn_wd, mla_w_dq, mla_q_norm, mla_w_uq, mla_w_dkv, mla_kv_norm,
           mla_w_ukv, mla_w_o, fourier_w, fourier_b, final_norm):
    A = lambda a: np.ascontiguousarray(np.asarray(a, dtype=np.float32))
    x_prompt, x_sample, cache_ckv, cache_kpe = A(x_prompt), A(x_sample), A(cache_ckv), A(cache_kpe)
    c, c_ctx, w_mod, b_mod, norm_g = A(c), A(c_ctx), A(w_mod), A(b_mod), A(norm_g)
    ffn_wg, ffn_wu, ffn_wd = A(ffn_wg), A(ffn_wu), A(ffn_wd)
    mla_w_dq, mla_q_norm, mla_w_uq, mla_w_dkv = A(mla_w_dq), A(mla_q_norm), A(mla_w_uq), A(mla_w_dkv)
    mla_kv_norm, mla_w_ukv, mla_w_o = A(mla_kv_norm), A(mla_w_ukv), A(mla_w_o)
    fourier_w, fourier_b, final_norm = A(fourier_w), A(fourier_b), A(final_norm)

    if "nc" not in _CACHE:
        _CACHE["nc"] = build_program(DEPTH_RUN)[0]
        _CACHE["tables"] = _const_tables()
    nc = _CACHE["nc"]
    dftC, dftP, dftS, rope = _CACHE["tables"]

    shared = {
        "bmodT": _vec(b_mod, 72), "normgT": _vec(norm_g, 8), "finalT": _vec(final_norm, 8),
        "qnT": _vec(mla_q_norm, 4), "kvnT": _vec(mla_kv_norm, 2), "fbT": _vec(fourier_b, 8),
        "wg": ffn_wg, "wu": ffn_wu, "wd": ffn_wd, "wdq": mla_w_dq, "wuq": mla_w_uq,
        "wuqs": _swap_cols(mla_w_uq, 8, 128, 192), "wdkv": mla_w_dkv,
        "wdkvs": _swap_cols(mla_w_dkv, 1, 256, 0), "wukv": mla_w_ukv, "wo": mla_w_o, "fw": fourier_w,
        "dftC": dftC, "dftP": dftP,
    }
    in_maps = []
    for r in range(8):
        b, qd = r // 4, r % 4
        xp = x_prompt[4 * r:4 * r + 4].reshape(1024, D)
        xs = x_sample[b, qd * 1024:(qd + 1) * 1024]
        m = dict(shared)
        m["xT"] = _fm(np.concatenate([xp, xs], axis=0))
        m["cckv"] = np.ascontiguousarray(cache_ckv[b].transpose(0, 2, 1))
        m["ckpe"] = np.ascontiguousarray(cache_kpe[b].transpose(0, 2, 1))
        m["condT"] = _vec(np.stack([c_ctx, c[b]]), 8).transpose(0, 2, 1).copy()
        m["wmod"] = np.ascontiguousarray(w_mod[:, :, qd * 2304:(qd + 1) * 2304])
        m["ropeT"] = rope[qd]
        m["dftS"] = dftS[qd]
        in_maps.append(m)

    res = run_bass_kernel_spmd(nc, in_maps, core_ids=list(range(8)))
    y_prompt = np.empty((32, 256, D), np.float32)
    y_sample = np.empty((2, 4096, D), np.float32)
    new_ckv = np.empty((32, 2, 256, 256), np.float32)
    new_kpe = np.empty((32, 2, 256, 64), np.float32)
    for r in range(8):
        b, qd = r // 4, r % 4
        o = res.results[r]
        y = np.asarray(o["yT"]).transpose(2, 1, 0).reshape(NT, D)
        y_prompt[4 * r:4 * r + 4] = y[:1024].reshape(4, 256, D)
        y_sample[b, qd * 1024:(qd + 1) * 1024] = y[1024:]
        ck = np.asarray(o["o_ckv"])
        kp = np.asarray(o["o_kpe"])
        new_ckv[4 * r:4 * r + 4] = ck.reshape(2, 256, 4, 256).transpose(2, 0, 3, 1)
        new_kpe[4 * r:4 * r + 4] = kp.reshape(2, 64, 4, 256).transpose(2, 0, 3, 1)
    return (y_prompt, y_sample, new_ckv, new_kpe)
```

```python
import math
from contextlib import ExitStack

import numpy as np
import ml_dtypes
import concourse.bass as bass
import concourse.mybir as mybir
from concourse.bass_utils import run_bass_kernel_spmd

F32 = mybir.dt.float32
BF16 = mybir.dt.bfloat16
AF = mybir.ActivationFunctionType
ALU = mybir.AluOpType

D = 1024
DFF = 2816
NCH = 8
TB = 512
NT = 2048
DEPTH = 4
EPS = 1e-6
ATTN_SCALE = 1.0 / math.sqrt(192.0)
GROUPS = [[0, 1, 2, 3], [4, 5, 6, 7]]
ENGINES = ("sp", "act", "dve", "pool", "pe")
SEM_ROT = 8000
SCR_BYTES = 102 * 1024
DBG_PHASES = "mf"


class Op:
    __slots__ = ("id", "eng", "fn", "deps", "is_dma", "slot", "sig", "inc", "needs_signal")

    def __init__(self, id, eng, fn, is_dma, slot, inc):
        self.id = id
        self.eng = eng
        self.fn = fn
        self.deps = set()
        self.is_dma = is_dma
        self.slot = slot
        self.sig = None
        self.inc = inc
        self.needs_signal = False


class Prog:
    def __init__(self):
        self.ops = []
        self.last_w = {}
        self.readers = {}
        self.eng_ops = {e: [] for e in ENGINES}
        self.fence_set = set()
        self.dma_since_fence = []
        self.fenced = {e: True for e in ENGINES}

    def fence(self):
        fs = set(self.dma_since_fence)
        for e in ENGINES:
            for o in reversed(self.eng_ops[e]):
                if not o.is_dma:
                    fs.add(o.id)
                    break
        self.fence_set = fs
        self.dma_since_fence = []
        self.fenced = {e: False for e in ENGINES}

    def op(self, eng, fn, reads=(), writes=(), dma=False, slot=None, inc=16):
        o = Op(len(self.ops), eng, fn, dma, slot, inc)
        deps = set()
        for k in reads:
            w = self.last_w.get(k)
            if w is not None:
                deps.add(w)
        for k in writes:
            w = self.last_w.get(k)
            if w is not None:
                deps.add(w)
            for r in self.readers.get(k, {}).values():
                if isinstance(r, list):
                    deps.update(r)
                else:
                    deps.add(r)
        if not self.fenced[eng]:
            deps |= self.fence_set
            self.fenced[eng] = True
        deps.discard(o.id)
        o.deps = deps
        for k in reads:
            rd = self.readers.setdefault(k, {})
            if dma:
                rd.setdefault("dma", []).append(o.id)
            else:
                rd[eng] = o.id
        for k in writes:
            self.last_w[k] = o.id
            self.readers[k] = {}
        if dma:
            if slot is None:
                o.slot = ("dma", writes[0])
            self.dma_since_fence.append(o.id)
        self.eng_ops[eng].append(o)
        self.ops.append(o)
        return o

    def emit(self, nc, final_waits=()):
        ops = self.ops
        for o in ops:
            for d in o.deps:
                p = ops[d]
                if p.is_dma:
                    p.needs_signal = True
                elif p.eng == "pe" and o.eng == "pe" and not o.is_dma:
                    continue
                else:
                    p.needs_signal = True
        for d in final_waits:
            d.needs_signal = True
        sem_keys = []
        comp_cnt = {e: 0 for e in ENGINES}
        slot_cnt = {}
        grp_total = {}
        for o in ops:
            if o.is_dma and isinstance(o.slot, tuple) and o.slot[0] == "grp":
                grp_total[o.slot] = grp_total.get(o.slot, 0) + o.inc
        for o in ops:
            if o.is_dma:
                c = slot_cnt.get(o.slot, 0) + o.inc
                slot_cnt[o.slot] = c
                o.sig = (o.slot, grp_total.get(o.slot, c))
                if o.slot not in slot_cnt or o.slot not in sem_keys:
                    sem_keys.append(o.slot)
            elif o.needs_signal:
                n = comp_cnt[o.eng]
                comp_cnt[o.eng] = n + 1
                key = ("c", o.eng, n // SEM_ROT)
                o.sig = (key, n % SEM_ROT + 1)
                if key not in sem_keys:
                    sem_keys.append(key)
        sem_keys = list(dict.fromkeys(sem_keys))
        self.n_sems = len(sem_keys)
        stack = ExitStack()
        sems = {}
        for i, k in enumerate(sem_keys):
            sems[k] = stack.enter_context(nc.semaphore("s%d" % i))

        def run(engname, e):
            waited = {}
            for o in self.eng_ops[engname]:
                need = {}
                for d in o.deps:
                    p = ops[d]
                    if p.sig is None:
                        continue
                    if (not p.is_dma) and p.eng == "pe" and engname == "pe" and not o.is_dma:
                        continue
                    k, c = p.sig
                    if need.get(k, 0) < c:
                        need[k] = c
                for k, c in need.items():
                    if waited.get(k, 0) >= c:
                        continue
                    e.wait_ge(sems[k], c)
                    waited[k] = c
                ins = o.fn(e)
                if o.sig is not None:
                    ins.then_inc(sems[o.sig[0]], o.inc if o.is_dma else 1)
            if engname == "sp":
                for d in final_waits:
                    k, c = d.sig
                    e.wait_ge(sems[k], c)

        with stack:
            with nc.Block() as block:
                @block.sync
                def _(e):
                    run("sp", e)

                @block.scalar
                def _(e):
                    run("act", e)

                @block.vector
                def _(e):
                    run("dve", e)

                @block.gpsimd
                def _(e):
                    run("pool", e)

                @block.tensor
                def _(e):
                    run("pe", e)


class Carve:
    def __init__(self, scr):
        self.scr = scr
        self.off = 0

    def reset(self, off=0):
        self.off = off

    def take(self, nelem, dtype, parts=128):
        nb = nelem * (4 if dtype == F32 else 2)
        nb = (nb + 63) // 64 * 64
        a = self.off // 2
        self.off += nb
        assert self.off <= SCR_BYTES, ("scratch overflow", self.off)
        v = self.scr[:, a:a + nb // 2]
        if dtype == F32:
            v = v.bitcast(F32)
        v = v[:, 0:nelem]
        return v


def build_program(depth=DEPTH):
    nc = bass.Bass("TRN2", target_bir_lowering=False)

    def din(name, shape, dt=F32):
        return nc.dram_tensor(name, list(shape), dt, kind="ExternalInput").ap()

    def dout(name, shape, dt=F32):
        return nc.dram_tensor(name, list(shape), dt, kind="ExternalOutput").ap()

    xT_d = din("xT", [128, NCH, NT])
    cckv_d = din("cckv", [2, 256, 512])
    ckpe_d = din("ckpe", [2, 64, 512])
    condT_d = din("condT", [128, NCH, 2])
    wmod_d = din("wmod", [4, D, 2304])
    bmodT_d = din("bmodT", [128, 4, 72])
    normgT_d = din("normgT", [128, 4, 3, 8])
    finalT_d = din("finalT", [128, 8])
    qnT_d = din("qnT", [128, 2, 4])
    kvnT_d = din("kvnT", [128, 2, 2])
    fbT_d = din("fbT", [128, 2, 8])
    wg_d = din("wg", [4, 2, D, DFF])
    wu_d = din("wu", [4, 2, D, DFF])
    wd_d = din("wd", [4, 2, DFF, D])
    wdq_d = din("wdq", [2, D, 512])
    wuq_d = din("wuq", [2, 512, 1536])
    wuqs_d = din("wuqs", [2, 512, 512])
    wdkv_d = din("wdkv", [2, D, 320])
    wdkvs_d = din("wdkvs", [2, D, 64])
    wukv_d = din("wukv", [2, 256, 2048])
    wo_d = din("wo", [2, D, D])
    fw_d = din("fw", [2, D, D])
    rope_d = din("ropeT", [2, 64, 1024])
    dftC_d = din("dftC", [256, 512])
    dftP_d = din("dftP", [2, 256, 256])
    dftS_d = din("dftS", [2, 4096, 1024])

    yT_d = dout("yT", [128, NCH, NT])
    ockv_d = dout("o_ckv", [2, 256, 1024])
    okpe_d = dout("o_kpe", [2, 64, 1024])

    mod_in = nc.dram_tensor("mod_in", [128, 144], F32).ap()
    mod_out = nc.dram_tensor("mod_out", [512, 144], F32).ap()
    x_in = [nc.dram_tensor("x_in%d" % j, [320, 1024], BF16).ap() for j in range(2)]
    x_out = [nc.dram_tensor("x_out%d" % j, [1280, 1024], BF16).ap() for j in range(2)]
    f_in = [[nc.dram_tensor("f_in%d_%d" % (j, q), [256, 2048], BF16).ap() for q in range(4)] for j in range(2)]
    f_out = [[nc.dram_tensor("f_out%d_%d" % (j, q), [1024, 2048], BF16).ap() for q in range(4)] for j in range(2)]

    dftS_bf = nc.dram_tensor("dftS_bf", [2, 4096, 1024], BF16).ap()
    DFT_KEYS = [("dftS_bf", a, q) for a in range(2) for q in range(4)]

    P = Prog()
    st = ExitStack()
    sb = lambda name, shape, dt: st.enter_context(nc.sbuf_tensor(name, list(shape), dt))
    xT = sb("xTs", [128, NCH, NT], F32)
    hT = sb("hTs", [128, NCH, NT], BF16)
    modS = sb("modS", [128, 4 * 72 * 2], F32)
    bmodT = sb("bmodTs", [128, 4, 72], F32)
    normgT = sb("normgTs", [128, 4, 3, 8], F32)
    finalT = sb("finalTs", [128, 8], F32)
    qnT = sb("qnTs", [128, 2, 4], F32)
    kvnT = sb("kvnTs", [128, 2, 2], F32)
    fbT = sb("fbTs", [128, 2, 8], F32)
    condT = sb("condTs", [128, NCH, 2], F32)
    scT = sb("scTs", [128, NCH, 2], F32)
    scB = sb("scBs", [128, NCH, 2], BF16)
    onesB = sb("onesB", [128, 128], BF16)
    onesF = sb("onesF", [128, 128], F32)
    epsT = sb("epsT", [128, 1], F32)
    scr = sb("scr", [128, SCR_BYTES // 2], BF16)
    ps = [st.enter_context(nc.psum_tensor("ps%d" % i, [128, 512], F32)) for i in range(8)]
    PK = [("ps", i) for i in range(8)]
    cv = Carve(scr)

    mod5 = modS[:].rearrange("p (l k c n) -> p l k c n", l=4, k=9, c=8)
    mod_lrx = modS[:].rearrange("p (l r x) -> p l r x", l=4, r=4)
    mod_lcn = modS[:].rearrange("p (l c n) -> p l c n", l=4, c=72)

    outs = []

    def mcol(l, k, c, cond):
        return mod5[:, l, k, c, cond:cond + 1]

    def cond_of(t):
        return 0 if t < 2 else 1

    def tsl(t):
        return slice(t * TB, (t + 1) * TB)

    cv.reset(0)
    sq = cv.take(2 * TB, BF16).rearrange("p (a b) -> p a b", a=2)
    rs = cv.take(2 * TB, F32).rearrange("p (a b) -> p a b", a=2)
    tmpf = cv.take(2 * TB, F32).rearrange("p (a b) -> p a b", a=2)
    COMMON = cv.off
    cnt = {"sq": 0, "tmpf": 0}

    def rot(name, n=2):
        v = cnt.get(name, 0)
        cnt[name] = v + 1
        return v % n

    P.op("sp", lambda e: e.dma_start(out=condT[:], in_=condT_d), writes=["condT"], dma=True, slot=("grp", "setup"))
    for (tl, td, nm) in ((bmodT, bmodT_d, "bmodT"), (normgT, normgT_d, "normgT"), (finalT, finalT_d, "finalT"),
                         (qnT, qnT_d, "qnT"), (kvnT, kvnT_d, "kvnT"), (fbT, fbT_d, "fbT")):
        P.op("sp", lambda e, tl=tl, td=td: e.dma_start(out=tl[:], in_=td), writes=[nm], dma=True, slot=("grp", "setup"))
    P.op("dve", lambda e: e.memset(onesB[:], 1.0), writes=["ones"])
    P.op("dve", lambda e: e.memset(onesF[:], 1.0), writes=["onesF"])
    P.op("dve", lambda e: e.memset(epsT[:], EPS), writes=["eps"])
    P.op("act", lambda e: e.activation(out=scB[:], in_=condT[:], func=AF.Silu), reads=["condT"], writes=["scT"])

    for c in range(NCH):
        P.op("sp", lambda e, c=c: e.dma_start(out=xT[:, c, :], in_=xT_d[:, c, :]),
             writes=[("xT", c, t) for t in range(4)], dma=True, slot=("grp", "xT"))

    cv.reset(COMMON)
    wm = [cv.take(NCH * 1152, BF16).rearrange("p (c f) -> p c f", c=NCH) for _ in range(4)]
    modin = cv.take(144, F32)
    for l in range(4):
        for hf in range(2):
            b = (l * 2 + hf) % 4
            src = wmod_d[l, :, hf * 1152:(hf + 1) * 1152].rearrange("(c p) f -> p c f", p=128)
            P.op("pool", lambda e, b=b, src=src: e.dma_start(out=wm[b], in_=src), writes=[("wm", b)], dma=True)
            for i in range(9):
                col = (hf * 9 + i) * 2
                for c in range(NCH):
                    P.op("pe", lambda e, b=b, i=i, c=c, col=col: e.matmul(
                        ps[0][:, col:col + 2], lhsT=wm[b][:, c, i * 128:(i + 1) * 128], rhs=scB[:, c, :],
                        start=(c == 0), stop=(c == NCH - 1)),
                        reads=[("wm", b), "scT"], writes=[PK[0]])
        P.op("dve", lambda e, l=l: e.tensor_copy(out=modin[:, l * 36:(l + 1) * 36], in_=ps[0][:, 0:36]),
             reads=[PK[0]], writes=["modin"])
    P.op("sp", lambda e: e.dma_start(out=mod_in, in_=modin), reads=["modin"], writes=["mod_in"], dma=True)
    P.op("pool", lambda e: e.collective_compute("AllGather", ALU.bypass, replica_groups=GROUPS,
                                                ins=[mod_in], outs=[mod_out]),
         reads=["mod_in"], writes=["mod_out"], dma=True, inc=1)
    for l in range(4):
        src = mod_out[:, l * 36:(l + 1) * 36].rearrange("(r p) x -> p r x", p=128)
        P.op("sp", lambda e, l=l, src=src: e.dma_start(out=mod_lrx[:, l], in_=src),
             reads=["mod_out"], writes=["modS"], dma=True, slot=("dma", "modS", l))
    for n in range(2):
        P.op("dve", lambda e, n=n: e.tensor_tensor(out=mod_lcn[:, :, :, n], in0=mod_lcn[:, :, :, n],
                                                   in1=bmodT[:], op=ALU.add),
             reads=["modS", "bmodT"], writes=["modS"])
    for i in range(3):
        for n in range(2):
            P.op("dve", lambda e, i=i, n=n: e.scalar_tensor_tensor(
                out=mod5[:, :, 3 * i + 1, :, n], in0=mod5[:, :, 3 * i + 1, :, n], scalar=1.0,
                in1=normgT[:, :, i, :], op0=ALU.add, op1=ALU.mult),
                reads=["modS", "normgT"], writes=["modS"])
            if i != 1:
                P.op("dve", lambda e, i=i, n=n: e.tensor_scalar(
                    out=mod5[:, :, 3 * i + 2, :, n], in0=mod5[:, :, 3 * i + 2, :, n], scalar1=0.5,
                    scalar2=None, op0=ALU.mult),
                    reads=["modS"], writes=["modS"])

    def rstd_block(src_keys, nchunks, get_src, inv_n, bank, extra_reads=()):
        for c in range(nchunks):
            s = rot("sq")
            src, rk = get_src(c)
            P.op("act", lambda e, s=s, src=src: e.activation(out=sq[:, s, :], in_=src, func=AF.Square),
                 reads=[rk], writes=[("sq", s)])
            P.op("pe", lambda e, s=s, c=c: e.matmul(ps[bank][:], lhsT=onesB[:], rhs=sq[:, s, :],
                                                    start=(c == 0), stop=(c == nchunks - 1)),
                 reads=[("sq", s), "ones"], writes=[PK[bank]])
        P.op("act", lambda e: e.activation(out=rs[:, 0, :], in_=ps[bank][:], func=AF.Ln,
                                           bias=epsT[:, 0:1], scale=inv_n),
             reads=[PK[bank], "eps"], writes=[("rs", 0)])
        P.op("act", lambda e: e.activation(out=rs[:, 1, :], in_=rs[:, 0, :], func=AF.Exp, scale=-0.5),
             reads=[("rs", 0)], writes=[("rs", 1)])

    def norm_phase(l, i, final=False):
        for t in range(4):
            n = cond_of(t)
            rstd_block(None, NCH, lambda c, t=t: (xT[:, c, tsl(t)], ("xT", c, t)), 1.0 / D, 7)
            for c in range(NCH):
                s = rot("tmpf")
                g = finalT[:, c:c + 1] if final else mcol(l, 3 * i + 1, c, n)
                P.op("dve", lambda e, s=s, c=c, t=t, g=g: e.scalar_tensor_tensor(
                    out=tmpf[:, s, :], in0=xT[:, c, tsl(t)], scalar=g, in1=rs[:, 1, :],
                    op0=ALU.mult, op1=ALU.mult),
                    reads=[("xT", c, t), ("rs", 1), "modS", "finalT"], writes=[("tmpf", s)])
                if final:
                    outs.append(P.op("sp", lambda e, s=s, c=c, t=t: e.dma_start(out=yT_d[:, c, tsl(t)], in_=tmpf[:, s, :]),
                                     reads=[("tmpf", s)], writes=[("yT", c, t)], dma=True, slot=("out", "tmpf", s)))
                else:
                    P.op("act", lambda e, s=s, c=c, t=t, l=l, i=i, n=n: e.activation(
                        out=hT[:, c, tsl(t)], in_=tmpf[:, s, :], func=AF.Identity,
                        bias=mcol(l, 3 * i, c, n), scale=1.0),
                        reads=[("tmpf", s), "modS"], writes=[("hT", c, t)])

    pending = []

    def norm_items(l, i, t, final=False):
        items = []
        n = cond_of(t)
        bank = 7

        def sq_item(c):
            def f():
                s = rot("sq")
                P.op("act", lambda e: e.activation(out=sq[:, s, :], in_=xT[:, c, tsl(t)], func=AF.Square),
                     reads=[("xT", c, t)], writes=[("sq", s)])
                P.op("pe", lambda e: e.matmul(ps[bank][:], lhsT=onesB[:], rhs=sq[:, s, :],
                                              start=(c == 0), stop=(c == NCH - 1)),
                     reads=[("sq", s), "ones"], writes=[PK[bank]])
            return f

        def rs_item():
            P.op("act", lambda e: e.activation(out=rs[:, 0, :], in_=ps[bank][:], func=AF.Ln,
                                               bias=epsT[:, 0:1], scale=1.0 / D),
                 reads=[PK[bank], "eps"], writes=[("rs", 0)])
            P.op("act", lambda e: e.activation(out=rs[:, 1, :], in_=rs[:, 0, :], func=AF.Exp, scale=-0.5),
                 reads=[("rs", 0)], writes=[("rs", 1)])

        def out_item(c):
            def f():
                s = rot("tmpf")
                g = finalT[:, c:c + 1] if final else mcol(l, 3 * i + 1, c, n)
                P.op("dve", lambda e: e.scalar_tensor_tensor(
                    out=tmpf[:, s, :], in0=xT[:, c, tsl(t)], scalar=g, in1=rs[:, 1, :],
                    op0=ALU.mult, op1=ALU.mult),
                    reads=[("xT", c, t), ("rs", 1), "modS", "finalT"], writes=[("tmpf", s)])
                if final:
                    outs.append(P.op("sp", lambda e: e.dma_start(out=yT_d[:, c, tsl(t)], in_=tmpf[:, s, :]),
                                     reads=[("tmpf", s)], writes=[("yT", c, t)], dma=True, slot=("out", "tmpf", s)))
                else:
                    P.op("act", lambda e: e.activation(
                        out=hT[:, c, tsl(t)], in_=tmpf[:, s, :], func=AF.Identity,
                        bias=mcol(l, 3 * i, c, n), scale=1.0),
                        reads=[("tmpf", s), "modS"], writes=[("hT", c, t)])
            return f

        for c in range(NCH):
            items.append((t, sq_item(c)))
        items.append((t, rs_item))
        for c in range(NCH):
            items.append((t, out_item(c)))
        return items

    def pop_pending(n):
        for _ in range(n):
            if pending:
                pending.pop(0)[1]()

    def flush_pending(upto_t=None):
        while pending and (upto_t is None or any(tg <= upto_t for tg, _ in pending)):
            pending.pop(0)[1]()

    def resid_update(bank, m, t, gate_ap, tok=None):
        sl = tsl(t) if tok is None else tok
        P.op("dve", lambda e: e.scalar_tensor_tensor(
            out=xT[:, m, sl], in0=ps[bank][:, 0:(sl.stop - sl.start)], scalar=gate_ap, in1=xT[:, m, sl],
            op0=ALU.mult, op1=ALU.add),
            reads=[PK[bank], ("xT", m, t), "modS"], writes=[("xT", m, t)])

    def ffn_phase(l, i, pre_normed=False, next_norm=None, skip_fence=False, pre_work=None, carry=False,
                  tail_order=(0, 1, 2, 3), tblocks=(0, 1, 2, 3)):
        if not skip_fence:
            P.fence()
        cv.reset(COMMON)
        NB = 3
        wgb = [cv.take(NCH * 512, BF16).rearrange("p (c f) -> p c f", c=NCH) for _ in range(NB)]
        wub = [cv.take(NCH * 512, BF16).rearrange("p (c f) -> p c f", c=NCH) for _ in range(NB)]
        wdb = [cv.take(4 * D, BF16).rearrange("p (j d) -> p j d", j=4) for _ in range(NB)]
        actb = [cv.take(4 * TB, BF16).rearrange("p (j t) -> p j t", j=4) for _ in range(2)]
        slb = [cv.take(TB, BF16) for _ in range(2)]
        sizes = [4, 4, 4, 4, 3, 3]
        f0s = [0, 4, 8, 12, 16, 19]
        kidx = 2 * i
        sub = 0 if i == 0 else 2

        def load_chunk(k):
            s = k % NB
            G = sizes[k]
            f0 = f0s[k] * 128
            srcg = wg_d[l, i, :, f0:f0 + G * 128].rearrange("(c p) f -> p c f", p=128)
            srcu = wu_d[l, i, :, f0:f0 + G * 128].rearrange("(c p) f -> p c f", p=128)
            srcd = wd_d[l, i, f0:f0 + G * 128, :].rearrange("(j p) d -> p j d", p=128)
            P.op("pool", lambda e: e.dma_start(out=wgb[s][:, :, 0:G * 128], in_=srcg), writes=[("wg", s)], dma=True)
            P.op("pool", lambda e: e.dma_start(out=wub[s][:, :, 0:G * 128], in_=srcu), writes=[("wu", s)], dma=True)
            P.op("pool", lambda e: e.dma_start(out=wdb[s][:, 0:G, :], in_=srcd), writes=[("wd", s)], dma=True)

        gu_cnt = [0]

        def gate_up(k, t):
            s = k % NB
            G = sizes[k]
            ab = (gu_cnt[0]) % 2
            gu_cnt[0] += 1
            for j in range(G):
                bg = j % 2
                bu = 2 + j % 2
                for c in range(NCH):
                    P.op("pe", lambda e, j=j, c=c, bg=bg: e.matmul(
                        ps[bg][:], lhsT=wgb[s][:, c, j * 128:(j + 1) * 128], rhs=hT[:, c, tsl(t)],
                        start=(c == 0), stop=(c == NCH - 1)),
                        reads=[("wg", s), ("hT", c, t)], writes=[PK[bg]])
                for c in range(NCH):
                    P.op("pe", lambda e, j=j, c=c, bu=bu: e.matmul(
                        ps[bu][:], lhsT=wub[s][:, c, j * 128:(j + 1) * 128], rhs=hT[:, c, tsl(t)],
                        start=(c == 0), stop=(c == NCH - 1)),
                        reads=[("wu", s), ("hT", c, t)], writes=[PK[bu]])
                sl_i = rot("slb")
                P.op("act", lambda e, bg=bg, sl_i=sl_i: e.activation(out=slb[sl_i], in_=ps[bg][:], func=AF.Silu),
                     reads=[PK[bg]], writes=[("slb", sl_i)])
                P.op("dve", lambda e, bu=bu, sl_i=sl_i, j=j, ab=ab: e.tensor_tensor(
                    out=actb[ab][:, j, :], in0=ps[bu][:], in1=slb[sl_i], op=ALU.mult),
                    reads=[PK[bu], ("slb", sl_i)], writes=[("act", ab, j)])
                pop_pending(2)
            return ab

        def down(k, t, ab):
            s = k % NB
            G = sizes[k]
            n = cond_of(t)
            for m in range(NCH):
                by = 4 + m % 3
                for j in range(G):
                    P.op("pe", lambda e, j=j, m=m, by=by: e.matmul(
                        ps[by][:], lhsT=wdb[s][:, j, m * 128:(m + 1) * 128], rhs=actb[ab][:, j, :],
                        start=(j == 0), stop=(j == G - 1)),
                        reads=[("wd", s), ("act", ab, j)], writes=[PK[by]])
                resid_update(by, m, t, mcol(l, 3 * sub + 2, m, n))
                pop_pending(1)

        if pre_work is not None:
            tiles = [(wgb[2][:, :, i_ * 128:(i_ + 1) * 128], ("wg", 2)) for i_ in range(4)] + \
                    [(wub[2][:, :, i_ * 128:(i_ + 1) * 128], ("wu", 2)) for i_ in range(4)]
            pre_work(lambda: (load_chunk(0), load_chunk(1)), tiles)
        if not pre_normed:
            for t in tblocks:
                pending.extend(norm_items(l, sub, t))
            flush_pending(tblocks[0])
        nk_ = len(sizes)
        work = [(k, t) for k in range(nk_ - 1) for t in tblocks]
        if pre_work is None:
            load_chunk(0)
            load_chunk(1)
        prev = None
        for idx, (k, t) in enumerate(work):
            if k == 0:
                flush_pending(t)
            ab = gate_up(k, t)
            if prev is not None:
                down(*prev)
            if t == tblocks[0] and k >= 1 and k + 1 < nk_:
                load_chunk(k + 1)
                if l == 0 and i == 1:
                    for a in range(2):
                        q = k - 1
                        P.op("pool", lambda e, a=a, q=q: e.dma_start(out=dftS_bf[a, q * 1024:(q + 1) * 1024, :],
                                                                     in_=dftS_d[a, q * 1024:(q + 1) * 1024, :]),
                             writes=[("dftS_bf", a, q)], dma=True, slot="dftcast")
            prev = (k, t, ab)
        k = nk_ - 1
        for t in [t_ for t_ in tail_order if t_ in tblocks]:
            ab = gate_up(k, t)
            if prev is not None:
                down(*prev)
                prev = None
            down(k, t, ab)
            if next_norm is not None:
                pending.extend(norm_items(next_norm[0], next_norm[1], t, final=next_norm[2]))
        if not carry:
            flush_pending()

    def mla_phase(l, pre_normed=False):
        jl = l // 2
        P.fence()
        flush_pending()
        cv.reset(COMMON)
        cqT_s = cv.take(4 * 1024, BF16).rearrange("p (c t) -> p c t", c=4)
        ckvT_s = cv.take(2 * 1024, BF16).rearrange("p (c t) -> p c t", c=2)
        kpeT_s = cv.take(1024, BF16)
        ropeT = cv.take(2 * 1024, F32).rearrange("p (a t) -> p a t", a=2)
        SHARED_S = cv.off
        cqT_p = cv.take(4 * 1024, BF16).rearrange("p (c t) -> p c t", c=4)
        ckvT_p = cv.take(2 * 1024, BF16).rearrange("p (c t) -> p c t", c=2)
        kpeT_p = cv.take(1024, BF16)
        SHARED_P = cv.off

        def lsl(t):
            return slice((t % 2) * TB, (t % 2 + 1) * TB)

        def cq(kc, t):
            return (cqT_p if t < 2 else cqT_s)[:, kc, lsl(t)]

        def ckv(mc, t):
            return (ckvT_p if t < 2 else ckvT_s)[:, mc, lsl(t)]

        def kpe(t):
            return (kpeT_p if t < 2 else kpeT_s)[0:64, lsl(t)]

        wdq = cv.take(NCH * 512, BF16).rearrange("p (c f) -> p c f", c=NCH)
        wdkv = cv.take(NCH * 384, BF16).rearrange("p (c f) -> p c f", c=NCH)
        cqraw = cv.take(4 * TB, F32).rearrange("p (c t) -> p c t", c=4)
        ckvf = [cv.take(2 * TB, F32).rearrange("p (c t) -> p c t", c=2) for _ in range(2)]
        kpef = [cv.take(TB, F32) for _ in range(2)]
        if not pre_normed:
            norm_phase(l, 1)
        P.op("pool", lambda e: e.dma_start(out=wdq, in_=wdq_d[jl].rearrange("(c p) f -> p c f", p=128)),
             writes=["wdq"], dma=True)
        P.op("pool", lambda e: e.dma_start(out=wdkv[:, :, 0:320], in_=wdkv_d[jl].rearrange("(c p) f -> p c f", p=128)),
             writes=["wdkv0"], dma=True)
        P.op("pool", lambda e: e.dma_start(out=wdkv[:, :, 320:384], in_=wdkvs_d[jl].rearrange("(c p) f -> p c f", p=128)),
             writes=["wdkv1"], dma=True)
        for a in range(2):
            P.op("sp", lambda e, a=a: e.dma_start(out=ropeT[0:64, a, :], in_=rope_d[a]), writes=[("rope", a)], dma=True)

        def stage_a(t):
            for mc in range(4):
                for c in range(NCH):
                    P.op("pe", lambda e, mc=mc, c=c: e.matmul(
                        ps[mc][:], lhsT=wdq[:, c, mc * 128:(mc + 1) * 128], rhs=hT[:, c, tsl(t)],
                        start=(c == 0), stop=(c == NCH - 1)),
                        reads=["wdq", ("hT", c, t)], writes=[PK[mc]])
                P.op("act", lambda e, mc=mc: e.activation(out=cqraw[:, mc, :], in_=ps[mc][:], func=AF.Identity),
                     reads=[PK[mc]], writes=[("cqraw", mc)])
            rstd_block(None, 4, lambda c: (cqraw[:, c, :], ("cqraw", c)), 1.0 / 512, 7)
            for mc in range(4):
                P.op("dve", lambda e, mc=mc: e.scalar_tensor_tensor(
                    out=cq(mc, t), in0=cqraw[:, mc, :], scalar=qnT[:, jl, mc:mc + 1], in1=rs[:, 1, :],
                    op0=ALU.mult, op1=ALU.mult),
                    reads=[("cqraw", mc), ("rs", 1), "qnT"], writes=[("cqT", mc, t)])
            fb = t % 2
            for mc in range(2):
                for c in range(NCH):
                    P.op("pe", lambda e, mc=mc, c=c: e.matmul(
                        ps[4 + mc][:], lhsT=wdkv[:, c, mc * 128:(mc + 1) * 128], rhs=hT[:, c, tsl(t)],
                        start=(c == 0), stop=(c == NCH - 1)),
                        reads=["wdkv0", ("hT", c, t)], writes=[PK[4 + mc]])
                P.op("act", lambda e, mc=mc: e.activation(out=cqraw[:, mc, :], in_=ps[4 + mc][:], func=AF.Identity),
                     reads=[PK[4 + mc]], writes=[("cqraw", mc)])
            rstd_block(None, 2, lambda c: (cqraw[:, c, :], ("cqraw", c)), 1.0 / 256, 7)
            for mc in range(2):
                if t < 2:
                    P.op("dve", lambda e, mc=mc: e.scalar_tensor_tensor(
                        out=ckvf[fb][:, mc, :], in0=cqraw[:, mc, :], scalar=kvnT[:, jl, mc:mc + 1], in1=rs[:, 1, :],
                        op0=ALU.mult, op1=ALU.mult),
                        reads=[("cqraw", mc), ("rs", 1), "kvnT"], writes=[("ckvf", fb, mc)])
                    P.op("act", lambda e, mc=mc: e.activation(out=ckv(mc, t), in_=ckvf[fb][:, mc, :], func=AF.Identity),
                         reads=[("ckvf", fb, mc)], writes=[("ckvT", mc, t)])
                else:
                    P.op("dve", lambda e, mc=mc: e.scalar_tensor_tensor(
                        out=ckv(mc, t), in0=cqraw[:, mc, :], scalar=kvnT[:, jl, mc:mc + 1], in1=rs[:, 1, :],
                        op0=ALU.mult, op1=ALU.mult),
                        reads=[("cqraw", mc), ("rs", 1), "kvnT"], writes=[("ckvT", mc, t)])
            if t < 2:
                outs.append(P.op("sp", lambda e: e.dma_start(
                    out=ockv_d[jl][:, tsl(t)].rearrange("(c p) t -> p c t", p=128), in_=ckvf[fb]),
                    reads=[("ckvf", fb, 0), ("ckvf", fb, 1)], writes=[("ockv", jl, t)], dma=True,
                    slot=("out", "ckvf", fb)))
            for c in range(NCH):
                P.op("pe", lambda e, c=c: e.matmul(
                    ps[6][0:64, :], lhsT=wdkv[:, c, 256:320], rhs=hT[:, c, tsl(t)],
                    start=(c == 0), stop=(c == NCH - 1)),
                    reads=["wdkv0", ("hT", c, t)], writes=[PK[6]])
            if t < 2:
                P.op("act", lambda e: e.activation(out=kpef[fb][0:64, :], in_=ps[6][0:64, :], func=AF.Identity),
                     reads=[PK[6]], writes=[("kpef", fb)])
                P.op("dve", lambda e: e.tensor_copy(out=kpe(t), in_=kpef[fb][0:64, :]),
                     reads=[("kpef", fb)], writes=[("kpeT", t)])
                outs.append(P.op("sp", lambda e: e.dma_start(out=okpe_d[jl][:, tsl(t)], in_=kpef[fb][0:64, :]),
                                 reads=[("kpef", fb)], writes=[("okpe", jl, t)], dma=True,
                                 slot=("out", "kpef", fb)))
            else:
                for c in range(NCH):
                    P.op("pe", lambda e, c=c: e.matmul(
                        ps[3][0:64, :], lhsT=wdkv[:, c, 320:384], rhs=hT[:, c, tsl(t)],
                        start=(c == 0), stop=(c == NCH - 1)),
                        reads=["wdkv1", ("hT", c, t)], writes=[PK[3]])
                tok = lsl(t)
                P.op("dve", lambda e: e.tensor_tensor(out=tmpf[0:64, 0, :], in0=ps[6][0:64, :],
                                                      in1=ropeT[0:64, 0, tok], op=ALU.mult),
                     reads=[PK[6], ("rope", 0)], writes=[("tmpf", 0)])
                P.op("dve", lambda e: e.tensor_tensor(out=tmpf[0:64, 1, :], in0=ps[3][0:64, :],
                                                      in1=ropeT[0:64, 1, tok], op=ALU.mult),
                     reads=[PK[3], ("rope", 1)], writes=[("tmpf", 1)])
                P.op("dve", lambda e: e.tensor_tensor(out=kpe(t), in0=tmpf[0:64, 0, :],
                                                      in1=tmpf[0:64, 1, :], op=ALU.add),
                     reads=[("tmpf", 0), ("tmpf", 1)], writes=[("kpeT", t)])

        for t in (2, 3, 0, 1):
            stage_a(t)
            if t == 3:
                P.op("sp", lambda e: e.dma_start(out=x_in[jl][0:256, :].rearrange("(c p) t -> p c t", p=128),
                                                 in_=ckvT_s),
                     reads=[("ckvT", 0, 2), ("ckvT", 0, 3), ("ckvT", 1, 2), ("ckvT", 1, 3)],
                     writes=[("x_in", jl, 0)], dma=True, slot=("grp", "x_in", jl))
                P.op("sp", lambda e: e.dma_start(out=x_in[jl][256:320, :], in_=kpeT_s[0:64, :]),
                     reads=[("kpeT", 2), ("kpeT", 3)], writes=[("x_in", jl, 1)], dma=True, slot=("grp", "x_in", jl))
                P.op("pool", lambda e: e.collective_compute("AllGather", ALU.bypass, replica_groups=GROUPS,
                                                            ins=[x_in[jl]], outs=[x_out[jl]]),
                     reads=[("x_in", jl, 0), ("x_in", jl, 1)], writes=[("x_out", jl)], dma=True, inc=1)

        def attention(stream):
            P.fence()
            cv.reset(SHARED_P if stream == 0 else SHARED_S)
            nk = 1024 if stream == 0 else 4608
            nkt = nk // 128
            tok0 = 0 if stream == 0 else 1024
            if stream == 1:
                ckv_all = cv.take(2 * nk, BF16).rearrange("p (c t) -> p c t", c=2)
                kpe_all = cv.take(nk, BF16)
            kn = cv.take(nk, BF16)
            Vh = cv.take(nk, BF16).rearrange("p (k d) -> p k d", d=128)
            qn = cv.take(1024, BF16)
            qr = cv.take(1024, BF16)
            Pt = [cv.take(TB, BF16) for _ in range(4)]
            dacc = [cv.take(TB, F32) for _ in range(2)]
            hw_q = [cv.take(4 * 256, BF16).rearrange("p (c f) -> p c f", c=4) for _ in range(2)]
            hw_kv = [cv.take(2 * 256, BF16).rearrange("p (c f) -> p c f", c=2) for _ in range(2)]
            wob = [cv.take(8 * 128, BF16).rearrange("p (h d) -> p h d", h=8) for _ in range(2)]
            rden = cv.take(TB, F32)
            P.op("dve", lambda e: e.memset(qr[64:128, :], 0.0), writes=["qr_pad"])
            if stream == 1:
                P.op("dve", lambda e: e.memset(kpe_all[64:128, :], 0.0), writes=["kpe_pad"])
            else:
                P.op("dve", lambda e: e.memset(kpeT_p[64:128, :], 0.0), writes=["kpe_pad"])
            if stream == 1:
                P.op("pool", lambda e: e.dma_start(out=ckv_all[:, :, 0:512],
                                                   in_=cckv_d[jl].rearrange("(c p) t -> p c t", p=128)),
                     writes=[("ckv_all", 0)], dma=True, slot=("grp", "kvall_p", jl))
                P.op("pool", lambda e: e.dma_start(out=kpe_all[0:64, 0:512], in_=ckpe_d[jl]),
                     writes=[("kpe_all", 0)], dma=True, slot=("grp", "kvall_p", jl))
                for r in range(4):
                    P.op("sp", lambda e, r=r: e.dma_start(
                        out=ckv_all[:, :, 512 + r * 1024:512 + (r + 1) * 1024],
                        in_=x_out[jl][r * 320:r * 320 + 256, :].rearrange("(c p) t -> p c t", p=128)),
                        reads=[("x_out", jl)], writes=[("ckv_all", 1 + r)], dma=True, slot=("grp", "kvall", jl))
                    P.op("sp", lambda e, r=r: e.dma_start(
                        out=kpe_all[0:64, 512 + r * 1024:512 + (r + 1) * 1024],
                        in_=x_out[jl][r * 320 + 256:(r + 1) * 320, :]),
                        reads=[("x_out", jl)], writes=[("kpe_all", 1 + r)], dma=True, slot=("grp", "kvall", jl))
                ckvsrc, kpesrc = ckv_all, kpe_all
                ckv_keys = [("ckv_all", r) for r in range(5)]
                kpe_keys = [("kpe_all", r) for r in range(5)]
            else:
                ckvsrc, kpesrc = ckvT_p, kpeT_p
                ckv_keys = [("ckvT", mc, t) for mc in range(2) for t in range(2)]
                kpe_keys = [("kpeT", 0), ("kpeT", 1)]

            gcnt = {"tile": 0}

            def head_tiles(h, qblocks):
                flat = [(qi, ki) for qi, (q0, qn_, kts) in enumerate(qblocks) for ki in range(len(kts))]
                info = {}

                def s_mm(qi, ki):
                    q0, qn_, kts = qblocks[qi]
                    kt = kts[ki]
                    sbk = gcnt["tile"] % 3
                    gcnt["tile"] += 1
                    da = dacc[qi % 2]
                    P.op("pe", lambda e: e.matmul(
                        ps[sbk][:, 0:qn_], lhsT=kn[:, kt * 128:(kt + 1) * 128], rhs=qn[:, q0:q0 + qn_],
                        start=True, stop=False),
                        reads=[("kn", kt // 4), ("qn", q0 // TB)], writes=[PK[sbk]])
                    P.op("pe", lambda e: e.matmul(
                        ps[sbk][:, 0:qn_], lhsT=kpesrc[:, kt * 128:(kt + 1) * 128], rhs=qr[:, q0:q0 + qn_],
                        start=False, stop=True),
                        reads=kpe_keys + [("qr", q0 // TB), "qr_pad", "kpe_pad"], writes=[PK[sbk]])
                    pi = rot("Pt", 4)
                    info[(qi, ki)] = pi
                    P.op("act", lambda e: e.activation(out=Pt[pi][:, 0:qn_], in_=ps[sbk][:, 0:qn_], func=AF.Exp),
                         reads=[PK[sbk]], writes=[("Pt", pi)])
                    if ki == 0:
                        P.op("dve", lambda e: e.tensor_copy(out=da[:, 0:qn_], in_=Pt[pi][:, 0:qn_]),
                             reads=[("Pt", pi)], writes=[("dacc", qi % 2)])
                    else:
                        P.op("dve", lambda e: e.tensor_tensor(out=da[:, 0:qn_], in0=da[:, 0:qn_], in1=Pt[pi][:, 0:qn_],
                                                              op=ALU.add),
                             reads=[("Pt", pi), ("dacc", qi % 2)], writes=[("dacc", qi % 2)])

                def pv_mm(qi, ki):
                    q0, qn_, kts = qblocks[qi]
                    kt = kts[ki]
                    pi = info[(qi, ki)]
                    ob = 4 + (qi % 2)
                    nkt_ = len(kts)
                    P.op("pe", lambda e: e.matmul(
                        ps[ob][:, 0:qn_], lhsT=Vh[:, kt, :], rhs=Pt[pi][:, 0:qn_],
                        start=(ki == 0), stop=(ki == nkt_ - 1)),
                        reads=[("Vh", kt // 4), ("Pt", pi)], writes=[PK[ob]])
                    if ki == nkt_ - 1:
                        db = 6 + (qi % 2)
                        da = dacc[qi % 2]
                        tq = (tok0 + q0) // TB
                        P.op("pe", lambda e: e.matmul(ps[db][:, 0:qn_], lhsT=onesF[:], rhs=da[:, 0:qn_],
                                                      start=True, stop=True),
                             reads=["onesF", ("dacc", qi % 2)], writes=[PK[db]])
                        P.op("act", lambda e: e.activation(out=rden[:, 0:qn_], in_=ps[db][:, 0:qn_], func=AF.Ln),
                             reads=[PK[db]], writes=["rden"])
                        P.op("act", lambda e: e.activation(out=rden[:, 0:qn_], in_=rden[:, 0:qn_], func=AF.Exp, scale=-1.0),
                             reads=["rden"], writes=["rden"])
                        P.op("dve", lambda e: e.tensor_tensor(out=hT[:, h, tok0 + q0:tok0 + q0 + qn_], in0=ps[ob][:, 0:qn_],
                                                              in1=rden[:, 0:qn_], op=ALU.mult),
                             reads=[PK[ob], "rden"], writes=[("hT", h, tq)])

                depth_ = 2
                for idx in range(len(flat) + depth_):
                    if idx < len(flat):
                        s_mm(*flat[idx])
                    if idx >= depth_:
                        pv_mm(*flat[idx - depth_])

            def head(h):
                hb = h % 2
                P.op("pool", lambda e: e.dma_start(
                    out=hw_q[hb][:, :, 0:192], in_=wuq_d[jl][:, h * 192:(h + 1) * 192].rearrange("(c p) f -> p c f", p=128)),
                    writes=[("hwq0", hb)], dma=True)
                P.op("pool", lambda e: e.dma_start(
                    out=hw_q[hb][:, :, 192:256], in_=wuqs_d[jl][:, h * 64:(h + 1) * 64].rearrange("(c p) f -> p c f", p=128)),
                    writes=[("hwq1", hb)], dma=True)
                P.op("pool", lambda e: e.dma_start(
                    out=hw_kv[hb], in_=wukv_d[jl][:, h * 256:(h + 1) * 256].rearrange("(c p) f -> p c f", p=128)),
                    writes=[("hwkv", hb)], dma=True)
                for tb in range(2):
                    t = (tok0 // TB) + tb
                    for kc in range(4):
                        P.op("pe", lambda e, kc=kc, t=t: e.matmul(
                            ps[0][:], lhsT=hw_q[hb][:, kc, 0:128], rhs=cq(kc, t),
                            start=(kc == 0), stop=(kc == 3)),
                            reads=[("hwq0", hb), ("cqT", kc, t)], writes=[PK[0]])
                    P.op("act", lambda e, tb=tb: e.activation(out=qn[:, tb * TB:(tb + 1) * TB], in_=ps[0][:],
                                                              func=AF.Identity, scale=ATTN_SCALE),
                         reads=[PK[0]], writes=[("qn", tb)])
                    for kc in range(4):
                        P.op("pe", lambda e, kc=kc, t=t: e.matmul(
                            ps[1][0:64, :], lhsT=hw_q[hb][:, kc, 128:192], rhs=cq(kc, t),
                            start=(kc == 0), stop=(kc == 3)),
                            reads=[("hwq0", hb), ("cqT", kc, t)], writes=[PK[1]])
                    if stream == 0:
                        P.op("act", lambda e, tb=tb: e.activation(out=qr[0:64, tb * TB:(tb + 1) * TB], in_=ps[1][0:64, :],
                                                                  func=AF.Identity, scale=ATTN_SCALE),
                             reads=[PK[1]], writes=[("qr", tb)])
                    else:
                        for kc in range(4):
                            P.op("pe", lambda e, kc=kc, t=t: e.matmul(
                                ps[2][0:64, :], lhsT=hw_q[hb][:, kc, 192:256], rhs=cq(kc, t),
                                start=(kc == 0), stop=(kc == 3)),
                                reads=[("hwq1", hb), ("cqT", kc, t)], writes=[PK[2]])
                        tok = slice(tb * TB, (tb + 1) * TB)
                        P.op("dve", lambda e, tok=tok: e.tensor_tensor(out=tmpf[0:64, 0, :], in0=ps[1][0:64, :],
                                                                       in1=ropeT[0:64, 0, tok], op=ALU.mult),
                             reads=[PK[1], ("rope", 0)], writes=[("tmpf", 0)])
                        P.op("dve", lambda e, tok=tok: e.tensor_tensor(out=tmpf[0:64, 1, :], in0=ps[2][0:64, :],
                                                                       in1=ropeT[0:64, 1, tok], op=ALU.mult),
                             reads=[PK[2], ("rope", 1)], writes=[("tmpf", 1)])
                        P.op("dve", lambda e: e.tensor_tensor(out=tmpf[0:64, 0, :], in0=tmpf[0:64, 0, :],
                                                              in1=tmpf[0:64, 1, :], op=ALU.add),
                             reads=[("tmpf", 0), ("tmpf", 1)], writes=[("tmpf", 0)])
                        P.op("act", lambda e, tb=tb: e.activation(out=qr[0:64, tb * TB:(tb + 1) * TB], in_=tmpf[0:64, 0, :],
                                                                  func=AF.Identity, scale=ATTN_SCALE),
                             reads=[("tmpf", 0)], writes=[("qr", tb)])
                for kb in range(nk // TB):
                    bank = 2 + kb % 2
                    for kc in range(2):
                        P.op("pe", lambda e, kc=kc, kb=kb, bank=bank: e.matmul(
                            ps[bank][:], lhsT=hw_kv[hb][:, kc, 0:128], rhs=ckvsrc[:, kc, kb * TB:(kb + 1) * TB],
                            start=(kc == 0), stop=(kc == 1)),
                            reads=[("hwkv", hb)] + ckv_keys, writes=[PK[bank]])
                    if kb % 2 == 0:
                        P.op("act", lambda e, kb=kb, bank=bank: e.activation(out=kn[:, kb * TB:(kb + 1) * TB], in_=ps[bank][:], func=AF.Identity),
                             reads=[PK[bank]], writes=[("kn", kb)])
                    else:
                        P.op("dve", lambda e, kb=kb, bank=bank: e.tensor_copy(out=kn[:, kb * TB:(kb + 1) * TB], in_=ps[bank][:]),
                             reads=[PK[bank]], writes=[("kn", kb)])
                for vb in range(nkt // 4):
                    bank = 2 + vb % 2
                    for q4 in range(4):
                        kt = vb * 4 + q4
                        for kc in range(2):
                            P.op("pe", lambda e, kc=kc, kt=kt, q4=q4, bank=bank: e.matmul(
                                ps[bank][:, q4 * 128:(q4 + 1) * 128], lhsT=ckvsrc[:, kc, kt * 128:(kt + 1) * 128],
                                rhs=hw_kv[hb][:, kc, 128:256], start=(kc == 0), stop=(kc == 1)),
                                reads=[("hwkv", hb)] + ckv_keys, writes=[PK[bank]])
                    vdst = Vh[:, vb * 4:(vb + 1) * 4, :]
                    vsrc = ps[bank][:].rearrange("p (k d) -> p k d", d=128)
                    if vb % 2 == 0:
                        P.op("dve", lambda e, vdst=vdst, vsrc=vsrc: e.tensor_copy(out=vdst, in_=vsrc),
                             reads=[PK[bank]], writes=[("Vh", vb)])
                    else:
                        P.op("act", lambda e, vdst=vdst, vsrc=vsrc: e.activation(out=vdst, in_=vsrc, func=AF.Identity),
                             reads=[PK[bank]], writes=[("Vh", vb)])
                if stream == 0:
                    qblocks = [(s * 256, 256, [2 * s, 2 * s + 1]) for s in range(4)]
                else:
                    qblocks = [(qb * TB, TB, list(range(nkt))) for qb in range(2)]
                head_tiles(h, qblocks)

            for h in range(8):
                head(h)
            if stream == 0:
                wo_stage(0, [(w_, None) for w_ in wob])

        def wo_stage(stream, wob, mid_loads=None):
            n = 0 if stream == 0 else 1
            tok0 = 0 if stream == 0 else 1024
            nb_ = len(wob)

            def ld(m):
                wb_i = m % nb_
                P.op("pool", lambda e: e.dma_start(
                    out=wob[wb_i][0], in_=wo_d[jl][:, m * 128:(m + 1) * 128].rearrange("(h p) d -> p h d", p=128)),
                    writes=[("wob", wb_i)], dma=True)

            for m in range(min(nb_, NCH)):
                ld(m)
            if mid_loads is not None:
                mid_loads()
            for m in range(NCH):
                wb_i = m % nb_
                for tb in range(2):
                    t = tok0 // TB + tb
                    bank = (m * 2 + tb) % 4
                    for hh in range(8):
                        P.op("pe", lambda e, hh=hh, t=t, bank=bank, wb_i=wb_i: e.matmul(
                            ps[bank][:], lhsT=wob[wb_i][0][:, hh, :], rhs=hT[:, hh, tsl(t)],
                            start=(hh == 0), stop=(hh == 7)),
                            reads=[("wob", wb_i), ("hT", hh, t)] + ([wob[wb_i][1]] if wob[wb_i][1] else []),
                            writes=[PK[bank]])
                    resid_update(bank, m, t, mcol(l, 5, m, n))
                if m + nb_ < NCH:
                    ld(m + nb_)

        attention(0)
        attention(1)

        def pre_work(mid_loads, tiles):
            wo_stage(1, tiles, mid_loads)
        return pre_work

    def fourier_phase(l, pre_normed=False):
        jl = l // 2
        P.fence()
        flush_pending()
        cv.reset(COMMON)
        cs = cv.take(2 * 512, BF16).rearrange("p (c f) -> p c f", c=2)
        dp = cv.take(2 * 2 * 256, BF16).rearrange("p (a n k) -> p a n k", a=2, n=2)
        ABp = cv.take(8 * 2048, BF16).rearrange("p (t f) -> p t f", t=8)
        ABs = cv.take(8 * 2048, BF16).rearrange("p (t f) -> p t f", t=8)
        NSB = 3
        abn = [cv.take(2048, BF16) for _ in range(NSB)]
        tbn = [cv.take(2 * 512, BF16).rearrange("p (a k) -> p a k", a=2) for _ in range(NSB)]
        fwm = [cv.take(NCH * 128, BF16).rearrange("p (c d) -> p c d", c=NCH) for _ in range(2)]
        if not pre_normed:
            norm_phase(l, 1)
        P.op("pool", lambda e: e.dma_start(out=cs, in_=dftC_d.rearrange("(c p) f -> p c f", p=128)), writes=["cs"], dma=True)
        for a in range(2):
            P.op("pool", lambda e, a=a: e.dma_start(out=dp[:, a], in_=dftP_d[a].rearrange("(n p) k -> p n k", p=128)),
                 writes=[("dp", a)], dma=True)
        for tile in list(range(8, 16)) + list(range(8)):
            dst = ABs if tile >= 8 else ABp
            ti = tile % 8
            t = tile // 4
            for g in range(4):
                bank = g % 4
                for kc in range(2):
                    P.op("pe", lambda e, g=g, kc=kc, tile=tile, bank=bank: e.matmul(
                        ps[bank][:], lhsT=hT[:, 2 * g + kc, tile * 128:(tile + 1) * 128], rhs=cs[:, kc, :],
                        start=(kc == 0), stop=(kc == 1)),
                        reads=["cs", ("hT", 2 * g + kc, t)], writes=[PK[bank]])
                dv = dst[:, ti, :].rearrange("p (s g c) -> p s g c", s=2, g=4)[:, :, g, :]
                sv = ps[bank][:].rearrange("p (s c) -> p s c", s=2)
                key = ("AB", tile)
                if g % 2 == 0:
                    P.op("act", lambda e, dv=dv, sv=sv: e.activation(out=dv, in_=sv, func=AF.Identity),
                         reads=[PK[bank]], writes=[(key, g)])
                else:
                    P.op("dve", lambda e, dv=dv, sv=sv: e.tensor_copy(out=dv, in_=sv),
                         reads=[PK[bank]], writes=[(key, g)])
            if tile >= 8 and tile % 2 == 1:
                part = (tile - 8) // 2
                P.op("sp", lambda e, part=part: e.dma_start(
                    out=f_in[jl][part].rearrange("(t p) f -> p t f", p=128), in_=ABs[:, 2 * part:2 * part + 2, :]),
                    reads=[(("AB", tl), g) for tl in (tile - 1, tile) for g in range(4)],
                    writes=[("f_in", jl, part)], dma=True)
                P.op("pool", lambda e, part=part: e.collective_compute(
                    "AllGather", ALU.bypass, replica_groups=GROUPS, ins=[f_in[jl][part]], outs=[f_out[jl][part]]),
                    reads=[("f_in", jl, part)], writes=[("f_out", jl, part)], dma=True, inc=1)
        for s in range(4):
            t = s // 2
            for m in range(NCH):
                bank = m % 4
                first = True
                for nt in range(2):
                    for a in range(2):
                        P.op("pe", lambda e, nt=nt, a=a, m=m, s=s, bank=bank, first=first: e.matmul(
                            ps[bank][:, 0:256], lhsT=ABp[:, s * 2 + nt, a * 1024 + m * 128:a * 1024 + (m + 1) * 128],
                            rhs=dp[:, a, nt, :], start=first, stop=(nt == 1 and a == 1)),
                            reads=[(("AB", s * 2 + nt), gg) for gg in range(4)] + [("dp", a)], writes=[PK[bank]])
                        first = False
                P.op("act", lambda e, m=m, s=s, bank=bank: e.activation(
                    out=hT[:, m, s * 256:(s + 1) * 256], in_=ps[bank][:, 0:256], func=AF.Identity, scale=1.0 / 256.0),
                    reads=[PK[bank]], writes=[("hT", m, t)])
        def fc_stage(tblocks, fwbuf, mid_loads=None):
            nb_ = len(fwbuf)

            def ld(m):
                wi = m % nb_
                P.op("pool", lambda e: e.dma_start(
                    out=fwbuf[wi][0], in_=fw_d[jl][:, m * 128:(m + 1) * 128].rearrange("(c p) d -> p c d", p=128)),
                    writes=[("fwm", wi)], dma=True)

            for m in range(min(nb_, NCH)):
                ld(m)
            if mid_loads is not None:
                mid_loads()
            for m in range(NCH):
                wi = m % nb_
                if m >= nb_ and False:
                    pass
                for t in tblocks:
                    n = cond_of(t)
                    bank = (m * 2 + t) % 4
                    for c in range(NCH):
                        P.op("pe", lambda e, c=c, t=t, bank=bank, wi=wi: e.matmul(
                            ps[bank][:], lhsT=fwbuf[wi][0][:, c, :], rhs=hT[:, c, tsl(t)],
                            start=(c == 0), stop=(c == NCH - 1)),
                            reads=[("fwm", wi), ("hT", c, t)] + ([fwbuf[wi][1]] if fwbuf[wi][1] else []),
                            writes=[PK[bank]])
                    s_ = rot("tmpf")
                    P.op("dve", lambda e, s_=s_, m=m, bank=bank, n=n: e.tensor_scalar(
                        out=tmpf[:, s_, :], in0=ps[bank][:], scalar1=fbT[:, jl, m:m + 1], scalar2=mcol(l, 5, m, n),
                        op0=ALU.add, op1=ALU.mult),
                        reads=[PK[bank], "fbT", "modS"], writes=[("tmpf", s_)])
                    P.op("dve", lambda e, s_=s_, m=m, t=t: e.tensor_tensor(out=xT[:, m, tsl(t)], in0=xT[:, m, tsl(t)],
                                                                           in1=tmpf[:, s_, :], op=ALU.add),
                         reads=[("tmpf", s_), ("xT", m, t)], writes=[("xT", m, t)])
                if m + nb_ < NCH:
                    ld(m + nb_)

        def pre_work1(mid_loads, tiles):
            fc_stage((0, 1), tiles, mid_loads)
        return pre_work1

    def fourier_part2(l):
        jl = l // 2
        P.fence()
        flush_pending()
        cv.reset(COMMON)
        NSB = 3
        abn = [cv.take(2048, BF16) for _ in range(NSB)]
        tbn = [cv.take(2 * 512, BF16).rearrange("p (a k) -> p a k", a=2) for _ in range(NSB)]

        def fc_stage(tblocks, fwbuf, mid_loads=None):
            nb_ = len(fwbuf)

            def ld(m):
                wi = m % nb_
                P.op("pool", lambda e: e.dma_start(
                    out=fwbuf[wi][0], in_=fw_d[jl][:, m * 128:(m + 1) * 128].rearrange("(c p) d -> p c d", p=128)),
                    writes=[("fwm", wi)], dma=True)

            for m in range(min(nb_, NCH)):
                ld(m)
            if mid_loads is not None:
                mid_loads()
            for m in range(NCH):
                wi = m % nb_
                for t in tblocks:
                    n = cond_of(t)
                    bank = (m * 2 + t) % 4
                    for c in range(NCH):
                        P.op("pe", lambda e, c=c, t=t, bank=bank, wi=wi: e.matmul(
                            ps[bank][:], lhsT=fwbuf[wi][0][:, c, :], rhs=hT[:, c, tsl(t)],
                            start=(c == 0), stop=(c == NCH - 1)),
                            reads=[("fwm", wi), ("hT", c, t)] + ([fwbuf[wi][1]] if fwbuf[wi][1] else []),
                            writes=[PK[bank]])
                    s_ = rot("tmpf")
                    P.op("dve", lambda e, s_=s_, m=m, bank=bank, n=n: e.tensor_scalar(
                        out=tmpf[:, s_, :], in0=ps[bank][:], scalar1=fbT[:, jl, m:m + 1], scalar2=mcol(l, 5, m, n),
                        op0=ALU.add, op1=ALU.mult),
                        reads=[PK[bank], "fbT", "modS"], writes=[("tmpf", s_)])
                    P.op("dve", lambda e, s_=s_, m=m, t=t: e.tensor_tensor(out=xT[:, m, tsl(t)], in0=xT[:, m, tsl(t)],
                                                                           in1=tmpf[:, s_, :], op=ALU.add),
                         reads=[("tmpf", s_), ("xT", m, t)], writes=[("xT", m, t)])
                if m + nb_ < NCH:
                    ld(m + nb_)

        for kb in range(2):
            t = 2 + kb
            nt_order = [r_ * 8 + part * 2 + j_ for part in range(4) for r_ in range(4) for j_ in range(2)]
            for ni, nt in enumerate(nt_order):
                b = (kb * 32 + ni) % NSB
                r_, w_ = nt // 8, nt % 8
                part, j_ = w_ // 2, w_ % 2
                src = f_out[jl][part][r_ * 256 + j_ * 128:r_ * 256 + (j_ + 1) * 128, :]
                P.op("sp", lambda e, src=src, b=b: e.dma_start(out=abn[b], in_=src),
                     reads=[("f_out", jl, part)], writes=[("abn", b)], dma=True)
                P.op("sp", lambda e, nt=nt, b=b, kb=kb: e.dma_start(
                    out=tbn[b], in_=dftS_bf[:, nt * 128:(nt + 1) * 128, kb * 512:(kb + 1) * 512].rearrange("a p k -> p a k")),
                    reads=DFT_KEYS, writes=[("tbn", b)], dma=True)
                for m in range(NCH):
                    for a in range(2):
                        P.op("pe", lambda e, ni=ni, a=a, m=m, b=b: e.matmul(
                            ps[m][:], lhsT=abn[b][:, a * 1024 + m * 128:a * 1024 + (m + 1) * 128], rhs=tbn[b][:, a, :],
                            start=(ni == 0 and a == 0), stop=(ni == 31 and a == 1)),
                            reads=[("abn", b), ("tbn", b)], writes=[PK[m]])
            for m in range(NCH):
                if m % 2 == 0:
                    P.op("act", lambda e, m=m, t=t: e.activation(out=hT[:, m, tsl(t)], in_=ps[m][:], func=AF.Identity,
                                                                 scale=1.0 / 1024.0),
                         reads=[PK[m]], writes=[("hT", m, t)])
                else:
                    P.op("dve", lambda e, m=m, t=t: e.tensor_scalar(out=hT[:, m, tsl(t)], in0=ps[m][:], scalar1=1.0 / 1024.0,
                                                                    scalar2=None, op0=ALU.mult),
                         reads=[PK[m]], writes=[("hT", m, t)])
        def pre_work(mid_loads, tiles):
            fc_stage((2, 3), tiles, mid_loads)
        return pre_work

    for l in range(depth):
        mixer_on = ("m" if l % 2 == 0 else "f") in DBG_PHASES
        ffn_phase(l, 0, pre_normed=(l > 0), next_norm=(l, 1, False) if mixer_on else None, skip_fence=(l > 0),
                  carry=mixer_on, tail_order=(2, 3, 0, 1) if mixer_on else (0, 1, 2, 3))
        last = (l == depth - 1)
        nn = (0, 0, True) if last else (l + 1, 0, False)
        if mixer_on and l % 2 == 1:
            pw1 = fourier_phase(l, pre_normed=True)
            ffn_phase(l, 1, pre_normed=False, next_norm=nn, pre_work=pw1, carry=True, tblocks=(0, 1))
            pw = fourier_part2(l)
            ffn_phase(l, 1, pre_normed=False, next_norm=nn, pre_work=pw, carry=not last, tblocks=(2, 3))
        else:
            pw = mla_phase(l, pre_normed=True) if mixer_on else None
            ffn_phase(l, 1, pre_normed=False, next_norm=nn, pre_work=pw, carry=not last)
    if depth == 0:
        P.fence()
        norm_phase(0, 0, final=True)
    P.emit(nc, final_waits=outs)
    st.close()
    return nc, P


def _fm(a):
    t = a.shape[0]
    return np.ascontiguousarray(a.T.reshape(NCH, 128, t).transpose(1, 0, 2))


def _vec(a, nch):
    sh = a.shape[:-1]
    b = a.reshape(sh + (nch, 128))
    return np.ascontiguousarray(np.moveaxis(b, -1, 0))


def _const_tables():
    f32 = np.float32
    k = np.arange(256)
    ang = 2 * np.pi * np.outer(k, k) / 256.0
    dftC = np.concatenate([np.cos(ang), np.sin(ang)], axis=1).astype(np.float32)
    dftP = np.stack([np.cos(ang), -np.sin(ang)]).astype(np.float32)
    n = np.arange(4096, dtype=np.int64)
    dftS = []
    for qd in range(4):
        kk = np.arange(qd * 1024, (qd + 1) * 1024, dtype=np.int64)
        a = 2 * np.pi * ((np.outer(n, kk) % 4096).astype(np.float64)) / 4096.0
        dftS.append(np.stack([np.cos(a), -np.sin(a)]).astype(np.float32))
    inv = 1.0 / (10000.0 ** (np.arange(16, dtype=np.float32) / 16.0))
    pos = np.arange(4096)
    row = (pos // 64).astype(np.float32)
    col = (pos % 64).astype(np.float32)
    ang = np.stack([row[:, None] * inv, col[:, None] * inv], axis=1).astype(np.float32)
    cos = np.cos(ang)
    sin = np.sin(ang)
    cosT = np.zeros((64, 4096), f32)
    sinT = np.zeros((64, 4096), f32)
    for a in range(2):
        for hf in range(2):
            for f in range(16):
                p = a * 32 + hf * 16 + f
                cosT[p] = cos[:, a, f]
                sinT[p] = -sin[:, a, f] if hf == 0 else sin[:, a, f]
    rope = [np.ascontiguousarray(np.stack([cosT[:, q * 1024:(q + 1) * 1024], sinT[:, q * 1024:(q + 1) * 1024]]))
            for q in range(4)]
    return dftC, dftP, dftS, rope


def _swap_cols(w, nheads, base, stride):
    cols = []
    for h in range(nheads):
        o = h * stride + base
        for a in range(2):
            cols += list(range(o + a * 32 + 16, o + a * 32 + 32)) + list(range(o + a * 32, o + a * 32 + 16))
    return np.ascontiguousarray(w[..., cols])


_CACHE = {}
DEPTH_RUN = DEPTH


def kernel(x_prompt, x_sample, cache_ckv, cache_kpe, c, c_ctx, w_mod, b_mod, norm_g,
           ffn_wg, ffn_wu, ffn_wd, mla_w_dq, mla_q_norm, mla_w_uq, mla_w_dkv, mla_kv_norm,
           mla_w_ukv, mla_w_o, fourier_w, fourier_b, final_norm):
    A = lambda a: np.ascontiguousarray(np.asarray(a, dtype=np.float32))
    x_prompt, x_sample, cache_ckv, cache_kpe = A(x_prompt), A(x_sample), A(cache_ckv), A(cache_kpe)
    c, c_ctx, w_mod, b_mod, norm_g = A(c), A(c_ctx), A(w_mod), A(b_mod), A(norm_g)
    ffn_wg, ffn_wu, ffn_wd = A(ffn_wg), A(ffn_wu), A(ffn_wd)
    mla_w_dq, mla_q_norm, mla_w_uq, mla_w_dkv = A(mla_w_dq), A(mla_q_norm), A(mla_w_uq), A(mla_w_dkv)
    mla_kv_norm, mla_w_ukv, mla_w_o = A(mla_kv_norm), A(mla_w_ukv), A(mla_w_o)
    fourier_w, fourier_b, final_norm = A(fourier_w), A(fourier_b), A(final_norm)

    if "nc" not in _CACHE:
        _CACHE["nc"] = build_program(DEPTH_RUN)[0]
        _CACHE["tables"] = _const_tables()
    nc = _CACHE["nc"]
    dftC, dftP, dftS, rope = _CACHE["tables"]

    shared = {
        "bmodT": _vec(b_mod, 72), "normgT": _vec(norm_g, 8), "finalT": _vec(final_norm, 8),
        "qnT": _vec(mla_q_norm, 4), "kvnT": _vec(mla_kv_norm, 2), "fbT": _vec(fourier_b, 8),
        "wg": ffn_wg, "wu": ffn_wu, "wd": ffn_wd, "wdq": mla_w_dq, "wuq": mla_w_uq,
        "wuqs": _swap_cols(mla_w_uq, 8, 128, 192), "wdkv": mla_w_dkv,
        "wdkvs": _swap_cols(mla_w_dkv, 1, 256, 0), "wukv": mla_w_ukv, "wo": mla_w_o, "fw": fourier_w,
        "dftC": dftC, "dftP": dftP,
    }
    in_maps = []
    for r in range(8):
        b, qd = r // 4, r % 4
        xp = x_prompt[4 * r:4 * r + 4].reshape(1024, D)
        xs = x_sample[b, qd * 1024:(qd + 1) * 1024]
        m = dict(shared)
        m["xT"] = _fm(np.concatenate([xp, xs], axis=0))
        m["cckv"] = np.ascontiguousarray(cache_ckv[b].transpose(0, 2, 1))
        m["ckpe"] = np.ascontiguousarray(cache_kpe[b].transpose(0, 2, 1))
        m["condT"] = _vec(np.stack([c_ctx, c[b]]), 8).transpose(0, 2, 1).copy()
        m["wmod"] = np.ascontiguousarray(w_mod[:, :, qd * 2304:(qd + 1) * 2304])
        m["ropeT"] = rope[qd]
        m["dftS"] = dftS[qd]
        in_maps.append(m)

    res = run_bass_kernel_spmd(nc, in_maps, core_ids=list(range(8)))
    y_prompt = np.empty((32, 256, D), np.float32)
    y_sample = np.empty((2, 4096, D), np.float32)
    new_ckv = np.empty((32, 2, 256, 256), np.float32)
    new_kpe = np.empty((32, 2, 256, 64), np.float32)
    for r in range(8):
        b, qd = r // 4, r % 4
        o = res.results[r]
        y = np.asarray(o["yT"]).transpose(2, 1, 0).reshape(NT, D)
        y_prompt[4 * r:4 * r + 4] = y[:1024].reshape(4, 256, D)
        y_sample[b, qd * 1024:(qd + 1) * 1024] = y[1024:]
        ck = np.asarray(o["o_ckv"])
        kp = np.asarray(o["o_kpe"])
        new_ckv[4 * r:4 * r + 4] = ck.reshape(2, 256, 4, 256).transpose(2, 0, 3, 1)
        new_kpe[4 * r:4 * r + 4] = kp.reshape(2, 64, 4, 256).transpose(2, 0, 3, 1)
    return (y_prompt, y_sample, new_ckv, new_kpe)
```

```python
import math
from contextlib import ExitStack

import numpy as np
import ml_dtypes
import concourse.bass as bass
import concourse.mybir as mybir
from concourse.bass_utils import run_bass_kernel_spmd

F32 = mybir.dt.float32
BF16 = mybir.dt.bfloat16
AF = mybir.ActivationFunctionType
ALU = mybir.AluOpType

D = 1024
DFF = 2816
NCH = 8
TB = 512
NT = 2048
DEPTH = 4
EPS = 1e-6
ATTN_SCALE = 1.0 / math.sqrt(192.0)
GROUPS = [[0, 1, 2, 3], [4, 5, 6, 7]]
ENGINES = ("sp", "act", "dve", "pool", "pe")
SEM_ROT = 8000
SCR_BYTES = 102 * 1024
DBG_PHASES = "mf"


class Op:
    __slots__ = ("id", "eng", "fn", "deps", "is_dma", "slot", "sig", "inc", "needs_signal")

    def __init__(self, id, eng, fn, is_dma, slot, inc):
        self.id = id
        self.eng = eng
        self.fn = fn
        self.deps = set()
        self.is_dma = is_dma
        self.slot = slot
        self.sig = None
        self.inc = inc
        self.needs_signal = False


class Prog:
    def __init__(self):
        self.ops = []
        self.last_w = {}
        self.readers = {}
        self.eng_ops = {e: [] for e in ENGINES}
        self.fence_set = set()
        self.dma_since_fence = []
        self.fenced = {e: True for e in ENGINES}

    def fence(self):
        fs = set(self.dma_since_fence)
        for e in ENGINES:
            for o in reversed(self.eng_ops[e]):
                if not o.is_dma:
                    fs.add(o.id)
                    break
        self.fence_set = fs
        self.dma_since_fence = []
        self.fenced = {e: False for e in ENGINES}

    def op(self, eng, fn, reads=(), writes=(), dma=False, slot=None, inc=16, nofence=False):
        o = Op(len(self.ops), eng, fn, dma, slot, inc)
        deps = set()
        for k in reads:
            w = self.last_w.get(k)
            if w is not None:
                deps.add(w)
        for k in writes:
            w = self.last_w.get(k)
            if w is not None:
                deps.add(w)
            for r in self.readers.get(k, {}).values():
                if isinstance(r, list):
                    deps.update(r)
                else:
                    deps.add(r)
        if not self.fenced[eng]:
            deps |= self.fence_set
            self.fenced[eng] = True
        deps.discard(o.id)
        o.deps = deps
        for k in reads:
            rd = self.readers.setdefault(k, {})
            if dma:
                rd.setdefault("dma", []).append(o.id)
            else:
                rd[eng] = o.id
        for k in writes:
            self.last_w[k] = o.id
            self.readers[k] = {}
        if dma:
            if slot is None:
                o.slot = ("dma", writes[0])
            if not nofence:
                self.dma_since_fence.append(o.id)
        self.eng_ops[eng].append(o)
        self.ops.append(o)
        return o

    def emit(self, nc, final_waits=()):
        ops = self.ops
        for o in ops:
            for d in o.deps:
                p = ops[d]
                if p.is_dma:
                    p.needs_signal = True
                elif p.eng == "pe" and o.eng == "pe" and not o.is_dma:
                    continue
                else:
                    p.needs_signal = True
        for d in final_waits:
            d.needs_signal = True
        sem_keys = []
        comp_cnt = {e: 0 for e in ENGINES}
        slot_cnt = {}
        grp_total = {}
        for o in ops:
            if o.is_dma and isinstance(o.slot, tuple) and o.slot[0] == "grp":
                grp_total[o.slot] = grp_total.get(o.slot, 0) + o.inc
        for o in ops:
            if o.is_dma:
                c = slot_cnt.get(o.slot, 0) + o.inc
                slot_cnt[o.slot] = c
                o.sig = (o.slot, grp_total.get(o.slot, c))
                if o.slot not in slot_cnt or o.slot not in sem_keys:
                    sem_keys.append(o.slot)
            elif o.needs_signal:
                n = comp_cnt[o.eng]
                comp_cnt[o.eng] = n + 1
                key = ("c", o.eng, n // SEM_ROT)
                o.sig = (key, n % SEM_ROT + 1)
                if key not in sem_keys:
                    sem_keys.append(key)
        sem_keys = list(dict.fromkeys(sem_keys))
        self.n_sems = len(sem_keys)
        stack = ExitStack()
        sems = {}
        for i, k in enumerate(sem_keys):
            sems[k] = stack.enter_context(nc.semaphore("s%d" % i))

        def run(engname, e):
            waited = {}
            for o in self.eng_ops[engname]:
                need = {}
                for d in o.deps:
                    p = ops[d]
                    if p.sig is None:
                        continue
                    if (not p.is_dma) and p.eng == "pe" and engname == "pe" and not o.is_dma:
                        continue
                    k, c = p.sig
                    if need.get(k, 0) < c:
                        need[k] = c
                for k, c in need.items():
                    if waited.get(k, 0) >= c:
                        continue
                    e.wait_ge(sems[k], c)
                    waited[k] = c
                ins = o.fn(e)
                if o.sig is not None:
                    ins.then_inc(sems[o.sig[0]], o.inc if o.is_dma else 1)
            if engname == "sp":
                for d in final_waits:
                    k, c = d.sig
                    e.wait_ge(sems[k], c)

        with stack:
            with nc.Block() as block:
                @block.sync
                def _(e):
                    run("sp", e)

                @block.scalar
                def _(e):
                    run("act", e)

                @block.vector
                def _(e):
                    run("dve", e)

                @block.gpsimd
                def _(e):
                    run("pool", e)

                @block.tensor
                def _(e):
                    run("pe", e)


class Carve:
    def __init__(self, scr):
        self.scr = scr
        self.off = 0

    def reset(self, off=0):
        self.off = off

    def take(self, nelem, dtype, parts=128):
        nb = nelem * (4 if dtype == F32 else 2)
        nb = (nb + 63) // 64 * 64
        a = self.off // 2
        self.off += nb
        assert self.off <= SCR_BYTES, ("scratch overflow", self.off)
        v = self.scr[:, a:a + nb // 2]
        if dtype == F32:
            v = v.bitcast(F32)
        v = v[:, 0:nelem]
        return v


def build_program(depth=DEPTH):
    nc = bass.Bass("TRN2", target_bir_lowering=False)

    def din(name, shape, dt=F32):
        return nc.dram_tensor(name, list(shape), dt, kind="ExternalInput").ap()

    def dout(name, shape, dt=F32):
        return nc.dram_tensor(name, list(shape), dt, kind="ExternalOutput").ap()

    xT_d = din("xT", [128, NCH, NT])
    cckv_d = din("cckv", [2, 256, 512])
    ckpe_d = din("ckpe", [2, 64, 512])
    condT_d = din("condT", [128, NCH, 2])
    wmod_d = din("wmod", [4, D, 2304])
    bmodT_d = din("bmodT", [128, 4, 72])
    normgT_d = din("normgT", [128, 4, 3, 8])
    finalT_d = din("finalT", [128, 8])
    qnT_d = din("qnT", [128, 2, 4])
    kvnT_d = din("kvnT", [128, 2, 2])
    fbT_d = din("fbT", [128, 2, 8])
    wg_d = din("wg", [4, 2, D, DFF])
    wu_d = din("wu", [4, 2, D, DFF])
    wd_d = din("wd", [4, 2, DFF, D])
    wdq_d = din("wdq", [2, D, 512])
    wuq_d = din("wuq", [2, 512, 1536])
    wuqs_d = din("wuqs", [2, 512, 512])
    wdkv_d = din("wdkv", [2, D, 320])
    wdkvs_d = din("wdkvs", [2, D, 64])
    wukv_d = din("wukv", [2, 256, 2048])
    wo_d = din("wo", [2, D, D])
    fw_d = din("fw", [2, D, D])
    rope_d = din("ropeT", [2, 64, 1024])
    dftC_d = din("dftC", [256, 512])
    dftP_d = din("dftP", [2, 256, 256])
    dftS_d = din("dftS", [2, 4096, 1024])

    yT_d = dout("yT", [128, NCH, NT])
    ockv_d = dout("o_ckv", [2, 256, 1024])
    okpe_d = dout("o_kpe", [2, 64, 1024])

    mod_in = nc.dram_tensor("mod_in", [128, 144], F32).ap()
    mod_out = nc.dram_tensor("mod_out", [512, 144], F32).ap()
    x_in = [nc.dram_tensor("x_in%d" % j, [320, 1024], BF16).ap() for j in range(2)]
    x_out = [nc.dram_tensor("x_out%d" % j, [1280, 1024], BF16).ap() for j in range(2)]
    f_in = [[nc.dram_tensor("f_in%d_%d" % (j, q), [256, 2048], BF16).ap() for q in range(4)] for j in range(2)]
    f_out = [[nc.dram_tensor("f_out%d_%d" % (j, q), [1024, 2048], BF16).ap() for q in range(4)] for j in range(2)]

    dftS_bf = nc.dram_tensor("dftS_bf", [2, 4096, 1024], BF16).ap()
    DFT_KEYS = [("dftS_bf", a, q) for a in range(2) for q in range(4)]

    P = Prog()
    st = ExitStack()
    sb = lambda name, shape, dt: st.enter_context(nc.sbuf_tensor(name, list(shape), dt))
    xT = sb("xTs", [128, NCH, NT], F32)
    hT = sb("hTs", [128, NCH, NT], BF16)
    modS = sb("modS", [128, 4 * 72 * 2], F32)
    bmodT = sb("bmodTs", [128, 4, 72], F32)
    normgT = sb("normgTs", [128, 4, 3, 8], F32)
    finalT = sb("finalTs", [128, 8], F32)
    qnT = sb("qnTs", [128, 2, 4], F32)
    kvnT = sb("kvnTs", [128, 2, 2], F32)
    fbT = sb("fbTs", [128, 2, 8], F32)
    condT = sb("condTs", [128, NCH, 2], F32)
    scT = sb("scTs", [128, NCH, 2], F32)
    scB = sb("scBs", [128, NCH, 2], BF16)
    onesB = sb("onesB", [128, 128], BF16)
    onesF = sb("onesF", [128, 128], F32)
    epsT = sb("epsT", [128, 1], F32)
    scr = sb("scr", [128, SCR_BYTES // 2], BF16)
    ps = [st.enter_context(nc.psum_tensor("ps%d" % i, [128, 512], F32)) for i in range(8)]
    PK = [("ps", i) for i in range(8)]
    cv = Carve(scr)

    mod5 = modS[:].rearrange("p (l k c n) -> p l k c n", l=4, k=9, c=8)
    mod_lrx = modS[:].rearrange("p (l r x) -> p l r x", l=4, r=4)
    mod_lcn = modS[:].rearrange("p (l c n) -> p l c n", l=4, c=72)

    outs = []

    def mcol(l, k, c, cond):
        return mod5[:, l, k, c, cond:cond + 1]

    def cond_of(t):
        return 0 if t < 2 else 1

    def tsl(t):
        return slice(t * TB, (t + 1) * TB)

    cv.reset(0)
    sq = cv.take(2 * TB, BF16).rearrange("p (a b) -> p a b", a=2)
    rs = cv.take(2 * TB, F32).rearrange("p (a b) -> p a b", a=2)
    tmpf = cv.take(2 * TB, F32).rearrange("p (a b) -> p a b", a=2)
    COMMON = cv.off
    cnt = {"sq": 0, "tmpf": 0}

    def rot(name, n=2):
        v = cnt.get(name, 0)
        cnt[name] = v + 1
        return v % n

    P.op("sp", lambda e: e.dma_start(out=condT[:], in_=condT_d), writes=["condT"], dma=True, slot=("grp", "setup"))
    for (tl, td, nm) in ((bmodT, bmodT_d, "bmodT"), (normgT, normgT_d, "normgT"), (finalT, finalT_d, "finalT"),
                         (qnT, qnT_d, "qnT"), (kvnT, kvnT_d, "kvnT"), (fbT, fbT_d, "fbT")):
        P.op("sp", lambda e, tl=tl, td=td: e.dma_start(out=tl[:], in_=td), writes=[nm], dma=True, slot=("grp", "setup"))
    P.op("dve", lambda e: e.memset(onesB[:], 1.0), writes=["ones"])
    P.op("dve", lambda e: e.memset(onesF[:], 1.0), writes=["onesF"])
    P.op("dve", lambda e: e.memset(epsT[:], EPS), writes=["eps"])
    P.op("act", lambda e: e.activation(out=scB[:], in_=condT[:], func=AF.Silu), reads=["condT"], writes=["scT"])

    for c in range(NCH):
        P.op("sp", lambda e, c=c: e.dma_start(out=xT[:, c, :], in_=xT_d[:, c, :]),
             writes=[("xT", c, t) for t in range(4)], dma=True, slot=("grp", "xT"))

    cv.reset(COMMON)
    wm = [cv.take(NCH * 1152, BF16).rearrange("p (c f) -> p c f", c=NCH) for _ in range(4)]
    modin = cv.take(144, F32)
    for l in range(4):
        for hf in range(2):
            b = (l * 2 + hf) % 4
            src = wmod_d[l, :, hf * 1152:(hf + 1) * 1152].rearrange("(c p) f -> p c f", p=128)
            P.op("pool", lambda e, b=b, src=src: e.dma_start(out=wm[b], in_=src), writes=[("wm", b)], dma=True)
            for i in range(9):
                col = (hf * 9 + i) * 2
                for c in range(NCH):
                    P.op("pe", lambda e, b=b, i=i, c=c, col=col: e.matmul(
                        ps[0][:, col:col + 2], lhsT=wm[b][:, c, i * 128:(i + 1) * 128], rhs=scB[:, c, :],
                        start=(c == 0), stop=(c == NCH - 1)),
                        reads=[("wm", b), "scT"], writes=[PK[0]])
        P.op("dve", lambda e, l=l: e.tensor_copy(out=modin[:, l * 36:(l + 1) * 36], in_=ps[0][:, 0:36]),
             reads=[PK[0]], writes=["modin"])
    P.op("sp", lambda e: e.dma_start(out=mod_in, in_=modin), reads=["modin"], writes=["mod_in"], dma=True)
    P.op("pool", lambda e: e.collective_compute("AllGather", ALU.bypass, replica_groups=GROUPS,
                                                ins=[mod_in], outs=[mod_out]),
         reads=["mod_in"], writes=["mod_out"], dma=True, inc=1)
    for l in range(4):
        src = mod_out[:, l * 36:(l + 1) * 36].rearrange("(r p) x -> p r x", p=128)
        P.op("sp", lambda e, l=l, src=src: e.dma_start(out=mod_lrx[:, l], in_=src),
             reads=["mod_out"], writes=["modS"], dma=True, slot=("dma", "modS", l))
    for n in range(2):
        P.op("dve", lambda e, n=n: e.tensor_tensor(out=mod_lcn[:, :, :, n], in0=mod_lcn[:, :, :, n],
                                                   in1=bmodT[:], op=ALU.add),
             reads=["modS", "bmodT"], writes=["modS"])
    for i in range(3):
        for n in range(2):
            P.op("dve", lambda e, i=i, n=n: e.scalar_tensor_tensor(
                out=mod5[:, :, 3 * i + 1, :, n], in0=mod5[:, :, 3 * i + 1, :, n], scalar=1.0,
                in1=normgT[:, :, i, :], op0=ALU.add, op1=ALU.mult),
                reads=["modS", "normgT"], writes=["modS"])
            if i != 1:
                P.op("dve", lambda e, i=i, n=n: e.tensor_scalar(
                    out=mod5[:, :, 3 * i + 2, :, n], in0=mod5[:, :, 3 * i + 2, :, n], scalar1=0.5,
                    scalar2=None, op0=ALU.mult),
                    reads=["modS"], writes=["modS"])

    def rstd_block(src_keys, nchunks, get_src, inv_n, bank, extra_reads=()):
        for c in range(nchunks):
            s = rot("sq")
            src, rk = get_src(c)
            P.op("act", lambda e, s=s, src=src: e.activation(out=sq[:, s, :], in_=src, func=AF.Square),
                 reads=[rk], writes=[("sq", s)])
            P.op("pe", lambda e, s=s, c=c: e.matmul(ps[bank][:], lhsT=onesB[:], rhs=sq[:, s, :],
                                                    start=(c == 0), stop=(c == nchunks - 1)),
                 reads=[("sq", s), "ones"], writes=[PK[bank]])
        P.op("act", lambda e: e.activation(out=rs[:, 0, :], in_=ps[bank][:], func=AF.Ln,
                                           bias=epsT[:, 0:1], scale=inv_n),
             reads=[PK[bank], "eps"], writes=[("rs", 0)])
        P.op("act", lambda e: e.activation(out=rs[:, 1, :], in_=rs[:, 0, :], func=AF.Exp, scale=-0.5),
             reads=[("rs", 0)], writes=[("rs", 1)])

    def norm_phase(l, i, final=False):
        for t in range(4):
            n = cond_of(t)
            rstd_block(None, NCH, lambda c, t=t: (xT[:, c, tsl(t)], ("xT", c, t)), 1.0 / D, 7)
            for c in range(NCH):
                s = rot("tmpf")
                g = finalT[:, c:c + 1] if final else mcol(l, 3 * i + 1, c, n)
                P.op("dve", lambda e, s=s, c=c, t=t, g=g: e.scalar_tensor_tensor(
                    out=tmpf[:, s, :], in0=xT[:, c, tsl(t)], scalar=g, in1=rs[:, 1, :],
                    op0=ALU.mult, op1=ALU.mult),
                    reads=[("xT", c, t), ("rs", 1), "modS", "finalT"], writes=[("tmpf", s)])
                if final:
                    outs.append(P.op("sp", lambda e, s=s, c=c, t=t: e.dma_start(out=yT_d[:, c, tsl(t)], in_=tmpf[:, s, :]),
                                     reads=[("tmpf", s)], writes=[("yT", c, t)], dma=True, slot=("out", "tmpf", s)))
                else:
                    P.op("act", lambda e, s=s, c=c, t=t, l=l, i=i, n=n: e.activation(
                        out=hT[:, c, tsl(t)], in_=tmpf[:, s, :], func=AF.Identity,
                        bias=mcol(l, 3 * i, c, n), scale=1.0),
                        reads=[("tmpf", s), "modS"], writes=[("hT", c, t)])

    pending = []

    def norm_items(l, i, t, final=False):
        items = []
        n = cond_of(t)
        bank = 7

        def sq_item(c):
            def f():
                s = rot("sq")
                P.op("act", lambda e: e.activation(out=sq[:, s, :], in_=xT[:, c, tsl(t)], func=AF.Square),
                     reads=[("xT", c, t)], writes=[("sq", s)])
                P.op("pe", lambda e: e.matmul(ps[bank][:], lhsT=onesB[:], rhs=sq[:, s, :],
                                              start=(c == 0), stop=(c == NCH - 1)),
                     reads=[("sq", s), "ones"], writes=[PK[bank]])
            return f

        def rs_item():
            P.op("act", lambda e: e.activation(out=rs[:, 0, :], in_=ps[bank][:], func=AF.Ln,
                                               bias=epsT[:, 0:1], scale=1.0 / D),
                 reads=[PK[bank], "eps"], writes=[("rs", 0)])
            P.op("act", lambda e: e.activation(out=rs[:, 1, :], in_=rs[:, 0, :], func=AF.Exp, scale=-0.5),
                 reads=[("rs", 0)], writes=[("rs", 1)])

        def out_item(c):
            def f():
                s = rot("tmpf")
                g = finalT[:, c:c + 1] if final else mcol(l, 3 * i + 1, c, n)
                P.op("dve", lambda e: e.scalar_tensor_tensor(
                    out=tmpf[:, s, :], in0=xT[:, c, tsl(t)], scalar=g, in1=rs[:, 1, :],
                    op0=ALU.mult, op1=ALU.mult),
                    reads=[("xT", c, t), ("rs", 1), "modS", "finalT"], writes=[("tmpf", s)])
                if final:
                    outs.append(P.op("sp", lambda e: e.dma_start(out=yT_d[:, c, tsl(t)], in_=tmpf[:, s, :]),
                                     reads=[("tmpf", s)], writes=[("yT", c, t)], dma=True, slot=("out", "tmpf", s)))
                else:
                    P.op("act", lambda e: e.activation(
                        out=hT[:, c, tsl(t)], in_=tmpf[:, s, :], func=AF.Identity,
                        bias=mcol(l, 3 * i, c, n), scale=1.0),
                        reads=[("tmpf", s), "modS"], writes=[("hT", c, t)])
            return f

        for c in range(NCH):
            items.append((t, sq_item(c)))
        items.append((t, rs_item))
        for c in range(NCH):
            items.append((t, out_item(c)))
        return items

    def pop_pending(n):
        for _ in range(n):
            if pending:
                pending.pop(0)[1]()

    def flush_pending(upto_t=None):
        while pending and (upto_t is None or any(tg <= upto_t for tg, _ in pending)):
            pending.pop(0)[1]()

    def resid_update(bank, m, t, gate_ap, tok=None):
        sl = tsl(t) if tok is None else tok
        P.op("dve", lambda e: e.scalar_tensor_tensor(
            out=xT[:, m, sl], in0=ps[bank][:, 0:(sl.stop - sl.start)], scalar=gate_ap, in1=xT[:, m, sl],
            op0=ALU.mult, op1=ALU.add),
            reads=[PK[bank], ("xT", m, t), "modS"], writes=[("xT", m, t)])

    def ffn_phase(l, i, pre_normed=False, next_norm=None, skip_fence=False, pre_work=None, carry=False,
                  tail_order=(0, 1, 2, 3), tblocks=(0, 1, 2, 3)):
        if not skip_fence:
            P.fence()
        cv.reset(COMMON)
        NB = 3
        wgb = [cv.take(NCH * 512, BF16).rearrange("p (c f) -> p c f", c=NCH) for _ in range(NB)]
        wub = [cv.take(NCH * 512, BF16).rearrange("p (c f) -> p c f", c=NCH) for _ in range(NB)]
        wdb = [cv.take(4 * D, BF16).rearrange("p (j d) -> p j d", j=4) for _ in range(NB)]
        actb = [cv.take(4 * TB, BF16).rearrange("p (j t) -> p j t", j=4) for _ in range(2)]
        slb = [cv.take(TB, BF16) for _ in range(2)]
        sizes = [4, 4, 4, 4, 3, 3]
        f0s = [0, 4, 8, 12, 16, 19]
        kidx = 2 * i
        sub = 0 if i == 0 else 2

        def load_chunk(k):
            s = k % NB
            G = sizes[k]
            f0 = f0s[k] * 128
            srcg = wg_d[l, i, :, f0:f0 + G * 128].rearrange("(c p) f -> p c f", p=128)
            srcu = wu_d[l, i, :, f0:f0 + G * 128].rearrange("(c p) f -> p c f", p=128)
            srcd = wd_d[l, i, f0:f0 + G * 128, :].rearrange("(j p) d -> p j d", p=128)
            P.op("pool", lambda e: e.dma_start(out=wgb[s][:, :, 0:G * 128], in_=srcg), writes=[("wg", s)], dma=True)
            P.op("pool", lambda e: e.dma_start(out=wub[s][:, :, 0:G * 128], in_=srcu), writes=[("wu", s)], dma=True)
            P.op("pool", lambda e: e.dma_start(out=wdb[s][:, 0:G, :], in_=srcd), writes=[("wd", s)], dma=True)

        gu_cnt = [0]

        def gate_up(k, t):
            s = k % NB
            G = sizes[k]
            ab = (gu_cnt[0]) % 2
            gu_cnt[0] += 1
            for j in range(G):
                bg = j % 2
                bu = 2 + j % 2
                for c in range(NCH):
                    P.op("pe", lambda e, j=j, c=c, bg=bg: e.matmul(
                        ps[bg][:], lhsT=wgb[s][:, c, j * 128:(j + 1) * 128], rhs=hT[:, c, tsl(t)],
                        start=(c == 0), stop=(c == NCH - 1)),
                        reads=[("wg", s), ("hT", c, t)], writes=[PK[bg]])
                for c in range(NCH):
                    P.op("pe", lambda e, j=j, c=c, bu=bu: e.matmul(
                        ps[bu][:], lhsT=wub[s][:, c, j * 128:(j + 1) * 128], rhs=hT[:, c, tsl(t)],
                        start=(c == 0), stop=(c == NCH - 1)),
                        reads=[("wu", s), ("hT", c, t)], writes=[PK[bu]])
                sl_i = rot("slb")
                P.op("act", lambda e, bg=bg, sl_i=sl_i: e.activation(out=slb[sl_i], in_=ps[bg][:], func=AF.Silu),
                     reads=[PK[bg]], writes=[("slb", sl_i)])
                P.op("dve", lambda e, bu=bu, sl_i=sl_i, j=j, ab=ab: e.tensor_tensor(
                    out=actb[ab][:, j, :], in0=ps[bu][:], in1=slb[sl_i], op=ALU.mult),
                    reads=[PK[bu], ("slb", sl_i)], writes=[("act", ab, j)])
                pop_pending(2)
            return ab

        def down(k, t, ab):
            s = k % NB
            G = sizes[k]
            n = cond_of(t)
            for m in range(NCH):
                by = 4 + m % 3
                for j in range(G):
                    P.op("pe", lambda e, j=j, m=m, by=by: e.matmul(
                        ps[by][:], lhsT=wdb[s][:, j, m * 128:(m + 1) * 128], rhs=actb[ab][:, j, :],
                        start=(j == 0), stop=(j == G - 1)),
                        reads=[("wd", s), ("act", ab, j)], writes=[PK[by]])
                resid_update(by, m, t, mcol(l, 3 * sub + 2, m, n))
                pop_pending(1)

        if pre_work is not None:
            tiles = [(wgb[2][:, :, i_ * 128:(i_ + 1) * 128], ("wg", 2)) for i_ in range(4)] + \
                    [(wub[2][:, :, i_ * 128:(i_ + 1) * 128], ("wu", 2)) for i_ in range(4)]
            pre_work(lambda: (load_chunk(0), load_chunk(1)), tiles)
        if not pre_normed:
            for t in tblocks:
                pending.extend(norm_items(l, sub, t))
            flush_pending(tblocks[0])
        nk_ = len(sizes)
        work = [(k, t) for k in range(nk_ - 1) for t in tblocks]
        if pre_work is None:
            load_chunk(0)
            load_chunk(1)
        prev = None
        for idx, (k, t) in enumerate(work):
            if k == 0:
                flush_pending(t)
            ab = gate_up(k, t)
            if prev is not None:
                down(*prev)
            if t == tblocks[0] and k >= 1 and k + 1 < nk_:
                load_chunk(k + 1)
                if l == 0 and i == 1:
                    for a in range(2):
                        q = k - 1
                        P.op("pool", lambda e, a=a, q=q: e.dma_start(out=dftS_bf[a, q * 1024:(q + 1) * 1024, :],
                                                                     in_=dftS_d[a, q * 1024:(q + 1) * 1024, :]),
                             writes=[("dftS_bf", a, q)], dma=True, slot="dftcast")
            prev = (k, t, ab)
        k = nk_ - 1
        for t in [t_ for t_ in tail_order if t_ in tblocks]:
            ab = gate_up(k, t)
            if prev is not None:
                down(*prev)
                prev = None
            down(k, t, ab)
            if next_norm is not None:
                pending.extend(norm_items(next_norm[0], next_norm[1], t, final=next_norm[2]))
        if not carry:
            flush_pending()

    def mla_phase(l, pre_normed=False):
        jl = l // 2
        P.fence()
        flush_pending()
        cv.reset(COMMON)
        cqT_s = cv.take(4 * 1024, BF16).rearrange("p (c t) -> p c t", c=4)
        ckvT_s = cv.take(2 * 1024, BF16).rearrange("p (c t) -> p c t", c=2)
        kpeT_s = cv.take(1024, BF16)
        ropeT = cv.take(2 * 1024, F32).rearrange("p (a t) -> p a t", a=2)
        SHARED_S = cv.off
        cqT_p = cv.take(4 * 1024, BF16).rearrange("p (c t) -> p c t", c=4)
        ckvT_p = cv.take(2 * 1024, BF16).rearrange("p (c t) -> p c t", c=2)
        kpeT_p = cv.take(1024, BF16)
        SHARED_P = cv.off

        def lsl(t):
            return slice((t % 2) * TB, (t % 2 + 1) * TB)

        def cq(kc, t):
            return (cqT_p if t < 2 else cqT_s)[:, kc, lsl(t)]

        def ckv(mc, t):
            return (ckvT_p if t < 2 else ckvT_s)[:, mc, lsl(t)]

        def kpe(t):
            return (kpeT_p if t < 2 else kpeT_s)[0:64, lsl(t)]

        wdq = cv.take(NCH * 512, BF16).rearrange("p (c f) -> p c f", c=NCH)
        wdkv = cv.take(NCH * 384, BF16).rearrange("p (c f) -> p c f", c=NCH)
        cqraw = cv.take(4 * TB, F32).rearrange("p (c t) -> p c t", c=4)
        ckvf = [cv.take(2 * TB, F32).rearrange("p (c t) -> p c t", c=2) for _ in range(2)]
        kpef = [cv.take(TB, F32) for _ in range(2)]
        if not pre_normed:
            norm_phase(l, 1)
        P.op("pool", lambda e: e.dma_start(out=wdq, in_=wdq_d[jl].rearrange("(c p) f -> p c f", p=128)),
             writes=["wdq"], dma=True)
        P.op("pool", lambda e: e.dma_start(out=wdkv[:, :, 0:320], in_=wdkv_d[jl].rearrange("(c p) f -> p c f", p=128)),
             writes=["wdkv0"], dma=True)
        P.op("pool", lambda e: e.dma_start(out=wdkv[:, :, 320:384], in_=wdkvs_d[jl].rearrange("(c p) f -> p c f", p=128)),
             writes=["wdkv1"], dma=True)
        for a in range(2):
            P.op("sp", lambda e, a=a: e.dma_start(out=ropeT[0:64, a, :], in_=rope_d[a]), writes=[("rope", a)], dma=True)

        def stage_a(t):
            for mc in range(4):
                for c in range(NCH):
                    P.op("pe", lambda e, mc=mc, c=c: e.matmul(
                        ps[mc][:], lhsT=wdq[:, c, mc * 128:(mc + 1) * 128], rhs=hT[:, c, tsl(t)],
                        start=(c == 0), stop=(c == NCH - 1)),
                        reads=["wdq", ("hT", c, t)], writes=[PK[mc]])
                P.op("act", lambda e, mc=mc: e.activation(out=cqraw[:, mc, :], in_=ps[mc][:], func=AF.Identity),
                     reads=[PK[mc]], writes=[("cqraw", mc)])
            rstd_block(None, 4, lambda c: (cqraw[:, c, :], ("cqraw", c)), 1.0 / 512, 7)
            for mc in range(4):
                P.op("dve", lambda e, mc=mc: e.scalar_tensor_tensor(
                    out=cq(mc, t), in0=cqraw[:, mc, :], scalar=qnT[:, jl, mc:mc + 1], in1=rs[:, 1, :],
                    op0=ALU.mult, op1=ALU.mult),
                    reads=[("cqraw", mc), ("rs", 1), "qnT"], writes=[("cqT", mc, t)])
            fb = t % 2
            for mc in range(2):
                for c in range(NCH):
                    P.op("pe", lambda e, mc=mc, c=c: e.matmul(
                        ps[4 + mc][:], lhsT=wdkv[:, c, mc * 128:(mc + 1) * 128], rhs=hT[:, c, tsl(t)],
                        start=(c == 0), stop=(c == NCH - 1)),
                        reads=["wdkv0", ("hT", c, t)], writes=[PK[4 + mc]])
                P.op("act", lambda e, mc=mc: e.activation(out=cqraw[:, mc, :], in_=ps[4 + mc][:], func=AF.Identity),
                     reads=[PK[4 + mc]], writes=[("cqraw", mc)])
            rstd_block(None, 2, lambda c: (cqraw[:, c, :], ("cqraw", c)), 1.0 / 256, 7)
            for mc in range(2):
                if t < 2:
                    P.op("dve", lambda e, mc=mc: e.scalar_tensor_tensor(
                        out=ckvf[fb][:, mc, :], in0=cqraw[:, mc, :], scalar=kvnT[:, jl, mc:mc + 1], in1=rs[:, 1, :],
                        op0=ALU.mult, op1=ALU.mult),
                        reads=[("cqraw", mc), ("rs", 1), "kvnT"], writes=[("ckvf", fb, mc)])
                    P.op("act", lambda e, mc=mc: e.activation(out=ckv(mc, t), in_=ckvf[fb][:, mc, :], func=AF.Identity),
                         reads=[("ckvf", fb, mc)], writes=[("ckvT", mc, t)])
                else:
                    P.op("dve", lambda e, mc=mc: e.scalar_tensor_tensor(
                        out=ckv(mc, t), in0=cqraw[:, mc, :], scalar=kvnT[:, jl, mc:mc + 1], in1=rs[:, 1, :],
                        op0=ALU.mult, op1=ALU.mult),
                        reads=[("cqraw", mc), ("rs", 1), "kvnT"], writes=[("ckvT", mc, t)])
            if t < 2:
                outs.append(P.op("sp", lambda e: e.dma_start(
                    out=ockv_d[jl][:, tsl(t)].rearrange("(c p) t -> p c t", p=128), in_=ckvf[fb]),
                    reads=[("ckvf", fb, 0), ("ckvf", fb, 1)], writes=[("ockv", jl, t)], dma=True,
                    slot=("out", "ckvf", fb)))
            for c in range(NCH):
                P.op("pe", lambda e, c=c: e.matmul(
                    ps[6][0:64, :], lhsT=wdkv[:, c, 256:320], rhs=hT[:, c, tsl(t)],
                    start=(c == 0), stop=(c == NCH - 1)),
                    reads=["wdkv0", ("hT", c, t)], writes=[PK[6]])
            if t < 2:
                P.op("act", lambda e: e.activation(out=kpef[fb][0:64, :], in_=ps[6][0:64, :], func=AF.Identity),
                     reads=[PK[6]], writes=[("kpef", fb)])
                P.op("dve", lambda e: e.tensor_copy(out=kpe(t), in_=kpef[fb][0:64, :]),
                     reads=[("kpef", fb)], writes=[("kpeT", t)])
                outs.append(P.op("sp", lambda e: e.dma_start(out=okpe_d[jl][:, tsl(t)], in_=kpef[fb][0:64, :]),
                                 reads=[("kpef", fb)], writes=[("okpe", jl, t)], dma=True,
                                 slot=("out", "kpef", fb)))
            else:
                for c in range(NCH):
                    P.op("pe", lambda e, c=c: e.matmul(
                        ps[3][0:64, :], lhsT=wdkv[:, c, 320:384], rhs=hT[:, c, tsl(t)],
                        start=(c == 0), stop=(c == NCH - 1)),
                        reads=["wdkv1", ("hT", c, t)], writes=[PK[3]])
                tok = lsl(t)
                P.op("dve", lambda e: e.tensor_tensor(out=tmpf[0:64, 0, :], in0=ps[6][0:64, :],
                                                      in1=ropeT[0:64, 0, tok], op=ALU.mult),
                     reads=[PK[6], ("rope", 0)], writes=[("tmpf", 0)])
                P.op("dve", lambda e: e.tensor_tensor(out=tmpf[0:64, 1, :], in0=ps[3][0:64, :],
                                                      in1=ropeT[0:64, 1, tok], op=ALU.mult),
                     reads=[PK[3], ("rope", 1)], writes=[("tmpf", 1)])
                P.op("dve", lambda e: e.tensor_tensor(out=kpe(t), in0=tmpf[0:64, 0, :],
                                                      in1=tmpf[0:64, 1, :], op=ALU.add),
                     reads=[("tmpf", 0), ("tmpf", 1)], writes=[("kpeT", t)])

        for t in (2, 3, 0, 1):
            stage_a(t)
            if t == 3:
                P.op("sp", lambda e: e.dma_start(out=x_in[jl][0:256, :].rearrange("(c p) t -> p c t", p=128),
                                                 in_=ckvT_s),
                     reads=[("ckvT", 0, 2), ("ckvT", 0, 3), ("ckvT", 1, 2), ("ckvT", 1, 3)],
                     writes=[("x_in", jl, 0)], dma=True, slot=("grp", "x_in", jl))
                P.op("sp", lambda e: e.dma_start(out=x_in[jl][256:320, :], in_=kpeT_s[0:64, :]),
                     reads=[("kpeT", 2), ("kpeT", 3)], writes=[("x_in", jl, 1)], dma=True, slot=("grp", "x_in", jl))
                P.op("pool", lambda e: e.collective_compute("AllGather", ALU.bypass, replica_groups=GROUPS,
                                                            ins=[x_in[jl]], outs=[x_out[jl]]),
                     reads=[("x_in", jl, 0), ("x_in", jl, 1)], writes=[("x_out", jl)], dma=True, inc=1)

        def attention(stream):
            P.fence()
            cv.reset(SHARED_P if stream == 0 else SHARED_S)
            nk = 1024 if stream == 0 else 4608
            nkt = nk // 128
            tok0 = 0 if stream == 0 else 1024
            if stream == 1:
                ckv_all = cv.take(2 * nk, BF16).rearrange("p (c t) -> p c t", c=2)
                kpe_all = cv.take(nk, BF16)
            kn = cv.take(nk, BF16)
            Vh = cv.take(nk, BF16).rearrange("p (k d) -> p k d", d=128)
            qn = cv.take(1024, BF16)
            qr = cv.take(1024, BF16)
            Pt = [cv.take(TB, BF16) for _ in range(4)]
            dacc = [cv.take(TB, F32) for _ in range(2)]
            hw_q = [cv.take(4 * 256, BF16).rearrange("p (c f) -> p c f", c=4) for _ in range(2)]
            hw_kv = [cv.take(2 * 256, BF16).rearrange("p (c f) -> p c f", c=2) for _ in range(2)]
            wob = [cv.take(8 * 128, BF16).rearrange("p (h d) -> p h d", h=8) for _ in range(2)]
            rden = cv.take(TB, F32)
            P.op("dve", lambda e: e.memset(qr[64:128, :], 0.0), writes=["qr_pad"])
            if stream == 1:
                P.op("dve", lambda e: e.memset(kpe_all[64:128, :], 0.0), writes=["kpe_pad"])
            else:
                P.op("dve", lambda e: e.memset(kpeT_p[64:128, :], 0.0), writes=["kpe_pad"])
            if stream == 1:
                P.op("pool", lambda e: e.dma_start(out=ckv_all[:, :, 0:512],
                                                   in_=cckv_d[jl].rearrange("(c p) t -> p c t", p=128)),
                     writes=[("ckv_all", 0)], dma=True, slot=("grp", "kvall_p", jl))
                P.op("pool", lambda e: e.dma_start(out=kpe_all[0:64, 0:512], in_=ckpe_d[jl]),
                     writes=[("kpe_all", 0)], dma=True, slot=("grp", "kvall_p", jl))
                for r in range(4):
                    P.op("sp", lambda e, r=r: e.dma_start(
                        out=ckv_all[:, :, 512 + r * 1024:512 + (r + 1) * 1024],
                        in_=x_out[jl][r * 320:r * 320 + 256, :].rearrange("(c p) t -> p c t", p=128)),
                        reads=[("x_out", jl)], writes=[("ckv_all", 1 + r)], dma=True, slot=("grp", "kvall", jl))
                    P.op("sp", lambda e, r=r: e.dma_start(
                        out=kpe_all[0:64, 512 + r * 1024:512 + (r + 1) * 1024],
                        in_=x_out[jl][r * 320 + 256:(r + 1) * 320, :]),
                        reads=[("x_out", jl)], writes=[("kpe_all", 1 + r)], dma=True, slot=("grp", "kvall", jl))
                ckvsrc, kpesrc = ckv_all, kpe_all
                ckv_keys = [("ckv_all", r) for r in range(5)]
                kpe_keys = [("kpe_all", r) for r in range(5)]
            else:
                ckvsrc, kpesrc = ckvT_p, kpeT_p
                ckv_keys = [("ckvT", mc, t) for mc in range(2) for t in range(2)]
                kpe_keys = [("kpeT", 0), ("kpeT", 1)]

            gcnt = {"tile": 0}

            def head_tiles(h, qblocks):
                flat = [(qi, ki) for qi, (q0, qn_, kts) in enumerate(qblocks) for ki in range(len(kts))]
                info = {}

                def s_mm(qi, ki):
                    q0, qn_, kts = qblocks[qi]
                    kt = kts[ki]
                    sbk = gcnt["tile"] % 3
                    gcnt["tile"] += 1
                    da = dacc[qi % 2]
                    P.op("pe", lambda e: e.matmul(
                        ps[sbk][:, 0:qn_], lhsT=kn[:, kt * 128:(kt + 1) * 128], rhs=qn[:, q0:q0 + qn_],
                        start=True, stop=False),
                        reads=[("kn", kt // 4), ("qn", q0 // TB)], writes=[PK[sbk]])
                    P.op("pe", lambda e: e.matmul(
                        ps[sbk][:, 0:qn_], lhsT=kpesrc[:, kt * 128:(kt + 1) * 128], rhs=qr[:, q0:q0 + qn_],
                        start=False, stop=True),
                        reads=kpe_keys + [("qr", q0 // TB), "qr_pad", "kpe_pad"], writes=[PK[sbk]])
                    pi = rot("Pt", 4)
                    info[(qi, ki)] = pi
                    P.op("act", lambda e: e.activation(out=Pt[pi][:, 0:qn_], in_=ps[sbk][:, 0:qn_], func=AF.Exp),
                         reads=[PK[sbk]], writes=[("Pt", pi)])
                    if ki == 0:
                        P.op("dve", lambda e: e.tensor_copy(out=da[:, 0:qn_], in_=Pt[pi][:, 0:qn_]),
                             reads=[("Pt", pi)], writes=[("dacc", qi % 2)])
                    else:
                        P.op("dve", lambda e: e.tensor_tensor(out=da[:, 0:qn_], in0=da[:, 0:qn_], in1=Pt[pi][:, 0:qn_],
                                                              op=ALU.add),
                             reads=[("Pt", pi), ("dacc", qi % 2)], writes=[("dacc", qi % 2)])

                def pv_mm(qi, ki):
                    q0, qn_, kts = qblocks[qi]
                    kt = kts[ki]
                    pi = info[(qi, ki)]
                    ob = 4 + (qi % 2)
                    nkt_ = len(kts)
                    P.op("pe", lambda e: e.matmul(
                        ps[ob][:, 0:qn_], lhsT=Vh[:, kt, :], rhs=Pt[pi][:, 0:qn_],
                        start=(ki == 0), stop=(ki == nkt_ - 1)),
                        reads=[("Vh", kt // 4), ("Pt", pi)], writes=[PK[ob]])
                    if ki == nkt_ - 1:
                        db = 6 + (qi % 2)
                        da = dacc[qi % 2]
                        tq = (tok0 + q0) // TB
                        P.op("pe", lambda e: e.matmul(ps[db][:, 0:qn_], lhsT=onesF[:], rhs=da[:, 0:qn_],
                                                      start=True, stop=True),
                             reads=["onesF", ("dacc", qi % 2)], writes=[PK[db]])
                        P.op("act", lambda e: e.activation(out=rden[:, 0:qn_], in_=ps[db][:, 0:qn_], func=AF.Ln),
                             reads=[PK[db]], writes=["rden"])
                        P.op("act", lambda e: e.activation(out=rden[:, 0:qn_], in_=rden[:, 0:qn_], func=AF.Exp, scale=-1.0),
                             reads=["rden"], writes=["rden"])
                        P.op("dve", lambda e: e.tensor_tensor(out=hT[:, h, tok0 + q0:tok0 + q0 + qn_], in0=ps[ob][:, 0:qn_],
                                                              in1=rden[:, 0:qn_], op=ALU.mult),
                             reads=[PK[ob], "rden"], writes=[("hT", h, tq)])

                depth_ = 2
                for idx in range(len(flat) + depth_):
                    if idx < len(flat):
                        s_mm(*flat[idx])
                    if idx >= depth_:
                        pv_mm(*flat[idx - depth_])

            def head(h):
                hb = h % 2
                P.op("pool", lambda e: e.dma_start(
                    out=hw_q[hb][:, :, 0:192], in_=wuq_d[jl][:, h * 192:(h + 1) * 192].rearrange("(c p) f -> p c f", p=128)),
                    writes=[("hwq0", hb)], dma=True)
                P.op("pool", lambda e: e.dma_start(
                    out=hw_q[hb][:, :, 192:256], in_=wuqs_d[jl][:, h * 64:(h + 1) * 64].rearrange("(c p) f -> p c f", p=128)),
                    writes=[("hwq1", hb)], dma=True)
                P.op("pool", lambda e: e.dma_start(
                    out=hw_kv[hb], in_=wukv_d[jl][:, h * 256:(h + 1) * 256].rearrange("(c p) f -> p c f", p=128)),
                    writes=[("hwkv", hb)], dma=True)
                for tb in range(2):
                    t = (tok0 // TB) + tb
                    for kc in range(4):
                        P.op("pe", lambda e, kc=kc, t=t: e.matmul(
                            ps[0][:], lhsT=hw_q[hb][:, kc, 0:128], rhs=cq(kc, t),
                            start=(kc == 0), stop=(kc == 3)),
                            reads=[("hwq0", hb), ("cqT", kc, t)], writes=[PK[0]])
                    P.op("act", lambda e, tb=tb: e.activation(out=qn[:, tb * TB:(tb + 1) * TB], in_=ps[0][:],
                                                              func=AF.Identity, scale=ATTN_SCALE),
                         reads=[PK[0]], writes=[("qn", tb)])
                    for kc in range(4):
                        P.op("pe", lambda e, kc=kc, t=t: e.matmul(
                            ps[1][0:64, :], lhsT=hw_q[hb][:, kc, 128:192], rhs=cq(kc, t),
                            start=(kc == 0), stop=(kc == 3)),
                            reads=[("hwq0", hb), ("cqT", kc, t)], writes=[PK[1]])
                    if stream == 0:
                        P.op("act", lambda e, tb=tb: e.activation(out=qr[0:64, tb * TB:(tb + 1) * TB], in_=ps[1][0:64, :],
                                                                  func=AF.Identity, scale=ATTN_SCALE),
                             reads=[PK[1]], writes=[("qr", tb)])
                    else:
                        for kc in range(4):
                            P.op("pe", lambda e, kc=kc, t=t: e.matmul(
                                ps[2][0:64, :], lhsT=hw_q[hb][:, kc, 192:256], rhs=cq(kc, t),
                                start=(kc == 0), stop=(kc == 3)),
                                reads=[("hwq1", hb), ("cqT", kc, t)], writes=[PK[2]])
                        tok = slice(tb * TB, (tb + 1) * TB)
                        P.op("dve", lambda e, tok=tok: e.tensor_tensor(out=tmpf[0:64, 0, :], in0=ps[1][0:64, :],
                                                                       in1=ropeT[0:64, 0, tok], op=ALU.mult),
                             reads=[PK[1], ("rope", 0)], writes=[("tmpf", 0)])
                        P.op("dve", lambda e, tok=tok: e.tensor_tensor(out=tmpf[0:64, 1, :], in0=ps[2][0:64, :],
                                                                       in1=ropeT[0:64, 1, tok], op=ALU.mult),
                             reads=[PK[2], ("rope", 1)], writes=[("tmpf", 1)])
                        P.op("dve", lambda e: e.tensor_tensor(out=tmpf[0:64, 0, :], in0=tmpf[0:64, 0, :],
                                                              in1=tmpf[0:64, 1, :], op=ALU.add),
                             reads=[("tmpf", 0), ("tmpf", 1)], writes=[("tmpf", 0)])
                        P.op("act", lambda e, tb=tb: e.activation(out=qr[0:64, tb * TB:(tb + 1) * TB], in_=tmpf[0:64, 0, :],
                                                                  func=AF.Identity, scale=ATTN_SCALE),
                             reads=[("tmpf", 0)], writes=[("qr", tb)])
                for kb in range(nk // TB):
                    bank = 2 + kb % 2
                    for kc in range(2):
                        P.op("pe", lambda e, kc=kc, kb=kb, bank=bank: e.matmul(
                            ps[bank][:], lhsT=hw_kv[hb][:, kc, 0:128], rhs=ckvsrc[:, kc, kb * TB:(kb + 1) * TB],
                            start=(kc == 0), stop=(kc == 1)),
                            reads=[("hwkv", hb)] + ckv_keys, writes=[PK[bank]])
                    if kb % 2 == 0:
                        P.op("act", lambda e, kb=kb, bank=bank: e.activation(out=kn[:, kb * TB:(kb + 1) * TB], in_=ps[bank][:], func=AF.Identity),
                             reads=[PK[bank]], writes=[("kn", kb)])
                    else:
                        P.op("dve", lambda e, kb=kb, bank=bank: e.tensor_copy(out=kn[:, kb * TB:(kb + 1) * TB], in_=ps[bank][:]),
                             reads=[PK[bank]], writes=[("kn", kb)])
                for vb in range(nkt // 4):
                    bank = 2 + vb % 2
                    for q4 in range(4):
                        kt = vb * 4 + q4
                        for kc in range(2):
                            P.op("pe", lambda e, kc=kc, kt=kt, q4=q4, bank=bank: e.matmul(
                                ps[bank][:, q4 * 128:(q4 + 1) * 128], lhsT=ckvsrc[:, kc, kt * 128:(kt + 1) * 128],
                                rhs=hw_kv[hb][:, kc, 128:256], start=(kc == 0), stop=(kc == 1)),
                                reads=[("hwkv", hb)] + ckv_keys, writes=[PK[bank]])
                    vdst = Vh[:, vb * 4:(vb + 1) * 4, :]
                    vsrc = ps[bank][:].rearrange("p (k d) -> p k d", d=128)
                    if vb % 2 == 0:
                        P.op("dve", lambda e, vdst=vdst, vsrc=vsrc: e.tensor_copy(out=vdst, in_=vsrc),
                             reads=[PK[bank]], writes=[("Vh", vb)])
                    else:
                        P.op("act", lambda e, vdst=vdst, vsrc=vsrc: e.activation(out=vdst, in_=vsrc, func=AF.Identity),
                             reads=[PK[bank]], writes=[("Vh", vb)])
                if stream == 0:
                    qblocks = [(s * 256, 256, [2 * s, 2 * s + 1]) for s in range(4)]
                else:
                    qblocks = [(qb * TB, TB, list(range(nkt))) for qb in range(2)]
                head_tiles(h, qblocks)

            for h in range(8):
                head(h)
            if stream == 0:
                wo_stage(0, [(w_, None) for w_ in wob])

        def wo_stage(stream, wob, mid_loads=None):
            n = 0 if stream == 0 else 1
            tok0 = 0 if stream == 0 else 1024
            nb_ = len(wob)

            def ld(m):
                wb_i = m % nb_
                P.op("pool", lambda e: e.dma_start(
                    out=wob[wb_i][0], in_=wo_d[jl][:, m * 128:(m + 1) * 128].rearrange("(h p) d -> p h d", p=128)),
                    writes=[("wob", wb_i)], dma=True)

            for m in range(min(nb_, NCH)):
                ld(m)
            if mid_loads is not None:
                mid_loads()
            for m in range(NCH):
                wb_i = m % nb_
                for tb in range(2):
                    t = tok0 // TB + tb
                    bank = (m * 2 + tb) % 4
                    for hh in range(8):
                        P.op("pe", lambda e, hh=hh, t=t, bank=bank, wb_i=wb_i: e.matmul(
                            ps[bank][:], lhsT=wob[wb_i][0][:, hh, :], rhs=hT[:, hh, tsl(t)],
                            start=(hh == 0), stop=(hh == 7)),
                            reads=[("wob", wb_i), ("hT", hh, t)] + ([wob[wb_i][1]] if wob[wb_i][1] else []),
                            writes=[PK[bank]])
                    resid_update(bank, m, t, mcol(l, 5, m, n))
                if m + nb_ < NCH:
                    ld(m + nb_)

        attention(0)
        attention(1)

        def pre_work(mid_loads, tiles):
            wo_stage(1, tiles, mid_loads)
        return pre_work

    def fourier_phase(l, pre_normed=False):
        jl = l // 2
        P.fence()
        flush_pending()
        cv.reset(COMMON)
        cs = cv.take(2 * 512, BF16).rearrange("p (c f) -> p c f", c=2)
        dp = cv.take(2 * 2 * 256, BF16).rearrange("p (a n k) -> p a n k", a=2, n=2)
        ABp = cv.take(8 * 2048, BF16).rearrange("p (t f) -> p t f", t=8)
        ABs = cv.take(8 * 2048, BF16).rearrange("p (t f) -> p t f", t=8)
        NSB = 3
        abn = [cv.take(2048, BF16) for _ in range(NSB)]
        tbn = [cv.take(2 * 512, BF16).rearrange("p (a k) -> p a k", a=2) for _ in range(NSB)]
        fwm = [cv.take(NCH * 128, BF16).rearrange("p (c d) -> p c d", c=NCH) for _ in range(2)]
        if not pre_normed:
            norm_phase(l, 1)
        P.op("pool", lambda e: e.dma_start(out=cs, in_=dftC_d.rearrange("(c p) f -> p c f", p=128)), writes=["cs"], dma=True)
        for a in range(2):
            P.op("pool", lambda e, a=a: e.dma_start(out=dp[:, a], in_=dftP_d[a].rearrange("(n p) k -> p n k", p=128)),
                 writes=[("dp", a)], dma=True)
        for tile in list(range(8, 16)) + list(range(8)):
            dst = ABs if tile >= 8 else ABp
            ti = tile % 8
            t = tile // 4
            for g in range(4):
                bank = g % 4
                for kc in range(2):
                    P.op("pe", lambda e, g=g, kc=kc, tile=tile, bank=bank: e.matmul(
                        ps[bank][:], lhsT=hT[:, 2 * g + kc, tile * 128:(tile + 1) * 128], rhs=cs[:, kc, :],
                        start=(kc == 0), stop=(kc == 1)),
                        reads=["cs", ("hT", 2 * g + kc, t)], writes=[PK[bank]])
                dv = dst[:, ti, :].rearrange("p (s g c) -> p s g c", s=2, g=4)[:, :, g, :]
                sv = ps[bank][:].rearrange("p (s c) -> p s c", s=2)
                key = ("AB", tile)
                if g % 2 == 0:
                    P.op("act", lambda e, dv=dv, sv=sv: e.activation(out=dv, in_=sv, func=AF.Identity),
                         reads=[PK[bank]], writes=[(key, g)])
                else:
                    P.op("dve", lambda e, dv=dv, sv=sv: e.tensor_copy(out=dv, in_=sv),
                         reads=[PK[bank]], writes=[(key, g)])
            if tile >= 8 and tile % 2 == 1:
                part = (tile - 8) // 2
                P.op("sp", lambda e, part=part: e.dma_start(
                    out=f_in[jl][part].rearrange("(t p) f -> p t f", p=128), in_=ABs[:, 2 * part:2 * part + 2, :]),
                    reads=[(("AB", tl), g) for tl in (tile - 1, tile) for g in range(4)],
                    writes=[("f_in", jl, part)], dma=True)
                P.op("pool", lambda e, part=part: e.collective_compute(
                    "AllGather", ALU.bypass, replica_groups=GROUPS, ins=[f_in[jl][part]], outs=[f_out[jl][part]]),
                    reads=[("f_in", jl, part)], writes=[("f_out", jl, part)], dma=True, inc=1, nofence=True)
        for s in range(4):
            t = s // 2
            for m in range(NCH):
                bank = m % 4
                first = True
                for nt in range(2):
                    for a in range(2):
                        P.op("pe", lambda e, nt=nt, a=a, m=m, s=s, bank=bank, first=first: e.matmul(
                            ps[bank][:, 0:256], lhsT=ABp[:, s * 2 + nt, a * 1024 + m * 128:a * 1024 + (m + 1) * 128],
                            rhs=dp[:, a, nt, :], start=first, stop=(nt == 1 and a == 1)),
                            reads=[(("AB", s * 2 + nt), gg) for gg in range(4)] + [("dp", a)], writes=[PK[bank]])
                        first = False
                P.op("act", lambda e, m=m, s=s, bank=bank: e.activation(
                    out=hT[:, m, s * 256:(s + 1) * 256], in_=ps[bank][:, 0:256], func=AF.Identity, scale=1.0 / 256.0),
                    reads=[PK[bank]], writes=[("hT", m, t)])
        def fc_stage(tblocks, fwbuf, mid_loads=None):
            nb_ = len(fwbuf)

            def ld(m):
                wi = m % nb_
                P.op("pool", lambda e: e.dma_start(
                    out=fwbuf[wi][0], in_=fw_d[jl][:, m * 128:(m + 1) * 128].rearrange("(c p) d -> p c d", p=128)),
                    writes=[("fwm", wi)], dma=True)

            for m in range(min(nb_, NCH)):
                ld(m)
            if mid_loads is not None:
                mid_loads()
            for m in range(NCH):
                wi = m % nb_
                if m >= nb_ and False:
                    pass
                for t in tblocks:
                    n = cond_of(t)
                    bank = (m * 2 + t) % 4
                    for c in range(NCH):
                        P.op("pe", lambda e, c=c, t=t, bank=bank, wi=wi: e.matmul(
                            ps[bank][:], lhsT=fwbuf[wi][0][:, c, :], rhs=hT[:, c, tsl(t)],
                            start=(c == 0), stop=(c == NCH - 1)),
                            reads=[("fwm", wi), ("hT", c, t)] + ([fwbuf[wi][1]] if fwbuf[wi][1] else []),
                            writes=[PK[bank]])
                    s_ = rot("tmpf")
                    P.op("dve", lambda e, s_=s_, m=m, bank=bank, n=n: e.tensor_scalar(
                        out=tmpf[:, s_, :], in0=ps[bank][:], scalar1=fbT[:, jl, m:m + 1], scalar2=mcol(l, 5, m, n),
                        op0=ALU.add, op1=ALU.mult),
                        reads=[PK[bank], "fbT", "modS"], writes=[("tmpf", s_)])
                    P.op("dve", lambda e, s_=s_, m=m, t=t: e.tensor_tensor(out=xT[:, m, tsl(t)], in0=xT[:, m, tsl(t)],
                                                                           in1=tmpf[:, s_, :], op=ALU.add),
                         reads=[("tmpf", s_), ("xT", m, t)], writes=[("xT", m, t)])
                if m + nb_ < NCH:
                    ld(m + nb_)

        def pre_work1(mid_loads, tiles):
            fc_stage((0, 1), tiles, mid_loads)
        return pre_work1

    def fourier_part2(l):
        jl = l // 2
        P.fence()
        flush_pending()
        cv.reset(COMMON)
        NSB = 3
        abn = [cv.take(2048, BF16) for _ in range(NSB)]
        tbn = [cv.take(2 * 512, BF16).rearrange("p (a k) -> p a k", a=2) for _ in range(NSB)]

        def fc_stage(tblocks, fwbuf, mid_loads=None):
            nb_ = len(fwbuf)

            def ld(m):
                wi = m % nb_
                P.op("pool", lambda e: e.dma_start(
                    out=fwbuf[wi][0], in_=fw_d[jl][:, m * 128:(m + 1) * 128].rearrange("(c p) d -> p c d", p=128)),
                    writes=[("fwm", wi)], dma=True)

            for m in range(min(nb_, NCH)):
                ld(m)
            if mid_loads is not None:
                mid_loads()
            for m in range(NCH):
                wi = m % nb_
                for t in tblocks:
                    n = cond_of(t)
                    bank = (m * 2 + t) % 4
                    for c in range(NCH):
                        P.op("pe", lambda e, c=c, t=t, bank=bank, wi=wi: e.matmul(
                            ps[bank][:], lhsT=fwbuf[wi][0][:, c, :], rhs=hT[:, c, tsl(t)],
                            start=(c == 0), stop=(c == NCH - 1)),
                            reads=[("fwm", wi), ("hT", c, t)] + ([fwbuf[wi][1]] if fwbuf[wi][1] else []),
                            writes=[PK[bank]])
                    s_ = rot("tmpf")
                    P.op("dve", lambda e, s_=s_, m=m, bank=bank, n=n: e.tensor_scalar(
                        out=tmpf[:, s_, :], in0=ps[bank][:], scalar1=fbT[:, jl, m:m + 1], scalar2=mcol(l, 5, m, n),
                        op0=ALU.add, op1=ALU.mult),
                        reads=[PK[bank], "fbT", "modS"], writes=[("tmpf", s_)])
                    P.op("dve", lambda e, s_=s_, m=m, t=t: e.tensor_tensor(out=xT[:, m, tsl(t)], in0=xT[:, m, tsl(t)],
                                                                           in1=tmpf[:, s_, :], op=ALU.add),
                         reads=[("tmpf", s_), ("xT", m, t)], writes=[("xT", m, t)])
                if m + nb_ < NCH:
                    ld(m + nb_)

        for kb in range(2):
            t = 2 + kb
            nt_order = [r_ * 8 + part * 2 + j_ for part in range(4) for r_ in range(4) for j_ in range(2)]
            for ni, nt in enumerate(nt_order):
                b = (kb * 32 + ni) % NSB
                r_, w_ = nt // 8, nt % 8
                part, j_ = w_ // 2, w_ % 2
                src = f_out[jl][part][r_ * 256 + j_ * 128:r_ * 256 + (j_ + 1) * 128, :]
                P.op("sp", lambda e, src=src, b=b: e.dma_start(out=abn[b], in_=src),
                     reads=[("f_out", jl, part)], writes=[("abn", b)], dma=True)
                P.op("sp", lambda e, nt=nt, b=b, kb=kb: e.dma_start(
                    out=tbn[b], in_=dftS_bf[:, nt * 128:(nt + 1) * 128, kb * 512:(kb + 1) * 512].rearrange("a p k -> p a k")),
                    reads=DFT_KEYS, writes=[("tbn", b)], dma=True)
                for m in range(NCH):
                    for a in range(2):
                        P.op("pe", lambda e, ni=ni, a=a, m=m, b=b: e.matmul(
                            ps[m][:], lhsT=abn[b][:, a * 1024 + m * 128:a * 1024 + (m + 1) * 128], rhs=tbn[b][:, a, :],
                            start=(ni == 0 and a == 0), stop=(ni == 31 and a == 1)),
                            reads=[("abn", b), ("tbn", b)], writes=[PK[m]])
            for m in range(NCH):
                if m % 2 == 0:
                    P.op("act", lambda e, m=m, t=t: e.activation(out=hT[:, m, tsl(t)], in_=ps[m][:], func=AF.Identity,
                                                                 scale=1.0 / 1024.0),
                         reads=[PK[m]], writes=[("hT", m, t)])
                else:
                    P.op("dve", lambda e, m=m, t=t: e.tensor_scalar(out=hT[:, m, tsl(t)], in0=ps[m][:], scalar1=1.0 / 1024.0,
                                                                    scalar2=None, op0=ALU.mult),
                         reads=[PK[m]], writes=[("hT", m, t)])
        def pre_work(mid_loads, tiles):
            fc_stage((2, 3), tiles, mid_loads)
        return pre_work

    for l in range(depth):
        mixer_on = ("m" if l % 2 == 0 else "f") in DBG_PHASES
        ffn_phase(l, 0, pre_normed=(l > 0), next_norm=(l, 1, False) if mixer_on else None, skip_fence=(l > 0),
                  carry=mixer_on, tail_order=(2, 3, 0, 1) if mixer_on else (0, 1, 2, 3))
        last = (l == depth - 1)
        nn = (0, 0, True) if last else (l + 1, 0, False)
        if mixer_on and l % 2 == 1:
            pw1 = fourier_phase(l, pre_normed=True)
            ffn_phase(l, 1, pre_normed=False, next_norm=nn, pre_work=pw1, carry=True, tblocks=(0, 1))
            pw = fourier_part2(l)
            ffn_phase(l, 1, pre_normed=False, next_norm=nn, pre_work=pw, carry=not last, tblocks=(2, 3))
        else:
            pw = mla_phase(l, pre_normed=True) if mixer_on else None
            ffn_phase(l, 1, pre_normed=False, next_norm=nn, pre_work=pw, carry=not last)
    if depth == 0:
        P.fence()
        norm_phase(0, 0, final=True)
    P.emit(nc, final_waits=outs)
    st.close()
    return nc, P


def _fm(a):
    t = a.shape[0]
    return np.ascontiguousarray(a.T.reshape(NCH, 128, t).transpose(1, 0, 2))


def _vec(a, nch):
    sh = a.shape[:-1]
    b = a.reshape(sh + (nch, 128))
    return np.ascontiguousarray(np.moveaxis(b, -1, 0))


def _const_tables():
    f32 = np.float32
    k = np.arange(256)
    ang = 2 * np.pi * np.outer(k, k) / 256.0
    dftC = np.concatenate([np.cos(ang), np.sin(ang)], axis=1).astype(np.float32)
    dftP = np.stack([np.cos(ang), -np.sin(ang)]).astype(np.float32)
    n = np.arange(4096, dtype=np.int64)
    dftS = []
    for qd in range(4):
        kk = np.arange(qd * 1024, (qd + 1) * 1024, dtype=np.int64)
        a = 2 * np.pi * ((np.outer(n, kk) % 4096).astype(np.float64)) / 4096.0
        dftS.append(np.stack([np.cos(a), -np.sin(a)]).astype(np.float32))
    inv = 1.0 / (10000.0 ** (np.arange(16, dtype=np.float32) / 16.0))
    pos = np.arange(4096)
    row = (pos // 64).astype(np.float32)
    col = (pos % 64).astype(np.float32)
    ang = np.stack([row[:, None] * inv, col[:, None] * inv], axis=1).astype(np.float32)
    cos = np.cos(ang)
    sin = np.sin(ang)
    cosT = np.zeros((64, 4096), f32)
    sinT = np.zeros((64, 4096), f32)
    for a in range(2):
        for hf in range(2):
            for f in range(16):
                p = a * 32 + hf * 16 + f
                cosT[p] = cos[:, a, f]
                sinT[p] = -sin[:, a, f] if hf == 0 else sin[:, a, f]
    rope = [np.ascontiguousarray(np.stack([cosT[:, q * 1024:(q + 1) * 1024], sinT[:, q * 1024:(q + 1) * 1024]]))
            for q in range(4)]
    return dftC, dftP, dftS, rope


def _swap_cols(w, nheads, base, stride):
    cols = []
    for h in range(nheads):
        o = h * stride + base
        for a in range(2):
            cols += list(range(o + a * 32 + 16, o + a * 32 + 32)) + list(range(o + a * 32, o + a * 32 + 16))
    return np.ascontiguousarray(w[..., cols])


_CACHE = {}
DEPTH_RUN = DEPTH


def kernel(x_prompt, x_sample, cache_ckv, cache_kpe, c, c_ctx, w_mod, b_mod, norm_g,
           ffn_wg, ffn_wu, ffn_wd, mla_w_dq, mla_q_norm, mla_w_uq, mla_w_dkv, mla_kv_norm,
           mla_w_ukv, mla_w_o, fourier_w, fourier_b, final_norm):
    A = lambda a: np.ascontiguousarray(np.asarray(a, dtype=np.float32))
    x_prompt, x_sample, cache_ckv, cache_kpe = A(x_prompt), A(x_sample), A(cache_ckv), A(cache_kpe)
    c, c_ctx, w_mod, b_mod, norm_g = A(c), A(c_ctx), A(w_mod), A(b_mod), A(norm_g)
    ffn_wg, ffn_wu, ffn_wd = A(ffn_wg), A(ffn_wu), A(ffn_wd)
    mla_w_dq, mla_q_norm, mla_w_uq, mla_w_dkv = A(mla_w_dq), A(mla_q_norm), A(mla_w_uq), A(mla_w_dkv)
    mla_kv_norm, mla_w_ukv, mla_w_o = A(mla_kv_norm), A(mla_w_ukv), A(mla_w_o)
    fourier_w, fourier_b, final_norm = A(fourier_w), A(fourier_b), A(final_norm)

    if "nc" not in _CACHE:
        _CACHE["nc"] = build_program(DEPTH_RUN)[0]
        _CACHE["tables"] = _const_tables()
    nc = _CACHE["nc"]
    dftC, dftP, dftS, rope = _CACHE["tables"]

    shared = {
        "bmodT": _vec(b_mod, 72), "normgT": _vec(norm_g, 8), "finalT": _vec(final_norm, 8),
        "qnT": _vec(mla_q_norm, 4), "kvnT": _vec(mla_kv_norm, 2), "fbT": _vec(fourier_b, 8),
        "wg": ffn_wg, "wu": ffn_wu, "wd": ffn_wd, "wdq": mla_w_dq, "wuq": mla_w_uq,
        "wuqs": _swap_cols(mla_w_uq, 8, 128, 192), "wdkv": mla_w_dkv,
        "wdkvs": _swap_cols(mla_w_dkv, 1, 256, 0), "wukv": mla_w_ukv, "wo": mla_w_o, "fw": fourier_w,
        "dftC": dftC, "dftP": dftP,
    }
    in_maps = []
    for r in range(8):
        b, qd = r // 4, r % 4
        xp = x_prompt[4 * r:4 * r + 4].reshape(1024, D)
        xs = x_sample[b, qd * 1024:(qd + 1) * 1024]
        m = dict(shared)
        m["xT"] = _fm(np.concatenate([xp, xs], axis=0))
        m["cckv"] = np.ascontiguousarray(cache_ckv[b].transpose(0, 2, 1))
        m["ckpe"] = np.ascontiguousarray(cache_kpe[b].transpose(0, 2, 1))
        m["condT"] = _vec(np.stack([c_ctx, c[b]]), 8).transpose(0, 2, 1).copy()
        m["wmod"] = np.ascontiguousarray(w_mod[:, :, qd * 2304:(qd + 1) * 2304])
        m["ropeT"] = rope[qd]
        m["dftS"] = dftS[qd]
        in_maps.append(m)

    res = run_bass_kernel_spmd(nc, in_maps, core_ids=list(range(8)))
    y_prompt = np.empty((32, 256, D), np.float32)
    y_sample = np.empty((2, 4096, D), np.float32)
    new_ckv = np.empty((32, 2, 256, 256), np.float32)
    new_kpe = np.empty((32, 2, 256, 64), np.float32)
    for r in range(8):
        b, qd = r // 4, r % 4
        o = res.results[r]
        y = np.asarray(o["yT"]).transpose(2, 1, 0).reshape(NT, D)
        y_prompt[4 * r:4 * r + 4] = y[:1024].reshape(4, 256, D)
        y_sample[b, qd * 1024:(qd + 1) * 1024] = y[1024:]
        ck = np.asarray(o["o_ckv"])
        kp = np.asarray(o["o_kpe"])
        new_ckv[4 * r:4 * r + 4] = ck.reshape(2, 256, 4, 256).transpose(2, 0, 3, 1)
        new_kpe[4 * r:4 * r + 4] = kp.reshape(2, 64, 4, 256).transpose(2, 0, 3, 1)
    return (y_prompt, y_sample, new_ckv, new_kpe)
```

```python
import math
from contextlib import ExitStack

import numpy as np
import ml_dtypes
import concourse.bass as bass
import concourse.mybir as mybir
from concourse.bass_utils import run_bass_kernel_spmd

F32 = mybir.dt.float32
BF16 = mybir.dt.bfloat16
AF = mybir.ActivationFunctionType
ALU = mybir.AluOpType

D = 1024
DFF = 2816
NCH = 8
TB = 512
NT = 2048
DEPTH = 4
EPS = 1e-6
ATTN_SCALE = 1.0 / math.sqrt(192.0)
GROUPS = [[0, 1, 2, 3], [4, 5, 6, 7]]
ENGINES = ("sp", "act", "dve", "pool", "pe")
SEM_ROT = 8000
SCR_BYTES = 102 * 1024
DBG_PHASES = "mf"


class Op:
    __slots__ = ("id", "eng", "fn", "deps", "is_dma", "slot", "sig", "inc", "needs_signal")

    def __init__(self, id, eng, fn, is_dma, slot, inc):
        self.id = id
        self.eng = eng
        self.fn = fn
        self.deps = set()
        self.is_dma = is_dma
        self.slot = slot
        self.sig = None
        self.inc = inc
        self.needs_signal = False


class Prog:
    def __init__(self):
        self.ops = []
        self.last_w = {}
        self.readers = {}
        self.eng_ops = {e: [] for e in ENGINES}
        self.fence_set = set()
        self.dma_since_fence = []
        self.fenced = {e: True for e in ENGINES}

    def fence(self):
        fs = set(self.dma_since_fence)
        for e in ENGINES:
            for o in reversed(self.eng_ops[e]):
                if not o.is_dma:
                    fs.add(o.id)
                    break
        self.fence_set = fs
        self.dma_since_fence = []
        self.fenced = {e: False for e in ENGINES}

    def op(self, eng, fn, reads=(), writes=(), dma=False, slot=None, inc=16, nofence=False):
        o = Op(len(self.ops), eng, fn, dma, slot, inc)
        deps = set()
        for k in reads:
            w = self.last_w.get(k)
            if w is not None:
                deps.add(w)
        for k in writes:
            w = self.last_w.get(k)
            if w is not None:
                deps.add(w)
            for r in self.readers.get(k, {}).values():
                if isinstance(r, list):
                    deps.update(r)
                else:
                    deps.add(r)
        if not self.fenced[eng]:
            deps |= self.fence_set
            self.fenced[eng] = True
        deps.discard(o.id)
        o.deps = deps
        for k in reads:
            rd = self.readers.setdefault(k, {})
            if dma:
                rd.setdefault("dma", []).append(o.id)
            else:
                rd[eng] = o.id
        for k in writes:
            self.last_w[k] = o.id
            self.readers[k] = {}
        if dma:
            if slot is None:
                o.slot = ("dma", writes[0])
            if not nofence:
                self.dma_since_fence.append(o.id)
        self.eng_ops[eng].append(o)
        self.ops.append(o)
        return o

    def emit(self, nc, final_waits=()):
        ops = self.ops
        for o in ops:
            for d in o.deps:
                p = ops[d]
                if p.is_dma:
                    p.needs_signal = True
                elif p.eng == "pe" and o.eng == "pe" and not o.is_dma:
                    continue
                else:
                    p.needs_signal = True
        for d in final_waits:
            d.needs_signal = True
        sem_keys = []
        comp_cnt = {e: 0 for e in ENGINES}
        slot_cnt = {}
        grp_total = {}
        for o in ops:
            if o.is_dma and isinstance(o.slot, tuple) and o.slot[0] == "grp":
                grp_total[o.slot] = grp_total.get(o.slot, 0) + o.inc
        for o in ops:
            if o.is_dma:
                c = slot_cnt.get(o.slot, 0) + o.inc
                slot_cnt[o.slot] = c
                o.sig = (o.slot, grp_total.get(o.slot, c))
                if o.slot not in slot_cnt or o.slot not in sem_keys:
                    sem_keys.append(o.slot)
            elif o.needs_signal:
                n = comp_cnt[o.eng]
                comp_cnt[o.eng] = n + 1
                key = ("c", o.eng, n // SEM_ROT)
                o.sig = (key, n % SEM_ROT + 1)
                if key not in sem_keys:
                    sem_keys.append(key)
        sem_keys = list(dict.fromkeys(sem_keys))
        self.n_sems = len(sem_keys)
        stack = ExitStack()
        sems = {}
        for i, k in enumerate(sem_keys):
            sems[k] = stack.enter_context(nc.semaphore("s%d" % i))

        def run(engname, e):
            waited = {}
            for o in self.eng_ops[engname]:
                need = {}
                for d in o.deps:
                    p = ops[d]
                    if p.sig is None:
                        continue
                    if (not p.is_dma) and p.eng == "pe" and engname == "pe" and not o.is_dma:
                        continue
                    k, c = p.sig
                    if need.get(k, 0) < c:
                        need[k] = c
                for k, c in need.items():
                    if waited.get(k, 0) >= c:
                        continue
                    e.wait_ge(sems[k], c)
                    waited[k] = c
                ins = o.fn(e)
                if o.sig is not None:
                    ins.then_inc(sems[o.sig[0]], o.inc if o.is_dma else 1)
            if engname == "sp":
                for d in final_waits:
                    k, c = d.sig
                    e.wait_ge(sems[k], c)

        with stack:
            with nc.Block() as block:
                @block.sync
                def _(e):
                    run("sp", e)

                @block.scalar
                def _(e):
                    run("act", e)

                @block.vector
                def _(e):
                    run("dve", e)

                @block.gpsimd
                def _(e):
                    run("pool", e)

                @block.tensor
                def _(e):
                    run("pe", e)


class Carve:
    def __init__(self, scr):
        self.scr = scr
        self.off = 0

    def reset(self, off=0):
        self.off = off

    def take(self, nelem, dtype, parts=128):
        nb = nelem * (4 if dtype == F32 else 2)
        nb = (nb + 63) // 64 * 64
        a = self.off // 2
        self.off += nb
        assert self.off <= SCR_BYTES, ("scratch overflow", self.off)
        v = self.scr[:, a:a + nb // 2]
        if dtype == F32:
            v = v.bitcast(F32)
        v = v[:, 0:nelem]
        return v


def build_program(depth=DEPTH):
    nc = bass.Bass("TRN2", target_bir_lowering=False)

    def din(name, shape, dt=F32):
        return nc.dram_tensor(name, list(shape), dt, kind="ExternalInput").ap()

    def dout(name, shape, dt=F32):
        return nc.dram_tensor(name, list(shape), dt, kind="ExternalOutput").ap()

    xT_d = din("xT", [128, NCH, NT])
    cckv_d = din("cckv", [2, 256, 512])
    ckpe_d = din("ckpe", [2, 64, 512])
    condT_d = din("condT", [128, NCH, 2])
    wmod_d = din("wmod", [4, D, 2304])
    bmodT_d = din("bmodT", [128, 4, 72])
    normgT_d = din("normgT", [128, 4, 3, 8])
    finalT_d = din("finalT", [128, 8])
    qnT_d = din("qnT", [128, 2, 4])
    kvnT_d = din("kvnT", [128, 2, 2])
    fbT_d = din("fbT", [128, 2, 8])
    wg_d = din("wg", [4, 2, D, DFF])
    wu_d = din("wu", [4, 2, D, DFF])
    wd_d = din("wd", [4, 2, DFF, D])
    wdq_d = din("wdq", [2, D, 512])
    wuq_d = din("wuq", [2, 512, 1536])
    wuqs_d = din("wuqs", [2, 512, 512])
    wdkv_d = din("wdkv", [2, D, 320])
    wdkvs_d = din("wdkvs", [2, D, 64])
    wukv_d = din("wukv", [2, 256, 2048])
    wo_d = din("wo", [2, D, D])
    fw_d = din("fw", [2, D, D])
    rope_d = din("ropeT", [2, 64, 1024])
    dftC_d = din("dftC", [256, 512])
    dftP_d = din("dftP", [2, 256, 256])
    dftS_d = din("dftS", [2, 4096, 1024])

    yT_d = dout("yT", [128, NCH, NT])
    ockv_d = dout("o_ckv", [2, 256, 1024])
    okpe_d = dout("o_kpe", [2, 64, 1024])

    mod_in = nc.dram_tensor("mod_in", [128, 144], F32).ap()
    mod_out = nc.dram_tensor("mod_out", [512, 144], F32).ap()
    x_in = [nc.dram_tensor("x_in%d" % j, [320, 1024], BF16).ap() for j in range(2)]
    x_out = [nc.dram_tensor("x_out%d" % j, [1280, 1024], BF16).ap() for j in range(2)]
    f_in = [[nc.dram_tensor("f_in%d_%d" % (j, q), [256, 2048], BF16).ap() for q in range(4)] for j in range(2)]
    f_out = [[nc.dram_tensor("f_out%d_%d" % (j, q), [1024, 2048], BF16).ap() for q in range(4)] for j in range(2)]

    dftS_bf = nc.dram_tensor("dftS_bf", [2, 4096, 1024], BF16).ap()
    DFT_KEYS = [("dftS_bf", a, q) for a in range(2) for q in range(4)]

    P = Prog()
    st = ExitStack()
    sb = lambda name, shape, dt: st.enter_context(nc.sbuf_tensor(name, list(shape), dt))
    xT = sb("xTs", [128, NCH, NT], F32)
    hT = sb("hTs", [128, NCH, NT], BF16)
    modS = sb("modS", [128, 4 * 72 * 2], F32)
    bmodT = sb("bmodTs", [128, 4, 72], F32)
    normgT = sb("normgTs", [128, 4, 3, 8], F32)
    finalT = sb("finalTs", [128, 8], F32)
    qnT = sb("qnTs", [128, 2, 4], F32)
    kvnT = sb("kvnTs", [128, 2, 2], F32)
    fbT = sb("fbTs", [128, 2, 8], F32)
    condT = sb("condTs", [128, NCH, 2], F32)
    scT = sb("scTs", [128, NCH, 2], F32)
    scB = sb("scBs", [128, NCH, 2], BF16)
    onesB = sb("onesB", [128, 128], BF16)
    onesF = sb("onesF", [128, 128], F32)
    epsT = sb("epsT", [128, 1], F32)
    scr = sb("scr", [128, SCR_BYTES // 2], BF16)
    ps = [st.enter_context(nc.psum_tensor("ps%d" % i, [128, 512], F32)) for i in range(8)]
    PK = [("ps", i) for i in range(8)]
    cv = Carve(scr)

    mod5 = modS[:].rearrange("p (l k c n) -> p l k c n", l=4, k=9, c=8)
    mod_lrx = modS[:].rearrange("p (l r x) -> p l r x", l=4, r=4)
    mod_lcn = modS[:].rearrange("p (l c n) -> p l c n", l=4, c=72)

    outs = []

    def mcol(l, k, c, cond):
        return mod5[:, l, k, c, cond:cond + 1]

    def cond_of(t):
        return 0 if t < 2 else 1

    def tsl(t):
        return slice(t * TB, (t + 1) * TB)

    cv.reset(0)
    sq = cv.take(2 * TB, BF16).rearrange("p (a b) -> p a b", a=2)
    rs = cv.take(2 * TB, F32).rearrange("p (a b) -> p a b", a=2)
    tmpf = cv.take(2 * TB, F32).rearrange("p (a b) -> p a b", a=2)
    COMMON = cv.off
    cnt = {"sq": 0, "tmpf": 0}

    def rot(name, n=2):
        v = cnt.get(name, 0)
        cnt[name] = v + 1
        return v % n

    P.op("sp", lambda e: e.dma_start(out=condT[:], in_=condT_d), writes=["condT"], dma=True, slot=("grp", "setup"))
    for (tl, td, nm) in ((bmodT, bmodT_d, "bmodT"), (normgT, normgT_d, "normgT"), (finalT, finalT_d, "finalT"),
                         (qnT, qnT_d, "qnT"), (kvnT, kvnT_d, "kvnT"), (fbT, fbT_d, "fbT")):
        P.op("sp", lambda e, tl=tl, td=td: e.dma_start(out=tl[:], in_=td), writes=[nm], dma=True, slot=("grp", "setup"))
    P.op("dve", lambda e: e.memset(onesB[:], 1.0), writes=["ones"])
    P.op("dve", lambda e: e.memset(onesF[:], 1.0), writes=["onesF"])
    P.op("dve", lambda e: e.memset(epsT[:], EPS), writes=["eps"])
    P.op("act", lambda e: e.activation(out=scB[:], in_=condT[:], func=AF.Silu), reads=["condT"], writes=["scT"])

    for c in range(NCH):
        P.op("sp", lambda e, c=c: e.dma_start(out=xT[:, c, :], in_=xT_d[:, c, :]),
             writes=[("xT", c, t) for t in range(4)], dma=True, slot=("grp", "xT"))

    cv.reset(COMMON)
    wm = [cv.take(NCH * 1152, BF16).rearrange("p (c f) -> p c f", c=NCH) for _ in range(4)]
    modin = cv.take(144, F32)
    for l in range(4):
        for hf in range(2):
            b = (l * 2 + hf) % 4
            src = wmod_d[l, :, hf * 1152:(hf + 1) * 1152].rearrange("(c p) f -> p c f", p=128)
            P.op("pool", lambda e, b=b, src=src: e.dma_start(out=wm[b], in_=src), writes=[("wm", b)], dma=True)
            for i in range(9):
                col = (hf * 9 + i) * 2
                for c in range(NCH):
                    P.op("pe", lambda e, b=b, i=i, c=c, col=col: e.matmul(
                        ps[0][:, col:col + 2], lhsT=wm[b][:, c, i * 128:(i + 1) * 128], rhs=scB[:, c, :],
                        start=(c == 0), stop=(c == NCH - 1)),
                        reads=[("wm", b), "scT"], writes=[PK[0]])
        P.op("dve", lambda e, l=l: e.tensor_copy(out=modin[:, l * 36:(l + 1) * 36], in_=ps[0][:, 0:36]),
             reads=[PK[0]], writes=["modin"])
    P.op("sp", lambda e: e.dma_start(out=mod_in, in_=modin), reads=["modin"], writes=["mod_in"], dma=True)
    P.op("pool", lambda e: e.collective_compute("AllGather", ALU.bypass, replica_groups=GROUPS,
                                                ins=[mod_in], outs=[mod_out]),
         reads=["mod_in"], writes=["mod_out"], dma=True, inc=1)
    for l in range(4):
        src = mod_out[:, l * 36:(l + 1) * 36].rearrange("(r p) x -> p r x", p=128)
        P.op("sp", lambda e, l=l, src=src: e.dma_start(out=mod_lrx[:, l], in_=src),
             reads=["mod_out"], writes=["modS"], dma=True, slot=("dma", "modS", l))
    for n in range(2):
        P.op("dve", lambda e, n=n: e.tensor_tensor(out=mod_lcn[:, :, :, n], in0=mod_lcn[:, :, :, n],
                                                   in1=bmodT[:], op=ALU.add),
             reads=["modS", "bmodT"], writes=["modS"])
    for i in range(3):
        for n in range(2):
            P.op("dve", lambda e, i=i, n=n: e.scalar_tensor_tensor(
                out=mod5[:, :, 3 * i + 1, :, n], in0=mod5[:, :, 3 * i + 1, :, n], scalar=1.0,
                in1=normgT[:, :, i, :], op0=ALU.add, op1=ALU.mult),
                reads=["modS", "normgT"], writes=["modS"])
            if i != 1:
                P.op("dve", lambda e, i=i, n=n: e.tensor_scalar(
                    out=mod5[:, :, 3 * i + 2, :, n], in0=mod5[:, :, 3 * i + 2, :, n], scalar1=0.5,
                    scalar2=None, op0=ALU.mult),
                    reads=["modS"], writes=["modS"])

    def rstd_block(src_keys, nchunks, get_src, inv_n, bank, extra_reads=()):
        for c in range(nchunks):
            s = rot("sq")
            src, rk = get_src(c)
            P.op("act", lambda e, s=s, src=src: e.activation(out=sq[:, s, :], in_=src, func=AF.Square),
                 reads=[rk], writes=[("sq", s)])
            P.op("pe", lambda e, s=s, c=c: e.matmul(ps[bank][:], lhsT=onesB[:], rhs=sq[:, s, :],
                                                    start=(c == 0), stop=(c == nchunks - 1)),
                 reads=[("sq", s), "ones"], writes=[PK[bank]])
        P.op("act", lambda e: e.activation(out=rs[:, 0, :], in_=ps[bank][:], func=AF.Ln,
                                           bias=epsT[:, 0:1], scale=inv_n),
             reads=[PK[bank], "eps"], writes=[("rs", 0)])
        P.op("act", lambda e: e.activation(out=rs[:, 1, :], in_=rs[:, 0, :], func=AF.Exp, scale=-0.5),
             reads=[("rs", 0)], writes=[("rs", 1)])

    def norm_phase(l, i, final=False):
        for t in range(4):
            n = cond_of(t)
            rstd_block(None, NCH, lambda c, t=t: (xT[:, c, tsl(t)], ("xT", c, t)), 1.0 / D, 7)
            for c in range(NCH):
                s = rot("tmpf")
                g = finalT[:, c:c + 1] if final else mcol(l, 3 * i + 1, c, n)
                P.op("dve", lambda e, s=s, c=c, t=t, g=g: e.scalar_tensor_tensor(
                    out=tmpf[:, s, :], in0=xT[:, c, tsl(t)], scalar=g, in1=rs[:, 1, :],
                    op0=ALU.mult, op1=ALU.mult),
                    reads=[("xT", c, t), ("rs", 1), "modS", "finalT"], writes=[("tmpf", s)])
                if final:
                    outs.append(P.op("sp", lambda e, s=s, c=c, t=t: e.dma_start(out=yT_d[:, c, tsl(t)], in_=tmpf[:, s, :]),
                                     reads=[("tmpf", s)], writes=[("yT", c, t)], dma=True, slot=("out", "tmpf", s)))
                else:
                    P.op("act", lambda e, s=s, c=c, t=t, l=l, i=i, n=n: e.activation(
                        out=hT[:, c, tsl(t)], in_=tmpf[:, s, :], func=AF.Identity,
                        bias=mcol(l, 3 * i, c, n), scale=1.0),
                        reads=[("tmpf", s), "modS"], writes=[("hT", c, t)])

    pending = []

    def norm_items(l, i, t, final=False):
        items = []
        n = cond_of(t)
        bank = 7

        def sq_item(c):
            def f():
                s = rot("sq")
                P.op("act", lambda e: e.activation(out=sq[:, s, :], in_=xT[:, c, tsl(t)], func=AF.Square),
                     reads=[("xT", c, t)], writes=[("sq", s)])
                P.op("pe", lambda e: e.matmul(ps[bank][:], lhsT=onesB[:], rhs=sq[:, s, :],
                                              start=(c == 0), stop=(c == NCH - 1)),
                     reads=[("sq", s), "ones"], writes=[PK[bank]])
            return f

        def rs_item():
            P.op("act", lambda e: e.activation(out=rs[:, 0, :], in_=ps[bank][:], func=AF.Ln,
                                               bias=epsT[:, 0:1], scale=1.0 / D),
                 reads=[PK[bank], "eps"], writes=[("rs", 0)])
            P.op("act", lambda e: e.activation(out=rs[:, 1, :], in_=rs[:, 0, :], func=AF.Exp, scale=-0.5),
                 reads=[("rs", 0)], writes=[("rs", 1)])

        def out_item(c):
            def f():
                s = rot("tmpf")
                g = finalT[:, c:c + 1] if final else mcol(l, 3 * i + 1, c, n)
                P.op("dve", lambda e: e.scalar_tensor_tensor(
                    out=tmpf[:, s, :], in0=xT[:, c, tsl(t)], scalar=g, in1=rs[:, 1, :],
                    op0=ALU.mult, op1=ALU.mult),
                    reads=[("xT", c, t), ("rs", 1), "modS", "finalT"], writes=[("tmpf", s)])
                if final:
                    outs.append(P.op("sp", lambda e: e.dma_start(out=yT_d[:, c, tsl(t)], in_=tmpf[:, s, :]),
                                     reads=[("tmpf", s)], writes=[("yT", c, t)], dma=True, slot=("out", "tmpf", s)))
                else:
                    P.op("act", lambda e: e.activation(
                        out=hT[:, c, tsl(t)], in_=tmpf[:, s, :], func=AF.Identity,
                        bias=mcol(l, 3 * i, c, n), scale=1.0),
                        reads=[("tmpf", s), "modS"], writes=[("hT", c, t)])
            return f

        for c in range(NCH):
            items.append((t, sq_item(c)))
        items.append((t, rs_item))
        for c in range(NCH):
            items.append((t, out_item(c)))
        return items

    def pop_pending(n):
        for _ in range(n):
            if pending:
                pending.pop(0)[1]()

    def flush_pending(upto_t=None):
        while pending and (upto_t is None or any(tg <= upto_t for tg, _ in pending)):
            pending.pop(0)[1]()

    def resid_update(bank, m, t, gate_ap, tok=None):
        sl = tsl(t) if tok is None else tok
        P.op("dve", lambda e: e.scalar_tensor_tensor(
            out=xT[:, m, sl], in0=ps[bank][:, 0:(sl.stop - sl.start)], scalar=gate_ap, in1=xT[:, m, sl],
            op0=ALU.mult, op1=ALU.add),
            reads=[PK[bank], ("xT", m, t), "modS"], writes=[("xT", m, t)])

    def ffn_phase(l, i, pre_normed=False, next_norm=None, skip_fence=False, pre_work=None, carry=False,
                  tail_order=(0, 1, 2, 3), tblocks=(0, 1, 2, 3), chunk0_preloaded=False):
        if not skip_fence:
            P.fence()
        cv.reset(COMMON)
        NB = 3
        wgb, wub, wdb = [], [], []
        for _ in range(NB):
            wgb.append(cv.take(NCH * 512, BF16).rearrange("p (c f) -> p c f", c=NCH))
            wub.append(cv.take(NCH * 512, BF16).rearrange("p (c f) -> p c f", c=NCH))
            wdb.append(cv.take(4 * D, BF16).rearrange("p (j d) -> p j d", j=4))
        actb = [cv.take(4 * TB, BF16).rearrange("p (j t) -> p j t", j=4) for _ in range(2)]
        slb = [cv.take(TB, BF16) for _ in range(2)]
        sizes = [4, 4, 4, 4, 3, 3]
        f0s = [0, 4, 8, 12, 16, 19]
        kidx = 2 * i
        sub = 0 if i == 0 else 2

        def load_chunk(k):
            s = k % NB
            G = sizes[k]
            f0 = f0s[k] * 128
            srcg = wg_d[l, i, :, f0:f0 + G * 128].rearrange("(c p) f -> p c f", p=128)
            srcu = wu_d[l, i, :, f0:f0 + G * 128].rearrange("(c p) f -> p c f", p=128)
            srcd = wd_d[l, i, f0:f0 + G * 128, :].rearrange("(j p) d -> p j d", p=128)
            P.op("pool", lambda e: e.dma_start(out=wgb[s][:, :, 0:G * 128], in_=srcg), writes=[("wg", s)], dma=True)
            P.op("pool", lambda e: e.dma_start(out=wub[s][:, :, 0:G * 128], in_=srcu), writes=[("wu", s)], dma=True)
            P.op("pool", lambda e: e.dma_start(out=wdb[s][:, 0:G, :], in_=srcd), writes=[("wd", s)], dma=True)

        gu_cnt = [0]

        def gate_up(k, t):
            s = k % NB
            G = sizes[k]
            ab = (gu_cnt[0]) % 2
            gu_cnt[0] += 1
            for j in range(G):
                bg = j % 2
                bu = 2 + j % 2
                for c in range(NCH):
                    P.op("pe", lambda e, j=j, c=c, bg=bg: e.matmul(
                        ps[bg][:], lhsT=wgb[s][:, c, j * 128:(j + 1) * 128], rhs=hT[:, c, tsl(t)],
                        start=(c == 0), stop=(c == NCH - 1)),
                        reads=[("wg", s), ("hT", c, t)], writes=[PK[bg]])
                for c in range(NCH):
                    P.op("pe", lambda e, j=j, c=c, bu=bu: e.matmul(
                        ps[bu][:], lhsT=wub[s][:, c, j * 128:(j + 1) * 128], rhs=hT[:, c, tsl(t)],
                        start=(c == 0), stop=(c == NCH - 1)),
                        reads=[("wu", s), ("hT", c, t)], writes=[PK[bu]])
                sl_i = rot("slb")
                P.op("act", lambda e, bg=bg, sl_i=sl_i: e.activation(out=slb[sl_i], in_=ps[bg][:], func=AF.Silu),
                     reads=[PK[bg]], writes=[("slb", sl_i)])
                P.op("dve", lambda e, bu=bu, sl_i=sl_i, j=j, ab=ab: e.tensor_tensor(
                    out=actb[ab][:, j, :], in0=ps[bu][:], in1=slb[sl_i], op=ALU.mult),
                    reads=[PK[bu], ("slb", sl_i)], writes=[("act", ab, j)])
                pop_pending(2)
            return ab

        def down(k, t, ab):
            s = k % NB
            G = sizes[k]
            n = cond_of(t)
            for m in range(NCH):
                by = 4 + m % 3
                for j in range(G):
                    P.op("pe", lambda e, j=j, m=m, by=by: e.matmul(
                        ps[by][:], lhsT=wdb[s][:, j, m * 128:(m + 1) * 128], rhs=actb[ab][:, j, :],
                        start=(j == 0), stop=(j == G - 1)),
                        reads=[("wd", s), ("act", ab, j)], writes=[PK[by]])
                resid_update(by, m, t, mcol(l, 3 * sub + 2, m, n))
                pop_pending(1)

        if pre_work is not None:
            tiles = [(wgb[2][:, :, i_ * 128:(i_ + 1) * 128], ("wg", 2)) for i_ in range(4)] + \
                    [(wub[2][:, :, i_ * 128:(i_ + 1) * 128], ("wu", 2)) for i_ in range(4)]
            pre_work(lambda: (None if chunk0_preloaded else load_chunk(0), load_chunk(1)), tiles)
        if not pre_normed:
            for t in tblocks:
                pending.extend(norm_items(l, sub, t))
            flush_pending(tblocks[0])
        nk_ = len(sizes)
        work = [(k, t) for k in range(nk_ - 1) for t in tblocks]
        if pre_work is None:
            load_chunk(0)
            load_chunk(1)
        prev = None
        for idx, (k, t) in enumerate(work):
            if k == 0:
                flush_pending(t)
            ab = gate_up(k, t)
            if prev is not None:
                down(*prev)
            if t == tblocks[0] and k >= 1 and k + 1 < nk_:
                load_chunk(k + 1)
                if l == 0 and i == 1:
                    for a in range(2):
                        q = k - 1
                        P.op("pool", lambda e, a=a, q=q: e.dma_start(out=dftS_bf[a, q * 1024:(q + 1) * 1024, :],
                                                                     in_=dftS_d[a, q * 1024:(q + 1) * 1024, :]),
                             writes=[("dftS_bf", a, q)], dma=True, slot="dftcast")
            prev = (k, t, ab)
        k = nk_ - 1
        for t in [t_ for t_ in tail_order if t_ in tblocks]:
            ab = gate_up(k, t)
            if prev is not None:
                down(*prev)
                prev = None
            down(k, t, ab)
            if next_norm is not None:
                pending.extend(norm_items(next_norm[0], next_norm[1], t, final=next_norm[2]))
        if not carry:
            flush_pending()

    def mla_phase(l, pre_normed=False):
        jl = l // 2
        P.fence()
        flush_pending()
        cv.reset(COMMON)
        cqT_s = cv.take(4 * 1024, BF16).rearrange("p (c t) -> p c t", c=4)
        ckvT_s = cv.take(2 * 1024, BF16).rearrange("p (c t) -> p c t", c=2)
        kpeT_s = cv.take(1024, BF16)
        ropeT = cv.take(2 * 1024, F32).rearrange("p (a t) -> p a t", a=2)
        SHARED_S = cv.off
        cqT_p = cv.take(4 * 1024, BF16).rearrange("p (c t) -> p c t", c=4)
        ckvT_p = cv.take(2 * 1024, BF16).rearrange("p (c t) -> p c t", c=2)
        kpeT_p = cv.take(1024, BF16)
        SHARED_P = cv.off

        def lsl(t):
            return slice((t % 2) * TB, (t % 2 + 1) * TB)

        def cq(kc, t):
            return (cqT_p if t < 2 else cqT_s)[:, kc, lsl(t)]

        def ckv(mc, t):
            return (ckvT_p if t < 2 else ckvT_s)[:, mc, lsl(t)]

        def kpe(t):
            return (kpeT_p if t < 2 else kpeT_s)[0:64, lsl(t)]

        wdq = cv.take(NCH * 512, BF16).rearrange("p (c f) -> p c f", c=NCH)
        wdkv = cv.take(NCH * 384, BF16).rearrange("p (c f) -> p c f", c=NCH)
        cqraw = cv.take(4 * TB, F32).rearrange("p (c t) -> p c t", c=4)
        ckvf = [cv.take(2 * TB, F32).rearrange("p (c t) -> p c t", c=2) for _ in range(2)]
        kpef = [cv.take(TB, F32) for _ in range(2)]
        if not pre_normed:
            norm_phase(l, 1)
        P.op("pool", lambda e: e.dma_start(out=wdq, in_=wdq_d[jl].rearrange("(c p) f -> p c f", p=128)),
             writes=["wdq"], dma=True)
        P.op("pool", lambda e: e.dma_start(out=wdkv[:, :, 0:320], in_=wdkv_d[jl].rearrange("(c p) f -> p c f", p=128)),
             writes=["wdkv0"], dma=True)
        P.op("pool", lambda e: e.dma_start(out=wdkv[:, :, 320:384], in_=wdkvs_d[jl].rearrange("(c p) f -> p c f", p=128)),
             writes=["wdkv1"], dma=True)
        for a in range(2):
            P.op("sp", lambda e, a=a: e.dma_start(out=ropeT[0:64, a, :], in_=rope_d[a]), writes=[("rope", a)], dma=True)

        def stage_a(t):
            for mc in range(4):
                for c in range(NCH):
                    P.op("pe", lambda e, mc=mc, c=c: e.matmul(
                        ps[mc][:], lhsT=wdq[:, c, mc * 128:(mc + 1) * 128], rhs=hT[:, c, tsl(t)],
                        start=(c == 0), stop=(c == NCH - 1)),
                        reads=["wdq", ("hT", c, t)], writes=[PK[mc]])
                P.op("act", lambda e, mc=mc: e.activation(out=cqraw[:, mc, :], in_=ps[mc][:], func=AF.Identity),
                     reads=[PK[mc]], writes=[("cqraw", mc)])
            rstd_block(None, 4, lambda c: (cqraw[:, c, :], ("cqraw", c)), 1.0 / 512, 7)
            for mc in range(4):
                P.op("dve", lambda e, mc=mc: e.scalar_tensor_tensor(
                    out=cq(mc, t), in0=cqraw[:, mc, :], scalar=qnT[:, jl, mc:mc + 1], in1=rs[:, 1, :],
                    op0=ALU.mult, op1=ALU.mult),
                    reads=[("cqraw", mc), ("rs", 1), "qnT"], writes=[("cqT", mc, t)])
            fb = t % 2
            for mc in range(2):
                for c in range(NCH):
                    P.op("pe", lambda e, mc=mc, c=c: e.matmul(
                        ps[4 + mc][:], lhsT=wdkv[:, c, mc * 128:(mc + 1) * 128], rhs=hT[:, c, tsl(t)],
                        start=(c == 0), stop=(c == NCH - 1)),
                        reads=["wdkv0", ("hT", c, t)], writes=[PK[4 + mc]])
                P.op("act", lambda e, mc=mc: e.activation(out=cqraw[:, mc, :], in_=ps[4 + mc][:], func=AF.Identity),
                     reads=[PK[4 + mc]], writes=[("cqraw", mc)])
            rstd_block(None, 2, lambda c: (cqraw[:, c, :], ("cqraw", c)), 1.0 / 256, 7)
            for mc in range(2):
                if t < 2:
                    P.op("dve", lambda e, mc=mc: e.scalar_tensor_tensor(
                        out=ckvf[fb][:, mc, :], in0=cqraw[:, mc, :], scalar=kvnT[:, jl, mc:mc + 1], in1=rs[:, 1, :],
                        op0=ALU.mult, op1=ALU.mult),
                        reads=[("cqraw", mc), ("rs", 1), "kvnT"], writes=[("ckvf", fb, mc)])
                    P.op("act", lambda e, mc=mc: e.activation(out=ckv(mc, t), in_=ckvf[fb][:, mc, :], func=AF.Identity),
                         reads=[("ckvf", fb, mc)], writes=[("ckvT", mc, t)])
                else:
                    P.op("dve", lambda e, mc=mc: e.scalar_tensor_tensor(
                        out=ckv(mc, t), in0=cqraw[:, mc, :], scalar=kvnT[:, jl, mc:mc + 1], in1=rs[:, 1, :],
                        op0=ALU.mult, op1=ALU.mult),
                        reads=[("cqraw", mc), ("rs", 1), "kvnT"], writes=[("ckvT", mc, t)])
            if t < 2:
                outs.append(P.op("sp", lambda e: e.dma_start(
                    out=ockv_d[jl][:, tsl(t)].rearrange("(c p) t -> p c t", p=128), in_=ckvf[fb]),
                    reads=[("ckvf", fb, 0), ("ckvf", fb, 1)], writes=[("ockv", jl, t)], dma=True,
                    slot=("out", "ckvf", fb)))
            for c in range(NCH):
                P.op("pe", lambda e, c=c: e.matmul(
                    ps[6][0:64, :], lhsT=wdkv[:, c, 256:320], rhs=hT[:, c, tsl(t)],
                    start=(c == 0), stop=(c == NCH - 1)),
                    reads=["wdkv0", ("hT", c, t)], writes=[PK[6]])
            if t < 2:
                P.op("act", lambda e: e.activation(out=kpef[fb][0:64, :], in_=ps[6][0:64, :], func=AF.Identity),
                     reads=[PK[6]], writes=[("kpef", fb)])
                P.op("dve", lambda e: e.tensor_copy(out=kpe(t), in_=kpef[fb][0:64, :]),
                     reads=[("kpef", fb)], writes=[("kpeT", t)])
                outs.append(P.op("sp", lambda e: e.dma_start(out=okpe_d[jl][:, tsl(t)], in_=kpef[fb][0:64, :]),
                                 reads=[("kpef", fb)], writes=[("okpe", jl, t)], dma=True,
                                 slot=("out", "kpef", fb)))
            else:
                for c in range(NCH):
                    P.op("pe", lambda e, c=c: e.matmul(
                        ps[3][0:64, :], lhsT=wdkv[:, c, 320:384], rhs=hT[:, c, tsl(t)],
                        start=(c == 0), stop=(c == NCH - 1)),
                        reads=["wdkv1", ("hT", c, t)], writes=[PK[3]])
                tok = lsl(t)
                P.op("dve", lambda e: e.tensor_tensor(out=tmpf[0:64, 0, :], in0=ps[6][0:64, :],
                                                      in1=ropeT[0:64, 0, tok], op=ALU.mult),
                     reads=[PK[6], ("rope", 0)], writes=[("tmpf", 0)])
                P.op("dve", lambda e: e.tensor_tensor(out=tmpf[0:64, 1, :], in0=ps[3][0:64, :],
                                                      in1=ropeT[0:64, 1, tok], op=ALU.mult),
                     reads=[PK[3], ("rope", 1)], writes=[("tmpf", 1)])
                P.op("dve", lambda e: e.tensor_tensor(out=kpe(t), in0=tmpf[0:64, 0, :],
                                                      in1=tmpf[0:64, 1, :], op=ALU.add),
                     reads=[("tmpf", 0), ("tmpf", 1)], writes=[("kpeT", t)])

        for t in (2, 3, 0, 1):
            stage_a(t)
            if t == 3:
                P.op("sp", lambda e: e.dma_start(out=x_in[jl][0:256, :].rearrange("(c p) t -> p c t", p=128),
                                                 in_=ckvT_s),
                     reads=[("ckvT", 0, 2), ("ckvT", 0, 3), ("ckvT", 1, 2), ("ckvT", 1, 3)],
                     writes=[("x_in", jl, 0)], dma=True, slot=("grp", "x_in", jl))
                P.op("sp", lambda e: e.dma_start(out=x_in[jl][256:320, :], in_=kpeT_s[0:64, :]),
                     reads=[("kpeT", 2), ("kpeT", 3)], writes=[("x_in", jl, 1)], dma=True, slot=("grp", "x_in", jl))
                P.op("pool", lambda e: e.collective_compute("AllGather", ALU.bypass, replica_groups=GROUPS,
                                                            ins=[x_in[jl]], outs=[x_out[jl]]),
                     reads=[("x_in", jl, 0), ("x_in", jl, 1)], writes=[("x_out", jl)], dma=True, inc=1)

        def attention(stream):
            P.fence()
            cv.reset(SHARED_P if stream == 0 else SHARED_S)
            nk = 1024 if stream == 0 else 4608
            nkt = nk // 128
            tok0 = 0 if stream == 0 else 1024
            if stream == 1:
                ckv_all = cv.take(2 * nk, BF16).rearrange("p (c t) -> p c t", c=2)
                kpe_all = cv.take(nk, BF16)
            kn = cv.take(nk, BF16)
            Vh = cv.take(nk, BF16).rearrange("p (k d) -> p k d", d=128)
            qn = cv.take(1024, BF16)
            qr = cv.take(1024, BF16)
            Pt = [cv.take(TB, BF16) for _ in range(4)]
            dacc = [cv.take(TB, F32) for _ in range(2)]
            hw_q = [cv.take(4 * 256, BF16).rearrange("p (c f) -> p c f", c=4) for _ in range(2)]
            hw_kv = [cv.take(2 * 256, BF16).rearrange("p (c f) -> p c f", c=2) for _ in range(2)]
            wob = [cv.take(8 * 128, BF16).rearrange("p (h d) -> p h d", h=8) for _ in range(2)]
            rden = cv.take(TB, F32)
            P.op("dve", lambda e: e.memset(qr[64:128, :], 0.0), writes=["qr_pad"])
            if stream == 1:
                P.op("dve", lambda e: e.memset(kpe_all[64:128, :], 0.0), writes=["kpe_pad"])
            else:
                P.op("dve", lambda e: e.memset(kpeT_p[64:128, :], 0.0), writes=["kpe_pad"])
            if stream == 1:
                P.op("pool", lambda e: e.dma_start(out=ckv_all[:, :, 0:512],
                                                   in_=cckv_d[jl].rearrange("(c p) t -> p c t", p=128)),
                     writes=[("ckv_all", 0)], dma=True, slot=("grp", "kvall_p", jl))
                P.op("pool", lambda e: e.dma_start(out=kpe_all[0:64, 0:512], in_=ckpe_d[jl]),
                     writes=[("kpe_all", 0)], dma=True, slot=("grp", "kvall_p", jl))
                for r in range(4):
                    P.op("sp", lambda e, r=r: e.dma_start(
                        out=ckv_all[:, :, 512 + r * 1024:512 + (r + 1) * 1024],
                        in_=x_out[jl][r * 320:r * 320 + 256, :].rearrange("(c p) t -> p c t", p=128)),
                        reads=[("x_out", jl)], writes=[("ckv_all", 1 + r)], dma=True, slot=("grp", "kvall", jl))
                    P.op("sp", lambda e, r=r: e.dma_start(
                        out=kpe_all[0:64, 512 + r * 1024:512 + (r + 1) * 1024],
                        in_=x_out[jl][r * 320 + 256:(r + 1) * 320, :]),
                        reads=[("x_out", jl)], writes=[("kpe_all", 1 + r)], dma=True, slot=("grp", "kvall", jl))
                ckvsrc, kpesrc = ckv_all, kpe_all
                ckv_keys = [("ckv_all", r) for r in range(5)]
                kpe_keys = [("kpe_all", r) for r in range(5)]
            else:
                ckvsrc, kpesrc = ckvT_p, kpeT_p
                ckv_keys = [("ckvT", mc, t) for mc in range(2) for t in range(2)]
                kpe_keys = [("kpeT", 0), ("kpeT", 1)]

            gcnt = {"tile": 0}

            def head_tiles(h, qblocks):
                flat = [(qi, ki) for qi, (q0, qn_, kts) in enumerate(qblocks) for ki in range(len(kts))]
                info = {}

                def s_mm(qi, ki):
                    q0, qn_, kts = qblocks[qi]
                    kt = kts[ki]
                    sbk = gcnt["tile"] % 3
                    gcnt["tile"] += 1
                    da = dacc[qi % 2]
                    P.op("pe", lambda e: e.matmul(
                        ps[sbk][:, 0:qn_], lhsT=kn[:, kt * 128:(kt + 1) * 128], rhs=qn[:, q0:q0 + qn_],
                        start=True, stop=False),
                        reads=[("kn", kt // 4), ("qn", q0 // TB)], writes=[PK[sbk]])
                    P.op("pe", lambda e: e.matmul(
                        ps[sbk][:, 0:qn_], lhsT=kpesrc[:, kt * 128:(kt + 1) * 128], rhs=qr[:, q0:q0 + qn_],
                        start=False, stop=True),
                        reads=kpe_keys + [("qr", q0 // TB), "qr_pad", "kpe_pad"], writes=[PK[sbk]])
                    pi = rot("Pt", 4)
                    info[(qi, ki)] = pi
                    P.op("act", lambda e: e.activation(out=Pt[pi][:, 0:qn_], in_=ps[sbk][:, 0:qn_], func=AF.Exp),
                         reads=[PK[sbk]], writes=[("Pt", pi)])
                    if ki == 0:
                        P.op("dve", lambda e: e.tensor_copy(out=da[:, 0:qn_], in_=Pt[pi][:, 0:qn_]),
                             reads=[("Pt", pi)], writes=[("dacc", qi % 2)])
                    else:
                        P.op("dve", lambda e: e.tensor_tensor(out=da[:, 0:qn_], in0=da[:, 0:qn_], in1=Pt[pi][:, 0:qn_],
                                                              op=ALU.add),
                             reads=[("Pt", pi), ("dacc", qi % 2)], writes=[("dacc", qi % 2)])

                def pv_mm(qi, ki):
                    q0, qn_, kts = qblocks[qi]
                    kt = kts[ki]
                    pi = info[(qi, ki)]
                    ob = 4 + (qi % 2)
                    nkt_ = len(kts)
                    P.op("pe", lambda e: e.matmul(
                        ps[ob][:, 0:qn_], lhsT=Vh[:, kt, :], rhs=Pt[pi][:, 0:qn_],
                        start=(ki == 0), stop=(ki == nkt_ - 1)),
                        reads=[("Vh", kt // 4), ("Pt", pi)], writes=[PK[ob]])
                    if ki == nkt_ - 1:
                        db = 6 + (qi % 2)
                        da = dacc[qi % 2]
                        tq = (tok0 + q0) // TB
                        P.op("pe", lambda e: e.matmul(ps[db][:, 0:qn_], lhsT=onesF[:], rhs=da[:, 0:qn_],
                                                      start=True, stop=True),
                             reads=["onesF", ("dacc", qi % 2)], writes=[PK[db]])
                        P.op("act", lambda e: e.activation(out=rden[:, 0:qn_], in_=ps[db][:, 0:qn_], func=AF.Ln),
                             reads=[PK[db]], writes=["rden"])
                        P.op("act", lambda e: e.activation(out=rden[:, 0:qn_], in_=rden[:, 0:qn_], func=AF.Exp, scale=-1.0),
                             reads=["rden"], writes=["rden"])
                        P.op("dve", lambda e: e.tensor_tensor(out=hT[:, h, tok0 + q0:tok0 + q0 + qn_], in0=ps[ob][:, 0:qn_],
                                                              in1=rden[:, 0:qn_], op=ALU.mult),
                             reads=[PK[ob], "rden"], writes=[("hT", h, tq)])

                depth_ = 2
                for idx in range(len(flat) + depth_):
                    if idx < len(flat):
                        s_mm(*flat[idx])
                    if idx >= depth_:
                        pv_mm(*flat[idx - depth_])

            def head(h):
                hb = h % 2
                P.op("pool", lambda e: e.dma_start(
                    out=hw_q[hb][:, :, 0:192], in_=wuq_d[jl][:, h * 192:(h + 1) * 192].rearrange("(c p) f -> p c f", p=128)),
                    writes=[("hwq0", hb)], dma=True)
                P.op("pool", lambda e: e.dma_start(
                    out=hw_q[hb][:, :, 192:256], in_=wuqs_d[jl][:, h * 64:(h + 1) * 64].rearrange("(c p) f -> p c f", p=128)),
                    writes=[("hwq1", hb)], dma=True)
                P.op("pool", lambda e: e.dma_start(
                    out=hw_kv[hb], in_=wukv_d[jl][:, h * 256:(h + 1) * 256].rearrange("(c p) f -> p c f", p=128)),
                    writes=[("hwkv", hb)], dma=True)
                for tb in range(2):
                    t = (tok0 // TB) + tb
                    for kc in range(4):
                        P.op("pe", lambda e, kc=kc, t=t: e.matmul(
                            ps[0][:], lhsT=hw_q[hb][:, kc, 0:128], rhs=cq(kc, t),
                            start=(kc == 0), stop=(kc == 3)),
                            reads=[("hwq0", hb), ("cqT", kc, t)], writes=[PK[0]])
                    P.op("act", lambda e, tb=tb: e.activation(out=qn[:, tb * TB:(tb + 1) * TB], in_=ps[0][:],
                                                              func=AF.Identity, scale=ATTN_SCALE),
                         reads=[PK[0]], writes=[("qn", tb)])
                    for kc in range(4):
                        P.op("pe", lambda e, kc=kc, t=t: e.matmul(
                            ps[1][0:64, :], lhsT=hw_q[hb][:, kc, 128:192], rhs=cq(kc, t),
                            start=(kc == 0), stop=(kc == 3)),
                            reads=[("hwq0", hb), ("cqT", kc, t)], writes=[PK[1]])
                    if stream == 0:
                        P.op("act", lambda e, tb=tb: e.activation(out=qr[0:64, tb * TB:(tb + 1) * TB], in_=ps[1][0:64, :],
                                                                  func=AF.Identity, scale=ATTN_SCALE),
                             reads=[PK[1]], writes=[("qr", tb)])
                    else:
                        for kc in range(4):
                            P.op("pe", lambda e, kc=kc, t=t: e.matmul(
                                ps[2][0:64, :], lhsT=hw_q[hb][:, kc, 192:256], rhs=cq(kc, t),
                                start=(kc == 0), stop=(kc == 3)),
                                reads=[("hwq1", hb), ("cqT", kc, t)], writes=[PK[2]])
                        tok = slice(tb * TB, (tb + 1) * TB)
                        P.op("dve", lambda e, tok=tok: e.tensor_tensor(out=tmpf[0:64, 0, :], in0=ps[1][0:64, :],
                                                                       in1=ropeT[0:64, 0, tok], op=ALU.mult),
                             reads=[PK[1], ("rope", 0)], writes=[("tmpf", 0)])
                        P.op("dve", lambda e, tok=tok: e.tensor_tensor(out=tmpf[0:64, 1, :], in0=ps[2][0:64, :],
                                                                       in1=ropeT[0:64, 1, tok], op=ALU.mult),
                             reads=[PK[2], ("rope", 1)], writes=[("tmpf", 1)])
                        P.op("dve", lambda e: e.tensor_tensor(out=tmpf[0:64, 0, :], in0=tmpf[0:64, 0, :],
                                                              in1=tmpf[0:64, 1, :], op=ALU.add),
                             reads=[("tmpf", 0), ("tmpf", 1)], writes=[("tmpf", 0)])
                        P.op("act", lambda e, tb=tb: e.activation(out=qr[0:64, tb * TB:(tb + 1) * TB], in_=tmpf[0:64, 0, :],
                                                                  func=AF.Identity, scale=ATTN_SCALE),
                             reads=[("tmpf", 0)], writes=[("qr", tb)])
                for kb in range(nk // TB):
                    bank = 2 + kb % 2
                    for kc in range(2):
                        P.op("pe", lambda e, kc=kc, kb=kb, bank=bank: e.matmul(
                            ps[bank][:], lhsT=hw_kv[hb][:, kc, 0:128], rhs=ckvsrc[:, kc, kb * TB:(kb + 1) * TB],
                            start=(kc == 0), stop=(kc == 1)),
                            reads=[("hwkv", hb)] + ckv_keys, writes=[PK[bank]])
                    if kb % 2 == 0:
                        P.op("act", lambda e, kb=kb, bank=bank: e.activation(out=kn[:, kb * TB:(kb + 1) * TB], in_=ps[bank][:], func=AF.Identity),
                             reads=[PK[bank]], writes=[("kn", kb)])
                    else:
                        P.op("dve", lambda e, kb=kb, bank=bank: e.tensor_copy(out=kn[:, kb * TB:(kb + 1) * TB], in_=ps[bank][:]),
                             reads=[PK[bank]], writes=[("kn", kb)])
                for vb in range(nkt // 4):
                    bank = 2 + vb % 2
                    for q4 in range(4):
                        kt = vb * 4 + q4
                        for kc in range(2):
                            P.op("pe", lambda e, kc=kc, kt=kt, q4=q4, bank=bank: e.matmul(
                                ps[bank][:, q4 * 128:(q4 + 1) * 128], lhsT=ckvsrc[:, kc, kt * 128:(kt + 1) * 128],
                                rhs=hw_kv[hb][:, kc, 128:256], start=(kc == 0), stop=(kc == 1)),
                                reads=[("hwkv", hb)] + ckv_keys, writes=[PK[bank]])
                    vdst = Vh[:, vb * 4:(vb + 1) * 4, :]
                    vsrc = ps[bank][:].rearrange("p (k d) -> p k d", d=128)
                    if vb % 2 == 0:
                        P.op("dve", lambda e, vdst=vdst, vsrc=vsrc: e.tensor_copy(out=vdst, in_=vsrc),
                             reads=[PK[bank]], writes=[("Vh", vb)])
                    else:
                        P.op("act", lambda e, vdst=vdst, vsrc=vsrc: e.activation(out=vdst, in_=vsrc, func=AF.Identity),
                             reads=[PK[bank]], writes=[("Vh", vb)])
                if stream == 0:
                    qblocks = [(s * 256, 256, [2 * s, 2 * s + 1]) for s in range(4)]
                else:
                    qblocks = [(qb * TB, TB, list(range(nkt))) for qb in range(2)]
                head_tiles(h, qblocks)

            for h in range(8):
                head(h)
            if stream == 0:
                wo_stage(0, [(w_, None) for w_ in wob])

        def wo_stage(stream, wob, mid_loads=None):
            n = 0 if stream == 0 else 1
            tok0 = 0 if stream == 0 else 1024
            nb_ = len(wob)

            def ld(m):
                wb_i = m % nb_
                P.op("pool", lambda e: e.dma_start(
                    out=wob[wb_i][0], in_=wo_d[jl][:, m * 128:(m + 1) * 128].rearrange("(h p) d -> p h d", p=128)),
                    writes=[("wob", wb_i)], dma=True)

            for m in range(min(nb_, NCH)):
                ld(m)
            if mid_loads is not None:
                mid_loads()
            for m in range(NCH):
                wb_i = m % nb_
                for tb in range(2):
                    t = tok0 // TB + tb
                    bank = (m * 2 + tb) % 4
                    for hh in range(8):
                        P.op("pe", lambda e, hh=hh, t=t, bank=bank, wb_i=wb_i: e.matmul(
                            ps[bank][:], lhsT=wob[wb_i][0][:, hh, :], rhs=hT[:, hh, tsl(t)],
                            start=(hh == 0), stop=(hh == 7)),
                            reads=[("wob", wb_i), ("hT", hh, t)] + ([wob[wb_i][1]] if wob[wb_i][1] else []),
                            writes=[PK[bank]])
                    resid_update(bank, m, t, mcol(l, 5, m, n))
                if m + nb_ < NCH:
                    ld(m + nb_)

        attention(0)
        attention(1)

        def pre_work(mid_loads, tiles):
            wo_stage(1, tiles, mid_loads)
        return pre_work

    def fourier_phase(l, pre_normed=False):
        jl = l // 2
        P.fence()
        flush_pending()
        cv.reset(COMMON)
        wg0 = cv.take(NCH * 512, BF16).rearrange("p (c f) -> p c f", c=NCH)
        wu0 = cv.take(NCH * 512, BF16).rearrange("p (c f) -> p c f", c=NCH)
        wd0 = cv.take(4 * D, BF16).rearrange("p (j d) -> p j d", j=4)
        P.op("pool", lambda e: e.dma_start(out=wg0[:, :, 0:512], in_=wg_d[l, 1, :, 0:512].rearrange("(c p) f -> p c f", p=128)),
             writes=[("wg", 0)], dma=True)
        P.op("pool", lambda e: e.dma_start(out=wu0[:, :, 0:512], in_=wu_d[l, 1, :, 0:512].rearrange("(c p) f -> p c f", p=128)),
             writes=[("wu", 0)], dma=True)
        P.op("pool", lambda e: e.dma_start(out=wd0[:, 0:4, :], in_=wd_d[l, 1, 0:512, :].rearrange("(j p) d -> p j d", p=128)),
             writes=[("wd", 0)], dma=True)
        cs = cv.take(2 * 512, BF16).rearrange("p (c f) -> p c f", c=2)
        dp = cv.take(2 * 2 * 256, BF16).rearrange("p (a n k) -> p a n k", a=2, n=2)
        ABp = cv.take(8 * 2048, BF16).rearrange("p (t f) -> p t f", t=8)
        ABs = cv.take(8 * 2048, BF16).rearrange("p (t f) -> p t f", t=8)
        if not pre_normed:
            norm_phase(l, 1)
        P.op("pool", lambda e: e.dma_start(out=cs, in_=dftC_d.rearrange("(c p) f -> p c f", p=128)), writes=["cs"], dma=True)
        for a in range(2):
            P.op("pool", lambda e, a=a: e.dma_start(out=dp[:, a], in_=dftP_d[a].rearrange("(n p) k -> p n k", p=128)),
                 writes=[("dp", a)], dma=True)
        for tile in list(range(8, 16)) + list(range(8)):
            dst = ABs if tile >= 8 else ABp
            ti = tile % 8
            t = tile // 4
            for g in range(4):
                bank = g % 4
                for kc in range(2):
                    P.op("pe", lambda e, g=g, kc=kc, tile=tile, bank=bank: e.matmul(
                        ps[bank][:], lhsT=hT[:, 2 * g + kc, tile * 128:(tile + 1) * 128], rhs=cs[:, kc, :],
                        start=(kc == 0), stop=(kc == 1)),
                        reads=["cs", ("hT", 2 * g + kc, t)], writes=[PK[bank]])
                dv = dst[:, ti, :].rearrange("p (s g c) -> p s g c", s=2, g=4)[:, :, g, :]
                sv = ps[bank][:].rearrange("p (s c) -> p s c", s=2)
                key = ("AB", tile)
                if g % 2 == 0:
                    P.op("act", lambda e, dv=dv, sv=sv: e.activation(out=dv, in_=sv, func=AF.Identity),
                         reads=[PK[bank]], writes=[(key, g)])
                else:
                    P.op("dve", lambda e, dv=dv, sv=sv: e.tensor_copy(out=dv, in_=sv),
                         reads=[PK[bank]], writes=[(key, g)])
            if tile >= 8 and tile % 2 == 1:
                part = (tile - 8) // 2
                P.op("sp", lambda e, part=part: e.dma_start(
                    out=f_in[jl][part].rearrange("(t p) f -> p t f", p=128), in_=ABs[:, 2 * part:2 * part + 2, :]),
                    reads=[(("AB", tl), g) for tl in (tile - 1, tile) for g in range(4)],
                    writes=[("f_in", jl, part)], dma=True)
                P.op("pool", lambda e, part=part: e.collective_compute(
                    "AllGather", ALU.bypass, replica_groups=GROUPS, ins=[f_in[jl][part]], outs=[f_out[jl][part]]),
                    reads=[("f_in", jl, part)], writes=[("f_out", jl, part)], dma=True, inc=1, nofence=True)
        for s in range(4):
            t = s // 2
            for m in range(NCH):
                bank = m % 4
                first = True
                for nt in range(2):
                    for a in range(2):
                        P.op("pe", lambda e, nt=nt, a=a, m=m, s=s, bank=bank, first=first: e.matmul(
                            ps[bank][:, 0:256], lhsT=ABp[:, s * 2 + nt, a * 1024 + m * 128:a * 1024 + (m + 1) * 128],
                            rhs=dp[:, a, nt, :], start=first, stop=(nt == 1 and a == 1)),
                            reads=[(("AB", s * 2 + nt), gg) for gg in range(4)] + [("dp", a)], writes=[PK[bank]])
                        first = False
                P.op("act", lambda e, m=m, s=s, bank=bank: e.activation(
                    out=hT[:, m, s * 256:(s + 1) * 256], in_=ps[bank][:, 0:256], func=AF.Identity, scale=1.0 / 256.0),
                    reads=[PK[bank]], writes=[("hT", m, t)])
        def fc_stage(tblocks, fwbuf, mid_loads=None):
            nb_ = len(fwbuf)

            def ld(m):
                wi = m % nb_
                P.op("pool", lambda e: e.dma_start(
                    out=fwbuf[wi][0], in_=fw_d[jl][:, m * 128:(m + 1) * 128].rearrange("(c p) d -> p c d", p=128)),
                    writes=[("fwm", wi)], dma=True)

            for m in range(min(nb_, NCH)):
                ld(m)
            if mid_loads is not None:
                mid_loads()
            for m in range(NCH):
                wi = m % nb_
                if m >= nb_ and False:
                    pass
                for t in tblocks:
                    n = cond_of(t)
                    bank = (m * 2 + t) % 4
                    for c in range(NCH):
                        P.op("pe", lambda e, c=c, t=t, bank=bank, wi=wi: e.matmul(
                            ps[bank][:], lhsT=fwbuf[wi][0][:, c, :], rhs=hT[:, c, tsl(t)],
                            start=(c == 0), stop=(c == NCH - 1)),
                            reads=[("fwm", wi), ("hT", c, t)] + ([fwbuf[wi][1]] if fwbuf[wi][1] else []),
                            writes=[PK[bank]])
                    s_ = rot("tmpf")
                    P.op("dve", lambda e, s_=s_, m=m, bank=bank, n=n: e.tensor_scalar(
                        out=tmpf[:, s_, :], in0=ps[bank][:], scalar1=fbT[:, jl, m:m + 1], scalar2=mcol(l, 5, m, n),
                        op0=ALU.add, op1=ALU.mult),
                        reads=[PK[bank], "fbT", "modS"], writes=[("tmpf", s_)])
                    P.op("dve", lambda e, s_=s_, m=m, t=t: e.tensor_tensor(out=xT[:, m, tsl(t)], in0=xT[:, m, tsl(t)],
                                                                           in1=tmpf[:, s_, :], op=ALU.add),
                         reads=[("tmpf", s_), ("xT", m, t)], writes=[("xT", m, t)])
                if m + nb_ < NCH:
                    ld(m + nb_)

        def pre_work1(mid_loads, tiles):
            fc_stage((0, 1), tiles, mid_loads)
        return pre_work1

    def fourier_part2(l):
        jl = l // 2
        P.fence()
        flush_pending()
        cv.reset(COMMON)
        NSB = 3
        abn = [cv.take(2048, BF16) for _ in range(NSB)]
        tbn = [cv.take(2 * 512, BF16).rearrange("p (a k) -> p a k", a=2) for _ in range(NSB)]

        def fc_stage(tblocks, fwbuf, mid_loads=None):
            nb_ = len(fwbuf)

            def ld(m):
                wi = m % nb_
                P.op("pool", lambda e: e.dma_start(
                    out=fwbuf[wi][0], in_=fw_d[jl][:, m * 128:(m + 1) * 128].rearrange("(c p) d -> p c d", p=128)),
                    writes=[("fwm", wi)], dma=True)

            for m in range(min(nb_, NCH)):
                ld(m)
            if mid_loads is not None:
                mid_loads()
            for m in range(NCH):
                wi = m % nb_
                for t in tblocks:
                    n = cond_of(t)
                    bank = (m * 2 + t) % 4
                    for c in range(NCH):
                        P.op("pe", lambda e, c=c, t=t, bank=bank, wi=wi: e.matmul(
                            ps[bank][:], lhsT=fwbuf[wi][0][:, c, :], rhs=hT[:, c, tsl(t)],
                            start=(c == 0), stop=(c == NCH - 1)),
                            reads=[("fwm", wi), ("hT", c, t)] + ([fwbuf[wi][1]] if fwbuf[wi][1] else []),
                            writes=[PK[bank]])
                    s_ = rot("tmpf")
                    P.op("dve", lambda e, s_=s_, m=m, bank=bank, n=n: e.tensor_scalar(
                        out=tmpf[:, s_, :], in0=ps[bank][:], scalar1=fbT[:, jl, m:m + 1], scalar2=mcol(l, 5, m, n),
                        op0=ALU.add, op1=ALU.mult),
                        reads=[PK[bank], "fbT", "modS"], writes=[("tmpf", s_)])
                    P.op("dve", lambda e, s_=s_, m=m, t=t: e.tensor_tensor(out=xT[:, m, tsl(t)], in0=xT[:, m, tsl(t)],
                                                                           in1=tmpf[:, s_, :], op=ALU.add),
                         reads=[("tmpf", s_), ("xT", m, t)], writes=[("xT", m, t)])
                if m + nb_ < NCH:
                    ld(m + nb_)

        for kb in range(2):
            t = 2 + kb
            nt_order = [r_ * 8 + part * 2 + j_ for part in range(4) for r_ in range(4) for j_ in range(2)]
            for ni, nt in enumerate(nt_order):
                b = (kb * 32 + ni) % NSB
                r_, w_ = nt // 8, nt % 8
                part, j_ = w_ // 2, w_ % 2
                src = f_out[jl][part][r_ * 256 + j_ * 128:r_ * 256 + (j_ + 1) * 128, :]
                P.op("sp", lambda e, src=src, b=b: e.dma_start(out=abn[b], in_=src),
                     reads=[("f_out", jl, part)], writes=[("abn", b)], dma=True)
                P.op("sp", lambda e, nt=nt, b=b, kb=kb: e.dma_start(
                    out=tbn[b], in_=dftS_bf[:, nt * 128:(nt + 1) * 128, kb * 512:(kb + 1) * 512].rearrange("a p k -> p a k")),
                    reads=DFT_KEYS, writes=[("tbn", b)], dma=True)
                for m in range(NCH):
                    for a in range(2):
                        P.op("pe", lambda e, ni=ni, a=a, m=m, b=b: e.matmul(
                            ps[m][:], lhsT=abn[b][:, a * 1024 + m * 128:a * 1024 + (m + 1) * 128], rhs=tbn[b][:, a, :],
                            start=(ni == 0 and a == 0), stop=(ni == 31 and a == 1)),
                            reads=[("abn", b), ("tbn", b)], writes=[PK[m]])
            for m in range(NCH):
                if m % 2 == 0:
                    P.op("act", lambda e, m=m, t=t: e.activation(out=hT[:, m, tsl(t)], in_=ps[m][:], func=AF.Identity,
                                                                 scale=1.0 / 1024.0),
                         reads=[PK[m]], writes=[("hT", m, t)])
                else:
                    P.op("dve", lambda e, m=m, t=t: e.tensor_scalar(out=hT[:, m, tsl(t)], in0=ps[m][:], scalar1=1.0 / 1024.0,
                                                                    scalar2=None, op0=ALU.mult),
                         reads=[PK[m]], writes=[("hT", m, t)])
        def pre_work(mid_loads, tiles):
            fc_stage((2, 3), tiles, mid_loads)
        return pre_work

    for l in range(depth):
        mixer_on = ("m" if l % 2 == 0 else "f") in DBG_PHASES
        ffn_phase(l, 0, pre_normed=(l > 0), next_norm=(l, 1, False) if mixer_on else None, skip_fence=(l > 0),
                  carry=mixer_on, tail_order=(2, 3, 0, 1) if mixer_on else (0, 1, 2, 3))
        last = (l == depth - 1)
        nn = (0, 0, True) if last else (l + 1, 0, False)
        if mixer_on and l % 2 == 1:
            pw1 = fourier_phase(l, pre_normed=True)
            ffn_phase(l, 1, pre_normed=False, next_norm=nn, pre_work=pw1, carry=True, tblocks=(0, 1),
                      chunk0_preloaded=True)
            pw = fourier_part2(l)
            ffn_phase(l, 1, pre_normed=False, next_norm=nn, pre_work=pw, carry=not last, tblocks=(2, 3))
        else:
            pw = mla_phase(l, pre_normed=True) if mixer_on else None
            ffn_phase(l, 1, pre_normed=False, next_norm=nn, pre_work=pw, carry=not last)
    if depth == 0:
        P.fence()
        norm_phase(0, 0, final=True)
    P.emit(nc, final_waits=outs)
    st.close()
    return nc, P


def _fm(a):
    t = a.shape[0]
    return np.ascontiguousarray(a.T.reshape(NCH, 128, t).transpose(1, 0, 2))


def _vec(a, nch):
    sh = a.shape[:-1]
    b = a.reshape(sh + (nch, 128))
    return np.ascontiguousarray(np.moveaxis(b, -1, 0))


def _const_tables():
    f32 = np.float32
    k = np.arange(256)
    ang = 2 * np.pi * np.outer(k, k) / 256.0
    dftC = np.concatenate([np.cos(ang), np.sin(ang)], axis=1).astype(np.float32)
    dftP = np.stack([np.cos(ang), -np.sin(ang)]).astype(np.float32)
    n = np.arange(4096, dtype=np.int64)
    dftS = []
    for qd in range(4):
        kk = np.arange(qd * 1024, (qd + 1) * 1024, dtype=np.int64)
        a = 2 * np.pi * ((np.outer(n, kk) % 4096).astype(np.float64)) / 4096.0
        dftS.append(np.stack([np.cos(a), -np.sin(a)]).astype(np.float32))
    inv = 1.0 / (10000.0 ** (np.arange(16, dtype=np.float32) / 16.0))
    pos = np.arange(4096)
    row = (pos // 64).astype(np.float32)
    col = (pos % 64).astype(np.float32)
    ang = np.stack([row[:, None] * inv, col[:, None] * inv], axis=1).astype(np.float32)
    cos = np.cos(ang)
    sin = np.sin(ang)
    cosT = np.zeros((64, 4096), f32)
    sinT = np.zeros((64, 4096), f32)
    for a in range(2):
        for hf in range(2):
            for f in range(16):
                p = a * 32 + hf * 16 + f
                cosT[p] = cos[:, a, f]
                sinT[p] = -sin[:, a, f] if hf == 0 else sin[:, a, f]
    rope = [np.ascontiguousarray(np.stack([cosT[:, q * 1024:(q + 1) * 1024], sinT[:, q * 1024:(q + 1) * 1024]]))
            for q in range(4)]
    return dftC, dftP, dftS, rope


def _swap_cols(w, nheads, base, stride):
    cols = []
    for h in range(nheads):
        o = h * stride + base
        for a in range(2):
            cols += list(range(o + a * 32 + 16, o + a * 32 + 32)) + list(range(o + a * 32, o + a * 32 + 16))
    return np.ascontiguousarray(w[..., cols])


_CACHE = {}
DEPTH_RUN = DEPTH


def kernel(x_prompt, x_sample, cache_ckv, cache_kpe, c, c_ctx, w_mod, b_mod, norm_g,
           ffn_wg, ffn_wu, ffn_wd, mla_w_dq, mla_q_norm, mla_w_uq, mla_w_dkv, mla_kv_norm,
           mla_w_ukv, mla_w_o, fourier_w, fourier_b, final_norm):
    A = lambda a: np.ascontiguousarray(np.asarray(a, dtype=np.float32))
    x_prompt, x_sample, cache_ckv, cache_kpe = A(x_prompt), A(x_sample), A(cache_ckv), A(cache_kpe)
    c, c_ctx, w_mod, b_mod, norm_g = A(c), A(c_ctx), A(w_mod), A(b_mod), A(norm_g)
    ffn_wg, ffn_wu, ffn_wd = A(ffn_wg), A(ffn_wu), A(ffn_wd)
    mla_w_dq, mla_q_norm, mla_w_uq, mla_w_dkv = A(mla_w_dq), A(mla_q_norm), A(mla_w_uq), A(mla_w_dkv)
    mla_kv_norm, mla_w_ukv, mla_w_o = A(mla_kv_norm), A(mla_w_ukv), A(mla_w_o)
    fourier_w, fourier_b, final_norm = A(fourier_w), A(fourier_b), A(final_norm)

    if "nc" not in _CACHE:
        _CACHE["nc"] = build_program(DEPTH_RUN)[0]
        _CACHE["tables"] = _const_tables()
    nc = _CACHE["nc"]
    dftC, dftP, dftS, rope = _CACHE["tables"]

    shared = {
        "bmodT": _vec(b_mod, 72), "normgT": _vec(norm_g, 8), "finalT": _vec(final_norm, 8),
        "qnT": _vec(mla_q_norm, 4), "kvnT": _vec(mla_kv_norm, 2), "fbT": _vec(fourier_b, 8),
        "wg": ffn_wg, "wu": ffn_wu, "wd": ffn_wd, "wdq": mla_w_dq, "wuq": mla_w_uq,
        "wuqs": _swap_cols(mla_w_uq, 8, 128, 192), "wdkv": mla_w_dkv,
        "wdkvs": _swap_cols(mla_w_dkv, 1, 256, 0), "wukv": mla_w_ukv, "wo": mla_w_o, "fw": fourier_w,
        "dftC": dftC, "dftP": dftP,
    }
    in_maps = []
    for r in range(8):
        b, qd = r // 4, r % 4
        xp = x_prompt[4 * r:4 * r + 4].reshape(1024, D)
        xs = x_sample[b, qd * 1024:(qd + 1) * 1024]
        m = dict(shared)
        m["xT"] = _fm(np.concatenate([xp, xs], axis=0))
        m["cckv"] = np.ascontiguousarray(cache_ckv[b].transpose(0, 2, 1))
        m["ckpe"] = np.ascontiguousarray(cache_kpe[b].transpose(0, 2, 1))
        m["condT"] = _vec(np.stack([c_ctx, c[b]]), 8).transpose(0, 2, 1).copy()
        m["wmod"] = np.ascontiguousarray(w_mod[:, :, qd * 2304:(qd + 1) * 2304])
        m["ropeT"] = rope[qd]
        m["dftS"] = dftS[qd]
        in_maps.append(m)

    res = run_bass_kernel_spmd(nc, in_maps, core_ids=list(range(8)))
    y_prompt = np.empty((32, 256, D), np.float32)
    y_sample = np.empty((2, 4096, D), np.float32)
    new_ckv = np.empty((32, 2, 256, 256), np.float32)
    new_kpe = np.empty((32, 2, 256, 64), np.float32)
    for r in range(8):
        b, qd = r // 4, r % 4
        o = res.results[r]
        y = np.asarray(o["yT"]).transpose(2, 1, 0).reshape(NT, D)
        y_prompt[4 * r:4 * r + 4] = y[:1024].reshape(4, 256, D)
        y_sample[b, qd * 1024:(qd + 1) * 1024] = y[1024:]
        ck = np.asarray(o["o_ckv"])
        kp = np.asarray(o["o_kpe"])
        new_ckv[4 * r:4 * r + 4] = ck.reshape(2, 256, 4, 256).transpose(2, 0, 3, 1)
        new_kpe[4 * r:4 * r + 4] = kp.reshape(2, 64, 4, 256).transpose(2, 0, 3, 1)
    return (y_prompt, y_sample, new_ckv, new_kpe)
```

```python
import math
from contextlib import ExitStack

import numpy as np
import ml_dtypes
import concourse.bass as bass
import concourse.mybir as mybir
from concourse.bass_utils import run_bass_kernel_spmd

F32 = mybir.dt.float32
BF16 = mybir.dt.bfloat16
AF = mybir.ActivationFunctionType
ALU = mybir.AluOpType

D = 1024
DFF = 2816
NCH = 8
TB = 512
NT = 2048
DEPTH = 4
EPS = 1e-6
ATTN_SCALE = 1.0 / math.sqrt(192.0)
GROUPS = [[0, 1, 2, 3], [4, 5, 6, 7]]
ENGINES = ("sp", "act", "dve", "pool", "pe")
SEM_ROT = 8000
SCR_BYTES = 102 * 1024
DBG_PHASES = "mf"


class Op:
    __slots__ = ("id", "eng", "fn", "deps", "is_dma", "slot", "sig", "inc", "needs_signal")

    def __init__(self, id, eng, fn, is_dma, slot, inc):
        self.id = id
        self.eng = eng
        self.fn = fn
        self.deps = set()
        self.is_dma = is_dma
        self.slot = slot
        self.sig = None
        self.inc = inc
        self.needs_signal = False


class Prog:
    def __init__(self):
        self.ops = []
        self.last_w = {}
        self.readers = {}
        self.eng_ops = {e: [] for e in ENGINES}
        self.fence_set = set()
        self.dma_since_fence = []
        self.fenced = {e: True for e in ENGINES}

    def fence(self):
        fs = set(self.dma_since_fence)
        for e in ENGINES:
            for o in reversed(self.eng_ops[e]):
                if not o.is_dma:
                    fs.add(o.id)
                    break
        self.fence_set = fs
        self.dma_since_fence = []
        self.fenced = {e: False for e in ENGINES}

    def op(self, eng, fn, reads=(), writes=(), dma=False, slot=None, inc=16, nofence=False):
        o = Op(len(self.ops), eng, fn, dma, slot, inc)
        deps = set()
        for k in reads:
            w = self.last_w.get(k)
            if w is not None:
                deps.add(w)
        for k in writes:
            w = self.last_w.get(k)
            if w is not None:
                deps.add(w)
            for r in self.readers.get(k, {}).values():
                if isinstance(r, list):
                    deps.update(r)
                else:
                    deps.add(r)
        if not self.fenced[eng]:
            deps |= self.fence_set
            self.fenced[eng] = True
        deps.discard(o.id)
        o.deps = deps
        for k in reads:
            rd = self.readers.setdefault(k, {})
            if dma:
                rd.setdefault("dma", []).append(o.id)
            else:
                rd[eng] = o.id
        for k in writes:
            self.last_w[k] = o.id
            self.readers[k] = {}
        if dma:
            if slot is None:
                o.slot = ("dma", writes[0])
            if not nofence:
                self.dma_since_fence.append(o.id)
        self.eng_ops[eng].append(o)
        self.ops.append(o)
        return o

    def emit(self, nc, final_waits=()):
        ops = self.ops
        for o in ops:
            for d in o.deps:
                p = ops[d]
                if p.is_dma:
                    p.needs_signal = True
                elif p.eng == "pe" and o.eng == "pe" and not o.is_dma:
                    continue
                else:
                    p.needs_signal = True
        for d in final_waits:
            d.needs_signal = True
        sem_keys = []
        comp_cnt = {e: 0 for e in ENGINES}
        slot_cnt = {}
        grp_total = {}
        for o in ops:
            if o.is_dma and isinstance(o.slot, tuple) and o.slot[0] == "grp":
                grp_total[o.slot] = grp_total.get(o.slot, 0) + o.inc
        for o in ops:
            if o.is_dma:
                c = slot_cnt.get(o.slot, 0) + o.inc
                slot_cnt[o.slot] = c
                o.sig = (o.slot, grp_total.get(o.slot, c))
                if o.slot not in slot_cnt or o.slot not in sem_keys:
                    sem_keys.append(o.slot)
            elif o.needs_signal:
                n = comp_cnt[o.eng]
                comp_cnt[o.eng] = n + 1
                key = ("c", o.eng, n // SEM_ROT)
                o.sig = (key, n % SEM_ROT + 1)
                if key not in sem_keys:
                    sem_keys.append(key)
        sem_keys = list(dict.fromkeys(sem_keys))
        self.n_sems = len(sem_keys)
        stack = ExitStack()
        sems = {}
        for i, k in enumerate(sem_keys):
            sems[k] = stack.enter_context(nc.semaphore("s%d" % i))

        def run(engname, e):
            waited = {}
            for o in self.eng_ops[engname]:
                need = {}
                for d in o.deps:
                    p = ops[d]
                    if p.sig is None:
                        continue
                    if (not p.is_dma) and p.eng == "pe" and engname == "pe" and not o.is_dma:
                        continue
                    k, c = p.sig
                    if need.get(k, 0) < c:
                        need[k] = c
                for k, c in need.items():
                    if waited.get(k, 0) >= c:
                        continue
                    e.wait_ge(sems[k], c)
                    waited[k] = c
                ins = o.fn(e)
                if o.sig is not None:
                    ins.then_inc(sems[o.sig[0]], o.inc if o.is_dma else 1)
            if engname == "sp":
                for d in final_waits:
                    k, c = d.sig
                    e.wait_ge(sems[k], c)

        with stack:
            with nc.Block() as block:
                @block.sync
                def _(e):
                    run("sp", e)

                @block.scalar
                def _(e):
                    run("act", e)

                @block.vector
                def _(e):
                    run("dve", e)

                @block.gpsimd
                def _(e):
                    run("pool", e)

                @block.tensor
                def _(e):
                    run("pe", e)


class Carve:
    def __init__(self, scr):
        self.scr = scr
        self.off = 0

    def reset(self, off=0):
        self.off = off

    def take(self, nelem, dtype, parts=128):
        nb = nelem * (4 if dtype == F32 else 2)
        nb = (nb + 63) // 64 * 64
        a = self.off // 2
        self.off += nb
        assert self.off <= SCR_BYTES, ("scratch overflow", self.off)
        v = self.scr[:, a:a + nb // 2]
        if dtype == F32:
            v = v.bitcast(F32)
        v = v[:, 0:nelem]
        return v


def build_program(depth=DEPTH):
    nc = bass.Bass("TRN2", target_bir_lowering=False)

    def din(name, shape, dt=F32):
        return nc.dram_tensor(name, list(shape), dt, kind="ExternalInput").ap()

    def dout(name, shape, dt=F32):
        return nc.dram_tensor(name, list(shape), dt, kind="ExternalOutput").ap()

    xT_d = din("xT", [128, NCH, NT])
    cckv_d = din("cckv", [2, 256, 512])
    ckpe_d = din("ckpe", [2, 64, 512])
    condT_d = din("condT", [128, NCH, 2])
    wmod_d = din("wmod", [4, D, 2304])
    bmodT_d = din("bmodT", [128, 4, 72])
    normgT_d = din("normgT", [128, 4, 3, 8])
    finalT_d = din("finalT", [128, 8])
    qnT_d = din("qnT", [128, 2, 4])
    kvnT_d = din("kvnT", [128, 2, 2])
    fbT_d = din("fbT", [128, 2, 8])
    wg_d = din("wg", [4, 2, D, DFF])
    wu_d = din("wu", [4, 2, D, DFF])
    wd_d = din("wd", [4, 2, DFF, D])
    wdq_d = din("wdq", [2, D, 512])
    wuq_d = din("wuq", [2, 512, 1536])
    wuqs_d = din("wuqs", [2, 512, 512])
    wdkv_d = din("wdkv", [2, D, 320])
    wdkvs_d = din("wdkvs", [2, D, 64])
    wukv_d = din("wukv", [2, 256, 2048])
    wo_d = din("wo", [2, D, D])
    fw_d = din("fw", [2, D, D])
    rope_d = din("ropeT", [2, 64, 1024])
    dftC_d = din("dftC", [256, 512])
    dftP_d = din("dftP", [2, 256, 256])
    dftS_d = din("dftS", [2, 4096, 1024])

    yT_d = dout("yT", [128, NCH, NT])
    ockv_d = dout("o_ckv", [2, 256, 1024])
    okpe_d = dout("o_kpe", [2, 64, 1024])

    mod_in = nc.dram_tensor("mod_in", [128, 144], F32).ap()
    mod_out = nc.dram_tensor("mod_out", [512, 144], F32).ap()
    x_in = [nc.dram_tensor("x_in%d" % j, [320, 1024], BF16).ap() for j in range(2)]
    x_out = [nc.dram_tensor("x_out%d" % j, [1280, 1024], BF16).ap() for j in range(2)]
    f_in = [[nc.dram_tensor("f_in%d_%d" % (j, q), [256, 2048], BF16).ap() for q in range(4)] for j in range(2)]
    f_out = [[nc.dram_tensor("f_out%d_%d" % (j, q), [1024, 2048], BF16).ap() for q in range(4)] for j in range(2)]

    dftS_bf = nc.dram_tensor("dftS_bf", [2, 4096, 1024], BF16).ap()
    DFT_KEYS = [("dftS_bf", a, q) for a in range(2) for q in range(4)]

    P = Prog()
    st = ExitStack()
    sb = lambda name, shape, dt: st.enter_context(nc.sbuf_tensor(name, list(shape), dt))
    xT = sb("xTs", [128, NCH, NT], F32)
    hT = sb("hTs", [128, NCH, NT], BF16)
    modS = sb("modS", [128, 4 * 72 * 2], F32)
    bmodT = sb("bmodTs", [128, 4, 72], F32)
    normgT = sb("normgTs", [128, 4, 3, 8], F32)
    finalT = sb("finalTs", [128, 8], F32)
    qnT = sb("qnTs", [128, 2, 4], F32)
    kvnT = sb("kvnTs", [128, 2, 2], F32)
    fbT = sb("fbTs", [128, 2, 8], F32)
    condT = sb("condTs", [128, NCH, 2], F32)
    scT = sb("scTs", [128, NCH, 2], F32)
    scB = sb("scBs", [128, NCH, 2], BF16)
    onesB = sb("onesB", [128, 128], BF16)
    onesF = sb("onesF", [128, 128], F32)
    epsT = sb("epsT", [128, 1], F32)
    scr = sb("scr", [128, SCR_BYTES // 2], BF16)
    ps = [st.enter_context(nc.psum_tensor("ps%d" % i, [128, 512], F32)) for i in range(8)]
    PK = [("ps", i) for i in range(8)]
    cv = Carve(scr)

    mod5 = modS[:].rearrange("p (l k c n) -> p l k c n", l=4, k=9, c=8)
    mod_lrx = modS[:].rearrange("p (l r x) -> p l r x", l=4, r=4)
    mod_lcn = modS[:].rearrange("p (l c n) -> p l c n", l=4, c=72)

    outs = []

    def mcol(l, k, c, cond):
        return mod5[:, l, k, c, cond:cond + 1]

    def cond_of(t):
        return 0 if t < 2 else 1

    def tsl(t):
        return slice(t * TB, (t + 1) * TB)

    cv.reset(0)
    sq = cv.take(2 * TB, BF16).rearrange("p (a b) -> p a b", a=2)
    rs = cv.take(2 * TB, F32).rearrange("p (a b) -> p a b", a=2)
    tmpf = cv.take(2 * TB, F32).rearrange("p (a b) -> p a b", a=2)
    COMMON = cv.off
    cnt = {"sq": 0, "tmpf": 0}

    def rot(name, n=2):
        v = cnt.get(name, 0)
        cnt[name] = v + 1
        return v % n

    P.op("sp", lambda e: e.dma_start(out=condT[:], in_=condT_d), writes=["condT"], dma=True, slot=("grp", "setup"))
    for (tl, td, nm) in ((bmodT, bmodT_d, "bmodT"), (normgT, normgT_d, "normgT"), (finalT, finalT_d, "finalT"),
                         (qnT, qnT_d, "qnT"), (kvnT, kvnT_d, "kvnT"), (fbT, fbT_d, "fbT")):
        P.op("sp", lambda e, tl=tl, td=td: e.dma_start(out=tl[:], in_=td), writes=[nm], dma=True, slot=("grp", "setup"))
    P.op("dve", lambda e: e.memset(onesB[:], 1.0), writes=["ones"])
    P.op("dve", lambda e: e.memset(onesF[:], 1.0), writes=["onesF"])
    P.op("dve", lambda e: e.memset(epsT[:], EPS), writes=["eps"])
    P.op("act", lambda e: e.activation(out=scB[:], in_=condT[:], func=AF.Silu), reads=["condT"], writes=["scT"])

    for c in range(NCH):
        P.op("sp", lambda e, c=c: e.dma_start(out=xT[:, c, :], in_=xT_d[:, c, :]),
             writes=[("xT", c, t) for t in range(4)], dma=True, slot=("grp", "xT"))

    cv.reset(COMMON)
    wm = [cv.take(NCH * 1152, BF16).rearrange("p (c f) -> p c f", c=NCH) for _ in range(4)]
    modin = cv.take(144, F32)
    for l in range(4):
        for hf in range(2):
            b = (l * 2 + hf) % 4
            src = wmod_d[l, :, hf * 1152:(hf + 1) * 1152].rearrange("(c p) f -> p c f", p=128)
            P.op("pool", lambda e, b=b, src=src: e.dma_start(out=wm[b], in_=src), writes=[("wm", b)], dma=True)
            for i in range(9):
                col = (hf * 9 + i) * 2
                for c in range(NCH):
                    P.op("pe", lambda e, b=b, i=i, c=c, col=col: e.matmul(
                        ps[0][:, col:col + 2], lhsT=wm[b][:, c, i * 128:(i + 1) * 128], rhs=scB[:, c, :],
                        start=(c == 0), stop=(c == NCH - 1)),
                        reads=[("wm", b), "scT"], writes=[PK[0]])
        P.op("dve", lambda e, l=l: e.tensor_copy(out=modin[:, l * 36:(l + 1) * 36], in_=ps[0][:, 0:36]),
             reads=[PK[0]], writes=["modin"])
    P.op("sp", lambda e: e.dma_start(out=mod_in, in_=modin), reads=["modin"], writes=["mod_in"], dma=True)
    P.op("pool", lambda e: e.collective_compute("AllGather", ALU.bypass, replica_groups=GROUPS,
                                                ins=[mod_in], outs=[mod_out]),
         reads=["mod_in"], writes=["mod_out"], dma=True, inc=1)
    for l in range(4):
        src = mod_out[:, l * 36:(l + 1) * 36].rearrange("(r p) x -> p r x", p=128)
        P.op("sp", lambda e, l=l, src=src: e.dma_start(out=mod_lrx[:, l], in_=src),
             reads=["mod_out"], writes=["modS"], dma=True, slot=("dma", "modS", l))
    for n in range(2):
        P.op("dve", lambda e, n=n: e.tensor_tensor(out=mod_lcn[:, :, :, n], in0=mod_lcn[:, :, :, n],
                                                   in1=bmodT[:], op=ALU.add),
             reads=["modS", "bmodT"], writes=["modS"])
    for i in range(3):
        for n in range(2):
            P.op("dve", lambda e, i=i, n=n: e.scalar_tensor_tensor(
                out=mod5[:, :, 3 * i + 1, :, n], in0=mod5[:, :, 3 * i + 1, :, n], scalar=1.0,
                in1=normgT[:, :, i, :], op0=ALU.add, op1=ALU.mult),
                reads=["modS", "normgT"], writes=["modS"])
            if i != 1:
                P.op("dve", lambda e, i=i, n=n: e.tensor_scalar(
                    out=mod5[:, :, 3 * i + 2, :, n], in0=mod5[:, :, 3 * i + 2, :, n], scalar1=0.5,
                    scalar2=None, op0=ALU.mult),
                    reads=["modS"], writes=["modS"])

    def rstd_block(src_keys, nchunks, get_src, inv_n, bank, extra_reads=()):
        for c in range(nchunks):
            s = rot("sq")
            src, rk = get_src(c)
            P.op("act", lambda e, s=s, src=src: e.activation(out=sq[:, s, :], in_=src, func=AF.Square),
                 reads=[rk], writes=[("sq", s)])
            P.op("pe", lambda e, s=s, c=c: e.matmul(ps[bank][:], lhsT=onesB[:], rhs=sq[:, s, :],
                                                    start=(c == 0), stop=(c == nchunks - 1)),
                 reads=[("sq", s), "ones"], writes=[PK[bank]])
        P.op("act", lambda e: e.activation(out=rs[:, 0, :], in_=ps[bank][:], func=AF.Ln,
                                           bias=epsT[:, 0:1], scale=inv_n),
             reads=[PK[bank], "eps"], writes=[("rs", 0)])
        P.op("act", lambda e: e.activation(out=rs[:, 1, :], in_=rs[:, 0, :], func=AF.Exp, scale=-0.5),
             reads=[("rs", 0)], writes=[("rs", 1)])

    def norm_phase(l, i, final=False):
        for t in range(4):
            n = cond_of(t)
            rstd_block(None, NCH, lambda c, t=t: (xT[:, c, tsl(t)], ("xT", c, t)), 1.0 / D, 7)
            for c in range(NCH):
                s = rot("tmpf")
                g = finalT[:, c:c + 1] if final else mcol(l, 3 * i + 1, c, n)
                P.op("dve", lambda e, s=s, c=c, t=t, g=g: e.scalar_tensor_tensor(
                    out=tmpf[:, s, :], in0=xT[:, c, tsl(t)], scalar=g, in1=rs[:, 1, :],
                    op0=ALU.mult, op1=ALU.mult),
                    reads=[("xT", c, t), ("rs", 1), "modS", "finalT"], writes=[("tmpf", s)])
                if final:
                    outs.append(P.op("sp", lambda e, s=s, c=c, t=t: e.dma_start(out=yT_d[:, c, tsl(t)], in_=tmpf[:, s, :]),
                                     reads=[("tmpf", s)], writes=[("yT", c, t)], dma=True, slot=("out", "tmpf", s)))
                else:
                    P.op("act", lambda e, s=s, c=c, t=t, l=l, i=i, n=n: e.activation(
                        out=hT[:, c, tsl(t)], in_=tmpf[:, s, :], func=AF.Identity,
                        bias=mcol(l, 3 * i, c, n), scale=1.0),
                        reads=[("tmpf", s), "modS"], writes=[("hT", c, t)])

    pending = []

    def norm_items(l, i, t, final=False):
        items = []
        n = cond_of(t)
        bank = 7

        def sq_item(c):
            def f():
                s = rot("sq")
                P.op("act", lambda e: e.activation(out=sq[:, s, :], in_=xT[:, c, tsl(t)], func=AF.Square),
                     reads=[("xT", c, t)], writes=[("sq", s)])
                P.op("pe", lambda e: e.matmul(ps[bank][:], lhsT=onesB[:], rhs=sq[:, s, :],
                                              start=(c == 0), stop=(c == NCH - 1)),
                     reads=[("sq", s), "ones"], writes=[PK[bank]])
            return f

        def rs_item():
            P.op("act", lambda e: e.activation(out=rs[:, 0, :], in_=ps[bank][:], func=AF.Ln,
                                               bias=epsT[:, 0:1], scale=1.0 / D),
                 reads=[PK[bank], "eps"], writes=[("rs", 0)])
            P.op("act", lambda e: e.activation(out=rs[:, 1, :], in_=rs[:, 0, :], func=AF.Exp, scale=-0.5),
                 reads=[("rs", 0)], writes=[("rs", 1)])

        def out_item(c):
            def f():
                s = rot("tmpf")
                g = finalT[:, c:c + 1] if final else mcol(l, 3 * i + 1, c, n)
                P.op("dve", lambda e: e.scalar_tensor_tensor(
                    out=tmpf[:, s, :], in0=xT[:, c, tsl(t)], scalar=g, in1=rs[:, 1, :],
                    op0=ALU.mult, op1=ALU.mult),
                    reads=[("xT", c, t), ("rs", 1), "modS", "finalT"], writes=[("tmpf", s)])
                if final:
                    outs.append(P.op("sp", lambda e: e.dma_start(out=yT_d[:, c, tsl(t)], in_=tmpf[:, s, :]),
                                     reads=[("tmpf", s)], writes=[("yT", c, t)], dma=True, slot=("out", "tmpf", s)))
                else:
                    P.op("act", lambda e: e.activation(
                        out=hT[:, c, tsl(t)], in_=tmpf[:, s, :], func=AF.Identity,
                        bias=mcol(l, 3 * i, c, n), scale=1.0),
                        reads=[("tmpf", s), "modS"], writes=[("hT", c, t)])
            return f

        for c in range(NCH):
            items.append((t, sq_item(c)))
        items.append((t, rs_item))
        for c in range(NCH):
            items.append((t, out_item(c)))
        return items

    def pop_pending(n):
        for _ in range(n):
            if pending:
                pending.pop(0)[1]()

    def flush_pending(upto_t=None):
        while pending and (upto_t is None or any(tg <= upto_t for tg, _ in pending)):
            pending.pop(0)[1]()

    def resid_update(bank, m, t, gate_ap, tok=None):
        sl = tsl(t) if tok is None else tok
        P.op("dve", lambda e: e.scalar_tensor_tensor(
            out=xT[:, m, sl], in0=ps[bank][:, 0:(sl.stop - sl.start)], scalar=gate_ap, in1=xT[:, m, sl],
            op0=ALU.mult, op1=ALU.add),
            reads=[PK[bank], ("xT", m, t), "modS"], writes=[("xT", m, t)])

    def ffn_phase(l, i, pre_normed=False, next_norm=None, skip_fence=False, pre_work=None, carry=False,
                  tail_order=(0, 1, 2, 3), tblocks=(0, 1, 2, 3), chunk0_preloaded=False, chunk1_preloaded=False):
        if not skip_fence:
            P.fence()
        cv.reset(COMMON)
        NB = 3
        wgb, wub, wdb = [], [], []
        for _ in range(NB):
            wgb.append(cv.take(NCH * 512, BF16).rearrange("p (c f) -> p c f", c=NCH))
            wub.append(cv.take(NCH * 512, BF16).rearrange("p (c f) -> p c f", c=NCH))
            wdb.append(cv.take(4 * D, BF16).rearrange("p (j d) -> p j d", j=4))
        actb = [cv.take(4 * TB, BF16).rearrange("p (j t) -> p j t", j=4) for _ in range(2)]
        slb = [cv.take(TB, BF16) for _ in range(2)]
        sizes = [4, 4, 4, 4, 3, 3]
        f0s = [0, 4, 8, 12, 16, 19]
        kidx = 2 * i
        sub = 0 if i == 0 else 2

        def load_chunk(k):
            s = k % NB
            G = sizes[k]
            f0 = f0s[k] * 128
            srcg = wg_d[l, i, :, f0:f0 + G * 128].rearrange("(c p) f -> p c f", p=128)
            srcu = wu_d[l, i, :, f0:f0 + G * 128].rearrange("(c p) f -> p c f", p=128)
            srcd = wd_d[l, i, f0:f0 + G * 128, :].rearrange("(j p) d -> p j d", p=128)
            P.op("pool", lambda e: e.dma_start(out=wgb[s][:, :, 0:G * 128], in_=srcg), writes=[("wg", s)], dma=True)
            P.op("pool", lambda e: e.dma_start(out=wub[s][:, :, 0:G * 128], in_=srcu), writes=[("wu", s)], dma=True)
            P.op("pool", lambda e: e.dma_start(out=wdb[s][:, 0:G, :], in_=srcd), writes=[("wd", s)], dma=True)

        gu_cnt = [0]

        def gate_up(k, t):
            s = k % NB
            G = sizes[k]
            ab = (gu_cnt[0]) % 2
            gu_cnt[0] += 1
            for j in range(G):
                bg = j % 2
                bu = 2 + j % 2
                for c in range(NCH):
                    P.op("pe", lambda e, j=j, c=c, bg=bg: e.matmul(
                        ps[bg][:], lhsT=wgb[s][:, c, j * 128:(j + 1) * 128], rhs=hT[:, c, tsl(t)],
                        start=(c == 0), stop=(c == NCH - 1)),
                        reads=[("wg", s), ("hT", c, t)], writes=[PK[bg]])
                for c in range(NCH):
                    P.op("pe", lambda e, j=j, c=c, bu=bu: e.matmul(
                        ps[bu][:], lhsT=wub[s][:, c, j * 128:(j + 1) * 128], rhs=hT[:, c, tsl(t)],
                        start=(c == 0), stop=(c == NCH - 1)),
                        reads=[("wu", s), ("hT", c, t)], writes=[PK[bu]])
                sl_i = rot("slb")
                P.op("act", lambda e, bg=bg, sl_i=sl_i: e.activation(out=slb[sl_i], in_=ps[bg][:], func=AF.Silu),
                     reads=[PK[bg]], writes=[("slb", sl_i)])
                P.op("dve", lambda e, bu=bu, sl_i=sl_i, j=j, ab=ab: e.tensor_tensor(
                    out=actb[ab][:, j, :], in0=ps[bu][:], in1=slb[sl_i], op=ALU.mult),
                    reads=[PK[bu], ("slb", sl_i)], writes=[("act", ab, j)])
                pop_pending(2)
            return ab

        def down(k, t, ab):
            s = k % NB
            G = sizes[k]
            n = cond_of(t)
            for m in range(NCH):
                by = 4 + m % 3
                for j in range(G):
                    P.op("pe", lambda e, j=j, m=m, by=by: e.matmul(
                        ps[by][:], lhsT=wdb[s][:, j, m * 128:(m + 1) * 128], rhs=actb[ab][:, j, :],
                        start=(j == 0), stop=(j == G - 1)),
                        reads=[("wd", s), ("act", ab, j)], writes=[PK[by]])
                resid_update(by, m, t, mcol(l, 3 * sub + 2, m, n))
                pop_pending(1)

        if pre_work is not None:
            tiles = [(wgb[2][:, :, i_ * 128:(i_ + 1) * 128], ("wg", 2)) for i_ in range(4)] + \
                    [(wub[2][:, :, i_ * 128:(i_ + 1) * 128], ("wu", 2)) for i_ in range(4)]
            pre_work(lambda: (None if chunk0_preloaded else load_chunk(0),
                              None if chunk1_preloaded else load_chunk(1)), tiles)
        if not pre_normed:
            for t in tblocks:
                pending.extend(norm_items(l, sub, t))
            flush_pending(tblocks[0])
        nk_ = len(sizes)
        work = [(k, t) for k in range(nk_ - 1) for t in tblocks]
        if pre_work is None:
            load_chunk(0)
            load_chunk(1)
        prev = None
        for idx, (k, t) in enumerate(work):
            if k == 0:
                flush_pending(t)
            ab = gate_up(k, t)
            if prev is not None:
                down(*prev)
            if t == tblocks[0] and k >= 1 and k + 1 < nk_:
                load_chunk(k + 1)
                if l == 0 and i == 1:
                    for a in range(2):
                        q = k - 1
                        P.op("pool", lambda e, a=a, q=q: e.dma_start(out=dftS_bf[a, q * 1024:(q + 1) * 1024, :],
                                                                     in_=dftS_d[a, q * 1024:(q + 1) * 1024, :]),
                             writes=[("dftS_bf", a, q)], dma=True, slot="dftcast")
            prev = (k, t, ab)
        k = nk_ - 1
        for t in [t_ for t_ in tail_order if t_ in tblocks]:
            ab = gate_up(k, t)
            if prev is not None:
                down(*prev)
                prev = None
            down(k, t, ab)
            if next_norm is not None:
                pending.extend(norm_items(next_norm[0], next_norm[1], t, final=next_norm[2]))
        if not carry:
            flush_pending()

    def mla_phase(l, pre_normed=False):
        jl = l // 2
        P.fence()
        flush_pending()
        cv.reset(COMMON)
        cqT_s = cv.take(4 * 1024, BF16).rearrange("p (c t) -> p c t", c=4)
        ckvT_s = cv.take(2 * 1024, BF16).rearrange("p (c t) -> p c t", c=2)
        kpeT_s = cv.take(1024, BF16)
        ropeT = cv.take(2 * 1024, F32).rearrange("p (a t) -> p a t", a=2)
        SHARED_S = cv.off
        cqT_p = cv.take(4 * 1024, BF16).rearrange("p (c t) -> p c t", c=4)
        ckvT_p = cv.take(2 * 1024, BF16).rearrange("p (c t) -> p c t", c=2)
        kpeT_p = cv.take(1024, BF16)
        SHARED_P = cv.off

        def lsl(t):
            return slice((t % 2) * TB, (t % 2 + 1) * TB)

        def cq(kc, t):
            return (cqT_p if t < 2 else cqT_s)[:, kc, lsl(t)]

        def ckv(mc, t):
            return (ckvT_p if t < 2 else ckvT_s)[:, mc, lsl(t)]

        def kpe(t):
            return (kpeT_p if t < 2 else kpeT_s)[0:64, lsl(t)]

        wdq = cv.take(NCH * 512, BF16).rearrange("p (c f) -> p c f", c=NCH)
        wdkv = cv.take(NCH * 384, BF16).rearrange("p (c f) -> p c f", c=NCH)
        cqraw = cv.take(4 * TB, F32).rearrange("p (c t) -> p c t", c=4)
        ckvf = [cv.take(2 * TB, F32).rearrange("p (c t) -> p c t", c=2) for _ in range(2)]
        kpef = [cv.take(TB, F32) for _ in range(2)]
        if not pre_normed:
            norm_phase(l, 1)
        P.op("pool", lambda e: e.dma_start(out=wdq, in_=wdq_d[jl].rearrange("(c p) f -> p c f", p=128)),
             writes=["wdq"], dma=True)
        P.op("pool", lambda e: e.dma_start(out=wdkv[:, :, 0:320], in_=wdkv_d[jl].rearrange("(c p) f -> p c f", p=128)),
             writes=["wdkv0"], dma=True)
        P.op("pool", lambda e: e.dma_start(out=wdkv[:, :, 320:384], in_=wdkvs_d[jl].rearrange("(c p) f -> p c f", p=128)),
             writes=["wdkv1"], dma=True)
        for a in range(2):
            P.op("sp", lambda e, a=a: e.dma_start(out=ropeT[0:64, a, :], in_=rope_d[a]), writes=[("rope", a)], dma=True)

        def stage_a(t):
            for mc in range(4):
                for c in range(NCH):
                    P.op("pe", lambda e, mc=mc, c=c: e.matmul(
                        ps[mc][:], lhsT=wdq[:, c, mc * 128:(mc + 1) * 128], rhs=hT[:, c, tsl(t)],
                        start=(c == 0), stop=(c == NCH - 1)),
                        reads=["wdq", ("hT", c, t)], writes=[PK[mc]])
                P.op("act", lambda e, mc=mc: e.activation(out=cqraw[:, mc, :], in_=ps[mc][:], func=AF.Identity),
                     reads=[PK[mc]], writes=[("cqraw", mc)])
            rstd_block(None, 4, lambda c: (cqraw[:, c, :], ("cqraw", c)), 1.0 / 512, 7)
            for mc in range(4):
                P.op("dve", lambda e, mc=mc: e.scalar_tensor_tensor(
                    out=cq(mc, t), in0=cqraw[:, mc, :], scalar=qnT[:, jl, mc:mc + 1], in1=rs[:, 1, :],
                    op0=ALU.mult, op1=ALU.mult),
                    reads=[("cqraw", mc), ("rs", 1), "qnT"], writes=[("cqT", mc, t)])
            fb = t % 2
            for mc in range(2):
                for c in range(NCH):
                    P.op("pe", lambda e, mc=mc, c=c: e.matmul(
                        ps[4 + mc][:], lhsT=wdkv[:, c, mc * 128:(mc + 1) * 128], rhs=hT[:, c, tsl(t)],
                        start=(c == 0), stop=(c == NCH - 1)),
                        reads=["wdkv0", ("hT", c, t)], writes=[PK[4 + mc]])
                P.op("act", lambda e, mc=mc: e.activation(out=cqraw[:, mc, :], in_=ps[4 + mc][:], func=AF.Identity),
                     reads=[PK[4 + mc]], writes=[("cqraw", mc)])
            rstd_block(None, 2, lambda c: (cqraw[:, c, :], ("cqraw", c)), 1.0 / 256, 7)
            for mc in range(2):
                if t < 2:
                    P.op("dve", lambda e, mc=mc: e.scalar_tensor_tensor(
                        out=ckvf[fb][:, mc, :], in0=cqraw[:, mc, :], scalar=kvnT[:, jl, mc:mc + 1], in1=rs[:, 1, :],
                        op0=ALU.mult, op1=ALU.mult),
                        reads=[("cqraw", mc), ("rs", 1), "kvnT"], writes=[("ckvf", fb, mc)])
                    P.op("act", lambda e, mc=mc: e.activation(out=ckv(mc, t), in_=ckvf[fb][:, mc, :], func=AF.Identity),
                         reads=[("ckvf", fb, mc)], writes=[("ckvT", mc, t)])
                else:
                    P.op("dve", lambda e, mc=mc: e.scalar_tensor_tensor(
                        out=ckv(mc, t), in0=cqraw[:, mc, :], scalar=kvnT[:, jl, mc:mc + 1], in1=rs[:, 1, :],
                        op0=ALU.mult, op1=ALU.mult),
                        reads=[("cqraw", mc), ("rs", 1), "kvnT"], writes=[("ckvT", mc, t)])
            if t < 2:
                outs.append(P.op("sp", lambda e: e.dma_start(
                    out=ockv_d[jl][:, tsl(t)].rearrange("(c p) t -> p c t", p=128), in_=ckvf[fb]),
                    reads=[("ckvf", fb, 0), ("ckvf", fb, 1)], writes=[("ockv", jl, t)], dma=True,
                    slot=("out", "ckvf", fb)))
            for c in range(NCH):
                P.op("pe", lambda e, c=c: e.matmul(
                    ps[6][0:64, :], lhsT=wdkv[:, c, 256:320], rhs=hT[:, c, tsl(t)],
                    start=(c == 0), stop=(c == NCH - 1)),
                    reads=["wdkv0", ("hT", c, t)], writes=[PK[6]])
            if t < 2:
                P.op("act", lambda e: e.activation(out=kpef[fb][0:64, :], in_=ps[6][0:64, :], func=AF.Identity),
                     reads=[PK[6]], writes=[("kpef", fb)])
                P.op("dve", lambda e: e.tensor_copy(out=kpe(t), in_=kpef[fb][0:64, :]),
                     reads=[("kpef", fb)], writes=[("kpeT", t)])
                outs.append(P.op("sp", lambda e: e.dma_start(out=okpe_d[jl][:, tsl(t)], in_=kpef[fb][0:64, :]),
                                 reads=[("kpef", fb)], writes=[("okpe", jl, t)], dma=True,
                                 slot=("out", "kpef", fb)))
            else:
                for c in range(NCH):
                    P.op("pe", lambda e, c=c: e.matmul(
                        ps[3][0:64, :], lhsT=wdkv[:, c, 320:384], rhs=hT[:, c, tsl(t)],
                        start=(c == 0), stop=(c == NCH - 1)),
                        reads=["wdkv1", ("hT", c, t)], writes=[PK[3]])
                tok = lsl(t)
                P.op("dve", lambda e: e.tensor_tensor(out=tmpf[0:64, 0, :], in0=ps[6][0:64, :],
                                                      in1=ropeT[0:64, 0, tok], op=ALU.mult),
                     reads=[PK[6], ("rope", 0)], writes=[("tmpf", 0)])
                P.op("dve", lambda e: e.tensor_tensor(out=tmpf[0:64, 1, :], in0=ps[3][0:64, :],
                                                      in1=ropeT[0:64, 1, tok], op=ALU.mult),
                     reads=[PK[3], ("rope", 1)], writes=[("tmpf", 1)])
                P.op("dve", lambda e: e.tensor_tensor(out=kpe(t), in0=tmpf[0:64, 0, :],
                                                      in1=tmpf[0:64, 1, :], op=ALU.add),
                     reads=[("tmpf", 0), ("tmpf", 1)], writes=[("kpeT", t)])

        for t in (2, 3, 0, 1):
            stage_a(t)
            if t == 3:
                P.op("sp", lambda e: e.dma_start(out=x_in[jl][0:256, :].rearrange("(c p) t -> p c t", p=128),
                                                 in_=ckvT_s),
                     reads=[("ckvT", 0, 2), ("ckvT", 0, 3), ("ckvT", 1, 2), ("ckvT", 1, 3)],
                     writes=[("x_in", jl, 0)], dma=True, slot=("grp", "x_in", jl))
                P.op("sp", lambda e: e.dma_start(out=x_in[jl][256:320, :], in_=kpeT_s[0:64, :]),
                     reads=[("kpeT", 2), ("kpeT", 3)], writes=[("x_in", jl, 1)], dma=True, slot=("grp", "x_in", jl))
                P.op("pool", lambda e: e.collective_compute("AllGather", ALU.bypass, replica_groups=GROUPS,
                                                            ins=[x_in[jl]], outs=[x_out[jl]]),
                     reads=[("x_in", jl, 0), ("x_in", jl, 1)], writes=[("x_out", jl)], dma=True, inc=1)

        def attention(stream):
            P.fence()
            cv.reset(SHARED_P if stream == 0 else SHARED_S)
            nk = 1024 if stream == 0 else 4608
            nkt = nk // 128
            tok0 = 0 if stream == 0 else 1024
            if stream == 1:
                ckv_all = cv.take(2 * nk, BF16).rearrange("p (c t) -> p c t", c=2)
                kpe_all = cv.take(nk, BF16)
            kn = cv.take(nk, BF16)
            Vh = cv.take(nk, BF16).rearrange("p (k d) -> p k d", d=128)
            qn = cv.take(1024, BF16)
            qr = cv.take(1024, BF16)
            Pt = [cv.take(TB, BF16) for _ in range(4)]
            dacc = [cv.take(TB, F32) for _ in range(2)]
            hw_q = [cv.take(4 * 256, BF16).rearrange("p (c f) -> p c f", c=4) for _ in range(2)]
            hw_kv = [cv.take(2 * 256, BF16).rearrange("p (c f) -> p c f", c=2) for _ in range(2)]
            wob = [cv.take(8 * 128, BF16).rearrange("p (h d) -> p h d", h=8) for _ in range(2)]
            rden = cv.take(TB, F32)
            P.op("dve", lambda e: e.memset(qr[64:128, :], 0.0), writes=["qr_pad"])
            if stream == 1:
                P.op("dve", lambda e: e.memset(kpe_all[64:128, :], 0.0), writes=["kpe_pad"])
            else:
                P.op("dve", lambda e: e.memset(kpeT_p[64:128, :], 0.0), writes=["kpe_pad"])
            if stream == 1:
                P.op("pool", lambda e: e.dma_start(out=ckv_all[:, :, 0:512],
                                                   in_=cckv_d[jl].rearrange("(c p) t -> p c t", p=128)),
                     writes=[("ckv_all", 0)], dma=True, slot=("grp", "kvall_p", jl))
                P.op("pool", lambda e: e.dma_start(out=kpe_all[0:64, 0:512], in_=ckpe_d[jl]),
                     writes=[("kpe_all", 0)], dma=True, slot=("grp", "kvall_p", jl))
                for r in range(4):
                    P.op("sp", lambda e, r=r: e.dma_start(
                        out=ckv_all[:, :, 512 + r * 1024:512 + (r + 1) * 1024],
                        in_=x_out[jl][r * 320:r * 320 + 256, :].rearrange("(c p) t -> p c t", p=128)),
                        reads=[("x_out", jl)], writes=[("ckv_all", 1 + r)], dma=True, slot=("grp", "kvall", jl))
                    P.op("sp", lambda e, r=r: e.dma_start(
                        out=kpe_all[0:64, 512 + r * 1024:512 + (r + 1) * 1024],
                        in_=x_out[jl][r * 320 + 256:(r + 1) * 320, :]),
                        reads=[("x_out", jl)], writes=[("kpe_all", 1 + r)], dma=True, slot=("grp", "kvall", jl))
                ckvsrc, kpesrc = ckv_all, kpe_all
                ckv_keys = [("ckv_all", r) for r in range(5)]
                kpe_keys = [("kpe_all", r) for r in range(5)]
            else:
                ckvsrc, kpesrc = ckvT_p, kpeT_p
                ckv_keys = [("ckvT", mc, t) for mc in range(2) for t in range(2)]
                kpe_keys = [("kpeT", 0), ("kpeT", 1)]

            gcnt = {"tile": 0}

            def head_tiles(h, qblocks):
                flat = [(qi, ki) for qi, (q0, qn_, kts) in enumerate(qblocks) for ki in range(len(kts))]
                info = {}

                def s_mm(qi, ki):
                    q0, qn_, kts = qblocks[qi]
                    kt = kts[ki]
                    sbk = gcnt["tile"] % 3
                    gcnt["tile"] += 1
                    da = dacc[qi % 2]
                    P.op("pe", lambda e: e.matmul(
                        ps[sbk][:, 0:qn_], lhsT=kn[:, kt * 128:(kt + 1) * 128], rhs=qn[:, q0:q0 + qn_],
                        start=True, stop=False),
                        reads=[("kn", kt // 4), ("qn", q0 // TB)], writes=[PK[sbk]])
                    P.op("pe", lambda e: e.matmul(
                        ps[sbk][:, 0:qn_], lhsT=kpesrc[:, kt * 128:(kt + 1) * 128], rhs=qr[:, q0:q0 + qn_],
                        start=False, stop=True),
                        reads=kpe_keys + [("qr", q0 // TB), "qr_pad", "kpe_pad"], writes=[PK[sbk]])
                    pi = rot("Pt", 4)
                    info[(qi, ki)] = pi
                    P.op("act", lambda e: e.activation(out=Pt[pi][:, 0:qn_], in_=ps[sbk][:, 0:qn_], func=AF.Exp),
                         reads=[PK[sbk]], writes=[("Pt", pi)])
                    if ki == 0:
                        P.op("dve", lambda e: e.tensor_copy(out=da[:, 0:qn_], in_=Pt[pi][:, 0:qn_]),
                             reads=[("Pt", pi)], writes=[("dacc", qi % 2)])
                    else:
                        P.op("dve", lambda e: e.tensor_tensor(out=da[:, 0:qn_], in0=da[:, 0:qn_], in1=Pt[pi][:, 0:qn_],
                                                              op=ALU.add),
                             reads=[("Pt", pi), ("dacc", qi % 2)], writes=[("dacc", qi % 2)])

                def pv_mm(qi, ki):
                    q0, qn_, kts = qblocks[qi]
                    kt = kts[ki]
                    pi = info[(qi, ki)]
                    ob = 4 + (qi % 2)
                    nkt_ = len(kts)
                    P.op("pe", lambda e: e.matmul(
                        ps[ob][:, 0:qn_], lhsT=Vh[:, kt, :], rhs=Pt[pi][:, 0:qn_],
                        start=(ki == 0), stop=(ki == nkt_ - 1)),
                        reads=[("Vh", kt // 4), ("Pt", pi)], writes=[PK[ob]])
                    if ki == nkt_ - 1:
                        db = 6 + (qi % 2)
                        da = dacc[qi % 2]
                        tq = (tok0 + q0) // TB
                        P.op("pe", lambda e: e.matmul(ps[db][:, 0:qn_], lhsT=onesF[:], rhs=da[:, 0:qn_],
                                                      start=True, stop=True),
                             reads=["onesF", ("dacc", qi % 2)], writes=[PK[db]])
                        P.op("act", lambda e: e.activation(out=rden[:, 0:qn_], in_=ps[db][:, 0:qn_], func=AF.Ln),
                             reads=[PK[db]], writes=["rden"])
                        P.op("act", lambda e: e.activation(out=rden[:, 0:qn_], in_=rden[:, 0:qn_], func=AF.Exp, scale=-1.0),
                             reads=["rden"], writes=["rden"])
                        P.op("dve", lambda e: e.tensor_tensor(out=hT[:, h, tok0 + q0:tok0 + q0 + qn_], in0=ps[ob][:, 0:qn_],
                                                              in1=rden[:, 0:qn_], op=ALU.mult),
                             reads=[PK[ob], "rden"], writes=[("hT", h, tq)])

                depth_ = 2
                for idx in range(len(flat) + depth_):
                    if idx < len(flat):
                        s_mm(*flat[idx])
                    if idx >= depth_:
                        pv_mm(*flat[idx - depth_])

            def head(h):
                hb = h % 2
                P.op("pool", lambda e: e.dma_start(
                    out=hw_q[hb][:, :, 0:192], in_=wuq_d[jl][:, h * 192:(h + 1) * 192].rearrange("(c p) f -> p c f", p=128)),
                    writes=[("hwq0", hb)], dma=True)
                P.op("pool", lambda e: e.dma_start(
                    out=hw_q[hb][:, :, 192:256], in_=wuqs_d[jl][:, h * 64:(h + 1) * 64].rearrange("(c p) f -> p c f", p=128)),
                    writes=[("hwq1", hb)], dma=True)
                P.op("pool", lambda e: e.dma_start(
                    out=hw_kv[hb], in_=wukv_d[jl][:, h * 256:(h + 1) * 256].rearrange("(c p) f -> p c f", p=128)),
                    writes=[("hwkv", hb)], dma=True)
                for tb in range(2):
                    t = (tok0 // TB) + tb
                    for kc in range(4):
                        P.op("pe", lambda e, kc=kc, t=t: e.matmul(
                            ps[0][:], lhsT=hw_q[hb][:, kc, 0:128], rhs=cq(kc, t),
                            start=(kc == 0), stop=(kc == 3)),
                            reads=[("hwq0", hb), ("cqT", kc, t)], writes=[PK[0]])
                    P.op("act", lambda e, tb=tb: e.activation(out=qn[:, tb * TB:(tb + 1) * TB], in_=ps[0][:],
                                                              func=AF.Identity, scale=ATTN_SCALE),
                         reads=[PK[0]], writes=[("qn", tb)])
                    for kc in range(4):
                        P.op("pe", lambda e, kc=kc, t=t: e.matmul(
                            ps[1][0:64, :], lhsT=hw_q[hb][:, kc, 128:192], rhs=cq(kc, t),
                            start=(kc == 0), stop=(kc == 3)),
                            reads=[("hwq0", hb), ("cqT", kc, t)], writes=[PK[1]])
                    if stream == 0:
                        P.op("act", lambda e, tb=tb: e.activation(out=qr[0:64, tb * TB:(tb + 1) * TB], in_=ps[1][0:64, :],
                                                                  func=AF.Identity, scale=ATTN_SCALE),
                             reads=[PK[1]], writes=[("qr", tb)])
                    else:
                        for kc in range(4):
                            P.op("pe", lambda e, kc=kc, t=t: e.matmul(
                                ps[2][0:64, :], lhsT=hw_q[hb][:, kc, 192:256], rhs=cq(kc, t),
                                start=(kc == 0), stop=(kc == 3)),
                                reads=[("hwq1", hb), ("cqT", kc, t)], writes=[PK[2]])
                        tok = slice(tb * TB, (tb + 1) * TB)
                        P.op("dve", lambda e, tok=tok: e.tensor_tensor(out=tmpf[0:64, 0, :], in0=ps[1][0:64, :],
                                                                       in1=ropeT[0:64, 0, tok], op=ALU.mult),
                             reads=[PK[1], ("rope", 0)], writes=[("tmpf", 0)])
                        P.op("dve", lambda e, tok=tok: e.tensor_tensor(out=tmpf[0:64, 1, :], in0=ps[2][0:64, :],
                                                                       in1=ropeT[0:64, 1, tok], op=ALU.mult),
                             reads=[PK[2], ("rope", 1)], writes=[("tmpf", 1)])
                        P.op("dve", lambda e: e.tensor_tensor(out=tmpf[0:64, 0, :], in0=tmpf[0:64, 0, :],
                                                              in1=tmpf[0:64, 1, :], op=ALU.add),
                             reads=[("tmpf", 0), ("tmpf", 1)], writes=[("tmpf", 0)])
                        P.op("act", lambda e, tb=tb: e.activation(out=qr[0:64, tb * TB:(tb + 1) * TB], in_=tmpf[0:64, 0, :],
                                                                  func=AF.Identity, scale=ATTN_SCALE),
                             reads=[("tmpf", 0)], writes=[("qr", tb)])
                for kb in range(nk // TB):
                    bank = 2 + kb % 2
                    for kc in range(2):
                        P.op("pe", lambda e, kc=kc, kb=kb, bank=bank: e.matmul(
                            ps[bank][:], lhsT=hw_kv[hb][:, kc, 0:128], rhs=ckvsrc[:, kc, kb * TB:(kb + 1) * TB],
                            start=(kc == 0), stop=(kc == 1)),
                            reads=[("hwkv", hb)] + ckv_keys, writes=[PK[bank]])
                    if kb % 2 == 0:
                        P.op("act", lambda e, kb=kb, bank=bank: e.activation(out=kn[:, kb * TB:(kb + 1) * TB], in_=ps[bank][:], func=AF.Identity),
                             reads=[PK[bank]], writes=[("kn", kb)])
                    else:
                        P.op("dve", lambda e, kb=kb, bank=bank: e.tensor_copy(out=kn[:, kb * TB:(kb + 1) * TB], in_=ps[bank][:]),
                             reads=[PK[bank]], writes=[("kn", kb)])
                for vb in range(nkt // 4):
                    bank = 2 + vb % 2
                    for q4 in range(4):
                        kt = vb * 4 + q4
                        for kc in range(2):
                            P.op("pe", lambda e, kc=kc, kt=kt, q4=q4, bank=bank: e.matmul(
                                ps[bank][:, q4 * 128:(q4 + 1) * 128], lhsT=ckvsrc[:, kc, kt * 128:(kt + 1) * 128],
                                rhs=hw_kv[hb][:, kc, 128:256], start=(kc == 0), stop=(kc == 1)),
                                reads=[("hwkv", hb)] + ckv_keys, writes=[PK[bank]])
                    vdst = Vh[:, vb * 4:(vb + 1) * 4, :]
                    vsrc = ps[bank][:].rearrange("p (k d) -> p k d", d=128)
                    if vb % 2 == 0:
                        P.op("dve", lambda e, vdst=vdst, vsrc=vsrc: e.tensor_copy(out=vdst, in_=vsrc),
                             reads=[PK[bank]], writes=[("Vh", vb)])
                    else:
                        P.op("act", lambda e, vdst=vdst, vsrc=vsrc: e.activation(out=vdst, in_=vsrc, func=AF.Identity),
                             reads=[PK[bank]], writes=[("Vh", vb)])
                if stream == 0:
                    qblocks = [(s * 256, 256, [2 * s, 2 * s + 1]) for s in range(4)]
                else:
                    qblocks = [(qb * TB, TB, list(range(nkt))) for qb in range(2)]
                head_tiles(h, qblocks)

            for h in range(8):
                head(h)
            if stream == 0:
                wo_stage(0, [(w_, None) for w_ in wob])

        def wo_stage(stream, wob, mid_loads=None):
            n = 0 if stream == 0 else 1
            tok0 = 0 if stream == 0 else 1024
            nb_ = len(wob)

            def ld(m):
                wb_i = m % nb_
                P.op("pool", lambda e: e.dma_start(
                    out=wob[wb_i][0], in_=wo_d[jl][:, m * 128:(m + 1) * 128].rearrange("(h p) d -> p h d", p=128)),
                    writes=[("wob", wb_i)], dma=True)

            for m in range(min(nb_, NCH)):
                ld(m)
            if mid_loads is not None:
                mid_loads()
            for m in range(NCH):
                wb_i = m % nb_
                for tb in range(2):
                    t = tok0 // TB + tb
                    bank = (m * 2 + tb) % 4
                    for hh in range(8):
                        P.op("pe", lambda e, hh=hh, t=t, bank=bank, wb_i=wb_i: e.matmul(
                            ps[bank][:], lhsT=wob[wb_i][0][:, hh, :], rhs=hT[:, hh, tsl(t)],
                            start=(hh == 0), stop=(hh == 7)),
                            reads=[("wob", wb_i), ("hT", hh, t)] + ([wob[wb_i][1]] if wob[wb_i][1] else []),
                            writes=[PK[bank]])
                    resid_update(bank, m, t, mcol(l, 5, m, n))
                if m + nb_ < NCH:
                    ld(m + nb_)

        attention(0)
        attention(1)

        def pre_work(mid_loads, tiles):
            wo_stage(1, tiles, mid_loads)
        return pre_work

    def fourier_phase(l, pre_normed=False):
        jl = l // 2
        P.fence()
        flush_pending()
        cv.reset(COMMON)
        wg0 = cv.take(NCH * 512, BF16).rearrange("p (c f) -> p c f", c=NCH)
        wu0 = cv.take(NCH * 512, BF16).rearrange("p (c f) -> p c f", c=NCH)
        wd0 = cv.take(4 * D, BF16).rearrange("p (j d) -> p j d", j=4)
        P.op("pool", lambda e: e.dma_start(out=wg0[:, :, 0:512], in_=wg_d[l, 1, :, 0:512].rearrange("(c p) f -> p c f", p=128)),
             writes=[("wg", 0)], dma=True)
        P.op("pool", lambda e: e.dma_start(out=wu0[:, :, 0:512], in_=wu_d[l, 1, :, 0:512].rearrange("(c p) f -> p c f", p=128)),
             writes=[("wu", 0)], dma=True)
        P.op("pool", lambda e: e.dma_start(out=wd0[:, 0:4, :], in_=wd_d[l, 1, 0:512, :].rearrange("(j p) d -> p j d", p=128)),
             writes=[("wd", 0)], dma=True)
        cs = cv.take(2 * 512, BF16).rearrange("p (c f) -> p c f", c=2)
        dp = cv.take(2 * 2 * 256, BF16).rearrange("p (a n k) -> p a n k", a=2, n=2)
        ABp = cv.take(8 * 2048, BF16).rearrange("p (t f) -> p t f", t=8)
        ABs = cv.take(8 * 2048, BF16).rearrange("p (t f) -> p t f", t=8)
        if not pre_normed:
            norm_phase(l, 1)
        P.op("pool", lambda e: e.dma_start(out=cs, in_=dftC_d.rearrange("(c p) f -> p c f", p=128)), writes=["cs"], dma=True)
        for a in range(2):
            P.op("pool", lambda e, a=a: e.dma_start(out=dp[:, a], in_=dftP_d[a].rearrange("(n p) k -> p n k", p=128)),
                 writes=[("dp", a)], dma=True)
        for tile in list(range(8, 16)) + list(range(8)):
            dst = ABs if tile >= 8 else ABp
            ti = tile % 8
            t = tile // 4
            for g in range(4):
                bank = g % 4
                for kc in range(2):
                    P.op("pe", lambda e, g=g, kc=kc, tile=tile, bank=bank: e.matmul(
                        ps[bank][:], lhsT=hT[:, 2 * g + kc, tile * 128:(tile + 1) * 128], rhs=cs[:, kc, :],
                        start=(kc == 0), stop=(kc == 1)),
                        reads=["cs", ("hT", 2 * g + kc, t)], writes=[PK[bank]])
                dv = dst[:, ti, :].rearrange("p (s g c) -> p s g c", s=2, g=4)[:, :, g, :]
                sv = ps[bank][:].rearrange("p (s c) -> p s c", s=2)
                key = ("AB", tile)
                if g % 2 == 0:
                    P.op("act", lambda e, dv=dv, sv=sv: e.activation(out=dv, in_=sv, func=AF.Identity),
                         reads=[PK[bank]], writes=[(key, g)])
                else:
                    P.op("dve", lambda e, dv=dv, sv=sv: e.tensor_copy(out=dv, in_=sv),
                         reads=[PK[bank]], writes=[(key, g)])
            if tile >= 8 and tile % 2 == 1:
                part = (tile - 8) // 2
                P.op("sp", lambda e, part=part: e.dma_start(
                    out=f_in[jl][part].rearrange("(t p) f -> p t f", p=128), in_=ABs[:, 2 * part:2 * part + 2, :]),
                    reads=[(("AB", tl), g) for tl in (tile - 1, tile) for g in range(4)],
                    writes=[("f_in", jl, part)], dma=True)
                P.op("pool", lambda e, part=part: e.collective_compute(
                    "AllGather", ALU.bypass, replica_groups=GROUPS, ins=[f_in[jl][part]], outs=[f_out[jl][part]]),
                    reads=[("f_in", jl, part)], writes=[("f_out", jl, part)], dma=True, inc=1, nofence=True)
        for s in range(4):
            t = s // 2
            for m in range(NCH):
                bank = m % 4
                first = True
                for nt in range(2):
                    for a in range(2):
                        P.op("pe", lambda e, nt=nt, a=a, m=m, s=s, bank=bank, first=first: e.matmul(
                            ps[bank][:, 0:256], lhsT=ABp[:, s * 2 + nt, a * 1024 + m * 128:a * 1024 + (m + 1) * 128],
                            rhs=dp[:, a, nt, :], start=first, stop=(nt == 1 and a == 1)),
                            reads=[(("AB", s * 2 + nt), gg) for gg in range(4)] + [("dp", a)], writes=[PK[bank]])
                        first = False
                P.op("act", lambda e, m=m, s=s, bank=bank: e.activation(
                    out=hT[:, m, s * 256:(s + 1) * 256], in_=ps[bank][:, 0:256], func=AF.Identity, scale=1.0 / 256.0),
                    reads=[PK[bank]], writes=[("hT", m, t)])
        def fc_stage(tblocks, fwbuf, mid_loads=None):
            nb_ = len(fwbuf)

            def ld(m):
                wi = m % nb_
                P.op("pool", lambda e: e.dma_start(
                    out=fwbuf[wi][0], in_=fw_d[jl][:, m * 128:(m + 1) * 128].rearrange("(c p) d -> p c d", p=128)),
                    writes=[("fwm", wi)], dma=True)

            for m in range(min(nb_, NCH)):
                ld(m)
            if mid_loads is not None:
                mid_loads()
            for m in range(NCH):
                wi = m % nb_
                if m >= nb_ and False:
                    pass
                for t in tblocks:
                    n = cond_of(t)
                    bank = (m * 2 + t) % 4
                    for c in range(NCH):
                        P.op("pe", lambda e, c=c, t=t, bank=bank, wi=wi: e.matmul(
                            ps[bank][:], lhsT=fwbuf[wi][0][:, c, :], rhs=hT[:, c, tsl(t)],
                            start=(c == 0), stop=(c == NCH - 1)),
                            reads=[("fwm", wi), ("hT", c, t)] + ([fwbuf[wi][1]] if fwbuf[wi][1] else []),
                            writes=[PK[bank]])
                    s_ = rot("tmpf")
                    P.op("dve", lambda e, s_=s_, m=m, bank=bank, n=n: e.tensor_scalar(
                        out=tmpf[:, s_, :], in0=ps[bank][:], scalar1=fbT[:, jl, m:m + 1], scalar2=mcol(l, 5, m, n),
                        op0=ALU.add, op1=ALU.mult),
                        reads=[PK[bank], "fbT", "modS"], writes=[("tmpf", s_)])
                    P.op("dve", lambda e, s_=s_, m=m, t=t: e.tensor_tensor(out=xT[:, m, tsl(t)], in0=xT[:, m, tsl(t)],
                                                                           in1=tmpf[:, s_, :], op=ALU.add),
                         reads=[("tmpf", s_), ("xT", m, t)], writes=[("xT", m, t)])
                if m + nb_ < NCH:
                    ld(m + nb_)

        def pre_work1(mid_loads, tiles):
            fc_stage((0, 1), tiles, mid_loads)
        return pre_work1

    def fourier_part2(l):
        jl = l // 2
        P.fence()
        flush_pending()
        cv.reset(COMMON)
        pre = []
        for s_ in range(3):
            pre.append((cv.take(NCH * 512, BF16).rearrange("p (c f) -> p c f", c=NCH),
                        cv.take(NCH * 512, BF16).rearrange("p (c f) -> p c f", c=NCH),
                        cv.take(4 * D, BF16).rearrange("p (j d) -> p j d", j=4)))
        for s_ in range(2):
            f0 = s_ * 512
            wg_, wu_, wd_ = pre[s_]
            P.op("pool", lambda e, wg_=wg_, f0=f0: e.dma_start(
                out=wg_[:, :, 0:512], in_=wg_d[l, 1, :, f0:f0 + 512].rearrange("(c p) f -> p c f", p=128)),
                writes=[("wg", s_)], dma=True)
            P.op("pool", lambda e, wu_=wu_, f0=f0: e.dma_start(
                out=wu_[:, :, 0:512], in_=wu_d[l, 1, :, f0:f0 + 512].rearrange("(c p) f -> p c f", p=128)),
                writes=[("wu", s_)], dma=True)
            P.op("pool", lambda e, wd_=wd_, f0=f0: e.dma_start(
                out=wd_[:, 0:4, :], in_=wd_d[l, 1, f0:f0 + 512, :].rearrange("(j p) d -> p j d", p=128)),
                writes=[("wd", s_)], dma=True)
        fw_tiles = [pre[2][0][:, :, i_ * 128:(i_ + 1) * 128] for i_ in range(4)] + \
                   [pre[2][1][:, :, i_ * 128:(i_ + 1) * 128] for i_ in range(4)]
        for m in range(NCH):
            P.op("pool", lambda e, m=m: e.dma_start(
                out=fw_tiles[m], in_=fw_d[jl][:, m * 128:(m + 1) * 128].rearrange("(c p) d -> p c d", p=128)),
                writes=[("fwm", m)], dma=True)
        NSB = 3
        abn = [cv.take(2048, BF16) for _ in range(NSB)]
        tbn = [cv.take(2 * 512, BF16).rearrange("p (a k) -> p a k", a=2) for _ in range(NSB)]

        def fc_stage(tblocks, fwbuf, mid_loads=None, preloaded=False):
            nb_ = len(fwbuf)

            def ld(m):
                wi = m % nb_
                P.op("pool", lambda e: e.dma_start(
                    out=fwbuf[wi][0], in_=fw_d[jl][:, m * 128:(m + 1) * 128].rearrange("(c p) d -> p c d", p=128)),
                    writes=[("fwm", wi)], dma=True)

            if not preloaded:
                for m in range(min(nb_, NCH)):
                    ld(m)
            if mid_loads is not None:
                mid_loads()
            for m in range(NCH):
                wi = m % nb_
                for t in tblocks:
                    n = cond_of(t)
                    bank = (m * 2 + t) % 4
                    for c in range(NCH):
                        P.op("pe", lambda e, c=c, t=t, bank=bank, wi=wi: e.matmul(
                            ps[bank][:], lhsT=fwbuf[wi][0][:, c, :], rhs=hT[:, c, tsl(t)],
                            start=(c == 0), stop=(c == NCH - 1)),
                            reads=[("fwm", wi), ("hT", c, t)] + ([fwbuf[wi][1]] if fwbuf[wi][1] else []),
                            writes=[PK[bank]])
                    s_ = rot("tmpf")
                    P.op("dve", lambda e, s_=s_, m=m, bank=bank, n=n: e.tensor_scalar(
                        out=tmpf[:, s_, :], in0=ps[bank][:], scalar1=fbT[:, jl, m:m + 1], scalar2=mcol(l, 5, m, n),
                        op0=ALU.add, op1=ALU.mult),
                        reads=[PK[bank], "fbT", "modS"], writes=[("tmpf", s_)])
                    P.op("dve", lambda e, s_=s_, m=m, t=t: e.tensor_tensor(out=xT[:, m, tsl(t)], in0=xT[:, m, tsl(t)],
                                                                           in1=tmpf[:, s_, :], op=ALU.add),
                         reads=[("tmpf", s_), ("xT", m, t)], writes=[("xT", m, t)])
                if m + nb_ < NCH:
                    ld(m + nb_)

        for kb in range(2):
            t = 2 + kb
            nt_order = [r_ * 8 + part * 2 + j_ for part in range(4) for r_ in range(4) for j_ in range(2)]
            for ni, nt in enumerate(nt_order):
                b = (kb * 32 + ni) % NSB
                r_, w_ = nt // 8, nt % 8
                part, j_ = w_ // 2, w_ % 2
                src = f_out[jl][part][r_ * 256 + j_ * 128:r_ * 256 + (j_ + 1) * 128, :]
                P.op("sp", lambda e, src=src, b=b: e.dma_start(out=abn[b], in_=src),
                     reads=[("f_out", jl, part)], writes=[("abn", b)], dma=True)
                P.op("sp", lambda e, nt=nt, b=b, kb=kb: e.dma_start(
                    out=tbn[b], in_=dftS_bf[:, nt * 128:(nt + 1) * 128, kb * 512:(kb + 1) * 512].rearrange("a p k -> p a k")),
                    reads=DFT_KEYS, writes=[("tbn", b)], dma=True)
                for m in range(NCH):
                    for a in range(2):
                        P.op("pe", lambda e, ni=ni, a=a, m=m, b=b: e.matmul(
                            ps[m][:], lhsT=abn[b][:, a * 1024 + m * 128:a * 1024 + (m + 1) * 128], rhs=tbn[b][:, a, :],
                            start=(ni == 0 and a == 0), stop=(ni == 31 and a == 1)),
                            reads=[("abn", b), ("tbn", b)], writes=[PK[m]])
            for m in range(NCH):
                if m % 2 == 0:
                    P.op("act", lambda e, m=m, t=t: e.activation(out=hT[:, m, tsl(t)], in_=ps[m][:], func=AF.Identity,
                                                                 scale=1.0 / 1024.0),
                         reads=[PK[m]], writes=[("hT", m, t)])
                else:
                    P.op("dve", lambda e, m=m, t=t: e.tensor_scalar(out=hT[:, m, tsl(t)], in0=ps[m][:], scalar1=1.0 / 1024.0,
                                                                    scalar2=None, op0=ALU.mult),
                         reads=[PK[m]], writes=[("hT", m, t)])
        def pre_work(mid_loads, tiles):
            fc_stage((2, 3), tiles, mid_loads, preloaded=True)
        return pre_work

    for l in range(depth):
        mixer_on = ("m" if l % 2 == 0 else "f") in DBG_PHASES
        ffn_phase(l, 0, pre_normed=(l > 0), next_norm=(l, 1, False) if mixer_on else None, skip_fence=(l > 0),
                  carry=mixer_on, tail_order=(2, 3, 0, 1) if mixer_on else (0, 1, 2, 3))
        last = (l == depth - 1)
        nn = (0, 0, True) if last else (l + 1, 0, False)
        if mixer_on and l % 2 == 1:
            pw1 = fourier_phase(l, pre_normed=True)
            ffn_phase(l, 1, pre_normed=False, next_norm=nn, pre_work=pw1, carry=True, tblocks=(0, 1),
                      chunk0_preloaded=True)
            pw = fourier_part2(l)
            ffn_phase(l, 1, pre_normed=False, next_norm=nn, pre_work=pw, carry=not last, tblocks=(2, 3),
                      chunk0_preloaded=True, chunk1_preloaded=True)
        else:
            pw = mla_phase(l, pre_normed=True) if mixer_on else None
            ffn_phase(l, 1, pre_normed=False, next_norm=nn, pre_work=pw, carry=not last)
    if depth == 0:
        P.fence()
        norm_phase(0, 0, final=True)
    P.emit(nc, final_waits=outs)
    st.close()
    return nc, P


def _fm(a):
    t = a.shape[0]
    return np.ascontiguousarray(a.T.reshape(NCH, 128, t).transpose(1, 0, 2))


def _vec(a, nch):
    sh = a.shape[:-1]
    b = a.reshape(sh + (nch, 128))
    return np.ascontiguousarray(np.moveaxis(b, -1, 0))


def _const_tables():
    f32 = np.float32
    k = np.arange(256)
    ang = 2 * np.pi * np.outer(k, k) / 256.0
    dftC = np.concatenate([np.cos(ang), np.sin(ang)], axis=1).astype(np.float32)
    dftP = np.stack([np.cos(ang), -np.sin(ang)]).astype(np.float32)
    n = np.arange(4096, dtype=np.int64)
    dftS = []
    for qd in range(4):
        kk = np.arange(qd * 1024, (qd + 1) * 1024, dtype=np.int64)
        a = 2 * np.pi * ((np.outer(n, kk) % 4096).astype(np.float64)) / 4096.0
        dftS.append(np.stack([np.cos(a), -np.sin(a)]).astype(np.float32))
    inv = 1.0 / (10000.0 ** (np.arange(16, dtype=np.float32) / 16.0))
    pos = np.arange(4096)
    row = (pos // 64).astype(np.float32)
    col = (pos % 64).astype(np.float32)
    ang = np.stack([row[:, None] * inv, col[:, None] * inv], axis=1).astype(np.float32)
    cos = np.cos(ang)
    sin = np.sin(ang)
    cosT = np.zeros((64, 4096), f32)
    sinT = np.zeros((64, 4096), f32)
    for a in range(2):
        for hf in range(2):
            for f in range(16):
                p = a * 32 + hf * 16 + f
                cosT[p] = cos[:, a, f]
                sinT[p] = -sin[:, a, f] if hf == 0 else sin[:, a, f]
    rope = [np.ascontiguousarray(np.stack([cosT[:, q * 1024:(q + 1) * 1024], sinT[:, q * 1024:(q + 1) * 1024]]))
            for q in range(4)]
    return dftC, dftP, dftS, rope


def _swap_cols(w, nheads, base, stride):
    cols = []
    for h in range(nheads):
        o = h * stride + base
        for a in range(2):
            cols += list(range(o + a * 32 + 16, o + a * 32 + 32)) + list(range(o + a * 32, o + a * 32 + 16))
    return np.ascontiguousarray(w[..., cols])


_CACHE = {}
DEPTH_RUN = DEPTH


def kernel(x_prompt, x_sample, cache_ckv, cache_kpe, c, c_ctx, w_mod, b_mod, norm_g,
           ffn_wg, ffn_wu, ffn_wd, mla_w_dq, mla_q_norm, mla_w_uq, mla_w_dkv, mla_kv_norm,
           mla_w_ukv, mla_w_o, fourier_w, fourier_b, final_norm):
    A = lambda a: np.ascontiguousarray(np.asarray(a, dtype=np.float32))
    x_prompt, x_sample, cache_ckv, cache_kpe = A(x_prompt), A(x_sample), A(cache_ckv), A(cache_kpe)
    c, c_ctx, w_mod, b_mod, norm_g = A(c), A(c_ctx), A(w_mod), A(b_mod), A(norm_g)
    ffn_wg, ffn_wu, ffn_wd = A(ffn_wg), A(ffn_wu), A(ffn_wd)
    mla_w_dq, mla_q_norm, mla_w_uq, mla_w_dkv = A(mla_w_dq), A(mla_q_norm), A(mla_w_uq), A(mla_w_dkv)
    mla_kv_norm, mla_w_ukv, mla_w_o = A(mla_kv_norm), A(mla_w_ukv), A(mla_w_o)
    fourier_w, fourier_b, final_norm = A(fourier_w), A(fourier_b), A(final_norm)

    if "nc" not in _CACHE:
        _CACHE["nc"] = build_program(DEPTH_RUN)[0]
        _CACHE["tables"] = _const_tables()
    nc = _CACHE["nc"]
    dftC, dftP, dftS, rope = _CACHE["tables"]

    shared = {
        "bmodT": _vec(b_mod, 72), "normgT": _vec(norm_g, 8), "finalT": _vec(final_norm, 8),
        "qnT": _vec(mla_q_norm, 4), "kvnT": _vec(mla_kv_norm, 2), "fbT": _vec(fourier_b, 8),
        "wg": ffn_wg, "wu": ffn_wu, "wd": ffn_wd, "wdq": mla_w_dq, "wuq": mla_w_uq,
        "wuqs": _swap_cols(mla_w_uq, 8, 128, 192), "wdkv": mla_w_dkv,
        "wdkvs": _swap_cols(mla_w_dkv, 1, 256, 0), "wukv": mla_w_ukv, "wo": mla_w_o, "fw": fourier_w,
        "dftC": dftC, "dftP": dftP,
    }
    in_maps = []
    for r in range(8):
        b, qd = r // 4, r % 4
        xp = x_prompt[4 * r:4 * r + 4].reshape(1024, D)
        xs = x_sample[b, qd * 1024:(qd + 1) * 1024]
        m = dict(shared)
        m["xT"] = _fm(np.concatenate([xp, xs], axis=0))
        m["cckv"] = np.ascontiguousarray(cache_ckv[b].transpose(0, 2, 1))
        m["ckpe"] = np.ascontiguousarray(cache_kpe[b].transpose(0, 2, 1))
        m["condT"] = _vec(np.stack([c_ctx, c[b]]), 8).transpose(0, 2, 1).copy()
        m["wmod"] = np.ascontiguousarray(w_mod[:, :, qd * 2304:(qd + 1) * 2304])
        m["ropeT"] = rope[qd]
        m["dftS"] = dftS[qd]
        in_maps.append(m)

    res = run_bass_kernel_spmd(nc, in_maps, core_ids=list(range(8)))
    y_prompt = np.empty((32, 256, D), np.float32)
    y_sample = np.empty((2, 4096, D), np.float32)
    new_ckv = np.empty((32, 2, 256, 256), np.float32)
    new_kpe = np.empty((32, 2, 256, 64), np.float32)
    for r in range(8):
        b, qd = r // 4, r % 4
        o = res.results[r]
        y = np.asarray(o["yT"]).transpose(2, 1, 0).reshape(NT, D)
        y_prompt[4 * r:4 * r + 4] = y[:1024].reshape(4, 256, D)
        y_sample[b, qd * 1024:(qd + 1) * 1024] = y[1024:]
        ck = np.asarray(o["o_ckv"])
        kp = np.asarray(o["o_kpe"])
        new_ckv[4 * r:4 * r + 4] = ck.reshape(2, 256, 4, 256).transpose(2, 0, 3, 1)
        new_kpe[4 * r:4 * r + 4] = kp.reshape(2, 64, 4, 256).transpose(2, 0, 3, 1)
    return (y_prompt, y_sample, new_ckv, new_kpe)
```

```python
import math
from contextlib import ExitStack

import numpy as np
import ml_dtypes
import concourse.bass as bass
import concourse.mybir as mybir
from concourse.bass_utils import run_bass_kernel_spmd

F32 = mybir.dt.float32
BF16 = mybir.dt.bfloat16
AF = mybir.ActivationFunctionType
ALU = mybir.AluOpType

D = 1024
DFF = 2816
NCH = 8
TB = 512
NT = 2048
DEPTH = 4
EPS = 1e-6
ATTN_SCALE = 1.0 / math.sqrt(192.0)
GROUPS = [[0, 1, 2, 3], [4, 5, 6, 7]]
ENGINES = ("sp", "act", "dve", "pool", "pe")
SEM_ROT = 8000
SCR_BYTES = 102 * 1024
DBG_PHASES = "mf"


class Op:
    __slots__ = ("id", "eng", "fn", "deps", "is_dma", "slot", "sig", "inc", "needs_signal")

    def __init__(self, id, eng, fn, is_dma, slot, inc):
        self.id = id
        self.eng = eng
        self.fn = fn
        self.deps = set()
        self.is_dma = is_dma
        self.slot = slot
        self.sig = None
        self.inc = inc
        self.needs_signal = False


class Prog:
    def __init__(self):
        self.ops = []
        self.last_w = {}
        self.readers = {}
        self.eng_ops = {e: [] for e in ENGINES}
        self.fence_set = set()
        self.dma_since_fence = []
        self.fenced = {e: True for e in ENGINES}

    def fence(self):
        fs = set(self.dma_since_fence)
        for e in ENGINES:
            for o in reversed(self.eng_ops[e]):
                if not o.is_dma:
                    fs.add(o.id)
                    break
        self.fence_set = fs
        self.dma_since_fence = []
        self.fenced = {e: False for e in ENGINES}

    def op(self, eng, fn, reads=(), writes=(), dma=False, slot=None, inc=16, nofence=False):
        o = Op(len(self.ops), eng, fn, dma, slot, inc)
        deps = set()
        for k in reads:
            w = self.last_w.get(k)
            if w is not None:
                deps.add(w)
        for k in writes:
            w = self.last_w.get(k)
            if w is not None:
                deps.add(w)
            for r in self.readers.get(k, {}).values():
                if isinstance(r, list):
                    deps.update(r)
                else:
                    deps.add(r)
        if not self.fenced[eng]:
            deps |= self.fence_set
            self.fenced[eng] = True
        deps.discard(o.id)
        o.deps = deps
        for k in reads:
            rd = self.readers.setdefault(k, {})
            if dma:
                rd.setdefault("dma", []).append(o.id)
            else:
                rd[eng] = o.id
        for k in writes:
            self.last_w[k] = o.id
            self.readers[k] = {}
        if dma:
            if slot is None:
                o.slot = ("dma", writes[0])
            if not nofence:
                self.dma_since_fence.append(o.id)
        self.eng_ops[eng].append(o)
        self.ops.append(o)
        return o

    def emit(self, nc, final_waits=()):
        ops = self.ops
        for o in ops:
            for d in o.deps:
                p = ops[d]
                if p.is_dma:
                    p.needs_signal = True
                elif p.eng == "pe" and o.eng == "pe" and not o.is_dma:
                    continue
                else:
                    p.needs_signal = True
        for d in final_waits:
            d.needs_signal = True
        sem_keys = []
        comp_cnt = {e: 0 for e in ENGINES}
        slot_cnt = {}
        grp_total = {}
        for o in ops:
            if o.is_dma and isinstance(o.slot, tuple) and o.slot[0] == "grp":
                grp_total[o.slot] = grp_total.get(o.slot, 0) + o.inc
        for o in ops:
            if o.is_dma:
                c = slot_cnt.get(o.slot, 0) + o.inc
                slot_cnt[o.slot] = c
                o.sig = (o.slot, grp_total.get(o.slot, c))
                if o.slot not in slot_cnt or o.slot not in sem_keys:
                    sem_keys.append(o.slot)
            elif o.needs_signal:
                n = comp_cnt[o.eng]
                comp_cnt[o.eng] = n + 1
                key = ("c", o.eng, n // SEM_ROT)
                o.sig = (key, n % SEM_ROT + 1)
                if key not in sem_keys:
                    sem_keys.append(key)
        sem_keys = list(dict.fromkeys(sem_keys))
        self.n_sems = len(sem_keys)
        stack = ExitStack()
        sems = {}
        for i, k in enumerate(sem_keys):
            sems[k] = stack.enter_context(nc.semaphore("s%d" % i))

        def run(engname, e):
            waited = {}
            for o in self.eng_ops[engname]:
                need = {}
                for d in o.deps:
                    p = ops[d]
                    if p.sig is None:
                        continue
                    if (not p.is_dma) and p.eng == "pe" and engname == "pe" and not o.is_dma:
                        continue
                    k, c = p.sig
                    if need.get(k, 0) < c:
                        need[k] = c
                for k, c in need.items():
                    if waited.get(k, 0) >= c:
                        continue
                    e.wait_ge(sems[k], c)
                    waited[k] = c
                ins = o.fn(e)
                if o.sig is not None:
                    ins.then_inc(sems[o.sig[0]], o.inc if o.is_dma else 1)
            if engname == "sp":
                for d in final_waits:
                    k, c = d.sig
                    e.wait_ge(sems[k], c)

        with stack:
            with nc.Block() as block:
                @block.sync
                def _(e):
                    run("sp", e)

                @block.scalar
                def _(e):
                    run("act", e)

                @block.vector
                def _(e):
                    run("dve", e)

                @block.gpsimd
                def _(e):
                    run("pool", e)

                @block.tensor
                def _(e):
                    run("pe", e)


class Carve:
    def __init__(self, scr):
        self.scr = scr
        self.off = 0

    def reset(self, off=0):
        self.off = off

    def take(self, nelem, dtype, parts=128):
        nb = nelem * (4 if dtype == F32 else 2)
        nb = (nb + 63) // 64 * 64
        a = self.off // 2
        self.off += nb
        assert self.off <= SCR_BYTES, ("scratch overflow", self.off)
        v = self.scr[:, a:a + nb // 2]
        if dtype == F32:
            v = v.bitcast(F32)
        v = v[:, 0:nelem]
        return v


def build_program(depth=DEPTH):
    nc = bass.Bass("TRN2", target_bir_lowering=False)

    def din(name, shape, dt=F32):
        return nc.dram_tensor(name, list(shape), dt, kind="ExternalInput").ap()

    def dout(name, shape, dt=F32):
        return nc.dram_tensor(name, list(shape), dt, kind="ExternalOutput").ap()

    xT_d = din("xT", [128, NCH, NT])
    cckv_d = din("cckv", [2, 256, 512])
    ckpe_d = din("ckpe", [2, 64, 512])
    condT_d = din("condT", [128, NCH, 2])
    wmod_d = din("wmod", [4, D, 2304])
    bmodT_d = din("bmodT", [128, 4, 72])
    normgT_d = din("normgT", [128, 4, 3, 8])
    finalT_d = din("finalT", [128, 8])
    qnT_d = din("qnT", [128, 2, 4])
    kvnT_d = din("kvnT", [128, 2, 2])
    fbT_d = din("fbT", [128, 2, 8])
    wg_d = din("wg", [4, 2, D, DFF])
    wu_d = din("wu", [4, 2, D, DFF])
    wd_d = din("wd", [4, 2, DFF, D])
    wdq_d = din("wdq", [2, D, 512])
    wuq_d = din("wuq", [2, 512, 1536])
    wuqs_d = din("wuqs", [2, 512, 512])
    wdkv_d = din("wdkv", [2, D, 320])
    wdkvs_d = din("wdkvs", [2, D, 64])
    wukv_d = din("wukv", [2, 256, 2048])
    wo_d = din("wo", [2, D, D])
    fw_d = din("fw", [2, D, D])
    rope_d = din("ropeT", [2, 64, 1024])
    dftC_d = din("dftC", [256, 512])
    dftP_d = din("dftP", [2, 256, 256])
    dftS_d = din("dftS", [2, 4096, 1024])

    yT_d = dout("yT", [128, NCH, NT])
    ockv_d = dout("o_ckv", [2, 256, 1024])
    okpe_d = dout("o_kpe", [2, 64, 1024])

    mod_in = nc.dram_tensor("mod_in", [128, 144], F32).ap()
    mod_out = nc.dram_tensor("mod_out", [512, 144], F32).ap()
    x_in = [nc.dram_tensor("x_in%d" % j, [320, 1024], BF16).ap() for j in range(2)]
    x_out = [nc.dram_tensor("x_out%d" % j, [1280, 1024], BF16).ap() for j in range(2)]
    f_in = [[nc.dram_tensor("f_in%d_%d" % (j, q), [256, 2048], BF16).ap() for q in range(4)] for j in range(2)]
    f_out = [[nc.dram_tensor("f_out%d_%d" % (j, q), [1024, 2048], BF16).ap() for q in range(4)] for j in range(2)]

    dftS_bf = nc.dram_tensor("dftS_bf", [2, 4096, 1024], BF16).ap()
    DFT_KEYS = [("dftS_bf", a, q) for a in range(2) for q in range(4)]

    P = Prog()
    st = ExitStack()
    sb = lambda name, shape, dt: st.enter_context(nc.sbuf_tensor(name, list(shape), dt))
    xT = sb("xTs", [128, NCH, NT], F32)
    hT = sb("hTs", [128, NCH, NT], BF16)
    modS = sb("modS", [128, 4 * 72 * 2], F32)
    bmodT = sb("bmodTs", [128, 4, 72], F32)
    normgT = sb("normgTs", [128, 4, 3, 8], F32)
    finalT = sb("finalTs", [128, 8], F32)
    qnT = sb("qnTs", [128, 2, 4], F32)
    kvnT = sb("kvnTs", [128, 2, 2], F32)
    fbT = sb("fbTs", [128, 2, 8], F32)
    condT = sb("condTs", [128, NCH, 2], F32)
    scT = sb("scTs", [128, NCH, 2], F32)
    scB = sb("scBs", [128, NCH, 2], BF16)
    onesB = sb("onesB", [128, 128], BF16)
    onesF = sb("onesF", [128, 128], F32)
    epsT = sb("epsT", [128, 1], F32)
    scr = sb("scr", [128, SCR_BYTES // 2], BF16)
    ps = [st.enter_context(nc.psum_tensor("ps%d" % i, [128, 512], F32)) for i in range(8)]
    PK = [("ps", i) for i in range(8)]
    cv = Carve(scr)

    mod5 = modS[:].rearrange("p (l k c n) -> p l k c n", l=4, k=9, c=8)
    mod_lrx = modS[:].rearrange("p (l r x) -> p l r x", l=4, r=4)
    mod_lcn = modS[:].rearrange("p (l c n) -> p l c n", l=4, c=72)

    outs = []

    def mcol(l, k, c, cond):
        return mod5[:, l, k, c, cond:cond + 1]

    def cond_of(t):
        return 0 if t < 2 else 1

    def tsl(t):
        return slice(t * TB, (t + 1) * TB)

    cv.reset(0)
    sq = cv.take(2 * TB, BF16).rearrange("p (a b) -> p a b", a=2)
    rs = cv.take(2 * TB, F32).rearrange("p (a b) -> p a b", a=2)
    tmpf = cv.take(2 * TB, F32).rearrange("p (a b) -> p a b", a=2)
    COMMON = cv.off
    cnt = {"sq": 0, "tmpf": 0}

    def rot(name, n=2):
        v = cnt.get(name, 0)
        cnt[name] = v + 1
        return v % n

    P.op("sp", lambda e: e.dma_start(out=condT[:], in_=condT_d), writes=["condT"], dma=True, slot=("grp", "setup"))
    for (tl, td, nm) in ((bmodT, bmodT_d, "bmodT"), (normgT, normgT_d, "normgT"), (finalT, finalT_d, "finalT"),
                         (qnT, qnT_d, "qnT"), (kvnT, kvnT_d, "kvnT"), (fbT, fbT_d, "fbT")):
        P.op("sp", lambda e, tl=tl, td=td: e.dma_start(out=tl[:], in_=td), writes=[nm], dma=True, slot=("grp", "setup"))
    P.op("dve", lambda e: e.memset(onesB[:], 1.0), writes=["ones"])
    P.op("dve", lambda e: e.memset(onesF[:], 1.0), writes=["onesF"])
    P.op("dve", lambda e: e.memset(epsT[:], EPS), writes=["eps"])
    P.op("act", lambda e: e.activation(out=scB[:], in_=condT[:], func=AF.Silu), reads=["condT"], writes=["scT"])

    for c in range(NCH):
        P.op("sp", lambda e, c=c: e.dma_start(out=xT[:, c, :], in_=xT_d[:, c, :]),
             writes=[("xT", c, t) for t in range(4)], dma=True, slot=("grp", "xT"))

    cv.reset(COMMON)
    wm = [cv.take(NCH * 1152, BF16).rearrange("p (c f) -> p c f", c=NCH) for _ in range(4)]
    modin = cv.take(144, F32)
    for l in range(4):
        for hf in range(2):
            b = (l * 2 + hf) % 4
            src = wmod_d[l, :, hf * 1152:(hf + 1) * 1152].rearrange("(c p) f -> p c f", p=128)
            P.op("pool", lambda e, b=b, src=src: e.dma_start(out=wm[b], in_=src), writes=[("wm", b)], dma=True)
            for i in range(9):
                col = (hf * 9 + i) * 2
                for c in range(NCH):
                    P.op("pe", lambda e, b=b, i=i, c=c, col=col: e.matmul(
                        ps[0][:, col:col + 2], lhsT=wm[b][:, c, i * 128:(i + 1) * 128], rhs=scB[:, c, :],
                        start=(c == 0), stop=(c == NCH - 1)),
                        reads=[("wm", b), "scT"], writes=[PK[0]])
        P.op("dve", lambda e, l=l: e.tensor_copy(out=modin[:, l * 36:(l + 1) * 36], in_=ps[0][:, 0:36]),
             reads=[PK[0]], writes=["modin"])
    P.op("sp", lambda e: e.dma_start(out=mod_in, in_=modin), reads=["modin"], writes=["mod_in"], dma=True)
    P.op("pool", lambda e: e.collective_compute("AllGather", ALU.bypass, replica_groups=GROUPS,
                                                ins=[mod_in], outs=[mod_out]),
         reads=["mod_in"], writes=["mod_out"], dma=True, inc=1)
    for l in range(4):
        src = mod_out[:, l * 36:(l + 1) * 36].rearrange("(r p) x -> p r x", p=128)
        P.op("sp", lambda e, l=l, src=src: e.dma_start(out=mod_lrx[:, l], in_=src),
             reads=["mod_out"], writes=["modS"], dma=True, slot=("dma", "modS", l))
    for n in range(2):
        P.op("dve", lambda e, n=n: e.tensor_tensor(out=mod_lcn[:, :, :, n], in0=mod_lcn[:, :, :, n],
                                                   in1=bmodT[:], op=ALU.add),
             reads=["modS", "bmodT"], writes=["modS"])
    for i in range(3):
        for n in range(2):
            P.op("dve", lambda e, i=i, n=n: e.scalar_tensor_tensor(
                out=mod5[:, :, 3 * i + 1, :, n], in0=mod5[:, :, 3 * i + 1, :, n], scalar=1.0,
                in1=normgT[:, :, i, :], op0=ALU.add, op1=ALU.mult),
                reads=["modS", "normgT"], writes=["modS"])
            if i != 1:
                P.op("dve", lambda e, i=i, n=n: e.tensor_scalar(
                    out=mod5[:, :, 3 * i + 2, :, n], in0=mod5[:, :, 3 * i + 2, :, n], scalar1=0.5,
                    scalar2=None, op0=ALU.mult),
                    reads=["modS"], writes=["modS"])

    def rstd_block(src_keys, nchunks, get_src, inv_n, bank, extra_reads=()):
        for c in range(nchunks):
            s = rot("sq")
            src, rk = get_src(c)
            P.op("act", lambda e, s=s, src=src: e.activation(out=sq[:, s, :], in_=src, func=AF.Square),
                 reads=[rk], writes=[("sq", s)])
            P.op("pe", lambda e, s=s, c=c: e.matmul(ps[bank][:], lhsT=onesB[:], rhs=sq[:, s, :],
                                                    start=(c == 0), stop=(c == nchunks - 1)),
                 reads=[("sq", s), "ones"], writes=[PK[bank]])
        P.op("act", lambda e: e.activation(out=rs[:, 0, :], in_=ps[bank][:], func=AF.Ln,
                                           bias=epsT[:, 0:1], scale=inv_n),
             reads=[PK[bank], "eps"], writes=[("rs", 0)])
        P.op("act", lambda e: e.activation(out=rs[:, 1, :], in_=rs[:, 0, :], func=AF.Exp, scale=-0.5),
             reads=[("rs", 0)], writes=[("rs", 1)])

    def norm_phase(l, i, final=False):
        for t in range(4):
            n = cond_of(t)
            rstd_block(None, NCH, lambda c, t=t: (xT[:, c, tsl(t)], ("xT", c, t)), 1.0 / D, 7)
            for c in range(NCH):
                s = rot("tmpf")
                g = finalT[:, c:c + 1] if final else mcol(l, 3 * i + 1, c, n)
                P.op("dve", lambda e, s=s, c=c, t=t, g=g: e.scalar_tensor_tensor(
                    out=tmpf[:, s, :], in0=xT[:, c, tsl(t)], scalar=g, in1=rs[:, 1, :],
                    op0=ALU.mult, op1=ALU.mult),
                    reads=[("xT", c, t), ("rs", 1), "modS", "finalT"], writes=[("tmpf", s)])
                if final:
                    outs.append(P.op("sp", lambda e, s=s, c=c, t=t: e.dma_start(out=yT_d[:, c, tsl(t)], in_=tmpf[:, s, :]),
                                     reads=[("tmpf", s)], writes=[("yT", c, t)], dma=True, slot=("out", "tmpf", s)))
                else:
                    P.op("act", lambda e, s=s, c=c, t=t, l=l, i=i, n=n: e.activation(
                        out=hT[:, c, tsl(t)], in_=tmpf[:, s, :], func=AF.Identity,
                        bias=mcol(l, 3 * i, c, n), scale=1.0),
                        reads=[("tmpf", s), "modS"], writes=[("hT", c, t)])

    pending = []

    def norm_items(l, i, t, final=False):
        items = []
        n = cond_of(t)
        bank = 7

        def sq_item(c):
            def f():
                s = rot("sq")
                P.op("act", lambda e: e.activation(out=sq[:, s, :], in_=xT[:, c, tsl(t)], func=AF.Square),
                     reads=[("xT", c, t)], writes=[("sq", s)])
                P.op("pe", lambda e: e.matmul(ps[bank][:], lhsT=onesB[:], rhs=sq[:, s, :],
                                              start=(c == 0), stop=(c == NCH - 1)),
                     reads=[("sq", s), "ones"], writes=[PK[bank]])
            return f

        def rs_item():
            P.op("act", lambda e: e.activation(out=rs[:, 0, :], in_=ps[bank][:], func=AF.Ln,
                                               bias=epsT[:, 0:1], scale=1.0 / D),
                 reads=[PK[bank], "eps"], writes=[("rs", 0)])
            P.op("act", lambda e: e.activation(out=rs[:, 1, :], in_=rs[:, 0, :], func=AF.Exp, scale=-0.5),
                 reads=[("rs", 0)], writes=[("rs", 1)])

        def out_item(c):
            def f():
                s = rot("tmpf")
                g = finalT[:, c:c + 1] if final else mcol(l, 3 * i + 1, c, n)
                P.op("dve", lambda e: e.scalar_tensor_tensor(
                    out=tmpf[:, s, :], in0=xT[:, c, tsl(t)], scalar=g, in1=rs[:, 1, :],
                    op0=ALU.mult, op1=ALU.mult),
                    reads=[("xT", c, t), ("rs", 1), "modS", "finalT"], writes=[("tmpf", s)])
                if final:
                    outs.append(P.op("sp", lambda e: e.dma_start(out=yT_d[:, c, tsl(t)], in_=tmpf[:, s, :]),
                                     reads=[("tmpf", s)], writes=[("yT", c, t)], dma=True, slot=("out", "tmpf", s)))
                else:
                    P.op("act", lambda e: e.activation(
                        out=hT[:, c, tsl(t)], in_=tmpf[:, s, :], func=AF.Identity,
                        bias=mcol(l, 3 * i, c, n), scale=1.0),
                        reads=[("tmpf", s), "modS"], writes=[("hT", c, t)])
            return f

        for c in range(NCH):
            items.append((t, sq_item(c)))
        items.append((t, rs_item))
        for c in range(NCH):
            items.append((t, out_item(c)))
        return items

    def pop_pending(n):
        for _ in range(n):
            if pending:
                pending.pop(0)[1]()

    def flush_pending(upto_t=None):
        while pending and (upto_t is None or any(tg <= upto_t for tg, _ in pending)):
            pending.pop(0)[1]()

    def resid_update(bank, m, t, gate_ap, tok=None):
        sl = tsl(t) if tok is None else tok
        P.op("dve", lambda e: e.scalar_tensor_tensor(
            out=xT[:, m, sl], in0=ps[bank][:, 0:(sl.stop - sl.start)], scalar=gate_ap, in1=xT[:, m, sl],
            op0=ALU.mult, op1=ALU.add),
            reads=[PK[bank], ("xT", m, t), "modS"], writes=[("xT", m, t)])

    def ffn_phase(l, i, pre_normed=False, next_norm=None, skip_fence=False, pre_work=None, carry=False,
                  tail_order=(0, 1, 2, 3), tblocks=(0, 1, 2, 3), chunk0_preloaded=False):
        if not skip_fence:
            P.fence()
        cv.reset(COMMON)
        NB = 3
        wgb, wub, wdb = [], [], []
        for _ in range(NB):
            wgb.append(cv.take(NCH * 512, BF16).rearrange("p (c f) -> p c f", c=NCH))
            wub.append(cv.take(NCH * 512, BF16).rearrange("p (c f) -> p c f", c=NCH))
            wdb.append(cv.take(4 * D, BF16).rearrange("p (j d) -> p j d", j=4))
        actb = [cv.take(4 * TB, BF16).rearrange("p (j t) -> p j t", j=4) for _ in range(2)]
        slb = [cv.take(TB, BF16) for _ in range(2)]
        sizes = [4, 4, 4, 4, 3, 3]
        f0s = [0, 4, 8, 12, 16, 19]
        kidx = 2 * i
        sub = 0 if i == 0 else 2

        def load_chunk(k):
            s = k % NB
            G = sizes[k]
            f0 = f0s[k] * 128
            srcg = wg_d[l, i, :, f0:f0 + G * 128].rearrange("(c p) f -> p c f", p=128)
            srcu = wu_d[l, i, :, f0:f0 + G * 128].rearrange("(c p) f -> p c f", p=128)
            srcd = wd_d[l, i, f0:f0 + G * 128, :].rearrange("(j p) d -> p j d", p=128)
            P.op("pool", lambda e: e.dma_start(out=wgb[s][:, :, 0:G * 128], in_=srcg), writes=[("wg", s)], dma=True)
            P.op("pool", lambda e: e.dma_start(out=wub[s][:, :, 0:G * 128], in_=srcu), writes=[("wu", s)], dma=True)
            P.op("pool", lambda e: e.dma_start(out=wdb[s][:, 0:G, :], in_=srcd), writes=[("wd", s)], dma=True)

        gu_cnt = [0]

        def gate_up(k, t):
            s = k % NB
            G = sizes[k]
            ab = (gu_cnt[0]) % 2
            gu_cnt[0] += 1
            for j in range(G):
                bg = j % 2
                bu = 2 + j % 2
                for c in range(NCH):
                    P.op("pe", lambda e, j=j, c=c, bg=bg: e.matmul(
                        ps[bg][:], lhsT=wgb[s][:, c, j * 128:(j + 1) * 128], rhs=hT[:, c, tsl(t)],
                        start=(c == 0), stop=(c == NCH - 1)),
                        reads=[("wg", s), ("hT", c, t)], writes=[PK[bg]])
                for c in range(NCH):
                    P.op("pe", lambda e, j=j, c=c, bu=bu: e.matmul(
                        ps[bu][:], lhsT=wub[s][:, c, j * 128:(j + 1) * 128], rhs=hT[:, c, tsl(t)],
                        start=(c == 0), stop=(c == NCH - 1)),
                        reads=[("wu", s), ("hT", c, t)], writes=[PK[bu]])
                sl_i = rot("slb")
                P.op("act", lambda e, bg=bg, sl_i=sl_i: e.activation(out=slb[sl_i], in_=ps[bg][:], func=AF.Silu),
                     reads=[PK[bg]], writes=[("slb", sl_i)])
                P.op("dve", lambda e, bu=bu, sl_i=sl_i, j=j, ab=ab: e.tensor_tensor(
                    out=actb[ab][:, j, :], in0=ps[bu][:], in1=slb[sl_i], op=ALU.mult),
                    reads=[PK[bu], ("slb", sl_i)], writes=[("act", ab, j)])
                pop_pending(2)
            return ab

        def down(k, t, ab):
            s = k % NB
            G = sizes[k]
            n = cond_of(t)
            for m in range(NCH):
                by = 4 + m % 3
                for j in range(G):
                    P.op("pe", lambda e, j=j, m=m, by=by: e.matmul(
                        ps[by][:], lhsT=wdb[s][:, j, m * 128:(m + 1) * 128], rhs=actb[ab][:, j, :],
                        start=(j == 0), stop=(j == G - 1)),
                        reads=[("wd", s), ("act", ab, j)], writes=[PK[by]])
                resid_update(by, m, t, mcol(l, 3 * sub + 2, m, n))
                pop_pending(1)

        if pre_work is not None:
            tiles = [(wgb[2][:, :, i_ * 128:(i_ + 1) * 128], ("wg", 2)) for i_ in range(4)] + \
                    [(wub[2][:, :, i_ * 128:(i_ + 1) * 128], ("wu", 2)) for i_ in range(4)]
            pre_work(lambda: (None if chunk0_preloaded else load_chunk(0), load_chunk(1)), tiles)
        if not pre_normed:
            for t in tblocks:
                pending.extend(norm_items(l, sub, t))
            flush_pending(tblocks[0])
        nk_ = len(sizes)
        work = [(k, t) for k in range(nk_ - 1) for t in tblocks]
        if pre_work is None:
            load_chunk(0)
            load_chunk(1)
        prev = None
        for idx, (k, t) in enumerate(work):
            if k == 0:
                flush_pending(t)
            ab = gate_up(k, t)
            if prev is not None:
                down(*prev)
            if t == tblocks[0] and k >= 1 and k + 1 < nk_:
                load_chunk(k + 1)
                if l == 0 and i == 1:
                    for a in range(2):
                        q = k - 1
                        P.op("pool", lambda e, a=a, q=q: e.dma_start(out=dftS_bf[a, q * 1024:(q + 1) * 1024, :],
                                                                     in_=dftS_d[a, q * 1024:(q + 1) * 1024, :]),
                             writes=[("dftS_bf", a, q)], dma=True, slot="dftcast")
            prev = (k, t, ab)
        k = nk_ - 1
        for t in [t_ for t_ in tail_order if t_ in tblocks]:
            ab = gate_up(k, t)
            if prev is not None:
                down(*prev)
                prev = None
            down(k, t, ab)
            if next_norm is not None:
                pending.extend(norm_items(next_norm[0], next_norm[1], t, final=next_norm[2]))
        if not carry:
            flush_pending()

    def mla_phase(l, pre_normed=False):
        jl = l // 2
        P.fence()
        flush_pending()
        cv.reset(COMMON)
        cqT_s = cv.take(4 * 1024, BF16).rearrange("p (c t) -> p c t", c=4)
        ckvT_s = cv.take(2 * 1024, BF16).rearrange("p (c t) -> p c t", c=2)
        kpeT_s = cv.take(1024, BF16)
        ropeT = cv.take(2 * 1024, F32).rearrange("p (a t) -> p a t", a=2)
        SHARED_S = cv.off
        cqT_p = cv.take(4 * 1024, BF16).rearrange("p (c t) -> p c t", c=4)
        ckvT_p = cv.take(2 * 1024, BF16).rearrange("p (c t) -> p c t", c=2)
        kpeT_p = cv.take(1024, BF16)
        SHARED_P = cv.off

        def lsl(t):
            return slice((t % 2) * TB, (t % 2 + 1) * TB)

        def cq(kc, t):
            return (cqT_p if t < 2 else cqT_s)[:, kc, lsl(t)]

        def ckv(mc, t):
            return (ckvT_p if t < 2 else ckvT_s)[:, mc, lsl(t)]

        def kpe(t):
            return (kpeT_p if t < 2 else kpeT_s)[0:64, lsl(t)]

        wdq = cv.take(NCH * 512, BF16).rearrange("p (c f) -> p c f", c=NCH)
        wdkv = cv.take(NCH * 384, BF16).rearrange("p (c f) -> p c f", c=NCH)
        cqraw = cv.take(4 * TB, F32).rearrange("p (c t) -> p c t", c=4)
        ckvf = [cv.take(2 * TB, F32).rearrange("p (c t) -> p c t", c=2) for _ in range(2)]
        kpef = [cv.take(TB, F32) for _ in range(2)]
        if not pre_normed:
            norm_phase(l, 1)
        P.op("pool", lambda e: e.dma_start(out=wdq, in_=wdq_d[jl].rearrange("(c p) f -> p c f", p=128)),
             writes=["wdq"], dma=True)
        P.op("pool", lambda e: e.dma_start(out=wdkv[:, :, 0:320], in_=wdkv_d[jl].rearrange("(c p) f -> p c f", p=128)),
             writes=["wdkv0"], dma=True)
        P.op("pool", lambda e: e.dma_start(out=wdkv[:, :, 320:384], in_=wdkvs_d[jl].rearrange("(c p) f -> p c f", p=128)),
             writes=["wdkv1"], dma=True)
        for a in range(2):
            P.op("sp", lambda e, a=a: e.dma_start(out=ropeT[0:64, a, :], in_=rope_d[a]), writes=[("rope", a)], dma=True)

        def stage_a(t):
            for mc in range(4):
                for c in range(NCH):
                    P.op("pe", lambda e, mc=mc, c=c: e.matmul(
                        ps[mc][:], lhsT=wdq[:, c, mc * 128:(mc + 1) * 128], rhs=hT[:, c, tsl(t)],
                        start=(c == 0), stop=(c == NCH - 1)),
                        reads=["wdq", ("hT", c, t)], writes=[PK[mc]])
                P.op("act", lambda e, mc=mc: e.activation(out=cqraw[:, mc, :], in_=ps[mc][:], func=AF.Identity),
                     reads=[PK[mc]], writes=[("cqraw", mc)])
            rstd_block(None, 4, lambda c: (cqraw[:, c, :], ("cqraw", c)), 1.0 / 512, 7)
            for mc in range(4):
                P.op("dve", lambda e, mc=mc: e.scalar_tensor_tensor(
                    out=cq(mc, t), in0=cqraw[:, mc, :], scalar=qnT[:, jl, mc:mc + 1], in1=rs[:, 1, :],
                    op0=ALU.mult, op1=ALU.mult),
                    reads=[("cqraw", mc), ("rs", 1), "qnT"], writes=[("cqT", mc, t)])
            fb = t % 2
            for mc in range(2):
                for c in range(NCH):
                    P.op("pe", lambda e, mc=mc, c=c: e.matmul(
                        ps[4 + mc][:], lhsT=wdkv[:, c, mc * 128:(mc + 1) * 128], rhs=hT[:, c, tsl(t)],
                        start=(c == 0), stop=(c == NCH - 1)),
                        reads=["wdkv0", ("hT", c, t)], writes=[PK[4 + mc]])
                P.op("act", lambda e, mc=mc: e.activation(out=cqraw[:, mc, :], in_=ps[4 + mc][:], func=AF.Identity),
                     reads=[PK[4 + mc]], writes=[("cqraw", mc)])
            rstd_block(None, 2, lambda c: (cqraw[:, c, :], ("cqraw", c)), 1.0 / 256, 7)
            for mc in range(2):
                if t < 2:
                    P.op("dve", lambda e, mc=mc: e.scalar_tensor_tensor(
                        out=ckvf[fb][:, mc, :], in0=cqraw[:, mc, :], scalar=kvnT[:, jl, mc:mc + 1], in1=rs[:, 1, :],
                        op0=ALU.mult, op1=ALU.mult),
                        reads=[("cqraw", mc), ("rs", 1), "kvnT"], writes=[("ckvf", fb, mc)])
                    P.op("act", lambda e, mc=mc: e.activation(out=ckv(mc, t), in_=ckvf[fb][:, mc, :], func=AF.Identity),
                         reads=[("ckvf", fb, mc)], writes=[("ckvT", mc, t)])
                else:
                    P.op("dve", lambda e, mc=mc: e.scalar_tensor_tensor(
                        out=ckv(mc, t), in0=cqraw[:, mc, :], scalar=kvnT[:, jl, mc:mc + 1], in1=rs[:, 1, :],
                        op0=ALU.mult, op1=ALU.mult),
                        reads=[("cqraw", mc), ("rs", 1), "kvnT"], writes=[("ckvT", mc, t)])
            if t < 2:
                outs.append(P.op("sp", lambda e: e.dma_start(
                    out=ockv_d[jl][:, tsl(t)].rearrange("(c p) t -> p c t", p=128), in_=ckvf[fb]),
                    reads=[("ckvf", fb, 0), ("ckvf", fb, 1)], writes=[("ockv", jl, t)], dma=True,
                    slot=("out", "ckvf", fb)))
            for c in range(NCH):
                P.op("pe", lambda e, c=c: e.matmul(
                    ps[6][0:64, :], lhsT=wdkv[:, c, 256:320], rhs=hT[:, c, tsl(t)],
                    start=(c == 0), stop=(c == NCH - 1)),
                    reads=["wdkv0", ("hT", c, t)], writes=[PK[6]])
            if t < 2:
                P.op("act", lambda e: e.activation(out=kpef[fb][0:64, :], in_=ps[6][0:64, :], func=AF.Identity),
                     reads=[PK[6]], writes=[("kpef", fb)])
                P.op("dve", lambda e: e.tensor_copy(out=kpe(t), in_=kpef[fb][0:64, :]),
                     reads=[("kpef", fb)], writes=[("kpeT", t)])
                outs.append(P.op("sp", lambda e: e.dma_start(out=okpe_d[jl][:, tsl(t)], in_=kpef[fb][0:64, :]),
                                 reads=[("kpef", fb)], writes=[("okpe", jl, t)], dma=True,
                                 slot=("out", "kpef", fb)))
            else:
                for c in range(NCH):
                    P.op("pe", lambda e, c=c: e.matmul(
                        ps[3][0:64, :], lhsT=wdkv[:, c, 320:384], rhs=hT[:, c, tsl(t)],
                        start=(c == 0), stop=(c == NCH - 1)),
                        reads=["wdkv1", ("hT", c, t)], writes=[PK[3]])
                tok = lsl(t)
                P.op("dve", lambda e: e.tensor_tensor(out=tmpf[0:64, 0, :], in0=ps[6][0:64, :],
                                                      in1=ropeT[0:64, 0, tok], op=ALU.mult),
                     reads=[PK[6], ("rope", 0)], writes=[("tmpf", 0)])
                P.op("dve", lambda e: e.tensor_tensor(out=tmpf[0:64, 1, :], in0=ps[3][0:64, :],
                                                      in1=ropeT[0:64, 1, tok], op=ALU.mult),
                     reads=[PK[3], ("rope", 1)], writes=[("tmpf", 1)])
                P.op("dve", lambda e: e.tensor_tensor(out=kpe(t), in0=tmpf[0:64, 0, :],
                                                      in1=tmpf[0:64, 1, :], op=ALU.add),
                     reads=[("tmpf", 0), ("tmpf", 1)], writes=[("kpeT", t)])

        for t in (2, 3, 0, 1):
            stage_a(t)
            if t == 3:
                P.op("sp", lambda e: e.dma_start(out=x_in[jl][0:256, :].rearrange("(c p) t -> p c t", p=128),
                                                 in_=ckvT_s),
                     reads=[("ckvT", 0, 2), ("ckvT", 0, 3), ("ckvT", 1, 2), ("ckvT", 1, 3)],
                     writes=[("x_in", jl, 0)], dma=True, slot=("grp", "x_in", jl))
                P.op("sp", lambda e: e.dma_start(out=x_in[jl][256:320, :], in_=kpeT_s[0:64, :]),
                     reads=[("kpeT", 2), ("kpeT", 3)], writes=[("x_in", jl, 1)], dma=True, slot=("grp", "x_in", jl))
                P.op("pool", lambda e: e.collective_compute("AllGather", ALU.bypass, replica_groups=GROUPS,
                                                            ins=[x_in[jl]], outs=[x_out[jl]]),
                     reads=[("x_in", jl, 0), ("x_in", jl, 1)], writes=[("x_out", jl)], dma=True, inc=1,
                     nofence=True)

        def attention(stream):
            P.fence()
            cv.reset(SHARED_P if stream == 0 else SHARED_S)
            nk = 1024 if stream == 0 else 4608
            nkt = nk // 128
            tok0 = 0 if stream == 0 else 1024
            if stream == 1:
                ckv_all = cv.take(2 * nk, BF16).rearrange("p (c t) -> p c t", c=2)
                kpe_all = cv.take(nk, BF16)
            kn = cv.take(nk, BF16)
            Vh = cv.take(nk, BF16).rearrange("p (k d) -> p k d", d=128)
            qn = cv.take(1024, BF16)
            qr = cv.take(1024, BF16)
            Pt = [cv.take(TB, BF16) for _ in range(4)]
            dacc = [cv.take(TB, F32) for _ in range(2)]
            hw_q = [cv.take(4 * 256, BF16).rearrange("p (c f) -> p c f", c=4) for _ in range(2)]
            hw_kv = [cv.take(2 * 256, BF16).rearrange("p (c f) -> p c f", c=2) for _ in range(2)]
            wob = [cv.take(8 * 128, BF16).rearrange("p (h d) -> p h d", h=8) for _ in range(2)]
            rden = cv.take(TB, F32)
            P.op("dve", lambda e: e.memset(qr[64:128, :], 0.0), writes=["qr_pad"])
            if stream == 1:
                P.op("dve", lambda e: e.memset(kpe_all[64:128, :], 0.0), writes=["kpe_pad"])
            else:
                P.op("dve", lambda e: e.memset(kpeT_p[64:128, :], 0.0), writes=["kpe_pad"])
            if stream == 1:
                P.op("pool", lambda e: e.dma_start(out=ckv_all[:, :, 0:512],
                                                   in_=cckv_d[jl].rearrange("(c p) t -> p c t", p=128)),
                     writes=[("ckv_all", 0)], dma=True, slot=("grp", "kvall_p", jl))
                P.op("pool", lambda e: e.dma_start(out=kpe_all[0:64, 0:512], in_=ckpe_d[jl]),
                     writes=[("kpe_all", 0)], dma=True, slot=("grp", "kvall_p", jl))
                for r in range(4):
                    P.op("sp", lambda e, r=r: e.dma_start(
                        out=ckv_all[:, :, 512 + r * 1024:512 + (r + 1) * 1024],
                        in_=x_out[jl][r * 320:r * 320 + 256, :].rearrange("(c p) t -> p c t", p=128)),
                        reads=[("x_out", jl)], writes=[("ckv_all", 1 + r)], dma=True, slot=("grp", "kvall", jl))
                    P.op("sp", lambda e, r=r: e.dma_start(
                        out=kpe_all[0:64, 512 + r * 1024:512 + (r + 1) * 1024],
                        in_=x_out[jl][r * 320 + 256:(r + 1) * 320, :]),
                        reads=[("x_out", jl)], writes=[("kpe_all", 1 + r)], dma=True, slot=("grp", "kvall", jl))
                ckvsrc, kpesrc = ckv_all, kpe_all
                ckv_keys = [("ckv_all", r) for r in range(5)]
                kpe_keys = [("kpe_all", r) for r in range(5)]
            else:
                ckvsrc, kpesrc = ckvT_p, kpeT_p
                ckv_keys = [("ckvT", mc, t) for mc in range(2) for t in range(2)]
                kpe_keys = [("kpeT", 0), ("kpeT", 1)]

            gcnt = {"tile": 0}

            def head_tiles(h, qblocks):
                flat = [(qi, ki) for qi, (q0, qn_, kts) in enumerate(qblocks) for ki in range(len(kts))]
                info = {}

                def s_mm(qi, ki):
                    q0, qn_, kts = qblocks[qi]
                    kt = kts[ki]
                    sbk = gcnt["tile"] % 3
                    gcnt["tile"] += 1
                    da = dacc[qi % 2]
                    P.op("pe", lambda e: e.matmul(
                        ps[sbk][:, 0:qn_], lhsT=kn[:, kt * 128:(kt + 1) * 128], rhs=qn[:, q0:q0 + qn_],
                        start=True, stop=False),
                        reads=[("kn", kt // 4), ("qn", q0 // TB)], writes=[PK[sbk]])
                    P.op("pe", lambda e: e.matmul(
                        ps[sbk][:, 0:qn_], lhsT=kpesrc[:, kt * 128:(kt + 1) * 128], rhs=qr[:, q0:q0 + qn_],
                        start=False, stop=True),
                        reads=kpe_keys + [("qr", q0 // TB), "qr_pad", "kpe_pad"], writes=[PK[sbk]])
                    pi = rot("Pt", 4)
                    info[(qi, ki)] = pi
                    P.op("act", lambda e: e.activation(out=Pt[pi][:, 0:qn_], in_=ps[sbk][:, 0:qn_], func=AF.Exp),
                         reads=[PK[sbk]], writes=[("Pt", pi)])
                    if ki == 0:
                        P.op("dve", lambda e: e.tensor_copy(out=da[:, 0:qn_], in_=Pt[pi][:, 0:qn_]),
                             reads=[("Pt", pi)], writes=[("dacc", qi % 2)])
                    else:
                        P.op("dve", lambda e: e.tensor_tensor(out=da[:, 0:qn_], in0=da[:, 0:qn_], in1=Pt[pi][:, 0:qn_],
                                                              op=ALU.add),
                             reads=[("Pt", pi), ("dacc", qi % 2)], writes=[("dacc", qi % 2)])

                def pv_mm(qi, ki):
                    q0, qn_, kts = qblocks[qi]
                    kt = kts[ki]
                    pi = info[(qi, ki)]
                    ob = 4 + (qi % 2)
                    nkt_ = len(kts)
                    P.op("pe", lambda e: e.matmul(
                        ps[ob][:, 0:qn_], lhsT=Vh[:, kt, :], rhs=Pt[pi][:, 0:qn_],
                        start=(ki == 0), stop=(ki == nkt_ - 1)),
                        reads=[("Vh", kt // 4), ("Pt", pi)], writes=[PK[ob]])
                    if ki == nkt_ - 1:
                        db = 6 + (qi % 2)
                        da = dacc[qi % 2]
                        tq = (tok0 + q0) // TB
                        P.op("pe", lambda e: e.matmul(ps[db][:, 0:qn_], lhsT=onesF[:], rhs=da[:, 0:qn_],
                                                      start=True, stop=True),
                             reads=["onesF", ("dacc", qi % 2)], writes=[PK[db]])
                        P.op("act", lambda e: e.activation(out=rden[:, 0:qn_], in_=ps[db][:, 0:qn_], func=AF.Ln),
                             reads=[PK[db]], writes=["rden"])
                        P.op("act", lambda e: e.activation(out=rden[:, 0:qn_], in_=rden[:, 0:qn_], func=AF.Exp, scale=-1.0),
                             reads=["rden"], writes=["rden"])
                        P.op("dve", lambda e: e.tensor_tensor(out=hT[:, h, tok0 + q0:tok0 + q0 + qn_], in0=ps[ob][:, 0:qn_],
                                                              in1=rden[:, 0:qn_], op=ALU.mult),
                             reads=[PK[ob], "rden"], writes=[("hT", h, tq)])

                depth_ = 2
                for idx in range(len(flat) + depth_):
                    if idx < len(flat):
                        s_mm(*flat[idx])
                    if idx >= depth_:
                        pv_mm(*flat[idx - depth_])

            def head(h):
                hb = h % 2
                P.op("pool", lambda e: e.dma_start(
                    out=hw_q[hb][:, :, 0:192], in_=wuq_d[jl][:, h * 192:(h + 1) * 192].rearrange("(c p) f -> p c f", p=128)),
                    writes=[("hwq0", hb)], dma=True)
                P.op("pool", lambda e: e.dma_start(
                    out=hw_q[hb][:, :, 192:256], in_=wuqs_d[jl][:, h * 64:(h + 1) * 64].rearrange("(c p) f -> p c f", p=128)),
                    writes=[("hwq1", hb)], dma=True)
                P.op("pool", lambda e: e.dma_start(
                    out=hw_kv[hb], in_=wukv_d[jl][:, h * 256:(h + 1) * 256].rearrange("(c p) f -> p c f", p=128)),
                    writes=[("hwkv", hb)], dma=True)
                for tb in range(2):
                    t = (tok0 // TB) + tb
                    for kc in range(4):
                        P.op("pe", lambda e, kc=kc, t=t: e.matmul(
                            ps[0][:], lhsT=hw_q[hb][:, kc, 0:128], rhs=cq(kc, t),
                            start=(kc == 0), stop=(kc == 3)),
                            reads=[("hwq0", hb), ("cqT", kc, t)], writes=[PK[0]])
                    P.op("act", lambda e, tb=tb: e.activation(out=qn[:, tb * TB:(tb + 1) * TB], in_=ps[0][:],
                                                              func=AF.Identity, scale=ATTN_SCALE),
                         reads=[PK[0]], writes=[("qn", tb)])
                    for kc in range(4):
                        P.op("pe", lambda e, kc=kc, t=t: e.matmul(
                            ps[1][0:64, :], lhsT=hw_q[hb][:, kc, 128:192], rhs=cq(kc, t),
                            start=(kc == 0), stop=(kc == 3)),
                            reads=[("hwq0", hb), ("cqT", kc, t)], writes=[PK[1]])
                    if stream == 0:
                        P.op("act", lambda e, tb=tb: e.activation(out=qr[0:64, tb * TB:(tb + 1) * TB], in_=ps[1][0:64, :],
                                                                  func=AF.Identity, scale=ATTN_SCALE),
                             reads=[PK[1]], writes=[("qr", tb)])
                    else:
                        for kc in range(4):
                            P.op("pe", lambda e, kc=kc, t=t: e.matmul(
                                ps[2][0:64, :], lhsT=hw_q[hb][:, kc, 192:256], rhs=cq(kc, t),
                                start=(kc == 0), stop=(kc == 3)),
                                reads=[("hwq1", hb), ("cqT", kc, t)], writes=[PK[2]])
                        tok = slice(tb * TB, (tb + 1) * TB)
                        P.op("dve", lambda e, tok=tok: e.tensor_tensor(out=tmpf[0:64, 0, :], in0=ps[1][0:64, :],
                                                                       in1=ropeT[0:64, 0, tok], op=ALU.mult),
                             reads=[PK[1], ("rope", 0)], writes=[("tmpf", 0)])
                        P.op("dve", lambda e, tok=tok: e.tensor_tensor(out=tmpf[0:64, 1, :], in0=ps[2][0:64, :],
                                                                       in1=ropeT[0:64, 1, tok], op=ALU.mult),
                             reads=[PK[2], ("rope", 1)], writes=[("tmpf", 1)])
                        P.op("dve", lambda e: e.tensor_tensor(out=tmpf[0:64, 0, :], in0=tmpf[0:64, 0, :],
                                                              in1=tmpf[0:64, 1, :], op=ALU.add),
                             reads=[("tmpf", 0), ("tmpf", 1)], writes=[("tmpf", 0)])
                        P.op("act", lambda e, tb=tb: e.activation(out=qr[0:64, tb * TB:(tb + 1) * TB], in_=tmpf[0:64, 0, :],
                                                                  func=AF.Identity, scale=ATTN_SCALE),
                             reads=[("tmpf", 0)], writes=[("qr", tb)])
                for kb in range(nk // TB):
                    bank = 2 + kb % 2
                    for kc in range(2):
                        P.op("pe", lambda e, kc=kc, kb=kb, bank=bank: e.matmul(
                            ps[bank][:], lhsT=hw_kv[hb][:, kc, 0:128], rhs=ckvsrc[:, kc, kb * TB:(kb + 1) * TB],
                            start=(kc == 0), stop=(kc == 1)),
                            reads=[("hwkv", hb)] + ckv_keys, writes=[PK[bank]])
                    if kb % 2 == 0:
                        P.op("act", lambda e, kb=kb, bank=bank: e.activation(out=kn[:, kb * TB:(kb + 1) * TB], in_=ps[bank][:], func=AF.Identity),
                             reads=[PK[bank]], writes=[("kn", kb)])
                    else:
                        P.op("dve", lambda e, kb=kb, bank=bank: e.tensor_copy(out=kn[:, kb * TB:(kb + 1) * TB], in_=ps[bank][:]),
                             reads=[PK[bank]], writes=[("kn", kb)])
                for vb in range(nkt // 4):
                    bank = 2 + vb % 2
                    for q4 in range(4):
                        kt = vb * 4 + q4
                        for kc in range(2):
                            P.op("pe", lambda e, kc=kc, kt=kt, q4=q4, bank=bank: e.matmul(
                                ps[bank][:, q4 * 128:(q4 + 1) * 128], lhsT=ckvsrc[:, kc, kt * 128:(kt + 1) * 128],
                                rhs=hw_kv[hb][:, kc, 128:256], start=(kc == 0), stop=(kc == 1)),
                                reads=[("hwkv", hb)] + ckv_keys, writes=[PK[bank]])
                    vdst = Vh[:, vb * 4:(vb + 1) * 4, :]
                    vsrc = ps[bank][:].rearrange("p (k d) -> p k d", d=128)
                    if vb % 2 == 0:
                        P.op("dve", lambda e, vdst=vdst, vsrc=vsrc: e.tensor_copy(out=vdst, in_=vsrc),
                             reads=[PK[bank]], writes=[("Vh", vb)])
                    else:
                        P.op("act", lambda e, vdst=vdst, vsrc=vsrc: e.activation(out=vdst, in_=vsrc, func=AF.Identity),
                             reads=[PK[bank]], writes=[("Vh", vb)])
                if stream == 0:
                    qblocks = [(s * 256, 256, [2 * s, 2 * s + 1]) for s in range(4)]
                else:
                    qblocks = [(qb * TB, TB, list(range(nkt))) for qb in range(2)]
                head_tiles(h, qblocks)

            for h in range(8):
                head(h)
            if stream == 0:
                wo_stage(0, [(w_, None) for w_ in wob])

        def wo_stage(stream, wob, mid_loads=None):
            n = 0 if stream == 0 else 1
            tok0 = 0 if stream == 0 else 1024
            nb_ = len(wob)

            def ld(m):
                wb_i = m % nb_
                P.op("pool", lambda e: e.dma_start(
                    out=wob[wb_i][0], in_=wo_d[jl][:, m * 128:(m + 1) * 128].rearrange("(h p) d -> p h d", p=128)),
                    writes=[("wob", wb_i)], dma=True)

            for m in range(min(nb_, NCH)):
                ld(m)
            if mid_loads is not None:
                mid_loads()
            for m in range(NCH):
                wb_i = m % nb_
                for tb in range(2):
                    t = tok0 // TB + tb
                    bank = (m * 2 + tb) % 4
                    for hh in range(8):
                        P.op("pe", lambda e, hh=hh, t=t, bank=bank, wb_i=wb_i: e.matmul(
                            ps[bank][:], lhsT=wob[wb_i][0][:, hh, :], rhs=hT[:, hh, tsl(t)],
                            start=(hh == 0), stop=(hh == 7)),
                            reads=[("wob", wb_i), ("hT", hh, t)] + ([wob[wb_i][1]] if wob[wb_i][1] else []),
                            writes=[PK[bank]])
                    resid_update(bank, m, t, mcol(l, 5, m, n))
                if m + nb_ < NCH:
                    ld(m + nb_)

        attention(0)
        attention(1)

        def pre_work(mid_loads, tiles):
            wo_stage(1, tiles, mid_loads)
        return pre_work

    def fourier_phase(l, pre_normed=False):
        jl = l // 2
        P.fence()
        flush_pending()
        cv.reset(COMMON)
        wg0 = cv.take(NCH * 512, BF16).rearrange("p (c f) -> p c f", c=NCH)
        wu0 = cv.take(NCH * 512, BF16).rearrange("p (c f) -> p c f", c=NCH)
        wd0 = cv.take(4 * D, BF16).rearrange("p (j d) -> p j d", j=4)
        P.op("pool", lambda e: e.dma_start(out=wg0[:, :, 0:512], in_=wg_d[l, 1, :, 0:512].rearrange("(c p) f -> p c f", p=128)),
             writes=[("wg", 0)], dma=True)
        P.op("pool", lambda e: e.dma_start(out=wu0[:, :, 0:512], in_=wu_d[l, 1, :, 0:512].rearrange("(c p) f -> p c f", p=128)),
             writes=[("wu", 0)], dma=True)
        P.op("pool", lambda e: e.dma_start(out=wd0[:, 0:4, :], in_=wd_d[l, 1, 0:512, :].rearrange("(j p) d -> p j d", p=128)),
             writes=[("wd", 0)], dma=True)
        cs = cv.take(2 * 512, BF16).rearrange("p (c f) -> p c f", c=2)
        dp = cv.take(2 * 2 * 256, BF16).rearrange("p (a n k) -> p a n k", a=2, n=2)
        ABp = cv.take(8 * 2048, BF16).rearrange("p (t f) -> p t f", t=8)
        ABs = cv.take(8 * 2048, BF16).rearrange("p (t f) -> p t f", t=8)
        if not pre_normed:
            norm_phase(l, 1)
        P.op("pool", lambda e: e.dma_start(out=cs, in_=dftC_d.rearrange("(c p) f -> p c f", p=128)), writes=["cs"], dma=True)
        for a in range(2):
            P.op("pool", lambda e, a=a: e.dma_start(out=dp[:, a], in_=dftP_d[a].rearrange("(n p) k -> p n k", p=128)),
                 writes=[("dp", a)], dma=True)
        for tile in list(range(8, 16)) + list(range(8)):
            dst = ABs if tile >= 8 else ABp
            ti = tile % 8
            t = tile // 4
            for g in range(4):
                bank = g % 4
                for kc in range(2):
                    P.op("pe", lambda e, g=g, kc=kc, tile=tile, bank=bank: e.matmul(
                        ps[bank][:], lhsT=hT[:, 2 * g + kc, tile * 128:(tile + 1) * 128], rhs=cs[:, kc, :],
                        start=(kc == 0), stop=(kc == 1)),
                        reads=["cs", ("hT", 2 * g + kc, t)], writes=[PK[bank]])
                dv = dst[:, ti, :].rearrange("p (s g c) -> p s g c", s=2, g=4)[:, :, g, :]
                sv = ps[bank][:].rearrange("p (s c) -> p s c", s=2)
                key = ("AB", tile)
                if g % 2 == 0:
                    P.op("act", lambda e, dv=dv, sv=sv: e.activation(out=dv, in_=sv, func=AF.Identity),
                         reads=[PK[bank]], writes=[(key, g)])
                else:
                    P.op("dve", lambda e, dv=dv, sv=sv: e.tensor_copy(out=dv, in_=sv),
                         reads=[PK[bank]], writes=[(key, g)])
            if tile >= 8 and tile % 2 == 1:
                part = (tile - 8) // 2
                P.op("sp", lambda e, part=part: e.dma_start(
                    out=f_in[jl][part].rearrange("(t p) f -> p t f", p=128), in_=ABs[:, 2 * part:2 * part + 2, :]),
                    reads=[(("AB", tl), g) for tl in (tile - 1, tile) for g in range(4)],
                    writes=[("f_in", jl, part)], dma=True)
                P.op("pool", lambda e, part=part: e.collective_compute(
                    "AllGather", ALU.bypass, replica_groups=GROUPS, ins=[f_in[jl][part]], outs=[f_out[jl][part]]),
                    reads=[("f_in", jl, part)], writes=[("f_out", jl, part)], dma=True, inc=1, nofence=True)
        for s in range(4):
            t = s // 2
            for m in range(NCH):
                bank = m % 4
                first = True
                for nt in range(2):
                    for a in range(2):
                        P.op("pe", lambda e, nt=nt, a=a, m=m, s=s, bank=bank, first=first: e.matmul(
                            ps[bank][:, 0:256], lhsT=ABp[:, s * 2 + nt, a * 1024 + m * 128:a * 1024 + (m + 1) * 128],
                            rhs=dp[:, a, nt, :], start=first, stop=(nt == 1 and a == 1)),
                            reads=[(("AB", s * 2 + nt), gg) for gg in range(4)] + [("dp", a)], writes=[PK[bank]])
                        first = False
                P.op("act", lambda e, m=m, s=s, bank=bank: e.activation(
                    out=hT[:, m, s * 256:(s + 1) * 256], in_=ps[bank][:, 0:256], func=AF.Identity, scale=1.0 / 256.0),
                    reads=[PK[bank]], writes=[("hT", m, t)])
        def fc_stage(tblocks, fwbuf, mid_loads=None):
            nb_ = len(fwbuf)

            def ld(m):
                wi = m % nb_
                P.op("pool", lambda e: e.dma_start(
                    out=fwbuf[wi][0], in_=fw_d[jl][:, m * 128:(m + 1) * 128].rearrange("(c p) d -> p c d", p=128)),
                    writes=[("fwm", wi)], dma=True)

            for m in range(min(nb_, NCH)):
                ld(m)
            if mid_loads is not None:
                mid_loads()
            for m in range(NCH):
                wi = m % nb_
                if m >= nb_ and False:
                    pass
                for t in tblocks:
                    n = cond_of(t)
                    bank = (m * 2 + t) % 4
                    for c in range(NCH):
                        P.op("pe", lambda e, c=c, t=t, bank=bank, wi=wi: e.matmul(
                            ps[bank][:], lhsT=fwbuf[wi][0][:, c, :], rhs=hT[:, c, tsl(t)],
                            start=(c == 0), stop=(c == NCH - 1)),
                            reads=[("fwm", wi), ("hT", c, t)] + ([fwbuf[wi][1]] if fwbuf[wi][1] else []),
                            writes=[PK[bank]])
                    s_ = rot("tmpf")
                    P.op("dve", lambda e, s_=s_, m=m, bank=bank, n=n: e.tensor_scalar(
                        out=tmpf[:, s_, :], in0=ps[bank][:], scalar1=fbT[:, jl, m:m + 1], scalar2=mcol(l, 5, m, n),
                        op0=ALU.add, op1=ALU.mult),
                        reads=[PK[bank], "fbT", "modS"], writes=[("tmpf", s_)])
                    P.op("dve", lambda e, s_=s_, m=m, t=t: e.tensor_tensor(out=xT[:, m, tsl(t)], in0=xT[:, m, tsl(t)],
                                                                           in1=tmpf[:, s_, :], op=ALU.add),
                         reads=[("tmpf", s_), ("xT", m, t)], writes=[("xT", m, t)])
                if m + nb_ < NCH:
                    ld(m + nb_)

        def pre_work1(mid_loads, tiles):
            fc_stage((0, 1), tiles, mid_loads)
        return pre_work1

    def fourier_part2(l):
        jl = l // 2
        P.fence()
        flush_pending()
        cv.reset(COMMON)
        NSB = 3
        abn = [cv.take(2048, BF16) for _ in range(NSB)]
        tbn = [cv.take(2 * 512, BF16).rearrange("p (a k) -> p a k", a=2) for _ in range(NSB)]

        def fc_stage(tblocks, fwbuf, mid_loads=None):
            nb_ = len(fwbuf)

            def ld(m):
                wi = m % nb_
                P.op("pool", lambda e: e.dma_start(
                    out=fwbuf[wi][0], in_=fw_d[jl][:, m * 128:(m + 1) * 128].rearrange("(c p) d -> p c d", p=128)),
                    writes=[("fwm", wi)], dma=True)

            for m in range(min(nb_, NCH)):
                ld(m)
            if mid_loads is not None:
                mid_loads()
            for m in range(NCH):
                wi = m % nb_
                for t in tblocks:
                    n = cond_of(t)
                    bank = (m * 2 + t) % 4
                    for c in range(NCH):
                        P.op("pe", lambda e, c=c, t=t, bank=bank, wi=wi: e.matmul(
                            ps[bank][:], lhsT=fwbuf[wi][0][:, c, :], rhs=hT[:, c, tsl(t)],
                            start=(c == 0), stop=(c == NCH - 1)),
                            reads=[("fwm", wi), ("hT", c, t)] + ([fwbuf[wi][1]] if fwbuf[wi][1] else []),
                            writes=[PK[bank]])
                    s_ = rot("tmpf")
                    P.op("dve", lambda e, s_=s_, m=m, bank=bank, n=n: e.tensor_scalar(
                        out=tmpf[:, s_, :], in0=ps[bank][:], scalar1=fbT[:, jl, m:m + 1], scalar2=mcol(l, 5, m, n),
                        op0=ALU.add, op1=ALU.mult),
                        reads=[PK[bank], "fbT", "modS"], writes=[("tmpf", s_)])
                    P.op("dve", lambda e, s_=s_, m=m, t=t: e.tensor_tensor(out=xT[:, m, tsl(t)], in0=xT[:, m, tsl(t)],
                                                                           in1=tmpf[:, s_, :], op=ALU.add),
                         reads=[("tmpf", s_), ("xT", m, t)], writes=[("xT", m, t)])
                if m + nb_ < NCH:
                    ld(m + nb_)

        for kb in range(2):
            t = 2 + kb
            nt_order = [r_ * 8 + part * 2 + j_ for part in range(4) for r_ in range(4) for j_ in range(2)]
            for ni, nt in enumerate(nt_order):
                b = (kb * 32 + ni) % NSB
                r_, w_ = nt // 8, nt % 8
                part, j_ = w_ // 2, w_ % 2
                src = f_out[jl][part][r_ * 256 + j_ * 128:r_ * 256 + (j_ + 1) * 128, :]
                P.op("sp", lambda e, src=src, b=b: e.dma_start(out=abn[b], in_=src),
                     reads=[("f_out", jl, part)], writes=[("abn", b)], dma=True)
                P.op("sp", lambda e, nt=nt, b=b, kb=kb: e.dma_start(
                    out=tbn[b], in_=dftS_bf[:, nt * 128:(nt + 1) * 128, kb * 512:(kb + 1) * 512].rearrange("a p k -> p a k")),
                    reads=DFT_KEYS, writes=[("tbn", b)], dma=True)
                for m in range(NCH):
                    for a in range(2):
                        P.op("pe", lambda e, ni=ni, a=a, m=m, b=b: e.matmul(
                            ps[m][:], lhsT=abn[b][:, a * 1024 + m * 128:a * 1024 + (m + 1) * 128], rhs=tbn[b][:, a, :],
                            start=(ni == 0 and a == 0), stop=(ni == 31 and a == 1)),
                            reads=[("abn", b), ("tbn", b)], writes=[PK[m]])
            for m in range(NCH):
                if m % 2 == 0:
                    P.op("act", lambda e, m=m, t=t: e.activation(out=hT[:, m, tsl(t)], in_=ps[m][:], func=AF.Identity,
                                                                 scale=1.0 / 1024.0),
                         reads=[PK[m]], writes=[("hT", m, t)])
                else:
                    P.op("dve", lambda e, m=m, t=t: e.tensor_scalar(out=hT[:, m, tsl(t)], in0=ps[m][:], scalar1=1.0 / 1024.0,
                                                                    scalar2=None, op0=ALU.mult),
                         reads=[PK[m]], writes=[("hT", m, t)])
        def pre_work(mid_loads, tiles):
            fc_stage((2, 3), tiles, mid_loads)
        return pre_work

    for l in range(depth):
        mixer_on = ("m" if l % 2 == 0 else "f") in DBG_PHASES
        ffn_phase(l, 0, pre_normed=(l > 0), next_norm=(l, 1, False) if mixer_on else None, skip_fence=(l > 0),
                  carry=mixer_on, tail_order=(2, 3, 0, 1) if mixer_on else (0, 1, 2, 3))
        last = (l == depth - 1)
        nn = (0, 0, True) if last else (l + 1, 0, False)
        if mixer_on and l % 2 == 1:
            pw1 = fourier_phase(l, pre_normed=True)
            ffn_phase(l, 1, pre_normed=False, next_norm=nn, pre_work=pw1, carry=True, tblocks=(0, 1),
                      chunk0_preloaded=True)
            pw = fourier_part2(l)
            ffn_phase(l, 1, pre_normed=False, next_norm=nn, pre_work=pw, carry=not last, tblocks=(2, 3))
        else:
            pw = mla_phase(l, pre_normed=True) if mixer_on else None
            ffn_phase(l, 1, pre_normed=False, next_norm=nn, pre_work=pw, carry=not last)
    if depth == 0:
        P.fence()
        norm_phase(0, 0, final=True)
    P.emit(nc, final_waits=outs)
    st.close()
    return nc, P


def _fm(a):
    t = a.shape[0]
    return np.ascontiguousarray(a.T.reshape(NCH, 128, t).transpose(1, 0, 2))


def _vec(a, nch):
    sh = a.shape[:-1]
    b = a.reshape(sh + (nch, 128))
    return np.ascontiguousarray(np.moveaxis(b, -1, 0))


def _const_tables():
    f32 = np.float32
    k = np.arange(256)
    ang = 2 * np.pi * np.outer(k, k) / 256.0
    dftC = np.concatenate([np.cos(ang), np.sin(ang)], axis=1).astype(np.float32)
    dftP = np.stack([np.cos(ang), -np.sin(ang)]).astype(np.float32)
    n = np.arange(4096, dtype=np.int64)
    dftS = []
    for qd in range(4):
        kk = np.arange(qd * 1024, (qd + 1) * 1024, dtype=np.int64)
        a = 2 * np.pi * ((np.outer(n, kk) % 4096).astype(np.float64)) / 4096.0
        dftS.append(np.stack([np.cos(a), -np.sin(a)]).astype(np.float32))
    inv = 1.0 / (10000.0 ** (np.arange(16, dtype=np.float32) / 16.0))
    pos = np.arange(4096)
    row = (pos // 64).astype(np.float32)
    col = (pos % 64).astype(np.float32)
    ang = np.stack([row[:, None] * inv, col[:, None] * inv], axis=1).astype(np.float32)
    cos = np.cos(ang)
    sin = np.sin(ang)
    cosT = np.zeros((64, 4096), f32)
    sinT = np.zeros((64, 4096), f32)
    for a in range(2):
        for hf in range(2):
            for f in range(16):
                p = a * 32 + hf * 16 + f
                cosT[p] = cos[:, a, f]
                sinT[p] = -sin[:, a, f] if hf == 0 else sin[:, a, f]
    rope = [np.ascontiguousarray(np.stack([cosT[:, q * 1024:(q + 1) * 1024], sinT[:, q * 1024:(q + 1) * 1024]]))
            for q in range(4)]
    return dftC, dftP, dftS, rope


def _swap_cols(w, nheads, base, stride):
    cols = []
    for h in range(nheads):
        o = h * stride + base
        for a in range(2):
            cols += list(range(o + a * 32 + 16, o + a * 32 + 32)) + list(range(o + a * 32, o + a * 32 + 16))
    return np.ascontiguousarray(w[..., cols])


_CACHE = {}
DEPTH_RUN = DEPTH


def kernel(x_prompt, x_sample, cache_ckv, cache_kpe, c, c_ctx, w_mod, b_mod, norm_g,
           ffn_wg, ffn_wu, ffn_wd, mla_w_dq, mla_q_norm, mla_w_uq, mla_w_dkv, mla_kv_norm,
           mla_w_ukv, mla_w_o, fourier_w, fourier_b, final_norm):
    A = lambda a: np.ascontiguousarray(np.asarray(a, dtype=np.float32))
    x_prompt, x_sample, cache_ckv, cache_kpe = A(x_prompt), A(x_sample), A(cache_ckv), A(cache_kpe)
    c, c_ctx, w_mod, b_mod, norm_g = A(c), A(c_ctx), A(w_mod), A(b_mod), A(norm_g)
    ffn_wg, ffn_wu, ffn_wd = A(ffn_wg), A(ffn_wu), A(ffn_wd)
    mla_w_dq, mla_q_norm, mla_w_uq, mla_w_dkv = A(mla_w_dq), A(mla_q_norm), A(mla_w_uq), A(mla_w_dkv)
    mla_kv_norm, mla_w_ukv, mla_w_o = A(mla_kv_norm), A(mla_w_ukv), A(mla_w_o)
    fourier_w, fourier_b, final_norm = A(fourier_w), A(fourier_b), A(final_norm)

    if "nc" not in _CACHE:
        _CACHE["nc"] = build_program(DEPTH_RUN)[0]
        _CACHE["tables"] = _const_tables()
    nc = _CACHE["nc"]
    dftC, dftP, dftS, rope = _CACHE["tables"]

    shared = {
        "bmodT": _vec(b_mod, 72), "normgT": _vec(norm_g, 8), "finalT": _vec(final_norm, 8),
        "qnT": _vec(mla_q_norm, 4), "kvnT": _vec(mla_kv_norm, 2), "fbT": _vec(fourier_b, 8),
        "wg": ffn_wg, "wu": ffn_wu, "wd": ffn_wd, "wdq": mla_w_dq, "wuq": mla_w_uq,
        "wuqs": _swap_cols(mla_w_uq, 8, 128, 192), "wdkv": mla_w_dkv,
        "wdkvs": _swap_cols(mla_w_dkv, 1, 256, 0), "wukv": mla_w_ukv, "wo": mla_w_o, "fw": fourier_w,
        "dftC": dftC, "dftP": dftP,
    }
    in_maps = []
    for r in range(8):
        b, qd = r // 4, r % 4
        xp = x_prompt[4 * r:4 * r + 4].reshape(1024, D)
        xs = x_sample[b, qd * 1024:(qd + 1) * 1024]
        m = dict(shared)
        m["xT"] = _fm(np.concatenate([xp, xs], axis=0))
        m["cckv"] = np.ascontiguousarray(cache_ckv[b].transpose(0, 2, 1))
        m["ckpe"] = np.ascontiguousarray(cache_kpe[b].transpose(0, 2, 1))
        m["condT"] = _vec(np.stack([c_ctx, c[b]]), 8).transpose(0, 2, 1).copy()
        m["wmod"] = np.ascontiguousarray(w_mod[:, :, qd * 2304:(qd + 1) * 2304])
        m["ropeT"] = rope[qd]
        m["dftS"] = dftS[qd]
        in_maps.append(m)

    res = run_bass_kernel_spmd(nc, in_maps, core_ids=list(range(8)))
    y_prompt = np.empty((32, 256, D), np.float32)
    y_sample = np.empty((2, 4096, D), np.float32)
    new_ckv = np.empty((32, 2, 256, 256), np.float32)
    new_kpe = np.empty((32, 2, 256, 64), np.float32)
    for r in range(8):
        b, qd = r // 4, r % 4
        o = res.results[r]
        y = np.asarray(o["yT"]).transpose(2, 1, 0).reshape(NT, D)
        y_prompt[4 * r:4 * r + 4] = y[:1024].reshape(4, 256, D)
        y_sample[b, qd * 1024:(qd + 1) * 1024] = y[1024:]
        ck = np.asarray(o["o_ckv"])
        kp = np.asarray(o["o_kpe"])
        new_ckv[4 * r:4 * r + 4] = ck.reshape(2, 256, 4, 256).transpose(2, 0, 3, 1)
        new_kpe[4 * r:4 * r + 4] = kp.reshape(2, 64, 4, 256).transpose(2, 0, 3, 1)
    return (y_prompt, y_sample, new_ckv, new_kpe)
```

```python
import math
from contextlib import ExitStack

import numpy as np
import ml_dtypes
import concourse.bass as bass
import concourse.mybir as mybir
from concourse.bass_utils import run_bass_kernel_spmd

F32 = mybir.dt.float32
BF16 = mybir.dt.bfloat16
AF = mybir.ActivationFunctionType
ALU = mybir.AluOpType

D = 1024
DFF = 2816
NCH = 8
TB = 512
NT = 2048
DEPTH = 4
EPS = 1e-6
ATTN_SCALE = 1.0 / math.sqrt(192.0)
GROUPS = [[0, 1, 2, 3], [4, 5, 6, 7]]
ENGINES = ("sp", "act", "dve", "pool", "pe")
SEM_ROT = 8000
SCR_BYTES = 102 * 1024
DBG_PHASES = "mf"


class Op:
    __slots__ = ("id", "eng", "fn", "deps", "is_dma", "slot", "sig", "inc", "needs_signal")

    def __init__(self, id, eng, fn, is_dma, slot, inc):
        self.id = id
        self.eng = eng
        self.fn = fn
        self.deps = set()
        self.is_dma = is_dma
        self.slot = slot
        self.sig = None
        self.inc = inc
        self.needs_signal = False


class Prog:
    def __init__(self):
        self.ops = []
        self.last_w = {}
        self.readers = {}
        self.eng_ops = {e: [] for e in ENGINES}
        self.fence_set = set()
        self.dma_since_fence = []
        self.fenced = {e: True for e in ENGINES}

    def fence(self):
        fs = set(self.dma_since_fence)
        for e in ENGINES:
            for o in reversed(self.eng_ops[e]):
                if not o.is_dma:
                    fs.add(o.id)
                    break
        self.fence_set = fs
        self.dma_since_fence = []
        self.fenced = {e: False for e in ENGINES}

    def op(self, eng, fn, reads=(), writes=(), dma=False, slot=None, inc=16, nofence=False):
        o = Op(len(self.ops), eng, fn, dma, slot, inc)
        deps = set()
        for k in reads:
            w = self.last_w.get(k)
            if w is not None:
                deps.add(w)
        for k in writes:
            w = self.last_w.get(k)
            if w is not None:
                deps.add(w)
            for r in self.readers.get(k, {}).values():
                if isinstance(r, list):
                    deps.update(r)
                else:
                    deps.add(r)
        if not self.fenced[eng]:
            deps |= self.fence_set
            self.fenced[eng] = True
        deps.discard(o.id)
        o.deps = deps
        for k in reads:
            rd = self.readers.setdefault(k, {})
            if dma:
                rd.setdefault("dma", []).append(o.id)
            else:
                rd[eng] = o.id
        for k in writes:
            self.last_w[k] = o.id
            self.readers[k] = {}
        if dma:
            if slot is None:
                o.slot = ("dma", writes[0])
            if not nofence:
                self.dma_since_fence.append(o.id)
        self.eng_ops[eng].append(o)
        self.ops.append(o)
        return o

    def emit(self, nc, final_waits=()):
        ops = self.ops
        for o in ops:
            for d in o.deps:
                p = ops[d]
                if p.is_dma:
                    p.needs_signal = True
                elif p.eng == "pe" and o.eng == "pe" and not o.is_dma:
                    continue
                else:
                    p.needs_signal = True
        for d in final_waits:
            d.needs_signal = True
        sem_keys = []
        comp_cnt = {e: 0 for e in ENGINES}
        slot_cnt = {}
        grp_total = {}
        for o in ops:
            if o.is_dma and isinstance(o.slot, tuple) and o.slot[0] == "grp":
                grp_total[o.slot] = grp_total.get(o.slot, 0) + o.inc
        for o in ops:
            if o.is_dma:
                c = slot_cnt.get(o.slot, 0) + o.inc
                slot_cnt[o.slot] = c
                o.sig = (o.slot, grp_total.get(o.slot, c))
                if o.slot not in slot_cnt or o.slot not in sem_keys:
                    sem_keys.append(o.slot)
            elif o.needs_signal:
                n = comp_cnt[o.eng]
                comp_cnt[o.eng] = n + 1
                key = ("c", o.eng, n // SEM_ROT)
                o.sig = (key, n % SEM_ROT + 1)
                if key not in sem_keys:
                    sem_keys.append(key)
        sem_keys = list(dict.fromkeys(sem_keys))
        self.n_sems = len(sem_keys)
        stack = ExitStack()
        sems = {}
        for i, k in enumerate(sem_keys):
            sems[k] = stack.enter_context(nc.semaphore("s%d" % i))

        def run(engname, e):
            waited = {}
            for o in self.eng_ops[engname]:
                need = {}
                for d in o.deps:
                    p = ops[d]
                    if p.sig is None:
                        continue
                    if (not p.is_dma) and p.eng == "pe" and engname == "pe" and not o.is_dma:
                        continue
                    k, c = p.sig
                    if need.get(k, 0) < c:
                        need[k] = c
                for k, c in need.items():
                    if waited.get(k, 0) >= c:
                        continue
                    e.wait_ge(sems[k], c)
                    waited[k] = c
                ins = o.fn(e)
                if o.sig is not None:
                    ins.then_inc(sems[o.sig[0]], o.inc if o.is_dma else 1)
            if engname == "sp":
                for d in final_waits:
                    k, c = d.sig
                    e.wait_ge(sems[k], c)

        with stack:
            with nc.Block() as block:
                @block.sync
                def _(e):
                    run("sp", e)

                @block.scalar
                def _(e):
                    run("act", e)

                @block.vector
                def _(e):
                    run("dve", e)

                @block.gpsimd
                def _(e):
                    run("pool", e)

                @block.tensor
                def _(e):
                    run("pe", e)


class Carve:
    def __init__(self, scr):
        self.scr = scr
        self.off = 0

    def reset(self, off=0):
        self.off = off

    def take(self, nelem, dtype, parts=128):
        nb = nelem * (4 if dtype == F32 else 2)
        nb = (nb + 63) // 64 * 64
        a = self.off // 2
        self.off += nb
        assert self.off <= SCR_BYTES, ("scratch overflow", self.off)
        v = self.scr[:, a:a + nb // 2]
        if dtype == F32:
            v = v.bitcast(F32)
        v = v[:, 0:nelem]
        return v


def build_program(depth=DEPTH):
    nc = bass.Bass("TRN2", target_bir_lowering=False)

    def din(name, shape, dt=F32):
        return nc.dram_tensor(name, list(shape), dt, kind="ExternalInput").ap()

    def dout(name, shape, dt=F32):
        return nc.dram_tensor(name, list(shape), dt, kind="ExternalOutput").ap()

    xT_d = din("xT", [128, NCH, NT])
    cckv_d = din("cckv", [2, 256, 512])
    ckpe_d = din("ckpe", [2, 64, 512])
    condT_d = din("condT", [128, NCH, 2])
    wmod_d = din("wmod", [4, D, 2304])
    bmodT_d = din("bmodT", [128, 4, 72])
    normgT_d = din("normgT", [128, 4, 3, 8])
    finalT_d = din("finalT", [128, 8])
    qnT_d = din("qnT", [128, 2, 4])
    kvnT_d = din("kvnT", [128, 2, 2])
    fbT_d = din("fbT", [128, 2, 8])
    wg_d = din("wg", [4, 2, D, DFF])
    wu_d = din("wu", [4, 2, D, DFF])
    wd_d = din("wd", [4, 2, DFF, D])
    wdq_d = din("wdq", [2, D, 512])
    wuq_d = din("wuq", [2, 512, 1536])
    wuqs_d = din("wuqs", [2, 512, 512])
    wdkv_d = din("wdkv", [2, D, 320])
    wdkvs_d = din("wdkvs", [2, D, 64])
    wukv_d = din("wukv", [2, 256, 2048])
    wo_d = din("wo", [2, D, D])
    fw_d = din("fw", [2, D, D])
    rope_d = din("ropeT", [2, 64, 1024])
    dftC_d = din("dftC", [256, 512])
    dftP_d = din("dftP", [2, 256, 256])
    dftS_d = din("dftS", [2, 4096, 1024])

    yT_d = dout("yT", [128, NCH, NT])
    ockv_d = dout("o_ckv", [2, 256, 1024])
    okpe_d = dout("o_kpe", [2, 64, 1024])

    mod_in = nc.dram_tensor("mod_in", [128, 144], F32).ap()
    mod_out = nc.dram_tensor("mod_out", [512, 144], F32).ap()
    x_in = [nc.dram_tensor("x_in%d" % j, [320, 1024], BF16).ap() for j in range(2)]
    x_out = [nc.dram_tensor("x_out%d" % j, [1280, 1024], BF16).ap() for j in range(2)]
    f_in = [[nc.dram_tensor("f_in%d_%d" % (j, q), [256, 2048], BF16).ap() for q in range(4)] for j in range(2)]
    f_out = [[nc.dram_tensor("f_out%d_%d" % (j, q), [1024, 2048], BF16).ap() for q in range(4)] for j in range(2)]

    dftS_bf = nc.dram_tensor("dftS_bf", [2, 4096, 1024], BF16).ap()
    DFT_KEYS = [("dftS_bf", a, q) for a in range(2) for q in range(4)]

    P = Prog()
    st = ExitStack()
    sb = lambda name, shape, dt: st.enter_context(nc.sbuf_tensor(name, list(shape), dt))
    xT = sb("xTs", [128, NCH, NT], F32)
    hT = sb("hTs", [128, NCH, NT], BF16)
    modS = sb("modS", [128, 4 * 72 * 2], F32)
    bmodT = sb("bmodTs", [128, 4, 72], F32)
    normgT = sb("normgTs", [128, 4, 3, 8], F32)
    finalT = sb("finalTs", [128, 8], F32)
    qnT = sb("qnTs", [128, 2, 4], F32)
    kvnT = sb("kvnTs", [128, 2, 2], F32)
    fbT = sb("fbTs", [128, 2, 8], F32)
    condT = sb("condTs", [128, NCH, 2], F32)
    scT = sb("scTs", [128, NCH, 2], F32)
    scB = sb("scBs", [128, NCH, 2], BF16)
    onesB = sb("onesB", [128, 128], BF16)
    onesF = sb("onesF", [128, 128], F32)
    epsT = sb("epsT", [128, 1], F32)
    scr = sb("scr", [128, SCR_BYTES // 2], BF16)
    ps = [st.enter_context(nc.psum_tensor("ps%d" % i, [128, 512], F32)) for i in range(8)]
    PK = [("ps", i) for i in range(8)]
    cv = Carve(scr)

    mod5 = modS[:].rearrange("p (l k c n) -> p l k c n", l=4, k=9, c=8)
    mod_lrx = modS[:].rearrange("p (l r x) -> p l r x", l=4, r=4)
    mod_lcn = modS[:].rearrange("p (l c n) -> p l c n", l=4, c=72)

    outs = []

    def mcol(l, k, c, cond):
        return mod5[:, l, k, c, cond:cond + 1]

    def cond_of(t):
        return 0 if t < 2 else 1

    def tsl(t):
        return slice(t * TB, (t + 1) * TB)

    cv.reset(0)
    sq = cv.take(2 * TB, BF16).rearrange("p (a b) -> p a b", a=2)
    rs = cv.take(2 * TB, F32).rearrange("p (a b) -> p a b", a=2)
    tmpf = cv.take(2 * TB, F32).rearrange("p (a b) -> p a b", a=2)
    COMMON = cv.off
    cnt = {"sq": 0, "tmpf": 0}

    def rot(name, n=2):
        v = cnt.get(name, 0)
        cnt[name] = v + 1
        return v % n

    P.op("sp", lambda e: e.dma_start(out=condT[:], in_=condT_d), writes=["condT"], dma=True, slot=("grp", "setup"))
    for (tl, td, nm) in ((bmodT, bmodT_d, "bmodT"), (normgT, normgT_d, "normgT"), (finalT, finalT_d, "finalT"),
                         (qnT, qnT_d, "qnT"), (kvnT, kvnT_d, "kvnT"), (fbT, fbT_d, "fbT")):
        P.op("sp", lambda e, tl=tl, td=td: e.dma_start(out=tl[:], in_=td), writes=[nm], dma=True, slot=("grp", "setup"))
    P.op("dve", lambda e: e.memset(onesB[:], 1.0), writes=["ones"])
    P.op("dve", lambda e: e.memset(onesF[:], 1.0), writes=["onesF"])
    P.op("dve", lambda e: e.memset(epsT[:], EPS), writes=["eps"])
    P.op("act", lambda e: e.activation(out=scB[:], in_=condT[:], func=AF.Silu), reads=["condT"], writes=["scT"])

    cv.reset(COMMON)
    wm = [cv.take(NCH * 1152, BF16).rearrange("p (c f) -> p c f", c=NCH) for _ in range(4)]
    modin = cv.take(144, F32)
    for l in range(4):
        for hf in range(2):
            b = (l * 2 + hf) % 4
            src = wmod_d[l, :, hf * 1152:(hf + 1) * 1152].rearrange("(c p) f -> p c f", p=128)
            P.op("pool", lambda e, b=b, src=src: e.dma_start(out=wm[b], in_=src), writes=[("wm", b)], dma=True)
            for i in range(9):
                col = (hf * 9 + i) * 2
                for c in range(NCH):
                    P.op("pe", lambda e, b=b, i=i, c=c, col=col: e.matmul(
                        ps[0][:, col:col + 2], lhsT=wm[b][:, c, i * 128:(i + 1) * 128], rhs=scB[:, c, :],
                        start=(c == 0), stop=(c == NCH - 1)),
                        reads=[("wm", b), "scT"], writes=[PK[0]])
        P.op("dve", lambda e, l=l: e.tensor_copy(out=modin[:, l * 36:(l + 1) * 36], in_=ps[0][:, 0:36]),
             reads=[PK[0]], writes=["modin"])
    P.op("sp", lambda e: e.dma_start(out=mod_in, in_=modin), reads=["modin"], writes=["mod_in"], dma=True)
    for c in range(NCH):
        P.op("sp", lambda e, c=c: e.dma_start(out=xT[:, c, :], in_=xT_d[:, c, :]),
             writes=[("xT", c, t) for t in range(4)], dma=True, slot=("grp", "xT"))
    P.op("pool", lambda e: e.collective_compute("AllGather", ALU.bypass, replica_groups=GROUPS,
                                                ins=[mod_in], outs=[mod_out]),
         reads=["mod_in"], writes=["mod_out"], dma=True, inc=1)
    for l in range(4):
        src = mod_out[:, l * 36:(l + 1) * 36].rearrange("(r p) x -> p r x", p=128)
        P.op("sp", lambda e, l=l, src=src: e.dma_start(out=mod_lrx[:, l], in_=src),
             reads=["mod_out"], writes=["modS"], dma=True, slot=("dma", "modS", l))
    for n in range(2):
        P.op("dve", lambda e, n=n: e.tensor_tensor(out=mod_lcn[:, :, :, n], in0=mod_lcn[:, :, :, n],
                                                   in1=bmodT[:], op=ALU.add),
             reads=["modS", "bmodT"], writes=["modS"])
    for i in range(3):
        for n in range(2):
            P.op("dve", lambda e, i=i, n=n: e.scalar_tensor_tensor(
                out=mod5[:, :, 3 * i + 1, :, n], in0=mod5[:, :, 3 * i + 1, :, n], scalar=1.0,
                in1=normgT[:, :, i, :], op0=ALU.add, op1=ALU.mult),
                reads=["modS", "normgT"], writes=["modS"])
            if i != 1:
                P.op("dve", lambda e, i=i, n=n: e.tensor_scalar(
                    out=mod5[:, :, 3 * i + 2, :, n], in0=mod5[:, :, 3 * i + 2, :, n], scalar1=0.5,
                    scalar2=None, op0=ALU.mult),
                    reads=["modS"], writes=["modS"])

    def rstd_block(src_keys, nchunks, get_src, inv_n, bank, extra_reads=()):
        for c in range(nchunks):
            s = rot("sq")
            src, rk = get_src(c)
            P.op("act", lambda e, s=s, src=src: e.activation(out=sq[:, s, :], in_=src, func=AF.Square),
                 reads=[rk], writes=[("sq", s)])
            P.op("pe", lambda e, s=s, c=c: e.matmul(ps[bank][:], lhsT=onesB[:], rhs=sq[:, s, :],
                                                    start=(c == 0), stop=(c == nchunks - 1)),
                 reads=[("sq", s), "ones"], writes=[PK[bank]])
        P.op("act", lambda e: e.activation(out=rs[:, 0, :], in_=ps[bank][:], func=AF.Ln,
                                           bias=epsT[:, 0:1], scale=inv_n),
             reads=[PK[bank], "eps"], writes=[("rs", 0)])
        P.op("act", lambda e: e.activation(out=rs[:, 1, :], in_=rs[:, 0, :], func=AF.Exp, scale=-0.5),
             reads=[("rs", 0)], writes=[("rs", 1)])

    def norm_phase(l, i, final=False):
        for t in range(4):
            n = cond_of(t)
            rstd_block(None, NCH, lambda c, t=t: (xT[:, c, tsl(t)], ("xT", c, t)), 1.0 / D, 7)
            for c in range(NCH):
                s = rot("tmpf")
                g = finalT[:, c:c + 1] if final else mcol(l, 3 * i + 1, c, n)
                P.op("dve", lambda e, s=s, c=c, t=t, g=g: e.scalar_tensor_tensor(
                    out=tmpf[:, s, :], in0=xT[:, c, tsl(t)], scalar=g, in1=rs[:, 1, :],
                    op0=ALU.mult, op1=ALU.mult),
                    reads=[("xT", c, t), ("rs", 1), "modS", "finalT"], writes=[("tmpf", s)])
                if final:
                    outs.append(P.op("sp", lambda e, s=s, c=c, t=t: e.dma_start(out=yT_d[:, c, tsl(t)], in_=tmpf[:, s, :]),
                                     reads=[("tmpf", s)], writes=[("yT", c, t)], dma=True, slot=("out", "tmpf", s)))
                else:
                    P.op("act", lambda e, s=s, c=c, t=t, l=l, i=i, n=n: e.activation(
                        out=hT[:, c, tsl(t)], in_=tmpf[:, s, :], func=AF.Identity,
                        bias=mcol(l, 3 * i, c, n), scale=1.0),
                        reads=[("tmpf", s), "modS"], writes=[("hT", c, t)])

    pending = []

    def norm_items(l, i, t, final=False):
        items = []
        n = cond_of(t)
        bank = 7

        def sq_item(c):
            def f():
                s = rot("sq")
                P.op("act", lambda e: e.activation(out=sq[:, s, :], in_=xT[:, c, tsl(t)], func=AF.Square),
                     reads=[("xT", c, t)], writes=[("sq", s)])
                P.op("pe", lambda e: e.matmul(ps[bank][:], lhsT=onesB[:], rhs=sq[:, s, :],
                                              start=(c == 0), stop=(c == NCH - 1)),
                     reads=[("sq", s), "ones"], writes=[PK[bank]])
            return f

        def rs_item():
            P.op("act", lambda e: e.activation(out=rs[:, 0, :], in_=ps[bank][:], func=AF.Ln,
                                               bias=epsT[:, 0:1], scale=1.0 / D),
                 reads=[PK[bank], "eps"], writes=[("rs", 0)])
            P.op("act", lambda e: e.activation(out=rs[:, 1, :], in_=rs[:, 0, :], func=AF.Exp, scale=-0.5),
                 reads=[("rs", 0)], writes=[("rs", 1)])

        def out_item(c):
            def f():
                s = rot("tmpf")
                g = finalT[:, c:c + 1] if final else mcol(l, 3 * i + 1, c, n)
                P.op("dve", lambda e: e.scalar_tensor_tensor(
                    out=tmpf[:, s, :], in0=xT[:, c, tsl(t)], scalar=g, in1=rs[:, 1, :],
                    op0=ALU.mult, op1=ALU.mult),
                    reads=[("xT", c, t), ("rs", 1), "modS", "finalT"], writes=[("tmpf", s)])
                if final:
                    outs.append(P.op("sp", lambda e: e.dma_start(out=yT_d[:, c, tsl(t)], in_=tmpf[:, s, :]),
                                     reads=[("tmpf", s)], writes=[("yT", c, t)], dma=True, slot=("out", "tmpf", s)))
                else:
                    P.op("act", lambda e: e.activation(
                        out=hT[:, c, tsl(t)], in_=tmpf[:, s, :], func=AF.Identity,
                        bias=mcol(l, 3 * i, c, n), scale=1.0),
                        reads=[("tmpf", s), "modS"], writes=[("hT", c, t)])
            return f

        for c in range(NCH):
            items.append((t, sq_item(c)))
        items.append((t, rs_item))
        for c in range(NCH):
            items.append((t, out_item(c)))
        return items

    def pop_pending(n):
        for _ in range(n):
            if pending:
                pending.pop(0)[1]()

    def flush_pending(upto_t=None):
        while pending and (upto_t is None or any(tg <= upto_t for tg, _ in pending)):
            pending.pop(0)[1]()

    def resid_update(bank, m, t, gate_ap, tok=None):
        sl = tsl(t) if tok is None else tok
        P.op("dve", lambda e: e.scalar_tensor_tensor(
            out=xT[:, m, sl], in0=ps[bank][:, 0:(sl.stop - sl.start)], scalar=gate_ap, in1=xT[:, m, sl],
            op0=ALU.mult, op1=ALU.add),
            reads=[PK[bank], ("xT", m, t), "modS"], writes=[("xT", m, t)])

    def ffn_phase(l, i, pre_normed=False, next_norm=None, skip_fence=False, pre_work=None, carry=False,
                  tail_order=(0, 1, 2, 3), tblocks=(0, 1, 2, 3), chunk0_preloaded=False):
        if not skip_fence:
            P.fence()
        cv.reset(COMMON)
        NB = 3
        wgb, wub, wdb = [], [], []
        for _ in range(NB):
            wgb.append(cv.take(NCH * 512, BF16).rearrange("p (c f) -> p c f", c=NCH))
            wub.append(cv.take(NCH * 512, BF16).rearrange("p (c f) -> p c f", c=NCH))
            wdb.append(cv.take(4 * D, BF16).rearrange("p (j d) -> p j d", j=4))
        actb = [cv.take(4 * TB, BF16).rearrange("p (j t) -> p j t", j=4) for _ in range(2)]
        slb = [cv.take(TB, BF16) for _ in range(2)]
        sizes = [4, 4, 4, 4, 3, 3]
        f0s = [0, 4, 8, 12, 16, 19]
        kidx = 2 * i
        sub = 0 if i == 0 else 2

        def load_chunk(k):
            s = k % NB
            G = sizes[k]
            f0 = f0s[k] * 128
            srcg = wg_d[l, i, :, f0:f0 + G * 128].rearrange("(c p) f -> p c f", p=128)
            srcu = wu_d[l, i, :, f0:f0 + G * 128].rearrange("(c p) f -> p c f", p=128)
            srcd = wd_d[l, i, f0:f0 + G * 128, :].rearrange("(j p) d -> p j d", p=128)
            P.op("pool", lambda e: e.dma_start(out=wgb[s][:, :, 0:G * 128], in_=srcg), writes=[("wg", s)], dma=True)
            P.op("pool", lambda e: e.dma_start(out=wub[s][:, :, 0:G * 128], in_=srcu), writes=[("wu", s)], dma=True)
            P.op("pool", lambda e: e.dma_start(out=wdb[s][:, 0:G, :], in_=srcd), writes=[("wd", s)], dma=True)

        gu_cnt = [0]

        def gate_up(k, t):
            s = k % NB
            G = sizes[k]
            ab = (gu_cnt[0]) % 2
            gu_cnt[0] += 1
            for j in range(G):
                bg = j % 2
                bu = 2 + j % 2
                for c in range(NCH):
                    P.op("pe", lambda e, j=j, c=c, bg=bg: e.matmul(
                        ps[bg][:], lhsT=wgb[s][:, c, j * 128:(j + 1) * 128], rhs=hT[:, c, tsl(t)],
                        start=(c == 0), stop=(c == NCH - 1)),
                        reads=[("wg", s), ("hT", c, t)], writes=[PK[bg]])
                for c in range(NCH):
                    P.op("pe", lambda e, j=j, c=c, bu=bu: e.matmul(
                        ps[bu][:], lhsT=wub[s][:, c, j * 128:(j + 1) * 128], rhs=hT[:, c, tsl(t)],
                        start=(c == 0), stop=(c == NCH - 1)),
                        reads=[("wu", s), ("hT", c, t)], writes=[PK[bu]])
                sl_i = rot("slb")
                P.op("act", lambda e, bg=bg, sl_i=sl_i: e.activation(out=slb[sl_i], in_=ps[bg][:], func=AF.Silu),
                     reads=[PK[bg]], writes=[("slb", sl_i)])
                P.op("dve", lambda e, bu=bu, sl_i=sl_i, j=j, ab=ab: e.tensor_tensor(
                    out=actb[ab][:, j, :], in0=ps[bu][:], in1=slb[sl_i], op=ALU.mult),
                    reads=[PK[bu], ("slb", sl_i)], writes=[("act", ab, j)])
                pop_pending(2)
            return ab

        def down(k, t, ab):
            s = k % NB
            G = sizes[k]
            n = cond_of(t)
            for m in range(NCH):
                by = 4 + m % 3
                for j in range(G):
                    P.op("pe", lambda e, j=j, m=m, by=by: e.matmul(
                        ps[by][:], lhsT=wdb[s][:, j, m * 128:(m + 1) * 128], rhs=actb[ab][:, j, :],
                        start=(j == 0), stop=(j == G - 1)),
                        reads=[("wd", s), ("act", ab, j)], writes=[PK[by]])
                resid_update(by, m, t, mcol(l, 3 * sub + 2, m, n))
                pop_pending(1)

        if pre_work is not None:
            tiles = [(wgb[2][:, :, i_ * 128:(i_ + 1) * 128], ("wg", 2)) for i_ in range(4)] + \
                    [(wub[2][:, :, i_ * 128:(i_ + 1) * 128], ("wu", 2)) for i_ in range(4)]
            pre_work(lambda: (None if chunk0_preloaded else load_chunk(0), load_chunk(1)), tiles)
        if not pre_normed:
            for t in tblocks:
                pending.extend(norm_items(l, sub, t))
            flush_pending(tblocks[0])
        nk_ = len(sizes)
        work = [(k, t) for k in range(nk_ - 1) for t in tblocks]
        if pre_work is None:
            load_chunk(0)
            load_chunk(1)
        prev = None
        for idx, (k, t) in enumerate(work):
            if k == 0:
                flush_pending(t)
            ab = gate_up(k, t)
            if prev is not None:
                down(*prev)
            if t == tblocks[0] and k >= 1 and k + 1 < nk_:
                load_chunk(k + 1)
                if l == 0 and i == 1:
                    for a in range(2):
                        q = k - 1
                        P.op("pool", lambda e, a=a, q=q: e.dma_start(out=dftS_bf[a, q * 1024:(q + 1) * 1024, :],
                                                                     in_=dftS_d[a, q * 1024:(q + 1) * 1024, :]),
                             writes=[("dftS_bf", a, q)], dma=True, slot="dftcast")
            prev = (k, t, ab)
        k = nk_ - 1
        for t in [t_ for t_ in tail_order if t_ in tblocks]:
            ab = gate_up(k, t)
            if prev is not None:
                down(*prev)
                prev = None
            down(k, t, ab)
            if next_norm is not None:
                pending.extend(norm_items(next_norm[0], next_norm[1], t, final=next_norm[2]))
        if not carry:
            flush_pending()

    def mla_phase(l, pre_normed=False):
        jl = l // 2
        P.fence()
        flush_pending()
        cv.reset(COMMON)
        cqT_s = cv.take(4 * 1024, BF16).rearrange("p (c t) -> p c t", c=4)
        ckvT_s = cv.take(2 * 1024, BF16).rearrange("p (c t) -> p c t", c=2)
        kpeT_s = cv.take(1024, BF16)
        ropeT = cv.take(2 * 1024, F32).rearrange("p (a t) -> p a t", a=2)
        SHARED_S = cv.off
        cqT_p = cv.take(4 * 1024, BF16).rearrange("p (c t) -> p c t", c=4)
        ckvT_p = cv.take(2 * 1024, BF16).rearrange("p (c t) -> p c t", c=2)
        kpeT_p = cv.take(1024, BF16)
        SHARED_P = cv.off

        def lsl(t):
            return slice((t % 2) * TB, (t % 2 + 1) * TB)

        def cq(kc, t):
            return (cqT_p if t < 2 else cqT_s)[:, kc, lsl(t)]

        def ckv(mc, t):
            return (ckvT_p if t < 2 else ckvT_s)[:, mc, lsl(t)]

        def kpe(t):
            return (kpeT_p if t < 2 else kpeT_s)[0:64, lsl(t)]

        wdq = cv.take(NCH * 512, BF16).rearrange("p (c f) -> p c f", c=NCH)
        wdkv = cv.take(NCH * 384, BF16).rearrange("p (c f) -> p c f", c=NCH)
        cqraw = cv.take(4 * TB, F32).rearrange("p (c t) -> p c t", c=4)
        ckvf = [cv.take(2 * TB, F32).rearrange("p (c t) -> p c t", c=2) for _ in range(2)]
        kpef = [cv.take(TB, F32) for _ in range(2)]
        if not pre_normed:
            norm_phase(l, 1)
        P.op("pool", lambda e: e.dma_start(out=wdq, in_=wdq_d[jl].rearrange("(c p) f -> p c f", p=128)),
             writes=["wdq"], dma=True)
        P.op("pool", lambda e: e.dma_start(out=wdkv[:, :, 0:320], in_=wdkv_d[jl].rearrange("(c p) f -> p c f", p=128)),
             writes=["wdkv0"], dma=True)
        P.op("pool", lambda e: e.dma_start(out=wdkv[:, :, 320:384], in_=wdkvs_d[jl].rearrange("(c p) f -> p c f", p=128)),
             writes=["wdkv1"], dma=True)
        for a in range(2):
            P.op("sp", lambda e, a=a: e.dma_start(out=ropeT[0:64, a, :], in_=rope_d[a]), writes=[("rope", a)], dma=True)

        def stage_a(t):
            for mc in range(4):
                for c in range(NCH):
                    P.op("pe", lambda e, mc=mc, c=c: e.matmul(
                        ps[mc][:], lhsT=wdq[:, c, mc * 128:(mc + 1) * 128], rhs=hT[:, c, tsl(t)],
                        start=(c == 0), stop=(c == NCH - 1)),
                        reads=["wdq", ("hT", c, t)], writes=[PK[mc]])
                P.op("act", lambda e, mc=mc: e.activation(out=cqraw[:, mc, :], in_=ps[mc][:], func=AF.Identity),
                     reads=[PK[mc]], writes=[("cqraw", mc)])
            rstd_block(None, 4, lambda c: (cqraw[:, c, :], ("cqraw", c)), 1.0 / 512, 7)
            for mc in range(4):
                P.op("dve", lambda e, mc=mc: e.scalar_tensor_tensor(
                    out=cq(mc, t), in0=cqraw[:, mc, :], scalar=qnT[:, jl, mc:mc + 1], in1=rs[:, 1, :],
                    op0=ALU.mult, op1=ALU.mult),
                    reads=[("cqraw", mc), ("rs", 1), "qnT"], writes=[("cqT", mc, t)])
            fb = t % 2
            for mc in range(2):
                for c in range(NCH):
                    P.op("pe", lambda e, mc=mc, c=c: e.matmul(
                        ps[4 + mc][:], lhsT=wdkv[:, c, mc * 128:(mc + 1) * 128], rhs=hT[:, c, tsl(t)],
                        start=(c == 0), stop=(c == NCH - 1)),
                        reads=["wdkv0", ("hT", c, t)], writes=[PK[4 + mc]])
                P.op("act", lambda e, mc=mc: e.activation(out=cqraw[:, mc, :], in_=ps[4 + mc][:], func=AF.Identity),
                     reads=[PK[4 + mc]], writes=[("cqraw", mc)])
            rstd_block(None, 2, lambda c: (cqraw[:, c, :], ("cqraw", c)), 1.0 / 256, 7)
            for mc in range(2):
                if t < 2:
                    P.op("dve", lambda e, mc=mc: e.scalar_tensor_tensor(
                        out=ckvf[fb][:, mc, :], in0=cqraw[:, mc, :], scalar=kvnT[:, jl, mc:mc + 1], in1=rs[:, 1, :],
                        op0=ALU.mult, op1=ALU.mult),
                        reads=[("cqraw", mc), ("rs", 1), "kvnT"], writes=[("ckvf", fb, mc)])
                    P.op("act", lambda e, mc=mc: e.activation(out=ckv(mc, t), in_=ckvf[fb][:, mc, :], func=AF.Identity),
                         reads=[("ckvf", fb, mc)], writes=[("ckvT", mc, t)])
                else:
                    P.op("dve", lambda e, mc=mc: e.scalar_tensor_tensor(
                        out=ckv(mc, t), in0=cqraw[:, mc, :], scalar=kvnT[:, jl, mc:mc + 1], in1=rs[:, 1, :],
                        op0=ALU.mult, op1=ALU.mult),
                        reads=[("cqraw", mc), ("rs", 1), "kvnT"], writes=[("ckvT", mc, t)])
            if t < 2:
                outs.append(P.op("sp", lambda e: e.dma_start(
                    out=ockv_d[jl][:, tsl(t)].rearrange("(c p) t -> p c t", p=128), in_=ckvf[fb]),
                    reads=[("ckvf", fb, 0), ("ckvf", fb, 1)], writes=[("ockv", jl, t)], dma=True,
                    slot=("out", "ckvf", fb)))
            for c in range(NCH):
                P.op("pe", lambda e, c=c: e.matmul(
                    ps[6][0:64, :], lhsT=wdkv[:, c, 256:320], rhs=hT[:, c, tsl(t)],
                    start=(c == 0), stop=(c == NCH - 1)),
                    reads=["wdkv0", ("hT", c, t)], writes=[PK[6]])
            if t < 2:
                P.op("act", lambda e: e.activation(out=kpef[fb][0:64, :], in_=ps[6][0:64, :], func=AF.Identity),
                     reads=[PK[6]], writes=[("kpef", fb)])
                P.op("dve", lambda e: e.tensor_copy(out=kpe(t), in_=kpef[fb][0:64, :]),
                     reads=[("kpef", fb)], writes=[("kpeT", t)])
                outs.append(P.op("sp", lambda e: e.dma_start(out=okpe_d[jl][:, tsl(t)], in_=kpef[fb][0:64, :]),
                                 reads=[("kpef", fb)], writes=[("okpe", jl, t)], dma=True,
                                 slot=("out", "kpef", fb)))
            else:
                for c in range(NCH):
                    P.op("pe", lambda e, c=c: e.matmul(
                        ps[3][0:64, :], lhsT=wdkv[:, c, 320:384], rhs=hT[:, c, tsl(t)],
                        start=(c == 0), stop=(c == NCH - 1)),
                        reads=["wdkv1", ("hT", c, t)], writes=[PK[3]])
                tok = lsl(t)
                P.op("dve", lambda e: e.tensor_tensor(out=tmpf[0:64, 0, :], in0=ps[6][0:64, :],
                                                      in1=ropeT[0:64, 0, tok], op=ALU.mult),
                     reads=[PK[6], ("rope", 0)], writes=[("tmpf", 0)])
                P.op("dve", lambda e: e.tensor_tensor(out=tmpf[0:64, 1, :], in0=ps[3][0:64, :],
                                                      in1=ropeT[0:64, 1, tok], op=ALU.mult),
                     reads=[PK[3], ("rope", 1)], writes=[("tmpf", 1)])
                P.op("dve", lambda e: e.tensor_tensor(out=kpe(t), in0=tmpf[0:64, 0, :],
                                                      in1=tmpf[0:64, 1, :], op=ALU.add),
                     reads=[("tmpf", 0), ("tmpf", 1)], writes=[("kpeT", t)])

        for t in (2, 3, 0, 1):
            stage_a(t)
            if t == 3:
                P.op("sp", lambda e: e.dma_start(out=x_in[jl][0:256, :].rearrange("(c p) t -> p c t", p=128),
                                                 in_=ckvT_s),
                     reads=[("ckvT", 0, 2), ("ckvT", 0, 3), ("ckvT", 1, 2), ("ckvT", 1, 3)],
                     writes=[("x_in", jl, 0)], dma=True, slot=("grp", "x_in", jl))
                P.op("sp", lambda e: e.dma_start(out=x_in[jl][256:320, :], in_=kpeT_s[0:64, :]),
                     reads=[("kpeT", 2), ("kpeT", 3)], writes=[("x_in", jl, 1)], dma=True, slot=("grp", "x_in", jl))
                P.op("pool", lambda e: e.collective_compute("AllGather", ALU.bypass, replica_groups=GROUPS,
                                                            ins=[x_in[jl]], outs=[x_out[jl]]),
                     reads=[("x_in", jl, 0), ("x_in", jl, 1)], writes=[("x_out", jl)], dma=True, inc=1,
                     nofence=True)

        def attention(stream):
            P.fence()
            cv.reset(SHARED_P if stream == 0 else SHARED_S)
            nk = 1024 if stream == 0 else 4608
            nkt = nk // 128
            tok0 = 0 if stream == 0 else 1024
            if stream == 1:
                ckv_all = cv.take(2 * nk, BF16).rearrange("p (c t) -> p c t", c=2)
                kpe_all = cv.take(nk, BF16)
            kn = cv.take(nk, BF16)
            Vh = cv.take(nk, BF16).rearrange("p (k d) -> p k d", d=128)
            qn = cv.take(1024, BF16)
            qr = cv.take(1024, BF16)
            Pt = [cv.take(TB, BF16) for _ in range(4)]
            dacc = [cv.take(TB, F32) for _ in range(2)]
            hw_q = [cv.take(4 * 256, BF16).rearrange("p (c f) -> p c f", c=4) for _ in range(2)]
            hw_kv = [cv.take(2 * 256, BF16).rearrange("p (c f) -> p c f", c=2) for _ in range(2)]
            wob = [cv.take(8 * 128, BF16).rearrange("p (h d) -> p h d", h=8) for _ in range(2)]
            rden = cv.take(TB, F32)
            P.op("dve", lambda e: e.memset(qr[64:128, :], 0.0), writes=["qr_pad"])
            if stream == 1:
                P.op("dve", lambda e: e.memset(kpe_all[64:128, :], 0.0), writes=["kpe_pad"])
            else:
                P.op("dve", lambda e: e.memset(kpeT_p[64:128, :], 0.0), writes=["kpe_pad"])
            if stream == 1:
                P.op("pool", lambda e: e.dma_start(out=ckv_all[:, :, 0:512],
                                                   in_=cckv_d[jl].rearrange("(c p) t -> p c t", p=128)),
                     writes=[("ckv_all", 0)], dma=True, slot=("grp", "kvall_p", jl))
                P.op("pool", lambda e: e.dma_start(out=kpe_all[0:64, 0:512], in_=ckpe_d[jl]),
                     writes=[("kpe_all", 0)], dma=True, slot=("grp", "kvall_p", jl))
                for r in range(4):
                    P.op("sp", lambda e, r=r: e.dma_start(
                        out=ckv_all[:, :, 512 + r * 1024:512 + (r + 1) * 1024],
                        in_=x_out[jl][r * 320:r * 320 + 256, :].rearrange("(c p) t -> p c t", p=128)),
                        reads=[("x_out", jl)], writes=[("ckv_all", 1 + r)], dma=True, slot=("grp", "kvall", jl))
                    P.op("sp", lambda e, r=r: e.dma_start(
                        out=kpe_all[0:64, 512 + r * 1024:512 + (r + 1) * 1024],
                        in_=x_out[jl][r * 320 + 256:(r + 1) * 320, :]),
                        reads=[("x_out", jl)], writes=[("kpe_all", 1 + r)], dma=True, slot=("grp", "kvall", jl))
                ckvsrc, kpesrc = ckv_all, kpe_all
                ckv_keys = [("ckv_all", r) for r in range(5)]
                kpe_keys = [("kpe_all", r) for r in range(5)]
            else:
                ckvsrc, kpesrc = ckvT_p, kpeT_p
                ckv_keys = [("ckvT", mc, t) for mc in range(2) for t in range(2)]
                kpe_keys = [("kpeT", 0), ("kpeT", 1)]

            gcnt = {"tile": 0}

            def head_tiles(h, qblocks):
                flat = [(qi, ki) for qi, (q0, qn_, kts) in enumerate(qblocks) for ki in range(len(kts))]
                info = {}

                def s_mm(qi, ki):
                    q0, qn_, kts = qblocks[qi]
                    kt = kts[ki]
                    sbk = gcnt["tile"] % 3
                    gcnt["tile"] += 1
                    da = dacc[qi % 2]
                    P.op("pe", lambda e: e.matmul(
                        ps[sbk][:, 0:qn_], lhsT=kn[:, kt * 128:(kt + 1) * 128], rhs=qn[:, q0:q0 + qn_],
                        start=True, stop=False),
                        reads=[("kn", kt // 4), ("qn", q0 // TB)], writes=[PK[sbk]])
                    P.op("pe", lambda e: e.matmul(
                        ps[sbk][:, 0:qn_], lhsT=kpesrc[:, kt * 128:(kt + 1) * 128], rhs=qr[:, q0:q0 + qn_],
                        start=False, stop=True),
                        reads=kpe_keys + [("qr", q0 // TB), "qr_pad", "kpe_pad"], writes=[PK[sbk]])
                    pi = rot("Pt", 4)
                    info[(qi, ki)] = pi
                    P.op("act", lambda e: e.activation(out=Pt[pi][:, 0:qn_], in_=ps[sbk][:, 0:qn_], func=AF.Exp),
                         reads=[PK[sbk]], writes=[("Pt", pi)])
                    if ki == 0:
                        P.op("dve", lambda e: e.tensor_copy(out=da[:, 0:qn_], in_=Pt[pi][:, 0:qn_]),
                             reads=[("Pt", pi)], writes=[("dacc", qi % 2)])
                    else:
                        P.op("dve", lambda e: e.tensor_tensor(out=da[:, 0:qn_], in0=da[:, 0:qn_], in1=Pt[pi][:, 0:qn_],
                                                              op=ALU.add),
                             reads=[("Pt", pi), ("dacc", qi % 2)], writes=[("dacc", qi % 2)])

                def pv_mm(qi, ki):
                    q0, qn_, kts = qblocks[qi]
                    kt = kts[ki]
                    pi = info[(qi, ki)]
                    ob = 4 + (qi % 2)
                    nkt_ = len(kts)
                    P.op("pe", lambda e: e.matmul(
                        ps[ob][:, 0:qn_], lhsT=Vh[:, kt, :], rhs=Pt[pi][:, 0:qn_],
                        start=(ki == 0), stop=(ki == nkt_ - 1)),
                        reads=[("Vh", kt // 4), ("Pt", pi)], writes=[PK[ob]])
                    if ki == nkt_ - 1:
                        db = 6 + (qi % 2)
                        da = dacc[qi % 2]
                        tq = (tok0 + q0) // TB
                        P.op("pe", lambda e: e.matmul(ps[db][:, 0:qn_], lhsT=onesF[:], rhs=da[:, 0:qn_],
                                                      start=True, stop=True),
                             reads=["onesF", ("dacc", qi % 2)], writes=[PK[db]])
                        P.op("act", lambda e: e.activation(out=rden[:, 0:qn_], in_=ps[db][:, 0:qn_], func=AF.Ln),
                             reads=[PK[db]], writes=["rden"])
                        P.op("act", lambda e: e.activation(out=rden[:, 0:qn_], in_=rden[:, 0:qn_], func=AF.Exp, scale=-1.0),
                             reads=["rden"], writes=["rden"])
                        P.op("dve", lambda e: e.tensor_tensor(out=hT[:, h, tok0 + q0:tok0 + q0 + qn_], in0=ps[ob][:, 0:qn_],
                                                              in1=rden[:, 0:qn_], op=ALU.mult),
                             reads=[PK[ob], "rden"], writes=[("hT", h, tq)])

                depth_ = 2
                for idx in range(len(flat) + depth_):
                    if idx < len(flat):
                        s_mm(*flat[idx])
                    if idx >= depth_:
                        pv_mm(*flat[idx - depth_])

            def head(h):
                hb = h % 2
                P.op("pool", lambda e: e.dma_start(
                    out=hw_q[hb][:, :, 0:192], in_=wuq_d[jl][:, h * 192:(h + 1) * 192].rearrange("(c p) f -> p c f", p=128)),
                    writes=[("hwq0", hb)], dma=True)
                P.op("pool", lambda e: e.dma_start(
                    out=hw_q[hb][:, :, 192:256], in_=wuqs_d[jl][:, h * 64:(h + 1) * 64].rearrange("(c p) f -> p c f", p=128)),
                    writes=[("hwq1", hb)], dma=True)
                P.op("pool", lambda e: e.dma_start(
                    out=hw_kv[hb], in_=wukv_d[jl][:, h * 256:(h + 1) * 256].rearrange("(c p) f -> p c f", p=128)),
                    writes=[("hwkv", hb)], dma=True)
                for tb in range(2):
                    t = (tok0 // TB) + tb
                    for kc in range(4):
                        P.op("pe", lambda e, kc=kc, t=t: e.matmul(
                            ps[0][:], lhsT=hw_q[hb][:, kc, 0:128], rhs=cq(kc, t),
                            start=(kc == 0), stop=(kc == 3)),
                            reads=[("hwq0", hb), ("cqT", kc, t)], writes=[PK[0]])
                    P.op("act", lambda e, tb=tb: e.activation(out=qn[:, tb * TB:(tb + 1) * TB], in_=ps[0][:],
                                                              func=AF.Identity, scale=ATTN_SCALE),
                         reads=[PK[0]], writes=[("qn", tb)])
                    for kc in range(4):
                        P.op("pe", lambda e, kc=kc, t=t: e.matmul(
                            ps[1][0:64, :], lhsT=hw_q[hb][:, kc, 128:192], rhs=cq(kc, t),
                            start=(kc == 0), stop=(kc == 3)),
                            reads=[("hwq0", hb), ("cqT", kc, t)], writes=[PK[1]])
                    if stream == 0:
                        P.op("act", lambda e, tb=tb: e.activation(out=qr[0:64, tb * TB:(tb + 1) * TB], in_=ps[1][0:64, :],
                                                                  func=AF.Identity, scale=ATTN_SCALE),
                             reads=[PK[1]], writes=[("qr", tb)])
                    else:
                        for kc in range(4):
                            P.op("pe", lambda e, kc=kc, t=t: e.matmul(
                                ps[2][0:64, :], lhsT=hw_q[hb][:, kc, 192:256], rhs=cq(kc, t),
                                start=(kc == 0), stop=(kc == 3)),
                                reads=[("hwq1", hb), ("cqT", kc, t)], writes=[PK[2]])
                        tok = slice(tb * TB, (tb + 1) * TB)
                        P.op("dve", lambda e, tok=tok: e.tensor_tensor(out=tmpf[0:64, 0, :], in0=ps[1][0:64, :],
                                                                       in1=ropeT[0:64, 0, tok], op=ALU.mult),
                             reads=[PK[1], ("rope", 0)], writes=[("tmpf", 0)])
                        P.op("dve", lambda e, tok=tok: e.tensor_tensor(out=tmpf[0:64, 1, :], in0=ps[2][0:64, :],
                                                                       in1=ropeT[0:64, 1, tok], op=ALU.mult),
                             reads=[PK[2], ("rope", 1)], writes=[("tmpf", 1)])
                        P.op("dve", lambda e: e.tensor_tensor(out=tmpf[0:64, 0, :], in0=tmpf[0:64, 0, :],
                                                              in1=tmpf[0:64, 1, :], op=ALU.add),
                             reads=[("tmpf", 0), ("tmpf", 1)], writes=[("tmpf", 0)])
                        P.op("act", lambda e, tb=tb: e.activation(out=qr[0:64, tb * TB:(tb + 1) * TB], in_=tmpf[0:64, 0, :],
                                                                  func=AF.Identity, scale=ATTN_SCALE),
                             reads=[("tmpf", 0)], writes=[("qr", tb)])
                for kb in range(nk // TB):
                    bank = 2 + kb % 2
                    for kc in range(2):
                        P.op("pe", lambda e, kc=kc, kb=kb, bank=bank: e.matmul(
                            ps[bank][:], lhsT=hw_kv[hb][:, kc, 0:128], rhs=ckvsrc[:, kc, kb * TB:(kb + 1) * TB],
                            start=(kc == 0), stop=(kc == 1)),
                            reads=[("hwkv", hb)] + ckv_keys, writes=[PK[bank]])
                    if kb % 2 == 0:
                        P.op("act", lambda e, kb=kb, bank=bank: e.activation(out=kn[:, kb * TB:(kb + 1) * TB], in_=ps[bank][:], func=AF.Identity),
                             reads=[PK[bank]], writes=[("kn", kb)])
                    else:
                        P.op("dve", lambda e, kb=kb, bank=bank: e.tensor_copy(out=kn[:, kb * TB:(kb + 1) * TB], in_=ps[bank][:]),
                             reads=[PK[bank]], writes=[("kn", kb)])
                for vb in range(nkt // 4):
                    bank = 2 + vb % 2
                    for q4 in range(4):
                        kt = vb * 4 + q4
                        for kc in range(2):
                            P.op("pe", lambda e, kc=kc, kt=kt, q4=q4, bank=bank: e.matmul(
                                ps[bank][:, q4 * 128:(q4 + 1) * 128], lhsT=ckvsrc[:, kc, kt * 128:(kt + 1) * 128],
                                rhs=hw_kv[hb][:, kc, 128:256], start=(kc == 0), stop=(kc == 1)),
                                reads=[("hwkv", hb)] + ckv_keys, writes=[PK[bank]])
                    vdst = Vh[:, vb * 4:(vb + 1) * 4, :]
                    vsrc = ps[bank][:].rearrange("p (k d) -> p k d", d=128)
                    if vb % 2 == 0:
                        P.op("dve", lambda e, vdst=vdst, vsrc=vsrc: e.tensor_copy(out=vdst, in_=vsrc),
                             reads=[PK[bank]], writes=[("Vh", vb)])
                    else:
                        P.op("act", lambda e, vdst=vdst, vsrc=vsrc: e.activation(out=vdst, in_=vsrc, func=AF.Identity),
                             reads=[PK[bank]], writes=[("Vh", vb)])
                if stream == 0:
                    qblocks = [(s * 256, 256, [2 * s, 2 * s + 1]) for s in range(4)]
                else:
                    qblocks = [(qb * TB, TB, list(range(nkt))) for qb in range(2)]
                head_tiles(h, qblocks)

            for h in range(8):
                head(h)
            if stream == 0:
                wo_stage(0, [(w_, None) for w_ in wob])

        def wo_stage(stream, wob, mid_loads=None):
            n = 0 if stream == 0 else 1
            tok0 = 0 if stream == 0 else 1024
            nb_ = len(wob)

            def ld(m):
                wb_i = m % nb_
                P.op("pool", lambda e: e.dma_start(
                    out=wob[wb_i][0], in_=wo_d[jl][:, m * 128:(m + 1) * 128].rearrange("(h p) d -> p h d", p=128)),
                    writes=[("wob", wb_i)], dma=True)

            for m in range(min(nb_, NCH)):
                ld(m)
            if mid_loads is not None:
                mid_loads()
            for m in range(NCH):
                wb_i = m % nb_
                for tb in range(2):
                    t = tok0 // TB + tb
                    bank = (m * 2 + tb) % 4
                    for hh in range(8):
                        P.op("pe", lambda e, hh=hh, t=t, bank=bank, wb_i=wb_i: e.matmul(
                            ps[bank][:], lhsT=wob[wb_i][0][:, hh, :], rhs=hT[:, hh, tsl(t)],
                            start=(hh == 0), stop=(hh == 7)),
                            reads=[("wob", wb_i), ("hT", hh, t)] + ([wob[wb_i][1]] if wob[wb_i][1] else []),
                            writes=[PK[bank]])
                    resid_update(bank, m, t, mcol(l, 5, m, n))
                if m + nb_ < NCH:
                    ld(m + nb_)

        attention(0)
        attention(1)

        def pre_work(mid_loads, tiles):
            wo_stage(1, tiles, mid_loads)
        return pre_work

    def fourier_phase(l, pre_normed=False):
        jl = l // 2
        P.fence()
        flush_pending()
        cv.reset(COMMON)
        wg0 = cv.take(NCH * 512, BF16).rearrange("p (c f) -> p c f", c=NCH)
        wu0 = cv.take(NCH * 512, BF16).rearrange("p (c f) -> p c f", c=NCH)
        wd0 = cv.take(4 * D, BF16).rearrange("p (j d) -> p j d", j=4)
        P.op("pool", lambda e: e.dma_start(out=wg0[:, :, 0:512], in_=wg_d[l, 1, :, 0:512].rearrange("(c p) f -> p c f", p=128)),
             writes=[("wg", 0)], dma=True)
        P.op("pool", lambda e: e.dma_start(out=wu0[:, :, 0:512], in_=wu_d[l, 1, :, 0:512].rearrange("(c p) f -> p c f", p=128)),
             writes=[("wu", 0)], dma=True)
        P.op("pool", lambda e: e.dma_start(out=wd0[:, 0:4, :], in_=wd_d[l, 1, 0:512, :].rearrange("(j p) d -> p j d", p=128)),
             writes=[("wd", 0)], dma=True)
        cs = cv.take(2 * 512, BF16).rearrange("p (c f) -> p c f", c=2)
        dp = cv.take(2 * 2 * 256, BF16).rearrange("p (a n k) -> p a n k", a=2, n=2)
        ABp = cv.take(8 * 2048, BF16).rearrange("p (t f) -> p t f", t=8)
        ABs = cv.take(8 * 2048, BF16).rearrange("p (t f) -> p t f", t=8)
        if not pre_normed:
            norm_phase(l, 1)
        P.op("pool", lambda e: e.dma_start(out=cs, in_=dftC_d.rearrange("(c p) f -> p c f", p=128)), writes=["cs"], dma=True)
        for a in range(2):
            P.op("pool", lambda e, a=a: e.dma_start(out=dp[:, a], in_=dftP_d[a].rearrange("(n p) k -> p n k", p=128)),
                 writes=[("dp", a)], dma=True)
        for tile in list(range(8, 16)) + list(range(8)):
            dst = ABs if tile >= 8 else ABp
            ti = tile % 8
            t = tile // 4
            for g in range(4):
                bank = g % 4
                for kc in range(2):
                    P.op("pe", lambda e, g=g, kc=kc, tile=tile, bank=bank: e.matmul(
                        ps[bank][:], lhsT=hT[:, 2 * g + kc, tile * 128:(tile + 1) * 128], rhs=cs[:, kc, :],
                        start=(kc == 0), stop=(kc == 1)),
                        reads=["cs", ("hT", 2 * g + kc, t)], writes=[PK[bank]])
                dv = dst[:, ti, :].rearrange("p (s g c) -> p s g c", s=2, g=4)[:, :, g, :]
                sv = ps[bank][:].rearrange("p (s c) -> p s c", s=2)
                key = ("AB", tile)
                if g % 2 == 0:
                    P.op("act", lambda e, dv=dv, sv=sv: e.activation(out=dv, in_=sv, func=AF.Identity),
                         reads=[PK[bank]], writes=[(key, g)])
                else:
                    P.op("dve", lambda e, dv=dv, sv=sv: e.tensor_copy(out=dv, in_=sv),
                         reads=[PK[bank]], writes=[(key, g)])
            if tile >= 8 and tile % 2 == 1:
                part = (tile - 8) // 2
                P.op("sp", lambda e, part=part: e.dma_start(
                    out=f_in[jl][part].rearrange("(t p) f -> p t f", p=128), in_=ABs[:, 2 * part:2 * part + 2, :]),
                    reads=[(("AB", tl), g) for tl in (tile - 1, tile) for g in range(4)],
                    writes=[("f_in", jl, part)], dma=True)
                P.op("pool", lambda e, part=part: e.collective_compute(
                    "AllGather", ALU.bypass, replica_groups=GROUPS, ins=[f_in[jl][part]], outs=[f_out[jl][part]]),
                    reads=[("f_in", jl, part)], writes=[("f_out", jl, part)], dma=True, inc=1, nofence=True)
        for s in range(4):
            t = s // 2
            for m in range(NCH):
                bank = m % 4
                first = True
                for nt in range(2):
                    for a in range(2):
                        P.op("pe", lambda e, nt=nt, a=a, m=m, s=s, bank=bank, first=first: e.matmul(
                            ps[bank][:, 0:256], lhsT=ABp[:, s * 2 + nt, a * 1024 + m * 128:a * 1024 + (m + 1) * 128],
                            rhs=dp[:, a, nt, :], start=first, stop=(nt == 1 and a == 1)),
                            reads=[(("AB", s * 2 + nt), gg) for gg in range(4)] + [("dp", a)], writes=[PK[bank]])
                        first = False
                P.op("act", lambda e, m=m, s=s, bank=bank: e.activation(
                    out=hT[:, m, s * 256:(s + 1) * 256], in_=ps[bank][:, 0:256], func=AF.Identity, scale=1.0 / 256.0),
                    reads=[PK[bank]], writes=[("hT", m, t)])
        def fc_stage(tblocks, fwbuf, mid_loads=None):
            nb_ = len(fwbuf)

            def ld(m):
                wi = m % nb_
                P.op("pool", lambda e: e.dma_start(
                    out=fwbuf[wi][0], in_=fw_d[jl][:, m * 128:(m + 1) * 128].rearrange("(c p) d -> p c d", p=128)),
                    writes=[("fwm", wi)], dma=True)

            for m in range(min(nb_, NCH)):
                ld(m)
            if mid_loads is not None:
                mid_loads()
            for m in range(NCH):
                wi = m % nb_
                if m >= nb_ and False:
                    pass
                for t in tblocks:
                    n = cond_of(t)
                    bank = (m * 2 + t) % 4
                    for c in range(NCH):
                        P.op("pe", lambda e, c=c, t=t, bank=bank, wi=wi: e.matmul(
                            ps[bank][:], lhsT=fwbuf[wi][0][:, c, :], rhs=hT[:, c, tsl(t)],
                            start=(c == 0), stop=(c == NCH - 1)),
                            reads=[("fwm", wi), ("hT", c, t)] + ([fwbuf[wi][1]] if fwbuf[wi][1] else []),
                            writes=[PK[bank]])
                    s_ = rot("tmpf")
                    P.op("dve", lambda e, s_=s_, m=m, bank=bank, n=n: e.tensor_scalar(
                        out=tmpf[:, s_, :], in0=ps[bank][:], scalar1=fbT[:, jl, m:m + 1], scalar2=mcol(l, 5, m, n),
                        op0=ALU.add, op1=ALU.mult),
                        reads=[PK[bank], "fbT", "modS"], writes=[("tmpf", s_)])
                    P.op("dve", lambda e, s_=s_, m=m, t=t: e.tensor_tensor(out=xT[:, m, tsl(t)], in0=xT[:, m, tsl(t)],
                                                                           in1=tmpf[:, s_, :], op=ALU.add),
                         reads=[("tmpf", s_), ("xT", m, t)], writes=[("xT", m, t)])
                if m + nb_ < NCH:
                    ld(m + nb_)

        def pre_work1(mid_loads, tiles):
            fc_stage((0, 1), tiles, mid_loads)
        return pre_work1

    def fourier_part2(l):
        jl = l // 2
        P.fence()
        flush_pending()
        cv.reset(COMMON)
        NSB = 3
        abn = [cv.take(2048, BF16) for _ in range(NSB)]
        tbn = [cv.take(2 * 512, BF16).rearrange("p (a k) -> p a k", a=2) for _ in range(NSB)]

        def fc_stage(tblocks, fwbuf, mid_loads=None):
            nb_ = len(fwbuf)

            def ld(m):
                wi = m % nb_
                P.op("pool", lambda e: e.dma_start(
                    out=fwbuf[wi][0], in_=fw_d[jl][:, m * 128:(m + 1) * 128].rearrange("(c p) d -> p c d", p=128)),
                    writes=[("fwm", wi)], dma=True)

            for m in range(min(nb_, NCH)):
                ld(m)
            if mid_loads is not None:
                mid_loads()
            for m in range(NCH):
                wi = m % nb_
                for t in tblocks:
                    n = cond_of(t)
                    bank = (m * 2 + t) % 4
                    for c in range(NCH):
                        P.op("pe", lambda e, c=c, t=t, bank=bank, wi=wi: e.matmul(
                            ps[bank][:], lhsT=fwbuf[wi][0][:, c, :], rhs=hT[:, c, tsl(t)],
                            start=(c == 0), stop=(c == NCH - 1)),
                            reads=[("fwm", wi), ("hT", c, t)] + ([fwbuf[wi][1]] if fwbuf[wi][1] else []),
                            writes=[PK[bank]])
                    s_ = rot("tmpf")
                    P.op("dve", lambda e, s_=s_, m=m, bank=bank, n=n: e.tensor_scalar(
                        out=tmpf[:, s_, :], in0=ps[bank][:], scalar1=fbT[:, jl, m:m + 1], scalar2=mcol(l, 5, m, n),
                        op0=ALU.add, op1=ALU.mult),
                        reads=[PK[bank], "fbT", "modS"], writes=[("tmpf", s_)])
                    P.op("dve", lambda e, s_=s_, m=m, t=t: e.tensor_tensor(out=xT[:, m, tsl(t)], in0=xT[:, m, tsl(t)],
                                                                           in1=tmpf[:, s_, :], op=ALU.add),
                         reads=[("tmpf", s_), ("xT", m, t)], writes=[("xT", m, t)])
                if m + nb_ < NCH:
                    ld(m + nb_)

        for kb in range(2):
            t = 2 + kb
            nt_order = [r_ * 8 + part * 2 + j_ for part in range(4) for r_ in range(4) for j_ in range(2)]
            for ni, nt in enumerate(nt_order):
                b = (kb * 32 + ni) % NSB
                r_, w_ = nt // 8, nt % 8
                part, j_ = w_ // 2, w_ % 2
                src = f_out[jl][part][r_ * 256 + j_ * 128:r_ * 256 + (j_ + 1) * 128, :]
                P.op("sp", lambda e, src=src, b=b: e.dma_start(out=abn[b], in_=src),
                     reads=[("f_out", jl, part)], writes=[("abn", b)], dma=True)
                P.op("sp", lambda e, nt=nt, b=b, kb=kb: e.dma_start(
                    out=tbn[b], in_=dftS_bf[:, nt * 128:(nt + 1) * 128, kb * 512:(kb + 1) * 512].rearrange("a p k -> p a k")),
                    reads=DFT_KEYS, writes=[("tbn", b)], dma=True)
                for m in range(NCH):
                    for a in range(2):
                        P.op("pe", lambda e, ni=ni, a=a, m=m, b=b: e.matmul(
                            ps[m][:], lhsT=abn[b][:, a * 1024 + m * 128:a * 1024 + (m + 1) * 128], rhs=tbn[b][:, a, :],
                            start=(ni == 0 and a == 0), stop=(ni == 31 and a == 1)),
                            reads=[("abn", b), ("tbn", b)], writes=[PK[m]])
            for m in range(NCH):
                if m % 2 == 0:
                    P.op("act", lambda e, m=m, t=t: e.activation(out=hT[:, m, tsl(t)], in_=ps[m][:], func=AF.Identity,
                                                                 scale=1.0 / 1024.0),
                         reads=[PK[m]], writes=[("hT", m, t)])
                else:
                    P.op("dve", lambda e, m=m, t=t: e.tensor_scalar(out=hT[:, m, tsl(t)], in0=ps[m][:], scalar1=1.0 / 1024.0,
                                                                    scalar2=None, op0=ALU.mult),
                         reads=[PK[m]], writes=[("hT", m, t)])
        def pre_work(mid_loads, tiles):
            fc_stage((2, 3), tiles, mid_loads)
        return pre_work

    for l in range(depth):
        mixer_on = ("m" if l % 2 == 0 else "f") in DBG_PHASES
        ffn_phase(l, 0, pre_normed=(l > 0), next_norm=(l, 1, False) if mixer_on else None, skip_fence=(l > 0),
                  carry=mixer_on, tail_order=(2, 3, 0, 1) if mixer_on else (0, 1, 2, 3))
        last = (l == depth - 1)
        nn = (0, 0, True) if last else (l + 1, 0, False)
        if mixer_on and l % 2 == 1:
            pw1 = fourier_phase(l, pre_normed=True)
            ffn_phase(l, 1, pre_normed=False, next_norm=nn, pre_work=pw1, carry=True, tblocks=(0, 1),
                      chunk0_preloaded=True)
            pw = fourier_part2(l)
            ffn_phase(l, 1, pre_normed=False, next_norm=nn, pre_work=pw, carry=not last, tblocks=(2, 3))
        else:
            pw = mla_phase(l, pre_normed=True) if mixer_on else None
            ffn_phase(l, 1, pre_normed=False, next_norm=nn, pre_work=pw, carry=not last)
    if depth == 0:
        P.fence()
        norm_phase(0, 0, final=True)
    P.emit(nc, final_waits=outs)
    st.close()
    return nc, P


def _fm(a):
    t = a.shape[0]
    return np.ascontiguousarray(a.T.reshape(NCH, 128, t).transpose(1, 0, 2))


def _vec(a, nch):
    sh = a.shape[:-1]
    b = a.reshape(sh + (nch, 128))
    return np.ascontiguousarray(np.moveaxis(b, -1, 0))


def _const_tables():
    f32 = np.float32
    k = np.arange(256)
    ang = 2 * np.pi * np.outer(k, k) / 256.0
    dftC = np.concatenate([np.cos(ang), np.sin(ang)], axis=1).astype(np.float32)
    dftP = np.stack([np.cos(ang), -np.sin(ang)]).astype(np.float32)
    n = np.arange(4096, dtype=np.int64)
    dftS = []
    for qd in range(4):
        kk = np.arange(qd * 1024, (qd + 1) * 1024, dtype=np.int64)
        a = 2 * np.pi * ((np.outer(n, kk) % 4096).astype(np.float64)) / 4096.0
        dftS.append(np.stack([np.cos(a), -np.sin(a)]).astype(np.float32))
    inv = 1.0 / (10000.0 ** (np.arange(16, dtype=np.float32) / 16.0))
    pos = np.arange(4096)
    row = (pos // 64).astype(np.float32)
    col = (pos % 64).astype(np.float32)
    ang = np.stack([row[:, None] * inv, col[:, None] * inv], axis=1).astype(np.float32)
    cos = np.cos(ang)
    sin = np.sin(ang)
    cosT = np.zeros((64, 4096), f32)
    sinT = np.zeros((64, 4096), f32)
    for a in range(2):
        for hf in range(2):
            for f in range(16):
                p = a * 32 + hf * 16 + f
                cosT[p] = cos[:, a, f]
                sinT[p] = -sin[:, a, f] if hf == 0 else sin[:, a, f]
    rope = [np.ascontiguousarray(np.stack([cosT[:, q * 1024:(q + 1) * 1024], sinT[:, q * 1024:(q + 1) * 1024]]))
            for q in range(4)]
    return dftC, dftP, dftS, rope


def _swap_cols(w, nheads, base, stride):
    cols = []
    for h in range(nheads):
        o = h * stride + base
        for a in range(2):
            cols += list(range(o + a * 32 + 16, o + a * 32 + 32)) + list(range(o + a * 32, o + a * 32 + 16))
    return np.ascontiguousarray(w[..., cols])


_CACHE = {}
DEPTH_RUN = DEPTH


def kernel(x_prompt, x_sample, cache_ckv, cache_kpe, c, c_ctx, w_mod, b_mod, norm_g,
           ffn_wg, ffn_wu, ffn_wd, mla_w_dq, mla_q_norm, mla_w_uq, mla_w_dkv, mla_kv_norm,
           mla_w_ukv, mla_w_o, fourier_w, fourier_b, final_norm):
    A = lambda a: np.ascontiguousarray(np.asarray(a, dtype=np.float32))
    x_prompt, x_sample, cache_ckv, cache_kpe = A(x_prompt), A(x_sample), A(cache_ckv), A(cache_kpe)
    c, c_ctx, w_mod, b_mod, norm_g = A(c), A(c_ctx), A(w_mod), A(b_mod), A(norm_g)
    ffn_wg, ffn_wu, ffn_wd = A(ffn_wg), A(ffn_wu), A(ffn_wd)
    mla_w_dq, mla_q_norm, mla_w_uq, mla_w_dkv = A(mla_w_dq), A(mla_q_norm), A(mla_w_uq), A(mla_w_dkv)
    mla_kv_norm, mla_w_ukv, mla_w_o = A(mla_kv_norm), A(mla_w_ukv), A(mla_w_o)
    fourier_w, fourier_b, final_norm = A(fourier_w), A(fourier_b), A(final_norm)

    if "nc" not in _CACHE:
        _CACHE["nc"] = build_program(DEPTH_RUN)[0]
        _CACHE["tables"] = _const_tables()
    nc = _CACHE["nc"]
    dftC, dftP, dftS, rope = _CACHE["tables"]

    shared = {
        "bmodT": _vec(b_mod, 72), "normgT": _vec(norm_g, 8), "finalT": _vec(final_norm, 8),
        "qnT": _vec(mla_q_norm, 4), "kvnT": _vec(mla_kv_norm, 2), "fbT": _vec(fourier_b, 8),
        "wg": ffn_wg, "wu": ffn_wu, "wd": ffn_wd, "wdq": mla_w_dq, "wuq": mla_w_uq,
        "wuqs": _swap_cols(mla_w_uq, 8, 128, 192), "wdkv": mla_w_dkv,
        "wdkvs": _swap_cols(mla_w_dkv, 1, 256, 0), "wukv": mla_w_ukv, "wo": mla_w_o, "fw": fourier_w,
        "dftC": dftC, "dftP": dftP,
    }
    in_maps = []
    for r in range(8):
        b, qd = r // 4, r % 4
        xp = x_prompt[4 * r:4 * r + 4].reshape(1024, D)
        xs = x_sample[b, qd * 1024:(qd + 1) * 1024]
        m = dict(shared)
        m["xT"] = _fm(np.concatenate([xp, xs], axis=0))
        m["cckv"] = np.ascontiguousarray(cache_ckv[b].transpose(0, 2, 1))
        m["ckpe"] = np.ascontiguousarray(cache_kpe[b].transpose(0, 2, 1))
        m["condT"] = _vec(np.stack([c_ctx, c[b]]), 8).transpose(0, 2, 1).copy()
        m["wmod"] = np.ascontiguousarray(w_mod[:, :, qd * 2304:(qd + 1) * 2304])
        m["ropeT"] = rope[qd]
        m["dftS"] = dftS[qd]
        in_maps.append(m)

    res = run_bass_kernel_spmd(nc, in_maps, core_ids=list(range(8)))
    y_prompt = np.empty((32, 256, D), np.float32)
    y_sample = np.empty((2, 4096, D), np.float32)
    new_ckv = np.empty((32, 2, 256, 256), np.float32)
    new_kpe = np.empty((32, 2, 256, 64), np.float32)
    for r in range(8):
        b, qd = r // 4, r % 4
        o = res.results[r]
        y = np.asarray(o["yT"]).transpose(2, 1, 0).reshape(NT, D)
        y_prompt[4 * r:4 * r + 4] = y[:1024].reshape(4, 256, D)
        y_sample[b, qd * 1024:(qd + 1) * 1024] = y[1024:]
        ck = np.asarray(o["o_ckv"])
        kp = np.asarray(o["o_kpe"])
        new_ckv[4 * r:4 * r + 4] = ck.reshape(2, 256, 4, 256).transpose(2, 0, 3, 1)
        new_kpe[4 * r:4 * r + 4] = kp.reshape(2, 64, 4, 256).transpose(2, 0, 3, 1)
    return (y_prompt, y_sample, new_ckv, new_kpe)
```
